# Optimizing a Trainium2 kernel written in Bass

```python
import math
import jax, jax.numpy as jnp
from jax import lax
import numpy as np

D_MODEL = 2048
BATCH = 8
SEQ = 2048
DEPTH = 1
DEC_BATCH = 128
DEC_SEQ = 8
PAST_LEN = 2048
PAGE_SIZE = 128

N_HEADS = 8
N_KV_HEADS = 2
HEAD_DIM = 128
GROUP = N_HEADS // N_KV_HEADS
D_ATTN = N_HEADS * HEAD_DIM
D_KV = N_KV_HEADS * HEAD_DIM
IDX_HEADS = 8
IDX_DIM = 64
TOPK_MAX = 256
Q_BLOCK = 128
N_BUCKETS = 32
MAX_DISTANCE = 128
D_CONV = D_MODEL // 2
CONV_WIDTH = 31
EPS = 1e-6
NEG = -1e30

SPLITS = (('q', D_ATTN), ('k', D_KV), ('v', D_KV), ('z_attn', D_ATTN),
          ('q_idx', IDX_HEADS * IDX_DIM), ('k_idx', IDX_DIM), ('w_idx', IDX_HEADS),
          ('glu_val', D_CONV), ('glu_gate', D_CONV), ('z_conv', D_CONV),
          ('gate_attn', D_MODEL), ('gate_conv', D_MODEL))
D_IN = sum(w for _, w in SPLITS)

kernel_name = 'dsa_conformer_gated_hybrid_step'


def rms_norm(x, g):
    xf = x.astype(jnp.float32)
    y = xf * lax.rsqrt(jnp.mean(xf * xf, axis=-1, keepdims=True) + EPS)
    return y.astype(x.dtype) * g


def layer_norm(x, g, b):
    xf = x.astype(jnp.float32)
    mu = jnp.mean(xf, axis=-1, keepdims=True)
    var = jnp.mean(jnp.square(xf - mu), axis=-1, keepdims=True)
    return ((xf - mu) * lax.rsqrt(var + EPS)).astype(x.dtype) * g + b


def t5_bucket(dist):
    n = jnp.maximum(dist, 0)
    max_exact = N_BUCKETS // 2
    ratio = jnp.log(jnp.maximum(n, 1).astype(jnp.float32) / max_exact) / math.log(MAX_DISTANCE / max_exact)
    large = jnp.minimum(max_exact + (ratio * (N_BUCKETS - max_exact)).astype(jnp.int32), N_BUCKETS - 1)
    return jnp.where(n < max_exact, n, large)


def split_columns(h):
    out = {}
    off = 0
    for name, w in SPLITS:
        out[name] = h[..., off:off + w]
        off += w
    return out


def dsa_block(q, q_idx, w_idx, q_pos, k, v, k_idx, rel_bias, top_k):
    f32 = jnp.float32
    B, Tq = q.shape[0], q.shape[1]
    L = k.shape[1]
    dots = jnp.einsum('bthd,bsd->bths', q_idx.astype(f32), k_idx.astype(f32)) * (IDX_DIM ** -0.5)
    score = jnp.einsum('bths,bth->bts', jax.nn.relu(dots), w_idx.astype(f32)) * (IDX_HEADS ** -0.5)
    causal = jnp.arange(L, dtype=jnp.int32)[None, :] <= q_pos[:, None]
    score = jnp.where(causal[None], score, -jnp.inf)
    _, sel = lax.top_k(score, top_k)
    valid = sel <= q_pos[None, :, None]
    gather = jax.vmap(lambda arr, idx: arr[idx])
    k_sel = gather(k, sel)
    v_sel = gather(v, sel)
    qg = q.reshape(B, Tq, N_KV_HEADS, GROUP, HEAD_DIM)
    logits = jnp.einsum('btkgd,btjkd->btkgj', qg.astype(f32), k_sel.astype(f32)) * (HEAD_DIM ** -0.5)
    bias = rel_bias.astype(f32)[t5_bucket(q_pos[None, :, None] - sel)]
    bias = bias.reshape(B, Tq, top_k, N_KV_HEADS, GROUP).transpose(0, 1, 3, 4, 2)
    logits = jnp.where(valid[:, :, None, None, :], logits + bias, NEG)
    probs = jax.nn.softmax(logits, axis=-1)
    out = jnp.einsum('btkgj,btjkd->btkgd', probs.astype(v.dtype), v_sel)
    return out.reshape(B, Tq, D_ATTN).astype(q.dtype)


def decoder_layer(x, k_past, v_past, kidx_past, conv_past, ln_g, w_in, conv_w, conv_b,
                  conv_norm_g, conv_norm_b, w_pw, b_pw, w_up_attn, w_up_conv, w_out, rel_bias):
    B, T, _ = x.shape
    P = k_past.shape[1]
    L = P + T
    top_k = min(TOPK_MAX, L // 4)
    c = split_columns(rms_norm(x, ln_g) @ w_in)

    q = c['q'].reshape(B, T, N_HEADS, HEAD_DIM)
    k_new = c['k'].reshape(B, T, N_KV_HEADS, HEAD_DIM)
    v_new = c['v'].reshape(B, T, N_KV_HEADS, HEAD_DIM)
    kidx_new = c['k_idx']
    q_idx = c['q_idx'].reshape(B, T, IDX_HEADS, IDX_DIM)
    w_idx = c['w_idx']
    k_all = jnp.concatenate([k_past, k_new], axis=1)
    v_all = jnp.concatenate([v_past, v_new], axis=1)
    kidx_all = jnp.concatenate([kidx_past, kidx_new], axis=1)
    qb = min(Q_BLOCK, T)
    if T % qb:
        qb = T
    nb = T // qb

    def to_blocks(a):
        return a.reshape(B, nb, qb, *a.shape[2:]).swapaxes(0, 1)

    q_pos = (P + jnp.arange(T, dtype=jnp.int32)).reshape(nb, qb)

    def attend(blk):
        qb_, qi_, wi_, pos_ = blk
        return dsa_block(qb_, qi_, wi_, pos_, k_all, v_all, kidx_all, rel_bias, top_k)

    attn = lax.map(attend, (to_blocks(q), to_blocks(q_idx), to_blocks(w_idx), q_pos))
    attn = attn.swapaxes(0, 1).reshape(B, T, D_ATTN)
    branch_attn = (attn * jax.nn.silu(c['z_attn'])) @ w_up_attn

    u = c['glu_val'] * jax.nn.sigmoid(c['glu_gate'])
    u_pad = jnp.concatenate([conv_past, u], axis=1)
    dw = lax.conv_general_dilated(u_pad, conv_w[:, None, :], window_strides=(1,), padding='VALID',
                                  dimension_numbers=('NWC', 'WIO', 'NWC'),
                                  feature_group_count=D_CONV) + conv_b
    conv_out = jax.nn.silu(layer_norm(dw, conv_norm_g, conv_norm_b)) @ w_pw + b_pw
    branch_conv = (conv_out * jax.nn.silu(c['z_conv'])) @ w_up_conv

    merged = jax.nn.sigmoid(c['gate_attn']) * branch_attn + jax.nn.sigmoid(c['gate_conv']) * branch_conv
    y = x + merged @ w_out
    return y, k_new, v_new, kidx_new, u_pad[:, -(CONV_WIDTH - 1):]


def setup_inputs(seed: int = 0) -> dict:
    key = jax.random.key(seed)
    ks = jax.random.split(key, 24)
    n_pages = PAST_LEN // PAGE_SIZE
    n_used = DEC_BATCH * n_pages
    n_phys = n_used + max(1, n_used // 4)
    nrm = jax.random.normal
    f32 = jnp.float32
    page_table = jax.random.permutation(ks[0], n_phys)[:n_used].astype(jnp.int32).reshape(DEC_BATCH, n_pages)
    return {
        'x_prompt': nrm(ks[1], (BATCH, SEQ, D_MODEL), f32),
        'x_sample': nrm(ks[2], (DEC_BATCH, DEC_SEQ, D_MODEL), f32),
        'cache_k': nrm(ks[3], (DEPTH, n_phys, PAGE_SIZE, N_KV_HEADS, HEAD_DIM), f32),
        'cache_v': nrm(ks[4], (DEPTH, n_phys, PAGE_SIZE, N_KV_HEADS, HEAD_DIM), f32),
        'cache_kidx': nrm(ks[5], (DEPTH, n_phys, PAGE_SIZE, IDX_DIM), f32),
        'state_conv': 0.5 * nrm(ks[6], (DEPTH, DEC_BATCH, CONV_WIDTH - 1, D_CONV), f32),
        'page_table': page_table,
        'ln_g': 1.0 + 0.02 * nrm(ks[7], (DEPTH, D_MODEL), f32),
        'w_in': nrm(ks[8], (DEPTH, D_MODEL, D_IN), f32) * D_MODEL ** -0.5,
        'conv_w': nrm(ks[9], (DEPTH, CONV_WIDTH, D_CONV), f32) * CONV_WIDTH ** -0.5,
        'conv_b': 0.02 * nrm(ks[10], (DEPTH, D_CONV), f32),
        'conv_norm_g': 1.0 + 0.02 * nrm(ks[11], (DEPTH, D_CONV), f32),
        'conv_norm_b': 0.02 * nrm(ks[12], (DEPTH, D_CONV), f32),
        'w_pw': nrm(ks[13], (DEPTH, D_CONV, D_CONV), f32) * D_CONV ** -0.5,
        'b_pw': 0.02 * nrm(ks[14], (DEPTH, D_CONV), f32),
        'w_up_attn': nrm(ks[15], (DEPTH, D_ATTN, D_MODEL), f32) * D_ATTN ** -0.5,
        'w_up_conv': nrm(ks[16], (DEPTH, D_CONV, D_MODEL), f32) * D_CONV ** -0.5,
        'w_out': nrm(ks[17], (DEPTH, D_MODEL, D_MODEL), f32) * D_MODEL ** -0.5,
        'rel_bias': 0.5 * nrm(ks[18], (N_BUCKETS, N_HEADS), f32),
        'final_g': 1.0 + 0.02 * nrm(ks[19], (D_MODEL,), f32),
    }


def reference(x_prompt, x_sample, cache_k, cache_v, cache_kidx, state_conv, page_table,
              ln_g, w_in, conv_w, conv_b, conv_norm_g, conv_norm_b, w_pw, b_pw,
              w_up_attn, w_up_conv, w_out, rel_bias, final_g):
    def gather_pages(pool):
        g = pool[page_table]
        return g.reshape(g.shape[0], g.shape[1] * g.shape[2], *g.shape[3:])

    B = x_prompt.shape[0]
    dt = x_prompt.dtype
    xp, xs = x_prompt, x_sample
    kp_l, vp_l, ip_l, cp_l = [], [], [], []
    ks_l, vs_l, is_l, cs_l = [], [], [], []
    for layer in range(DEPTH):
        params = (ln_g[layer], w_in[layer], conv_w[layer], conv_b[layer], conv_norm_g[layer],
                  conv_norm_b[layer], w_pw[layer], b_pw[layer], w_up_attn[layer],
                  w_up_conv[layer], w_out[layer], rel_bias)
        xp, kp, vp, ip, cp = decoder_layer(
            xp,
            jnp.zeros((B, 0, N_KV_HEADS, HEAD_DIM), dt),
            jnp.zeros((B, 0, N_KV_HEADS, HEAD_DIM), dt),
            jnp.zeros((B, 0, IDX_DIM), dt),
            jnp.zeros((B, CONV_WIDTH - 1, D_CONV), dt),
            *params)
        xs, ks_, vs_, is_, cs_ = decoder_layer(
            xs,
            gather_pages(cache_k[layer]),
            gather_pages(cache_v[layer]),
            gather_pages(cache_kidx[layer]),
            state_conv[layer],
            *params)
        kp_l.append(kp); vp_l.append(vp); ip_l.append(ip); cp_l.append(cp)
        ks_l.append(ks_); vs_l.append(vs_); is_l.append(is_); cs_l.append(cs_)
    y_prompt = rms_norm(xp, final_g)
    y_sample = rms_norm(xs, final_g)
    return (y_prompt, y_sample,
            jnp.stack(kp_l), jnp.stack(vp_l), jnp.stack(ip_l), jnp.stack(cp_l),
            jnp.stack(ks_l), jnp.stack(vs_l), jnp.stack(is_l), jnp.stack(cs_l))
```

```python
import math
import numpy as np
import concourse.bass as bass
import concourse.mybir as mybir
from concourse.bass_utils import run_bass_kernel_spmd

F32 = mybir.dt.float32
BF16 = mybir.dt.bfloat16
I32 = mybir.dt.int32
U8 = mybir.dt.uint8
AF = mybir.ActivationFunctionType
ALU = mybir.AluOpType
AX = mybir.AxisListType
ET = mybir.EngineType

D = 2048
DIN = 10312
C_Q, C_K, C_V, C_ZA, C_QI, C_KI, C_WI, C_GV, C_GG, C_ZC, C_GA, C_GC = (
    0, 1024, 1280, 1536, 2560, 3072, 3136, 3144, 4168, 5192, 6216, 8264)
NPHYS = 2560
NEGB = -30000.0
SCALE = 128.0 ** -0.5
SQ128 = 128.0 ** 0.5
NBIS = 18
EPS = 1e-6


class Res:
    __slots__ = ("w", "r")

    def __init__(self):
        self.w = None
        self.r = {}


class Eng:
    def __init__(self, nc, e, name):
        self.e = e
        self.sem = nc.alloc_semaphore("p_" + name)
        self.n = 0
        self.seen = {}

    def wait(self, tok):
        if tok is None:
            return
        sem, val = tok
        k = id(sem)
        if self.seen.get(k, 0) >= val:
            return
        self.e.wait_ge(sem, val)
        self.seen[k] = val


class Slot:
    def __init__(self, nc, name, scratch=False):
        self.sem = nc.alloc_semaphore("d_" + name)
        self.n = 0
        self.scratch = scratch


class KB:
    def __init__(self, nc):
        self.nc = nc
        self.eng = {
            "pe": Eng(nc, nc.tensor, "pe"), "act": Eng(nc, nc.scalar, "act"),
            "dve": Eng(nc, nc.vector, "dve"), "pool": Eng(nc, nc.gpsimd, "pool"),
            "sp": Eng(nc, nc.sync, "sp"),
        }
        self.slots = []
        self.slotd = {}
        self.out_toks = []
        self.flip = 0

    def slot(self, name, scratch=False):
        if name in self.slotd:
            return self.slotd[name]
        s = Slot(self.nc, name, scratch)
        self.slots.append(s)
        self.slotd[name] = s
        return s

    def _deps(self, E, reads, writes):
        for r in reads:
            if r.w:
                E.wait(r.w)
        for w in writes:
            if w.w:
                E.wait(w.w)
            for t in w.r.values():
                E.wait(t)

    def _mark(self, tok, reads, writes):
        for r in reads:
            r.r[id(tok[0])] = tok
        for w in writes:
            w.w = tok
            w.r = {}

    def op(self, en, fn, reads=(), writes=()):
        E = self.eng[en]
        self._deps(E, reads, writes)
        ins = fn()
        E.n += 1
        ins.then_inc(E.sem, 1)
        tok = (E.sem, E.n)
        self._mark(tok, reads, writes)
        return tok

    def mm(self, fns, reads=(), writes=()):
        E = self.eng["pe"]
        self._deps(E, reads, writes)
        ins = None
        for f in fns:
            ins = f()
        E.n += 1
        ins.then_inc(E.sem, 1)
        tok = (E.sem, E.n)
        self._mark(tok, reads, writes)
        return tok

    def dma(self, q, out, in_, slot, reads=(), writes=(), is_out=False, **kw):
        E = self.eng[q]
        self._deps(E, reads, writes)
        ins = E.e.dma_start(out=out, in_=in_, **kw)
        slot.n += 16
        ins.then_inc(slot.sem, 16)
        tok = (slot.sem, slot.n)
        self._mark(tok, reads, writes)
        if is_out:
            self.out_toks.append(tok)
        return tok

    def idma(self, out, in_, idx_ap, slot, reads=(), writes=()):
        E = self.eng["pool"]
        self._deps(E, reads, writes)
        ins = self.nc.gpsimd.indirect_dma_start(out=out, out_offset=None, in_=in_,
                                                in_offset=bass.IndirectOffsetOnAxis(ap=idx_ap, axis=0))
        slot.n += 16
        ins.then_inc(slot.sem, 16)
        tok = (slot.sem, slot.n)
        self._mark(tok, reads, writes)
        return tok

    def barrier(self):
        toks = [(E.sem, E.n) for k, E in self.eng.items() if E.n > 0 and k != "sp"]
        toks += [(s.sem, s.n) for s in self.slots if s.scratch and s.n > 0]
        for E in self.eng.values():
            for t in toks:
                E.wait(t)

    def ev(self):
        self.flip ^= 1
        return "act" if self.flip else "dve"


def t5_bucket_np(n):
    n = np.maximum(n, 0)
    max_exact = 16
    ratio = np.log(np.maximum(n, 1).astype(np.float32) / np.float32(max_exact)) / np.float32(math.log(128 / max_exact))
    large = np.minimum(max_exact + (ratio.astype(np.float32) * np.float32(16)).astype(np.int32), 31)
    return np.where(n < max_exact, n, large)


def host_consts():
    c = {}
    c["identf"] = np.eye(128, dtype=np.float32)
    m = np.arange(384)
    n = m - 128
    oh = np.zeros((33, 384), np.float32)
    bk = t5_bucket_np(n)
    for i in range(384):
        if n[i] >= 0:
            oh[bk[i], i] = 1.0
        else:
            oh[32, i] = 1.0
    c["oh33"] = oh
    q = np.arange(128)[:, None]
    s = np.arange(128)[None, :]
    c["causneg"] = np.where(s <= q, 0.0, -1e30).astype(np.float32)
    same = (q // 8) == (s // 8)
    c["sampneg"] = np.where(same & ((s % 8) <= (q % 8)), 0.0, -1e30).astype(np.float32)
    c["sampbm"] = same.astype(np.float32)
    c["sampnb"] = np.where(same, 0.0, NEGB * SQ128).astype(np.float32)
    tok = np.arange(128)
    dq = np.zeros((128, 8, 8), np.float32)
    for t in range(128):
        dq[t, :, t % 8] = 1.0
    c["dq"] = dq.reshape(128, 64)
    bs = np.zeros((128, 16), np.float32)
    bs[tok, tok // 8] = 1.0
    c["bsel"] = bs
    sb = np.zeros((64, 248), np.float32)
    for hq in range(64):
        sb[hq, 120 + hq % 8] = 1.0
    c["selbase"] = sb
    c["oh8T"] = (np.arange(128)[None, :] // 16 == np.arange(8)[:, None]).astype(np.float32)
    c["fvec"] = np.tile((2.0 ** -(np.arange(32, dtype=np.float64) + 1)).astype(np.float32)[None, :], (128, 1))
    c["pidmat"] = np.tile((np.arange(128, dtype=np.float32) % 16)[:, None], (1, 32))
    return c


CONST_SHAPES = {"identf": [128, 128], "oh33": [33, 384], "causneg": [128, 128], "sampneg": [128, 128],
                "sampbm": [128, 128], "sampnb": [128, 128], "dq": [128, 64], "bsel": [128, 16],
                "selbase": [64, 248], "pidmat": [128, 32], "fvec": [128, 32], "oh8T": [8, 128]}


def build():
    nc = bass.Bass("TRN2", target_bir_lowering=False)
    kb = KB(nc)

    def din(name, shape, dt=F32):
        return nc.dram_tensor(name, shape, dt, kind="ExternalInput").ap()

    def dout(name, shape):
        return nc.dram_tensor(name, shape, F32, kind="ExternalOutput").ap()

    x = din("x", [2176, D])
    cache_k = din("cache_k", [NPHYS * 16, 2048])
    cache_v = din("cache_v", [NPHYS * 16, 2048])
    cache_ki = din("cache_ki", [NPHYS * 16, 512])
    state = din("state", [16, 30, 1024])
    ptab = din("ptab", [1, 256], I32)
    ln_g = din("ln_g", [D])
    w_in = din("w_in", [D, DIN])
    conv_w = din("conv_w", [31, 1024])
    conv_b = din("conv_b", [1024])
    cn_g = din("cn_g", [1024])
    cn_b = din("cn_b", [1024])
    w_pw = din("w_pw", [1024, 1024])
    b_pw = din("b_pw", [1024])
    w_ua = din("w_ua", [1024, D])
    w_uc = din("w_uc", [1024, D])
    w_out = din("w_out", [D, D])
    rel_bias = din("rel_bias", [32, 8])
    final_g = din("final_g", [1, D])
    cin = {k: din("c_" + k, v) for k, v in CONST_SHAPES.items()}

    y = dout("y", [2176, D])
    kk = dout("kk", [2176, 256])
    vv = dout("vv", [2176, 256])
    kio = dout("kio", [2176, 64])
    convp = dout("convp", [30, 1024])
    convs = dout("convs", [16, 30, 1024])
    dper = nc.dram_tensor("dper", [8, 129 * 384], F32, kind="Internal").ap()

    w_in_v = w_in.rearrange("(kc p) f -> p kc f", p=128)
    w_out_v = w_out.rearrange("(kc p) f -> p kc f", p=128)
    w_pw_v = w_pw.rearrange("(kc p) f -> p kc f", p=128)
    w_ua_v = w_ua.rearrange("(kc p) f -> p kc f", p=128)
    w_uc_v = w_uc.rearrange("(kc p) f -> p kc f", p=128)

    wsc = {}

    def wconv(key, src, kcn, ncols, col0=0, wtot=None):
        wtot = ncols if wtot is None else wtot
        if key not in wsc:
            t = nc.dram_tensor("wsc_" + key, [128, kcn * wtot], BF16, kind="Internal").ap()
            wsc[key] = (t, Res(), kb.slot("cv_" + key))
        t, r, sl = wsc[key]
        tv = t.rearrange("p (k c) -> p k c", k=kcn)
        kb.dma("pool", tv[:, :, col0:col0 + ncols], src, sl, writes=[r])

    def conv_first():
        for b2 in range(2):
            wconv(f"q{b2}", w_in_v[:, :, C_Q + b2 * 512: C_Q + (b2 + 1) * 512], 16, 512)
        wconv("kv", w_in_v[:, :, C_K:C_K + 512], 16, 512)
        for b2 in range(2):
            wconv(f"za{b2}", w_in_v[:, :, C_ZA + b2 * 512: C_ZA + (b2 + 1) * 512], 16, 512)
        wconv("qi", w_in_v[:, :, C_QI:C_QI + 512], 16, 512)
        wconv("ki", w_in_v[:, :, C_KI:C_KI + 72], 16, 72)

    def conv_rest(after):
        kb._deps(kb.eng["pool"], after, [])
        for b4 in range(4):
            wconv(f"ua{b4}", w_ua_v[:, :, b4 * 512:(b4 + 1) * 512], 8, 512)
        for b4 in range(4):
            wconv(f"ga{b4}", w_in_v[:, :, C_GA + b4 * 512: C_GA + (b4 + 1) * 512], 16, 512)
        for b2 in range(2):
            wconv(f"gg{b2}", w_in_v[:, :, C_GG + b2 * 512: C_GG + (b2 + 1) * 512], 16, 512)
        for b2 in range(2):
            wconv(f"gv{b2}", w_in_v[:, :, C_GV + b2 * 512: C_GV + (b2 + 1) * 512], 16, 512)
        for b2 in range(2):
            wconv(f"zc{b2}", w_in_v[:, :, C_ZC + b2 * 512: C_ZC + (b2 + 1) * 512], 16, 512)
        for b2 in range(2):
            wconv(f"pw{b2}", w_pw_v[:, :, b2 * 512:(b2 + 1) * 512], 8, 512)
        for b4 in range(4):
            wconv(f"uc{b4}", w_uc_v[:, :, b4 * 512:(b4 + 1) * 512], 8, 512)
            wconv(f"gc{b4}", w_in_v[:, :, C_GC + b4 * 512: C_GC + (b4 + 1) * 512], 16, 512)
        for b4 in range(4):
            wconv(f"wo{b4}", w_out_v[:, :, b4 * 512:(b4 + 1) * 512], 16, 512)


    def sb(name, shape, dt):
        return nc.alloc_sbuf_tensor(name, shape, dt)

    identf = sb("identf", [128, 128], F32)
    identb = sb("identb", [128, 128], BF16)
    onesb = sb("onesb", [128, 128], BF16)
    onesf = sb("onesf", [128, 128], F32)
    causneg = sb("causneg", [128, 128], F32)
    sampneg = sb("sampneg", [128, 128], F32)
    sampbm = sb("sampbm", [128, 128], F32)
    sampnb = sb("sampnb", [128, 128], F32)
    dq = sb("dq", [128, 64], F32)
    bsel = sb("bsel", [128, 16], F32)
    selbase = sb("selbase", [64, 248], F32)
    prmT = sb("prmT", [128, 48], F32)
    lngT = prmT[:, 0:16]
    cbT = prmT[:, 16:24]
    cngT = prmT[:, 24:32]
    cnbT = prmT[:, 32:40]
    bpwT = prmT[:, 40:48]
    cwT = sb("cwT", [128, 8, 31], F32)
    oh8T = sb("oh8T", [8, 128], F32)
    epsT = sb("epsT", [128, 1], F32)
    pidmat = sb("pidmat", [128, 32], F32)
    fvec = sb("fvec", [128, 32], F32)
    idxall = sb("idxall", [128, 32], I32)
    biasT = sb("biasT", [128, 4, 8, 128], BF16)
    xnT = sb("xnT", [128, 16, 512], BF16)
    mT = sb("mT", [128, 16, 512], BF16)
    wbuf = [sb(f"wbuf{i}", [128, 8192], BF16) for i in range(2)]
    KT = sb("KT", [128, 2, 2048], BF16)
    VA = sb("VA", [128, 16, 2, 128], BF16)
    KiT = sb("KiT", [64, 2048], BF16)
    uhalo = sb("uhalo", [128, 8, 30], BF16)
    SCR = 102 * 1024
    scr = sb("scr", [128, SCR], U8)
    psum_all = nc.alloc_psum_tensor("ps", [128, 4096], F32)

    class Arena:
        def __init__(self):
            self.off = 0

        def reset(self):
            self.off = 0

        def get(self, shape, dt):
            esz = 4 if dt in (F32, I32) else 2
            n = int(np.prod(shape[1:])) * esz
            n = (n + 63) // 64 * 64
            assert self.off + n <= SCR, (self.off, n)
            ap = scr[0:shape[0], self.off:self.off + n].bitcast(dt)
            self.off += n
            if len(shape) == 3:
                ap = ap[:, 0:shape[1] * shape[2]].rearrange("p (a b) -> p a b", a=shape[1])
            elif len(shape) == 4:
                ap = ap[:, 0:shape[1] * shape[2] * shape[3]].rearrange("p (a b c) -> p a b c", a=shape[1], b=shape[2])
            else:
                ap = ap[:, 0:shape[1]]
            return ap

    ar = Arena()

    banks = [psum_all[:, i * 512:(i + 1) * 512] for i in range(8)]
    bres = [Res() for _ in range(8)]
    pinned = set()
    bstate = {"i": 0}

    def bank():
        while True:
            i = bstate["i"] % 8
            bstate["i"] += 1
            if i not in pinned:
                return i, banks[i], bres[i]

    s_setup = kb.slot("setup")
    R_const = Res()
    ar.reset()
    ar.off = 56 * 1024
    prm = ar.get([48, 128], F32)
    cwst = ar.get([31, 1024], F32)
    ptJ = ar.get([8, 32], I32)
    ptJf = ar.get([8, 32], F32)
    with nc.allow_non_contiguous_dma(reason="tiny page table transpose"):
        for t_, a_ in [(identf, cin["identf"]), (causneg, cin["causneg"]), (sampneg, cin["sampneg"]),
                       (sampbm, cin["sampbm"]), (sampnb, cin["sampnb"]), (dq, cin["dq"]), (bsel, cin["bsel"]),
                       (selbase, cin["selbase"]), (pidmat, cin["pidmat"]), (fvec, cin["fvec"]), (oh8T, cin["oh8T"])]:
            kb.dma("sp", t_[:], a_, s_setup, writes=[R_const])
        kb.dma("sp", ptJ[:], ptab.rearrange("o (bh j) -> (o j) bh", j=8), s_setup, writes=[R_const])
        kb.dma("sp", prm[0:16, :], ln_g.rearrange("(kc p) -> kc p", p=128), s_setup, writes=[R_const])
        for k_, a_ in enumerate([conv_b, cn_g, cn_b, b_pw]):
            kb.dma("sp", prm[16 + 8 * k_:24 + 8 * k_, :], a_.rearrange("(c p) -> c p", p=128), s_setup, writes=[R_const])
        kb.dma("sp", cwst[:], conv_w, s_setup, writes=[R_const])
    bi, bk_, br = bank()
    kb.mm([lambda: nc.tensor.transpose(out=bk_[:, 0:48], in_=prm[0:48, :], identity=identf[0:48, 0:48])],
          reads=[R_const], writes=[br])
    kb.op("dve", lambda: nc.vector.tensor_copy(out=prmT[:], in_=bk_[:, 0:48]), reads=[br], writes=[R_const])
    bi, bk_, br = bank()
    kb.mm([(lambda c_=c_: nc.tensor.transpose(out=bk_[:, c_ * 31:(c_ + 1) * 31], in_=cwst[0:31, c_ * 128:(c_ + 1) * 128],
                                              identity=identf[0:31, 0:31])) for c_ in range(8)],
          reads=[R_const], writes=[br])
    kb.op("dve", lambda: nc.vector.tensor_copy(out=cwT[:].rearrange("p a b -> p (a b)"), in_=bk_[:, 0:248]),
          reads=[br], writes=[R_const])
    kb.op("dve", lambda: nc.vector.tensor_copy(out=ptJf[:], in_=ptJ[:]), reads=[R_const], writes=[R_const])
    bi, bk_, br = bank()
    kb.mm([lambda: nc.tensor.matmul(bk_[:, 0:32], oh8T[:, :], ptJf[:, :], start=True, stop=True)],
          reads=[R_const], writes=[br])
    kb.op("dve", lambda: nc.vector.scalar_tensor_tensor(out=idxall[:], in0=bk_[:, 0:32], scalar=16.0, in1=pidmat[:],
                                                        op0=ALU.mult, op1=ALU.add), reads=[br, R_const], writes=[R_const])
    kb.op("dve", lambda: nc.vector.tensor_copy(out=identb[:], in_=identf[:]), reads=[R_const], writes=[R_const])
    kb.op("dve", lambda: nc.vector.memset(onesb[:], 1.0), writes=[R_const])
    kb.op("dve", lambda: nc.vector.memset(onesf[:], 1.0 / 1024.0), writes=[R_const])
    kb.op("dve", lambda: nc.vector.memset(epsT[:], EPS), writes=[R_const])
    kb.op("dve", lambda: nc.vector.memset(uhalo[:], 0.0), writes=[R_const])

    LS = {}

    rb33 = ar.get([33, 8], F32)
    oh33 = ar.get([33, 384], F32)
    dx = ar.get([8, 384], F32)
    dx2 = ar.get([8, 384], F32)
    R_b0 = Res()
    s_b = kb.slot("btab", scratch=True)
    kb.op("dve", lambda: nc.vector.memset(rb33[0:33, :], NEGB), writes=[R_b0])
    kb.dma("sp", rb33[0:32, :], rel_bias, s_b, writes=[R_b0])
    kb.dma("sp", oh33[:], cin["oh33"], s_b, writes=[R_b0])
    bi, bk_, br = bank()
    kb.mm([lambda: nc.tensor.matmul(bk_[0:8, 0:384], rb33[:, :], oh33[:, :], start=True, stop=True)],
          reads=[R_b0], writes=[br])
    kb.op("dve", lambda: nc.vector.tensor_copy(out=dx[:], in_=bk_[0:8, 0:384]), reads=[br], writes=[R_b0])
    kb.op("dve", lambda: nc.vector.tensor_scalar(out=dx2[:], in0=dx[:], scalar1=dx[:, 383:384], scalar2=SQ128,
                                                 op0=ALU.subtract, op1=ALU.mult), reads=[R_b0], writes=[R_b0])
    R_dper = Res()
    kb.dma("sp", dper.rearrange("h (r m) -> h r m", m=384), dx2[:].unsqueeze(1).broadcast_to([8, 129, 384]),
           s_b, reads=[R_b0], writes=[R_dper])

    def late_setup():
        ar.off = 72 * 1024
        tst = ar.get([128, 8, 128], F32)
        thf = ar.get([128, 8, 128], F32)
        R_b = Res()
        R_bias = Res()

        def mk_table(cofs, dst_hi, dst_lo):
            src = dper[:, cofs:cofs + 383 * 128].rearrange("h (s x) -> s h x", x=383)[:, :, 0:128]
            kb.dma("sp", tst[:], src, s_b, reads=[R_dper], writes=[R_b])
            kb.op("dve", lambda: nc.vector.tensor_copy(out=dst_hi, in_=tst[:]), reads=[R_b], writes=[R_bias])
            kb.op("dve", lambda: nc.vector.tensor_copy(out=thf[:], in_=dst_hi), reads=[R_bias], writes=[R_b])
            kb.op("dve", lambda: nc.vector.tensor_tensor(out=dst_lo, in0=tst[:], in1=thf[:], op=ALU.subtract),
                  reads=[R_b], writes=[R_bias])

        with nc.allow_non_contiguous_dma(reason="toeplitz"):
            mk_table(128, biasT[:, 0], biasT[:, 1])
            mk_table(256, biasT[:, 2], biasT[:, 3])
        dsc = [nc.dram_tensor(f"dsc{c}", [128, 31 * 128], BF16, kind="Internal").ap() for c in range(8)]
        R_dsc = [Res() for _ in range(8)]
        dgt = [ar.get([128, 31, 128], BF16) for _ in range(2)]
        R_dgt = [Res(), Res()]
        s_dg = [kb.slot("dg0", scratch=True), kb.slot("dg1", scratch=True)]
        for c in range(8):
            di = c % 2
            kb.op("dve", lambda: nc.vector.tensor_tensor(out=dgt[di][:], in0=identf[:].unsqueeze(1).broadcast_to([128, 31, 128]),
                                                          in1=cwT[:, c, :].unsqueeze(2).broadcast_to([128, 31, 128]), op=ALU.mult),
                  reads=[R_const], writes=[R_dgt[di]])
            kb.dma("sp", dsc[c], dgt[di][:].rearrange("p a b -> p (a b)"), s_dg[di], reads=[R_dgt[di]], writes=[R_dsc[c]])
        LS.update(dict(R_dper=R_dper, R_bias=R_bias, dsc=dsc, R_dsc=R_dsc, s_dg=s_dg))

    R_uh = Res()
    wres = [Res(), Res()]
    wsl = [kb.slot("w0"), kb.slot("w1")]
    wst = {"n": 0}

    def wload(key, parts, kcn, ncols):
        i = wst["n"] % 2
        wst["n"] += 1
        flat = wbuf[i][:, 0:kcn * ncols]
        view = flat.rearrange("p (k c) -> p k c", k=kcn)
        t, r, sl = wsc[key]
        kb.dma("sp", flat, t, wsl[i], reads=[r], writes=[wres[i]])
        return view, wres[i]

    prefetched = {}

    def run_jobs(jobs, next_first=None, hook=None):
        loaded = {}
        order = [k for k, j in enumerate(jobs) if j[0] is not None]

        def ensure(k):
            if k not in loaded:
                key = jobs[k][0][0]
                if key in prefetched:
                    loaded[k] = prefetched.pop(key)
                else:
                    loaded[k] = wload(*jobs[k][0])
        pos = 0
        for k, (ls, fn) in enumerate(jobs):
            if ls is not None:
                ensure(k)
                pos = order.index(k)
                if pos + 1 < len(order):
                    ensure(order[pos + 1])
                elif next_first is not None:
                    prefetched[next_first[0]] = wload(*next_first)
                fn(*loaded.pop(k))
            else:
                fn()
            if hook is not None and hook[0] == k:
                hook[1]()

    R_xn = Res()

    def p1_alloc(gj):
        nt_ = 1 if gj == 4 else 4
        pb = {}
        pb["xb"] = [ar.get([128, D], F32) for _ in range(2)]
        pb["xs"] = [ar.get([128, D], F32) for _ in range(nt_)]
        pb["st"] = ar.get([128, 4 * nt_], F32)
        pb["R_xb"] = [Res(), Res()]
        pb["R_xs"] = [Res() for _ in range(nt_)]
        pb["R_st"] = Res()
        pb["nt"] = nt_
        return pb

    def p1_norm(gj, pb):
        tok0_ = gj * 512
        s_x = [kb.slot(f"x_{i}", scratch=True) for i in range(2)]
        xb, xs, st = pb["xb"], pb["xs"], pb["st"]
        for t in range(pb["nt"]):
            i = t % 2
            c0 = 4 * t
            kb.dma("sp", xb[i][:], x[tok0_ + t * 128: tok0_ + (t + 1) * 128, :], s_x[i], writes=[pb["R_xb"][i]])
            kb.op("act", lambda: nc.scalar.activation(out=xs[t][:], in_=xb[i][:], func=AF.Square,
                                                      accum_out=st[:, c0:c0 + 1]), reads=[pb["R_xb"][i]],
                  writes=[pb["R_xs"][t], pb["R_st"]])
            kb.op("act", lambda: nc.scalar.activation(out=st[:, c0 + 1:c0 + 2], in_=st[:, c0:c0 + 1], func=AF.Sqrt,
                                                      bias=epsT[:, 0:1], scale=1.0 / D), reads=[pb["R_st"], R_const],
                  writes=[pb["R_st"]])
            kb.op("dve", lambda: nc.vector.reciprocal(out=st[:, c0 + 2:c0 + 3], in_=st[:, c0 + 1:c0 + 2]), reads=[pb["R_st"]],
                  writes=[pb["R_st"]])
            kb.op("act", lambda: nc.scalar.activation(out=xs[t][:], in_=xb[i][:], func=AF.Copy, scale=st[:, c0 + 2:c0 + 3]),
                  reads=[pb["R_xb"][i], pb["R_st"]], writes=[pb["R_xs"][t]])

    def p1_tr(gj, pb):
        NT_ = 128 if gj == 4 else 512
        xn_ = xnT[:, :, 0:NT_]
        xs = pb["xs"]
        for t in range(pb["nt"]):
            for q4 in range(4):
                bi, bk, br = bank()
                kb.mm([(lambda kc=q4 * 4 + k, k=k: nc.tensor.transpose(out=bk[:, k * 128:(k + 1) * 128],
                                                                       in_=xs[t][:, kc * 128:(kc + 1) * 128],
                                                                       identity=identf[:])) for k in range(4)],
                      reads=[pb["R_xs"][t], R_const], writes=[br])
                for k in range(4):
                    kc = q4 * 4 + k
                    if kb.ev() == "act":
                        kb.op("act", lambda: nc.scalar.activation(out=xn_[:, kc, t * 128:(t + 1) * 128],
                                                                  in_=bk[:, k * 128:(k + 1) * 128], func=AF.Copy,
                                                                  scale=lngT[:, kc:kc + 1]),
                              reads=[br, R_const], writes=[R_xn])
                    else:
                        kb.op("dve", lambda: nc.vector.tensor_scalar(out=xn_[:, kc, t * 128:(t + 1) * 128],
                                                                     in0=bk[:, k * 128:(k + 1) * 128],
                                                                     scalar1=lngT[:, kc:kc + 1], scalar2=None,
                                                                     op0=ALU.mult),
                              reads=[br, R_const], writes=[R_xn])

    def group(gi):
        samp = gi == 4
        NT = 128 if samp else 512
        ntile = NT // 128
        tok0 = gi * 512
        xn = xnT[:, :, 0:NT]
        m_T = mT[:, :, 0:NT]
        R_m = Res()

        if gi == 0:
            ar.reset()
            pb = p1_alloc(0)
            p1_norm(0, pb)
            kb._deps(kb.eng["pool"], [R_const, pb["R_xb"][0], pb["R_xb"][1]], [])
            with nc.allow_non_contiguous_dma(reason="narrow idx weight block"):
                conv_first()
            p1_tr(0, pb)
        if gi == 0:
            late_setup()
            conv_rest([LS["R_bias"]] + LS["R_dsc"])
        R_dper, R_bias, dsc, R_dsc, s_dg = LS["R_dper"], LS["R_bias"], LS["dsc"], LS["R_dsc"], LS["s_dg"]
        kb.barrier()

        ar.reset()
        qT = ar.get([128, 8, NT], BF16)
        zsT = ar.get([128, 8, NT], BF16)
        qiT = ar.get([64, 8, NT], BF16)
        gT = ar.get([128, 8, NT], BF16)
        kvst = [ar.get([128, 512], F32) for _ in range(2)]
        kist = [ar.get([128, 72], F32) for _ in range(2)]
        wab = ar.get([128, ntile, 16], F32)
        R_q, R_zs, R_qi, R_g, R_wab = Res(), Res(), Res(), Res(), Res()
        R_kvst = [Res(), Res()]
        R_kist = [Res(), Res()]
        s_kv = [kb.slot(f"kv_{i}", scratch=True) for i in range(2)]
        s_ki = [kb.slot(f"ki_{i}", scratch=True) for i in range(2)]
        R_KT, R_VA, R_KiT = Res(), Res(), Res()
        if samp:
            KTn = ar.get([128, 2, 128], BF16)
            VAn = ar.get([128, 2, 128], BF16)
            KiTn = ar.get([64, 128], BF16)

        def formF(wv, wr, nchunk, M, evac, kcn=16, src=None, R_src=None):
            src = xn if src is None else src
            R_src = R_xn if R_src is None else R_src
            for c in range(nchunk):
                bi, bk, br = bank()
                kb.mm([(lambda kc=kc: nc.tensor.matmul(bk[0:M, 0:NT], wv[:, kc, c * M:(c + 1) * M], src[:, kc, :],
                                                       start=(kc == 0), stop=(kc == kcn - 1))) for kc in range(kcn)],
                      reads=[wr, R_src], writes=[br])
                evac(c, bk[0:M, 0:NT], br)

        def formT(wv, wr, ncols, evac, c0=0, kcn=16):
            for t in range(ntile):
                bi, bk, br = bank()
                kb.mm([(lambda kc=kc: nc.tensor.matmul(bk[:, 0:ncols], xn[:, kc, t * 128:(t + 1) * 128],
                                                       wv[:, kc, c0:c0 + ncols],
                                                       start=(kc == 0), stop=(kc == kcn - 1))) for kc in range(kcn)],
                      reads=[wr, R_xn], writes=[br])
                evac(t, bk[:, 0:ncols], br)

        def copy_evac(dst_fn, R_dst, func=None):
            def f(c, ps, br):
                if func is not None:
                    kb.op("act", lambda: nc.scalar.activation(out=dst_fn(c), in_=ps, func=func), reads=[br], writes=[R_dst])
                elif kb.ev() == "act":
                    kb.op("act", lambda: nc.scalar.copy(out=dst_fn(c), in_=ps), reads=[br], writes=[R_dst])
                else:
                    kb.op("dve", lambda: nc.vector.tensor_copy(out=dst_fn(c), in_=ps), reads=[br], writes=[R_dst])
            return f

        jobs = []
        for b2 in range(2):
            jobs.append(((f"q{b2}", [(0, w_in_v[:, :, C_Q + b2 * 512: C_Q + (b2 + 1) * 512])], 16, 512),
                         lambda wv, wr, b2=b2: formF(wv, wr, 4, 128, copy_evac(lambda c: qT[:, b2 * 4 + c, :], R_q))))

        def kv_job(wv, wr):
            if samp:
                formF(wv, wr, 2, 128, copy_evac(lambda c: KTn[:, c, :], R_KT))
            else:
                formF(wv, wr, 2, 128, copy_evac(lambda c: KT[:, c, tok0:tok0 + NT], R_KT))

            def ev(t, ps, br):
                i = t % 2
                kb.op("act", lambda: nc.scalar.copy(out=kvst[i][:, :], in_=ps), reads=[br], writes=[R_kvst[i]])
                kb.dma("sp", kk[tok0 + t * 128: tok0 + (t + 1) * 128, :], kvst[i][:, 0:256], s_kv[i],
                       reads=[R_kvst[i]], is_out=True)
                kb.dma("sp", vv[tok0 + t * 128: tok0 + (t + 1) * 128, :], kvst[i][:, 256:512], s_kv[i],
                       reads=[R_kvst[i]], is_out=True)
                dst = VAn[:, :, :] if samp else VA[:, gi * 4 + t, :, :]
                kb.op("dve", lambda: nc.vector.tensor_copy(out=dst, in_=kvst[i][:, 256:512].rearrange("p (g d) -> p g d", g=2)),
                      reads=[R_kvst[i]], writes=[R_VA])
            formT(wv, wr, 512, ev)
        jobs.append((("kv", [(0, w_in_v[:, :, C_K:C_K + 512])], 16, 512), kv_job))
        for b2 in range(2):
            jobs.append(((f"za{b2}", [(0, w_in_v[:, :, C_ZA + b2 * 512: C_ZA + (b2 + 1) * 512])], 16, 512),
                         lambda wv, wr, b2=b2: formF(wv, wr, 4, 128, copy_evac(lambda c: zsT[:, b2 * 4 + c, :], R_zs, AF.Silu))))
        jobs.append((("qi", [(0, w_in_v[:, :, C_QI: C_QI + 512])], 16, 512),
                     lambda wv, wr: formF(wv, wr, 8, 64, copy_evac(lambda c: qiT[:, c, :], R_qi))))

        def idx_job(wv, wr):
            if samp:
                formF(wv, wr, 1, 64, copy_evac(lambda c: KiTn[:, :], R_KiT))
            else:
                formF(wv, wr, 1, 64, copy_evac(lambda c: KiT[:, tok0:tok0 + NT], R_KiT))

            def ev(t, ps, br):
                i = t % 2
                kb.op("act", lambda: nc.scalar.copy(out=kist[i][:, :], in_=ps), reads=[br], writes=[R_kist[i]])
                kb.dma("sp", kio[tok0 + t * 128: tok0 + (t + 1) * 128, :], kist[i][:, 0:64], s_ki[i],
                       reads=[R_kist[i]], is_out=True)
                kb.op("act", lambda: nc.scalar.activation(out=wab[:, t, 0:8], in_=kist[i][:, 64:72], func=AF.Abs),
                      reads=[R_kist[i]], writes=[R_wab])
                kb.op("act", lambda: nc.scalar.activation(out=wab[:, t, 8:16], in_=kist[i][:, 64:72], func=AF.Sign),
                      reads=[R_kist[i]], writes=[R_wab])
            formT(wv, wr, 72, ev)
        jobs.append((("ki", [(0, w_in_v[:, :, C_KI:C_KI + 72])], 16, 72), idx_job))
        with nc.allow_non_contiguous_dma(reason="narrow idx weight block"):
            run_jobs(jobs, ("ua0", [], 8, 512))
        if gi == 0:
            for E_ in kb.eng.values():
                kb._deps(E_, [R_bias] + R_dsc, [])

        class SelBuf:
            pass
        nset = 1 if samp else 2
        SB = []
        for k_ in range(nset):
            S = SelBuf()
            S.score = ar.get([128, 2176], F32)
            S.junk = ar.get([128, 2176], BF16)
            S.m01 = ar.get([128, 2176], BF16)
            S.maskT = ar.get([128, 17, 128], BF16)
            S.bis = ar.get([128, 80], F32)
            S.R_score, S.R_junk, S.R_m01, S.R_maskT, S.R_bis = Res(), Res(), Res(), Res(), Res()
            SB.append(S)
        S0 = SB[0]
        score, maskT, bis = S0.score, S0.maskT, S0.bis
        R_score, R_maskT, R_bis = S0.R_score, S0.R_maskT, S0.R_bis
        rtmp = [ar.get([128, 512], F32) for _ in range(2)]
        et = [ar.get([128, 512], BF16) for _ in range(3)]
        pt = [ar.get([128, 512], BF16) for _ in range(2)]
        rd = ar.get([128, 512], F32)
        tmpf = ar.get([128, 512], F32)
        R_rd, R_tmpf = Res(), Res()
        R_rtmp = [Res(), Res()]
        R_et = [Res(), Res(), Res()]
        R_pt = [Res(), Res()]
        cnt = {"r": 0, "e": 0}
        NEGM = NEGB * SQ128

        def gen_select(S, L, nblk, additive):
            bis_ = S.bis
            Rb = S.R_bis
            kb.op("dve", lambda: nc.vector.tensor_reduce(out=bis_[:, 0:1], in_=S.score[:, 0:L], axis=AX.X, op=ALU.max),
                  reads=[S.R_score], writes=[Rb])
            yield
            kb.op("dve", lambda: nc.vector.tensor_sub(out=bis_[:, 2:3], in0=bis_[:, 0:1], in1=bis_[:, 1:2]),
                  reads=[Rb], writes=[Rb])
            for k in range(NBIS + 1):
                pass
            kb.op("dve", lambda: nc.vector.tensor_scalar(out=bis_[:, 40:40 + NBIS + 1],
                                                         in0=fvec[:, 0:NBIS + 1], scalar1=bis_[:, 2:3], scalar2=None,
                                                         op0=ALU.mult), reads=[Rb, R_const], writes=[Rb])
            kb.op("dve", lambda: nc.vector.tensor_tensor(out=bis_[:, 8:9], in0=bis_[:, 1:2], in1=bis_[:, 40:41], op=ALU.add),
                  reads=[Rb], writes=[Rb])
            yield
            for k in range(NBIS):
                mid = bis_[:, 8 + k:9 + k]
                kb.op("dve", lambda: nc.vector.tensor_scalar(out=S.junk[:, 0:L], in0=S.score[:, 0:L], scalar1=mid,
                                                             scalar2=0.0, op0=ALU.is_ge, op1=ALU.add,
                                                             accum_out=bis_[:, 4:5]),
                      reads=[Rb, S.R_score], writes=[S.R_junk, Rb])
                yield
                kb.op("dve", lambda: nc.vector.tensor_scalar(out=bis_[:, 5:6], in0=bis_[:, 4:5], scalar1=255.5, scalar2=-0.5,
                                                             op0=ALU.is_ge, op1=ALU.add), reads=[Rb], writes=[Rb])
                kb.op("dve", lambda: nc.vector.scalar_tensor_tensor(out=bis_[:, 9 + k:10 + k], in0=bis_[:, 5:6],
                                                                    scalar=bis_[:, 40 + k:41 + k], in1=mid,
                                                                    op0=ALU.mult, op1=ALU.add), reads=[Rb], writes=[Rb])
                yield
            kb.op("dve", lambda: nc.vector.tensor_sub(out=bis_[:, 6:7], in0=bis_[:, 8 + NBIS:9 + NBIS],
                                                      in1=bis_[:, 40 + NBIS:41 + NBIS]), reads=[Rb], writes=[Rb])
            if additive:
                kb.op("dve", lambda: nc.vector.tensor_scalar(out=S.m01[:, 0:L], in0=S.score[:, 0:L], scalar1=bis_[:, 6:7],
                                                             scalar2=NEGM, op0=ALU.is_lt, op1=ALU.mult),
                      reads=[Rb, S.R_score], writes=[S.R_m01])
            else:
                kb.op("dve", lambda: nc.vector.tensor_scalar(out=S.m01[:, 0:L], in0=S.score[:, 0:L], scalar1=bis_[:, 6:7],
                                                             scalar2=None, op0=ALU.is_ge),
                      reads=[Rb, S.R_score], writes=[S.R_m01])
            yield
            j = 0
            while j < nblk:
                n = min(8, nblk - j)
                bi, bk, br = bank()
                bkb = bk.bitcast(BF16)
                kb.mm([(lambda jj=jj: nc.tensor.transpose(out=bkb[:, jj * 128:(jj + 1) * 128],
                                                          in_=S.m01[:, (j + jj) * 128:(j + jj + 1) * 128],
                                                          identity=identb[:])) for jj in range(n)],
                      reads=[S.R_m01, R_const], writes=[br])
                kb.op("act", lambda: nc.scalar.copy(out=S.maskT[:, j:j + n, :],
                                                    in_=bkb[:, 0:n * 128].rearrange("p (a b) -> p a b", a=n)),
                      reads=[br], writes=[S.R_maskT])
                j += n
                yield

        def select_mask(L, nblk):
            for _ in gen_select(S0, L, nblk, False):
                pass

        def finalize(g, bo, ro, bd, rdn, cols):
            kb.op("dve", lambda: nc.vector.reciprocal(out=rd[:], in_=bd[:, :]), reads=[rdn], writes=[R_rd])
            kb.op("dve", lambda: nc.vector.tensor_tensor(out=tmpf[:], in0=bo[:, :], in1=rd[:], op=ALU.mult),
                  reads=[ro, R_rd], writes=[R_tmpf])
            kb.op("dve", lambda: nc.vector.tensor_tensor(out=gT[:, 4 * g:4 * g + 4, cols],
                                                         in0=tmpf[:].rearrange("p (h q) -> p h q", h=4),
                                                         in1=zsT[:, 4 * g:4 * g + 4, cols], op=ALU.mult),
                  reads=[R_tmpf, R_zs], writes=[R_g])

        def attn_block(g, lhsK, cols, near, tabs, mask_ap, bo, ro, bd, rdn, lhsV, first, rK, rV, addmask=None, R_mk=None):
            bi, bk, br = bank()
            nmm = 1 + (2 if near else 0) + (1 if addmask is not None else 0)
            fns = [lambda: nc.tensor.matmul(bk[:, :], lhsK, qT[:, 4 * g:4 * g + 4, cols], start=True, stop=(nmm == 1),
                                            skip_group_check=True)]
            if near:
                fns.append(lambda: nc.tensor.matmul(bk[:, :], identb[:], tabs[0][:, 4 * g:4 * g + 4, :], start=False,
                                                    stop=False, skip_group_check=True))
                fns.append(lambda: nc.tensor.matmul(bk[:, :], identb[:], tabs[1][:, 4 * g:4 * g + 4, :], start=False,
                                                    stop=(addmask is None), skip_group_check=True))
            rds = [rK, R_q, R_bias, R_const]
            if addmask is not None:
                fns.append(lambda: nc.tensor.matmul(bk[:, :], identb[:], addmask.unsqueeze(1).broadcast_to([128, 4, 128]),
                                                    start=False, stop=True, skip_group_check=True))
                rds.append(R_mk)
            kb.mm(fns, reads=rds, writes=[br])
            ei = cnt["e"] % 3
            cnt["e"] += 1
            kb.op("act", lambda: nc.scalar.activation(out=et[ei][:], in_=bk[:, :], func=AF.Exp, scale=SCALE),
                  reads=[br], writes=[R_et[ei]])
            if mask_ap is not None:
                pi = ei % 2
                kb.op("dve", lambda: nc.vector.tensor_tensor(out=pt[pi][:].rearrange("p (h q) -> p h q", h=4),
                                                             in0=et[ei][:].rearrange("p (h q) -> p h q", h=4),
                                                             in1=mask_ap.unsqueeze(1).broadcast_to([128, 4, 128]),
                                                             op=ALU.mult),
                      reads=[R_et[ei], R_maskT], writes=[R_pt[pi]])
                src, rs = pt[pi], R_pt[pi]
            else:
                src, rs = et[ei], R_et[ei]
            kb.mm([lambda: nc.tensor.matmul(bo[:, :], lhsV, src[:], start=first, stop=False, skip_group_check=True),
                   lambda: nc.tensor.matmul(bd[:, :], onesb[:], src[:], start=first, stop=False, skip_group_check=True)],
                  reads=[rs, rV, R_const], writes=[ro, rdn])

        def pin():
            i, b, r = bank()
            pinned.add(i)
            return i, b, r

        if not samp:
            def gen_sel_tile(t):
                ti = gi * 4 + t
                S = SB[t % 2]
                L = 128 * (ti + 1)
                cols = slice(t * 128, (t + 1) * 128)
                if ti < 2:
                    return
                nch = (L + 511) // 512
                for h in range(8):
                    for ch in range(nch):
                        w = min(512, L - ch * 512)
                        bi, bk, br = bank()
                        kb.mm([lambda: nc.tensor.matmul(bk[:, 0:w], qiT[:, h, cols], KiT[:, ch * 512:ch * 512 + w],
                                                        start=True, stop=True)], reads=[R_qi, R_KiT], writes=[br])
                        ri = cnt["r"] % 2
                        cnt["r"] += 1
                        kb.op("act", lambda: nc.scalar.activation(out=rtmp[ri][:, 0:w], in_=bk[:, 0:w], func=AF.Relu,
                                                                  scale=wab[:, t, h:h + 1]),
                              reads=[br, R_wab], writes=[R_rtmp[ri]])
                        sc = S.score[:, ch * 512:ch * 512 + w]
                        if h == 0:
                            kb.op("dve", lambda: nc.vector.tensor_scalar(out=sc, in0=rtmp[ri][:, 0:w],
                                                                         scalar1=wab[:, t, 8 + h:9 + h], scalar2=None,
                                                                         op0=ALU.mult),
                                  reads=[R_rtmp[ri], R_wab], writes=[S.R_score])
                        else:
                            kb.op("dve", lambda: nc.vector.scalar_tensor_tensor(out=sc, in0=rtmp[ri][:, 0:w],
                                                                                scalar=wab[:, t, 8 + h:9 + h], in1=sc,
                                                                                op0=ALU.mult, op1=ALU.add),
                                  reads=[R_rtmp[ri], R_wab], writes=[S.R_score])
                        yield
                kb.op("dve", lambda: nc.vector.tensor_reduce(out=S.bis[:, 1:2], in_=S.score[:, 0:L], axis=AX.X, op=ALU.min),
                      reads=[S.R_score], writes=[S.R_bis])
                kb.op("dve", lambda: nc.vector.tensor_tensor(out=S.score[:, L - 128:L], in0=S.score[:, L - 128:L],
                                                             in1=causneg[:], op=ALU.add),
                      reads=[R_const], writes=[S.R_score])
                yield
                for _ in gen_select(S, L, ti + 1, True):
                    yield

            def gen_attn_tile(t):
                ti = gi * 4 + t
                S = SB[t % 2]
                cols = slice(t * 128, (t + 1) * 128)
                for g in range(2):
                    io, bo, ro = pin()
                    idn, bd, rdn = pin()
                    for j in range(ti + 1):
                        near = ti - j <= 1
                        tabs = (biasT[:, 2 * (ti - j)], biasT[:, 2 * (ti - j) + 1]) if near else None
                        attn_block(g, KT[:, g, j * 128:(j + 1) * 128], cols, near, tabs, None, bo, ro, bd, rdn,
                                   VA[:, j, g, :], j == 0, R_KT, R_VA,
                                   addmask=(S.maskT[:, j, :] if ti >= 2 else None), R_mk=S.R_maskT)
                        yield
                    finalize(g, bo, ro, bd, rdn, cols)
                    pinned.discard(io)
                    pinned.discard(idn)
                    yield

            def nsteps_sel(t):
                ti = gi * 4 + t
                if ti < 2:
                    return 0
                L = 128 * (ti + 1)
                return 8 * ((L + 511) // 512) + 3 + 2 * NBIS + 1 + (ti + 8) // 8

            def zipper(ga, na, gb, nb):
                da = db = 0
                alive_a = alive_b = True
                while alive_a or alive_b:
                    fa = da / max(na, 1) if alive_a else 2.0
                    fb = db / max(nb, 1) if alive_b else 2.0
                    if fa <= fb:
                        try:
                            next(ga)
                            da += 1
                        except StopIteration:
                            alive_a = False
                    else:
                        try:
                            next(gb)
                            db += 1
                        except StopIteration:
                            alive_b = False

            for _ in gen_sel_tile(0):
                pass
            for t in range(4):
                ga = gen_attn_tile(t)
                na = 2 * (gi * 4 + t + 2)
                if t + 1 < 4:
                    zipper(ga, na, gen_sel_tile(t + 1), nsteps_sel(t + 1))
                else:
                    for _ in ga:
                        pass
        else:
            cols = slice(0, 128)
            ts = ar.get([128, 2, 8, 128], BF16)
            tab16 = ar.get([128, 8, 2, 64], BF16)
            ar_mark = ar.off
            tst2 = ar.get([128, 8, 128], F32)
            thf2 = ar.get([128, 8, 128], F32)
            stg = ar.get([128, 8, 64], F32)
            stgh = ar.get([128, 8, 64], F32)
            R_ts = Res()
            s_ts = kb.slot("ts", scratch=True)
            with nc.allow_non_contiguous_dma(reason="toeplitz"):
                src_ = dper[:, 128:128 + 383 * 128].rearrange("h (s x) -> s h x", x=383)[:, :, 0:128]
                kb.dma("sp", tst2[:], src_, s_ts, reads=[R_dper], writes=[R_ts])
                kb.op("dve", lambda: nc.vector.memset(stg[:], 0.0), writes=[R_ts])
                for pl in range(8):
                    off = 256 - pl
                    srcp = dper[:, off:off + 376 * 16].rearrange("h (pp x) -> pp h x", x=376)[:, :, 0:8]
                    kb.dma("sp", stg[112:128, pl, :].rearrange("p (h q) -> p h q", h=8), srcp, s_ts, reads=[R_dper], writes=[R_ts])
            kb.op("dve", lambda: nc.vector.tensor_tensor(out=tst2[:], in0=tst2[:],
                                                         in1=sampbm[:].unsqueeze(1).broadcast_to([128, 8, 128]),
                                                         op=ALU.mult), reads=[R_const], writes=[R_ts])
            kb.op("dve", lambda: nc.vector.tensor_tensor(out=tst2[:], in0=tst2[:],
                                                         in1=sampnb[:].unsqueeze(1).broadcast_to([128, 8, 128]),
                                                         op=ALU.add), reads=[R_const], writes=[R_ts])
            kb.op("dve", lambda: nc.vector.tensor_copy(out=ts[:, 0], in_=tst2[:]), writes=[R_ts])
            kb.op("dve", lambda: nc.vector.tensor_copy(out=thf2[:], in_=ts[:, 0]), writes=[R_ts])
            kb.op("dve", lambda: nc.vector.tensor_tensor(out=ts[:, 1], in0=tst2[:], in1=thf2[:], op=ALU.subtract),
                  writes=[R_ts])
            kb.op("dve", lambda: nc.vector.tensor_copy(out=tab16[:, :, 0, :], in_=stg[:]), writes=[R_ts])
            kb.op("dve", lambda: nc.vector.tensor_copy(out=stgh[:], in_=tab16[:, :, 0, :]), writes=[R_ts])
            kb.op("dve", lambda: nc.vector.tensor_tensor(out=tab16[:, :, 1, :], in0=stg[:], in1=stgh[:], op=ALU.subtract),
                  writes=[R_ts, R_bias])
            kb.barrier()
            ar.off = ar_mark
            wcol = ar.get([64, 16], F32)
            amat = ar.get([128, 64], F32)
            lq = [ar.get([64, 64], BF16) for _ in range(2)]
            kic = [ar.get([128, 2, 8, 64], BF16) for _ in range(2)]
            kitb = [ar.get([64, 2048], BF16) for _ in range(2)]
            rwt = [ar.get([64, 512], F32) for _ in range(4)]
            rwh = [ar.get([64, 512], BF16) for _ in range(4)]
            rwl = [ar.get([64, 512], BF16) for _ in range(4)]
            selb = ar.get([64, 248], BF16)
            R_rwh = [Res() for _ in range(4)]
            R_rwl = [Res() for _ in range(4)]
            kb.op("dve", lambda: nc.vector.tensor_copy(out=selb[:], in_=selbase[:]), reads=[R_const], writes=[R_const])
            R_wcol = Res()
            R_lq = [Res(), Res()]
            R_kic = [Res(), Res()]
            R_kitb = [Res(), Res()]
            R_rwt = [Res() for _ in range(4)]
            s_kic = [kb.slot(f"kic{i}", scratch=True) for i in range(2)]
            kb.op("dve", lambda: nc.vector.tensor_tensor(out=amat[:].rearrange("p (h q) -> p h q", h=8),
                                                         in0=dq[:].rearrange("p (h q) -> p h q", h=8),
                                                         in1=kist[0][:, 64:72].unsqueeze(2).broadcast_to([128, 8, 8]),
                                                         op=ALU.mult), reads=[R_kist[0], R_const], writes=[R_wcol])
            bi, bk, br = bank()
            kb.mm([lambda: nc.tensor.matmul(bk[0:64, 0:16], amat[:, :], bsel[:, :], start=True, stop=True)],
                  reads=[R_wcol, R_const], writes=[br])
            kb.op("dve", lambda: nc.vector.tensor_copy(out=wcol[:], in_=bk[0:64, 0:16]), reads=[br], writes=[R_wcol])
            sbk = [pin() for _ in range(5)]
            prepared = set()

            def prep(b):
                i = b % 2
                for hf in range(2):
                    kb.idma(kic[i][:, hf].rearrange("p a e -> p (a e)"), cache_ki, idxall[:, 2 * b + hf:2 * b + hf + 1],
                            s_kic[i], reads=[R_const], writes=[R_kic[i]])
                for hf in range(2):
                    bi, bk, br = bank()
                    bkb = bk.bitcast(BF16)
                    kb.mm([(lambda pl=pl: nc.tensor.transpose(out=bkb[0:64, pl * 128:(pl + 1) * 128], in_=kic[i][:, hf, pl, :],
                                                              identity=identb[:])) for pl in range(8)],
                          reads=[R_kic[i], R_const], writes=[br])
                    if kb.ev() == "act":
                        kb.op("act", lambda: nc.scalar.copy(out=kitb[i][:, hf * 1024:(hf + 1) * 1024], in_=bkb[0:64, :]),
                              reads=[br], writes=[R_kitb[i]])
                    else:
                        kb.op("dve", lambda: nc.vector.tensor_copy(out=kitb[i][:, hf * 1024:(hf + 1) * 1024], in_=bkb[0:64, :]),
                              reads=[br], writes=[R_kitb[i]])
                kb.op("dve", lambda: nc.vector.tensor_copy(out=lq[i][:].rearrange("p (h q) -> p h q", h=8),
                                                           in_=qiT[:, :, 8 * b:8 * b + 8]), reads=[R_qi], writes=[R_lq[i]])

            items = [(b, ch) for b in range(16) for ch in range(5)]

            def stageA(k):
                b, ch = items[k]
                i = b % 2
                if b not in prepared:
                    prepared.add(b)
                    prep(b)
                w = 512 if ch < 4 else 128
                rhs = kitb[i][:, ch * 512:(ch + 1) * 512] if ch < 4 else KiTn[:, :]
                bi, bk, br = pin()
                kb.mm([lambda: nc.tensor.matmul(bk[0:64, 0:w], lq[i][:, :], rhs, start=True, stop=True)],
                      reads=[R_lq[i], R_kitb[i], R_KiT], writes=[br])
                return bk, br, bi
            pend = {}
            LOOK = 1
            for k in range(min(LOOK, len(items))):
                pend[k] = stageA(k)
            for k in range(len(items)):
                if k + LOOK < len(items):
                    pend[k + LOOK] = stageA(k + LOOK)
                b, ch = items[k]
                w = 512 if ch < 4 else 128
                bk, br, bi_ = pend.pop(k)
                ri = k % 4
                kb.op("dve", lambda: nc.vector.tensor_scalar(out=rwt[ri][:, 0:w], in0=bk[0:64, 0:w], scalar1=0.0,
                                                             scalar2=wcol[:, b:b + 1], op0=ALU.max, op1=ALU.mult),
                      reads=[br, R_wcol], writes=[R_rwt[ri]])
                pinned.discard(bi_)
                kb.op("act", lambda: nc.scalar.copy(out=rwh[ri][:, 0:w], in_=rwt[ri][:, 0:w]), reads=[R_rwt[ri]],
                      writes=[R_rwh[ri]])
                kb.op("dve", lambda: nc.vector.tensor_tensor(out=rwl[ri][:, 0:w], in0=rwt[ri][:, 0:w], in1=rwh[ri][:, 0:w],
                                                             op=ALU.subtract), reads=[R_rwt[ri], R_rwh[ri]], writes=[R_rwl[ri]])
                sbi, sbk_, sbr = sbk[ch]
                kb.mm([lambda: nc.tensor.matmul(sbk_[:, 0:w], selb[:, 120 - 8 * b:248 - 8 * b], rwh[ri][:, 0:w],
                                                start=(b == 0), stop=False, skip_group_check=True),
                       lambda: nc.tensor.matmul(sbk_[:, 0:w], selb[:, 120 - 8 * b:248 - 8 * b], rwl[ri][:, 0:w],
                                                start=False, stop=(b == 15), skip_group_check=True)],
                      reads=[R_rwh[ri], R_rwl[ri], R_const], writes=[sbr])
            for ch in range(5):
                w = 512 if ch < 4 else 128
                sbi, sbk_, sbr = sbk[ch]
                kb.op("act", lambda: nc.scalar.copy(out=score[:, ch * 512:ch * 512 + w], in_=sbk_[:, 0:w]),
                      reads=[sbr], writes=[R_score])
                pinned.discard(sbi)
            L = 2176
            kb.op("dve", lambda: nc.vector.tensor_reduce(out=bis[:, 1:2], in_=score[:, 0:L], axis=AX.X, op=ALU.min),
                  reads=[R_score], writes=[R_bis])
            kb.op("dve", lambda: nc.vector.tensor_tensor(out=score[:, 2048:2176], in0=score[:, 2048:2176],
                                                         in1=sampneg[:], op=ALU.add), reads=[R_const], writes=[R_score])
            select_mask(L, 17)
            kb.barrier()
            ar.off = ar_mark
            acc = [(pin(), pin()) for _ in range(2)]
            for g in range(2):
                (io, bo, ro), (idn, bd, rdn) = acc[g]
                attn_block(g, KTn[:, g, :], cols, True, (ts[:, 0], ts[:, 1]), maskT[:, 16, :], bo, ro, bd, rdn,
                           VAn[:, g, :], True, R_KT, R_VA)
            kcb = [ar.get([128, 2, 8, 256], BF16) for _ in range(2)]
            vb = [ar.get([128, 2, 8, 256], BF16) for _ in range(2)]
            ktb = ar.get([128, 2, 2048], BF16)
            es = ar.get([128, 16, 64], BF16)
            ps_ = ar.get([128, 16, 64], BF16)
            R_kc = [Res(), Res()]
            R_vb = [Res(), Res()]
            R_ktb, R_es, R_ps = Res(), Res(), Res()
            s_kc = [kb.slot(f"kc{i}", scratch=True) for i in range(2)]
            s_vc = [kb.slot(f"vc{i}", scratch=True) for i in range(2)]
            il0, bl0, rl0 = pin()
            il1, bl1, rl1 = pin()
            lgb = [(bl0, rl0), (bl1, rl1)]

            def gather(b):
                i = b % 2
                for hf in range(2):
                    ix = idxall[:, 2 * b + hf:2 * b + hf + 1]
                    kb.idma(kcb[i][:, hf].rearrange("p a e -> p (a e)"), cache_k, ix, s_kc[i], reads=[R_const], writes=[R_kc[i]])
                    kb.idma(vb[i][:, hf].rearrange("p a e -> p (a e)"), cache_v, ix, s_vc[i], reads=[R_const], writes=[R_vb[i]])
            gather(0)
            for b in range(16):
                i = b % 2
                if b + 1 < 16:
                    gather(b + 1)
                for hf in range(2):
                    for g in range(2):
                        bi, bk, br = bank()
                        bkb = bk.bitcast(BF16)
                        kb.mm([(lambda pl=pl: nc.tensor.transpose(out=bkb[:, pl * 128:(pl + 1) * 128],
                                                                  in_=kcb[i][:, hf, pl, g * 128:(g + 1) * 128],
                                                                  identity=identb[:])) for pl in range(8)],
                              reads=[R_kc[i], R_const], writes=[br])
                        if kb.ev() == "act":
                            kb.op("act", lambda: nc.scalar.copy(out=ktb[:, g, hf * 1024:(hf + 1) * 1024], in_=bkb[:, :]),
                                  reads=[br], writes=[R_ktb])
                        else:
                            kb.op("dve", lambda: nc.vector.tensor_copy(out=ktb[:, g, hf * 1024:(hf + 1) * 1024], in_=bkb[:, :]),
                                  reads=[br], writes=[R_ktb])
                for hb in range(2):
                    bl, rl = lgb[hb]
                    blv = bl[:, :].rearrange("p (j c) -> p j c", j=8)
                    fns = []
                    for jj in range(8):
                        j = hb * 8 + jj
                        for g in range(2):
                            fns.append(lambda j=j, jj=jj, g=g, blv=blv: nc.tensor.matmul(
                                blv[:, jj, g * 32:(g + 1) * 32], ktb[:, g, j * 128:(j + 1) * 128],
                                qT[:, 4 * g:4 * g + 4, 8 * b:8 * b + 8], start=True, stop=(hb == 0), skip_group_check=True))
                            if hb == 1:
                                for hl in range(2):
                                    fns.append(lambda jj=jj, g=g, hl=hl, blv=blv: nc.tensor.matmul(
                                        blv[:, jj, g * 32:(g + 1) * 32], identb[:],
                                        tab16[:, jj, hl, :].rearrange("p (h q) -> p h q", h=8)[:, 4 * g:4 * g + 4, :],
                                        start=False, stop=(hl == 1), skip_group_check=True))
                    kb.mm(fns, reads=[R_ktb, R_q, R_bias, R_ts, R_const], writes=[rl])
                    kb.op("act", lambda: nc.scalar.activation(out=es[:, hb * 8:(hb + 1) * 8, :], in_=blv, func=AF.Exp,
                                                              scale=SCALE), reads=[rl], writes=[R_es])
                kb.op("dve", lambda: nc.vector.tensor_tensor(
                    out=ps_[:].rearrange("p j (h q) -> p j h q", h=8), in0=es[:].rearrange("p j (h q) -> p j h q", h=8),
                    in1=maskT[:, 0:16, 8 * b:8 * b + 8].unsqueeze(2).broadcast_to([128, 16, 8, 8]), op=ALU.mult),
                    reads=[R_es, R_maskT], writes=[R_ps])
                for g in range(2):
                    (io, bo, ro), (idn, bd, rdn) = acc[g]
                    bov = bo[:, :].rearrange("p (h q) -> p h q", h=4)[:, :, 8 * b:8 * b + 8]
                    bdv = bd[:, :].rearrange("p (h q) -> p h q", h=4)[:, :, 8 * b:8 * b + 8]
                    fns = []
                    for j in range(16):
                        fns.append(lambda j=j, g=g, bov=bov: nc.tensor.matmul(
                            bov, vb[i][:, j // 8, j % 8, g * 128:(g + 1) * 128], ps_[:, j, g * 32:(g + 1) * 32],
                            start=False, stop=False, skip_group_check=True))
                        fns.append(lambda j=j, g=g, bdv=bdv: nc.tensor.matmul(bdv, onesb[:], ps_[:, j, g * 32:(g + 1) * 32],
                                                                            start=False, stop=False, skip_group_check=True))
                    kb.mm(fns, reads=[R_ps, R_vb[i], R_const], writes=[ro, rdn])
            pinned.discard(il0)
            pinned.discard(il1)
            for g in range(2):
                (io, bo, ro), (idn, bd, rdn) = acc[g]
                finalize(g, bo, ro, bd, rdn, cols)
                pinned.discard(io)
                pinned.discard(idn)

        jobs = []
        for b4 in range(4):
            def ua_job(wv, wr, b4=b4):
                formF(wv, wr, 4, 128, copy_evac(lambda c: m_T[:, b4 * 4 + c, :], R_m), kcn=8, src=gT, R_src=R_g)
            jobs.append(((f"ua{b4}", [(0, w_ua_v[:, :, b4 * 512:(b4 + 1) * 512])], 8, 512), ua_job))
        sgt = [ar.get([128, NT], F32) for _ in range(2)]
        R_sgt = [Res(), Res()]
        sgc = {"n": 0}
        for b4 in range(4):
            def ga_job(wv, wr, b4=b4):
                def ev(c, ps, br):
                    i = sgc["n"] % 2
                    sgc["n"] += 1
                    kb.op("act", lambda: nc.scalar.activation(out=sgt[i][:], in_=ps, func=AF.Sigmoid), reads=[br], writes=[R_sgt[i]])
                    kb.op("dve", lambda: nc.vector.tensor_tensor(out=m_T[:, b4 * 4 + c, :], in0=m_T[:, b4 * 4 + c, :],
                                                                 in1=sgt[i][:], op=ALU.mult), reads=[R_sgt[i]], writes=[R_m])
                formF(wv, wr, 4, 128, ev)
            jobs.append(((f"ga{b4}", [(0, w_in_v[:, :, C_GA + b4 * 512: C_GA + (b4 + 1) * 512])], 16, 512), ga_job))
        run_jobs(jobs, ("gg0", [], 16, 512))
        kb.barrier()

        ar.reset()
        NB_ = 16 if samp else 1
        TW = 38 if samp else NT + 30
        uT = ar.get([128, 8, NB_ * TW], BF16)
        zcT = ar.get([128, 8, NT], BF16)
        dwf = ar.get([128, 8, NT], F32)
        aT = ar.get([128, 8, NT], BF16)
        cT = ar.get([128, 8, NT], BF16)
        diag = [ar.get([128, 31, 128], BF16) for _ in range(2)]
        sqt = [ar.get([128, NT], F32) for _ in range(2)]
        lnm = ar.get([128, NT], F32)
        lnv = ar.get([128, NT], F32)
        lnr = ar.get([128, NT], F32)
        t1 = [ar.get([128, NT], F32) for _ in range(2)]
        sgt2 = [ar.get([128, NT], F32) for _ in range(2)]
        tt2 = [ar.get([128, NT], F32) for _ in range(2)]
        R_u, R_zc, R_dw, R_a, R_c, R_ln = Res(), Res(), Res(), Res(), Res(), Res()
        R_diag = [Res(), Res()]
        R_sq = [Res(), Res()]
        R_t1 = [Res(), Res()]
        R_sg2 = [Res(), Res()]
        R_tt2 = [Res(), Res()]
        need_tok = samp or gi == 3
        if need_tok:
            sgtok = ar.get([128, 1024], F32)
            utok = ar.get([128, 1024], F32)
            R_sgtok, R_utok = Res(), Res()
            s_ut = kb.slot(f"ut", scratch=True)
        if samp:
            uv = uT[:].rearrange("p c (b w) -> p c b w", b=16)

            def ucols(c):
                return uv[:, c, :, 30:38]
            stt = ar.get([120, 1024], F32)
            R_stt = Res()
            s_stt = kb.slot("stt", scratch=True)
            for q4 in range(4):
                kb.dma("sp", stt[:], state[q4 * 4:(q4 + 1) * 4].rearrange("b r c -> (b r) c"), s_stt, writes=[R_stt])
                for c2 in range(2):
                    bi, bk, br = bank()
                    kb.mm([(lambda k=k: nc.tensor.transpose(out=bk[:, k * 120:(k + 1) * 120],
                                                            in_=stt[:, (c2 * 4 + k) * 128:(c2 * 4 + k + 1) * 128],
                                                            identity=identf[0:120, 0:120])) for k in range(4)],
                          reads=[R_stt, R_const], writes=[br])
                    kb.op("dve", lambda: nc.vector.tensor_copy(
                        out=uv[:, c2 * 4:(c2 + 1) * 4, q4 * 4:(q4 + 1) * 4, 0:30],
                        in_=bk[:, 0:480].rearrange("p (k b r) -> p k b r", k=4, b=4)), reads=[br], writes=[R_u])
            kb.dma("sp", convs[:, 0:22, :], state[:, 8:30, :], s_stt, is_out=True)
        else:
            def ucols(c):
                return uT[:, c, 30:30 + NT]
            kb.op("dve", lambda: nc.vector.tensor_copy(out=uT[:, :, 0:30], in_=uhalo[:]), reads=[R_const, R_uh], writes=[R_u])

        jobs = []
        for b2 in range(2):
            def gg_job(wv, wr, b2=b2):
                formF(wv, wr, 4, 128, copy_evac(lambda c: ucols(b2 * 4 + c), R_u, AF.Sigmoid))
                if need_tok:
                    t = ntile - 1
                    bi, bk, br = bank()
                    kb.mm([(lambda kc=kc: nc.tensor.matmul(bk[:, 0:512], xn[:, kc, t * 128:(t + 1) * 128], wv[:, kc, :],
                                                           start=(kc == 0), stop=(kc == 15))) for kc in range(16)],
                          reads=[wr, R_xn], writes=[br])
                    kb.op("act", lambda: nc.scalar.activation(out=sgtok[:, b2 * 512:(b2 + 1) * 512], in_=bk[:, 0:512],
                                                              func=AF.Sigmoid), reads=[br], writes=[R_sgtok])
            jobs.append(((f"gg{b2}", [(0, w_in_v[:, :, C_GG + b2 * 512: C_GG + (b2 + 1) * 512])], 16, 512), gg_job))
        for b2 in range(2):
            def gv_job(wv, wr, b2=b2):
                def ev(c, ps, br):
                    kb.op("dve", lambda: nc.vector.tensor_tensor(out=ucols(b2 * 4 + c), in0=ps if not samp else
                                                                 ps.rearrange("p (b t) -> p b t", b=16),
                                                                 in1=ucols(b2 * 4 + c), op=ALU.mult), reads=[br], writes=[R_u])
                formF(wv, wr, 4, 128, ev)
                if need_tok:
                    t = ntile - 1
                    bi, bk, br = bank()
                    kb.mm([(lambda kc=kc: nc.tensor.matmul(bk[:, 0:512], xn[:, kc, t * 128:(t + 1) * 128], wv[:, kc, :],
                                                           start=(kc == 0), stop=(kc == 15))) for kc in range(16)],
                          reads=[wr, R_xn], writes=[br])
                    kb.op("dve", lambda: nc.vector.tensor_tensor(out=utok[:, b2 * 512:(b2 + 1) * 512], in0=bk[:, 0:512],
                                                                 in1=sgtok[:, b2 * 512:(b2 + 1) * 512], op=ALU.mult),
                          reads=[br, R_sgtok], writes=[R_utok])
            jobs.append(((f"gv{b2}", [(0, w_in_v[:, :, C_GV + b2 * 512: C_GV + (b2 + 1) * 512])], 16, 512), gv_job))
        def conv_job():
            if need_tok:
                if samp:
                    for b in range(16):
                        kb.dma("sp", convs[b, 22:30, :], utok[8 * b:8 * b + 8, :], s_ut, reads=[R_utok], is_out=True)
                else:
                    kb.dma("sp", convp[:, :], utok[98:128, :], s_ut, reads=[R_utok], is_out=True)
            if not samp:
                kb.op("dve", lambda: nc.vector.tensor_copy(out=uhalo[:], in_=uT[:, :, NT:NT + 30]), reads=[R_u], writes=[R_uh])
            for c in range(8):
                di = c % 2
                kb.dma("sp", diag[di][:].rearrange("p a b -> p (a b)"), dsc[c], s_dg[di], reads=[R_dsc[c]], writes=[R_diag[di]])
                bi, bk, br = bank()
                if samp:
                    rhsf = lambda j: uv[:, c, :, j:j + 8]
                else:
                    rhsf = lambda j: uT[:, c, j:j + NT]
                kb.mm([(lambda j=j: nc.tensor.matmul(bk[:, 0:NT], diag[di][:, j, :], rhsf(j), start=(j == 0), stop=(j == 30)))
                       for j in range(31)], reads=[R_diag[di], R_u], writes=[br])
                kb.op("act", lambda: nc.scalar.activation(out=dwf[:, c, :], in_=bk[:, 0:NT], func=AF.Identity,
                                                          bias=cbT[:, c:c + 1], scale=1.0), reads=[br, R_const], writes=[R_dw])
            im, bm, rm = pin()
            iq, bq, rq = pin()
            for c in range(8):
                si = c % 2
                kb.op("act", lambda: nc.scalar.activation(out=sqt[si][:], in_=dwf[:, c, :], func=AF.Square),
                      reads=[R_dw], writes=[R_sq[si]])
                kb.mm([lambda: nc.tensor.matmul(bm[:, 0:NT], onesf[:], dwf[:, c, :], start=(c == 0), stop=(c == 7),
                                                skip_group_check=True)], reads=[R_dw, R_const], writes=[rm])
                kb.mm([lambda: nc.tensor.matmul(bq[:, 0:NT], onesf[:], sqt[si][:], start=(c == 0), stop=(c == 7),
                                                skip_group_check=True)], reads=[R_sq[si], R_const], writes=[rq])
            kb.op("dve", lambda: nc.vector.tensor_copy(out=lnm[:], in_=bm[:, 0:NT]), reads=[rm], writes=[R_ln])
            kb.op("dve", lambda: nc.vector.scalar_tensor_tensor(out=lnv[:], in0=lnm[:], scalar=-1.0, in1=lnm[:],
                                                                op0=ALU.mult, op1=ALU.mult), reads=[R_ln], writes=[R_ln])
            kb.op("dve", lambda: nc.vector.tensor_tensor(out=lnv[:], in0=bq[:, 0:NT], in1=lnv[:], op=ALU.add),
                  reads=[rq, R_ln], writes=[R_ln])
            kb.op("act", lambda: nc.scalar.activation(out=lnv[:], in_=lnv[:], func=AF.Sqrt, bias=epsT[:, 0:1], scale=1.0),
                  reads=[R_ln, R_const], writes=[R_ln])
            kb.op("dve", lambda: nc.vector.reciprocal(out=lnr[:], in_=lnv[:]), reads=[R_ln], writes=[R_ln])
            pinned.discard(im)
            pinned.discard(iq)
            for c in range(8):
                i = c % 2
                kb.op("dve", lambda: nc.vector.tensor_tensor(out=t1[i][:], in0=dwf[:, c, :], in1=lnm[:], op=ALU.subtract),
                      reads=[R_dw, R_ln], writes=[R_t1[i]])
                kb.op("dve", lambda: nc.vector.tensor_tensor(out=t1[i][:], in0=t1[i][:], in1=lnr[:], op=ALU.mult),
                      reads=[R_ln], writes=[R_t1[i]])
                kb.op("act", lambda: nc.scalar.activation(out=aT[:, c, :], in_=t1[i][:], func=AF.Silu,
                                                          bias=cnbT[:, c:c + 1], scale=cngT[:, c:c + 1]),
                      reads=[R_t1[i], R_const], writes=[R_a])
        jobs.append((None, conv_job))
        for b2 in range(2):
            jobs.append(((f"zc{b2}", [(0, w_in_v[:, :, C_ZC + b2 * 512: C_ZC + (b2 + 1) * 512])], 16, 512),
                         lambda wv, wr, b2=b2: formF(wv, wr, 4, 128, copy_evac(lambda c: zcT[:, b2 * 4 + c, :], R_zc, AF.Silu))))

        for b2 in range(2):
            def pw_job(wv, wr, b2=b2):
                def ev(c, ps, br):
                    cc = b2 * 4 + c
                    kb.op("dve", lambda: nc.vector.scalar_tensor_tensor(out=cT[:, cc, :], in0=ps, scalar=bpwT[:, cc:cc + 1],
                                                                        in1=zcT[:, cc, :], op0=ALU.add, op1=ALU.mult),
                          reads=[br, R_zc, R_const], writes=[R_c])
                formF(wv, wr, 4, 128, ev, kcn=8, src=aT, R_src=R_a)
            jobs.append(((f"pw{b2}", [(0, w_pw_v[:, :, b2 * 512:(b2 + 1) * 512])], 8, 512), pw_job))
        bcb = {}
        for b4 in range(4):
            def uc_job(wv, wr, b4=b4):
                for c in range(4):
                    bi, bk, br = pin()
                    kb.mm([(lambda kc=kc: nc.tensor.matmul(bk[:, 0:NT], wv[:, kc, c * 128:(c + 1) * 128], cT[:, kc, :],
                                                           start=(kc == 0), stop=(kc == 7))) for kc in range(8)],
                          reads=[wr, R_c], writes=[br])
                    bcb[(b4, c)] = (bi, bk, br)
            jobs.append(((f"uc{b4}", [(0, w_uc_v[:, :, b4 * 512:(b4 + 1) * 512])], 8, 512), uc_job))

            def gc_job(wv, wr, b4=b4):
                def ev(c, ps, br):
                    i = sgc["n"] % 2
                    sgc["n"] += 1
                    bi2, bk2, br2 = bcb.pop((b4, c))
                    kb.op("act", lambda: nc.scalar.activation(out=sgt2[i][:], in_=ps, func=AF.Sigmoid), reads=[br], writes=[R_sg2[i]])
                    kb.op("dve", lambda: nc.vector.tensor_tensor(out=tt2[i][:], in0=bk2[:, 0:NT], in1=sgt2[i][:], op=ALU.mult),
                          reads=[br2, R_sg2[i]], writes=[R_tt2[i]])
                    pinned.discard(bi2)
                    kb.op("dve", lambda: nc.vector.tensor_tensor(out=m_T[:, b4 * 4 + c, :], in0=m_T[:, b4 * 4 + c, :],
                                                                 in1=tt2[i][:], op=ALU.add), reads=[R_tt2[i]], writes=[R_m])
                formF(wv, wr, 4, 128, ev)
            jobs.append(((f"gc{b4}", [(0, w_in_v[:, :, C_GC + b4 * 512: C_GC + (b4 + 1) * 512])], 16, 512), gc_job))
        run_jobs(jobs, ("wo0", [], 16, 512))
        kb.barrier()

        ar.reset()
        yb = [ar.get([128, D], F32) for _ in range(ntile)]
        gbc = ar.get([128, D], F32)
        jk = ar.get([128, D], F32)
        st2 = ar.get([128, 16], F32)
        R_yb = [Res() for _ in range(ntile)]
        R_gbc, R_jk = Res(), Res()
        R_st2 = [Res() for _ in range(ntile)]
        s_y = [kb.slot(f"y_{i}", scratch=True) for i in range(ntile)]
        s_g = kb.slot(f"g", scratch=True)
        for t in range(ntile):
            kb.dma("sp", yb[t][:], x[tok0 + t * 128: tok0 + (t + 1) * 128, :], s_y[t], writes=[R_yb[t]])
        kb.dma("sp", gbc[:], final_g.broadcast_to([128, D]), s_g, writes=[R_gbc])

        def final_norm(t):
            c0 = 4 * t
            kb.op("act", lambda: nc.scalar.activation(out=jk[:], in_=yb[t][:], func=AF.Square, accum_out=st2[:, c0:c0 + 1]),
                  reads=[R_yb[t]], writes=[R_jk, R_st2[t]])
            kb.op("act", lambda: nc.scalar.activation(out=st2[:, c0 + 1:c0 + 2], in_=st2[:, c0:c0 + 1], func=AF.Sqrt,
                                                      bias=epsT[:, 0:1], scale=1.0 / D), reads=[R_st2[t], R_const],
                  writes=[R_st2[t]])
            kb.op("dve", lambda: nc.vector.reciprocal(out=st2[:, c0 + 2:c0 + 3], in_=st2[:, c0 + 1:c0 + 2]), reads=[R_st2[t]],
                  writes=[R_st2[t]])
            kb.op("dve", lambda: nc.vector.scalar_tensor_tensor(out=yb[t][:], in0=yb[t][:], scalar=st2[:, c0 + 2:c0 + 3],
                                                                in1=gbc[:], op0=ALU.mult, op1=ALU.mult),
                  reads=[R_st2[t], R_gbc], writes=[R_yb[t]])
            kb.dma("sp", y[tok0 + t * 128: tok0 + (t + 1) * 128, :], yb[t][:], s_y[t], reads=[R_yb[t]], is_out=True)

        jobs = []
        for b4 in range(4):
            def wo_job(wv, wr, b4=b4):
                for t in range(ntile):
                    bi, bk, br = bank()
                    kb.mm([(lambda kc=kc: nc.tensor.matmul(bk[:, 0:512], m_T[:, kc, t * 128:(t + 1) * 128], wv[:, kc, :],
                                                           start=(kc == 0), stop=(kc == 15))) for kc in range(16)],
                          reads=[wr, R_m], writes=[br])
                    ysl = yb[t][:, b4 * 512:(b4 + 1) * 512]
                    kb.op("dve", lambda: nc.vector.tensor_tensor(out=ysl, in0=bk[:, 0:512], in1=ysl, op=ALU.add),
                          reads=[br], writes=[R_yb[t]])
                    if b4 == 3:
                        final_norm(t)
            jobs.append(((f"wo{b4}", [(0, w_out_v[:, :, b4 * 512:(b4 + 1) * 512])], 16, 512), wo_job))
        if gi < 4:
            pbn = p1_alloc(gi + 1)
            run_jobs(jobs, ("q0", [], 16, 512), hook=(1, lambda: p1_norm(gi + 1, pbn)))
            p1_tr(gi + 1, pbn)
        else:
            run_jobs(jobs, None)
        kb.barrier()

    for gi in range(5):
        group(gi)
    last = {}
    for sem, val in kb.out_toks:
        last[id(sem)] = (sem, max(val, last.get(id(sem), (None, 0))[1]))
    for tok in last.values():
        kb.eng["sp"].wait(tok)
    return nc


_NC = None


def kernel(x_prompt, x_sample, cache_k, cache_v, cache_kidx, state_conv, page_table, ln_g, w_in, conv_w, conv_b,
           conv_norm_g, conv_norm_b, w_pw, b_pw, w_up_attn, w_up_conv, w_out, rel_bias, final_g):
    global _NC
    if _NC is None:
        _NC = build()
    nc = _NC
    f = lambda a: np.ascontiguousarray(np.asarray(a, dtype=np.float32))
    consts = host_consts()
    ck = f(cache_k)[0].reshape(NPHYS * 16, 2048)
    cv = f(cache_v)[0].reshape(NPHYS * 16, 2048)
    cki = f(cache_kidx)[0].reshape(NPHYS * 16, 512)
    shared = {
        "cache_k": ck, "cache_v": cv, "cache_ki": cki, "ln_g": f(ln_g)[0], "w_in": f(w_in)[0], "conv_w": f(conv_w)[0],
        "conv_b": f(conv_b)[0], "cn_g": f(conv_norm_g)[0], "cn_b": f(conv_norm_b)[0], "w_pw": f(w_pw)[0],
        "b_pw": f(b_pw)[0], "w_ua": f(w_up_attn)[0], "w_uc": f(w_up_conv)[0], "w_out": f(w_out)[0],
        "rel_bias": f(rel_bias), "final_g": f(final_g).reshape(1, D),
    }
    for k, v in consts.items():
        shared["c_" + k] = v
    xp = f(x_prompt)
    xs = f(x_sample)
    st = f(state_conv)[0]
    pt = np.ascontiguousarray(np.asarray(page_table, dtype=np.int32))
    in_maps = []
    for c in range(8):
        m = dict(shared)
        m["x"] = np.ascontiguousarray(np.concatenate([xp[c], xs[16 * c:16 * c + 16].reshape(128, D)], axis=0))
        m["state"] = np.ascontiguousarray(st[16 * c:16 * c + 16])
        m["ptab"] = np.ascontiguousarray(pt[16 * c:16 * c + 16].reshape(1, 256))
        in_maps.append(m)
    res = run_bass_kernel_spmd(nc, in_maps, core_ids=list(range(8)))
    R = res.results
    y_p = np.stack([R[c]["y"][0:2048] for c in range(8)])
    y_s = np.concatenate([R[c]["y"][2048:2176].reshape(16, 8, D) for c in range(8)], axis=0)
    k_p = np.stack([R[c]["kk"][0:2048].reshape(2048, 2, 128) for c in range(8)])[None]
    v_p = np.stack([R[c]["vv"][0:2048].reshape(2048, 2, 128) for c in range(8)])[None]
    ki_p = np.stack([R[c]["kio"][0:2048] for c in range(8)])[None]
    cp = np.stack([R[c]["convp"] for c in range(8)])[None]
    k_s = np.concatenate([R[c]["kk"][2048:2176].reshape(16, 8, 2, 128) for c in range(8)], axis=0)[None]
    v_s = np.concatenate([R[c]["vv"][2048:2176].reshape(16, 8, 2, 128) for c in range(8)], axis=0)[None]
    ki_s = np.concatenate([R[c]["kio"][2048:2176].reshape(16, 8, 64) for c in range(8)], axis=0)[None]
    cs = np.concatenate([R[c]["convs"] for c in range(8)], axis=0)[None]
    return (y_p.astype(np.float32), y_s.astype(np.float32), k_p.astype(np.float32), v_p.astype(np.float32),
            ki_p.astype(np.float32), cp.astype(np.float32), k_s.astype(np.float32), v_s.astype(np.float32),
            ki_s.astype(np.float32), cs.astype(np.float32))
```

```python
import math
import numpy as np
import concourse.bass as bass
import concourse.mybir as mybir
from concourse.bass_utils import run_bass_kernel_spmd

F32 = mybir.dt.float32
BF16 = mybir.dt.bfloat16
I32 = mybir.dt.int32
U8 = mybir.dt.uint8
AF = mybir.ActivationFunctionType
ALU = mybir.AluOpType
AX = mybir.AxisListType
ET = mybir.EngineType

D = 2048
DIN = 10312
C_Q, C_K, C_V, C_ZA, C_QI, C_KI, C_WI, C_GV, C_GG, C_ZC, C_GA, C_GC = (
    0, 1024, 1280, 1536, 2560, 3072, 3136, 3144, 4168, 5192, 6216, 8264)
NPHYS = 2560
NEGB = -30000.0
SCALE = 128.0 ** -0.5
SQ128 = 128.0 ** 0.5
NBIS = 18
EPS = 1e-6


class Res:
    __slots__ = ("w", "r")

    def __init__(self):
        self.w = None
        self.r = {}


class Eng:
    def __init__(self, nc, e, name):
        self.e = e
        self.sem = nc.alloc_semaphore("p_" + name)
        self.n = 0
        self.seen = {}

    def wait(self, tok):
        if tok is None:
            return
        sem, val = tok
        k = id(sem)
        if self.seen.get(k, 0) >= val:
            return
        self.e.wait_ge(sem, val)
        self.seen[k] = val


class Slot:
    def __init__(self, nc, name, scratch=False):
        self.sem = nc.alloc_semaphore("d_" + name)
        self.n = 0
        self.scratch = scratch


class KB:
    def __init__(self, nc):
        self.nc = nc
        self.eng = {
            "pe": Eng(nc, nc.tensor, "pe"), "act": Eng(nc, nc.scalar, "act"),
            "dve": Eng(nc, nc.vector, "dve"), "pool": Eng(nc, nc.gpsimd, "pool"),
            "sp": Eng(nc, nc.sync, "sp"),
        }
        self.slots = []
        self.slotd = {}
        self.out_toks = []
        self.flip = 0

    def slot(self, name, scratch=False):
        if name in self.slotd:
            return self.slotd[name]
        s = Slot(self.nc, name, scratch)
        self.slots.append(s)
        self.slotd[name] = s
        return s

    def _deps(self, E, reads, writes):
        for r in reads:
            if r.w:
                E.wait(r.w)
        for w in writes:
            if w.w:
                E.wait(w.w)
            for t in w.r.values():
                E.wait(t)

    def _mark(self, tok, reads, writes):
        for r in reads:
            r.r[id(tok[0])] = tok
        for w in writes:
            w.w = tok
            w.r = {}

    def op(self, en, fn, reads=(), writes=()):
        E = self.eng[en]
        self._deps(E, reads, writes)
        ins = fn()
        E.n += 1
        ins.then_inc(E.sem, 1)
        tok = (E.sem, E.n)
        self._mark(tok, reads, writes)
        return tok

    def mm(self, fns, reads=(), writes=()):
        E = self.eng["pe"]
        self._deps(E, reads, writes)
        ins = None
        for f in fns:
            ins = f()
        E.n += 1
        ins.then_inc(E.sem, 1)
        tok = (E.sem, E.n)
        self._mark(tok, reads, writes)
        return tok

    def dma(self, q, out, in_, slot, reads=(), writes=(), is_out=False, **kw):
        E = self.eng[q]
        self._deps(E, reads, writes)
        ins = E.e.dma_start(out=out, in_=in_, **kw)
        slot.n += 16
        ins.then_inc(slot.sem, 16)
        tok = (slot.sem, slot.n)
        self._mark(tok, reads, writes)
        if is_out:
            self.out_toks.append(tok)
        return tok

    def idma(self, out, in_, idx_ap, slot, reads=(), writes=()):
        E = self.eng["pool"]
        self._deps(E, reads, writes)
        ins = self.nc.gpsimd.indirect_dma_start(out=out, out_offset=None, in_=in_,
                                                in_offset=bass.IndirectOffsetOnAxis(ap=idx_ap, axis=0))
        slot.n += 16
        ins.then_inc(slot.sem, 16)
        tok = (slot.sem, slot.n)
        self._mark(tok, reads, writes)
        return tok

    def barrier(self):
        toks = [(E.sem, E.n) for k, E in self.eng.items() if E.n > 0 and k != "sp"]
        toks += [(s.sem, s.n) for s in self.slots if s.scratch and s.n > 0]
        for E in self.eng.values():
            for t in toks:
                E.wait(t)

    def ev(self):
        self.flip ^= 1
        return "act" if self.flip else "dve"


def t5_bucket_np(n):
    n = np.maximum(n, 0)
    max_exact = 16
    ratio = np.log(np.maximum(n, 1).astype(np.float32) / np.float32(max_exact)) / np.float32(math.log(128 / max_exact))
    large = np.minimum(max_exact + (ratio.astype(np.float32) * np.float32(16)).astype(np.int32), 31)
    return np.where(n < max_exact, n, large)


def host_consts():
    c = {}
    c["identf"] = np.eye(128, dtype=np.float32)
    m = np.arange(384)
    n = m - 128
    oh = np.zeros((33, 384), np.float32)
    bk = t5_bucket_np(n)
    for i in range(384):
        if n[i] >= 0:
            oh[bk[i], i] = 1.0
        else:
            oh[32, i] = 1.0
    c["oh33"] = oh
    q = np.arange(128)[:, None]
    s = np.arange(128)[None, :]
    c["causneg"] = np.where(s <= q, 0.0, -1e30).astype(np.float32)
    same = (q // 8) == (s // 8)
    c["sampneg"] = np.where(same & ((s % 8) <= (q % 8)), 0.0, -1e30).astype(np.float32)
    c["sampbm"] = same.astype(np.float32)
    c["sampnb"] = np.where(same, 0.0, NEGB * SQ128).astype(np.float32)
    tok = np.arange(128)
    dq = np.zeros((128, 8, 8), np.float32)
    for t in range(128):
        dq[t, :, t % 8] = 1.0
    c["dq"] = dq.reshape(128, 64)
    bs = np.zeros((128, 16), np.float32)
    bs[tok, tok // 8] = 1.0
    c["bsel"] = bs
    sb = np.zeros((64, 248), np.float32)
    for hq in range(64):
        sb[hq, 120 + hq % 8] = 1.0
    c["selbase"] = sb
    c["oh8T"] = (np.arange(128)[None, :] // 16 == np.arange(8)[:, None]).astype(np.float32)
    c["fvec"] = np.tile((2.0 ** -(np.arange(32, dtype=np.float64) + 1)).astype(np.float32)[None, :], (128, 1))
    c["pidmat"] = np.tile((np.arange(128, dtype=np.float32) % 16)[:, None], (1, 32))
    return c


CONST_SHAPES = {"identf": [128, 128], "oh33": [33, 384], "causneg": [128, 128], "sampneg": [128, 128],
                "sampbm": [128, 128], "sampnb": [128, 128], "dq": [128, 64], "bsel": [128, 16],
                "selbase": [64, 248], "pidmat": [128, 32], "fvec": [128, 32], "oh8T": [8, 128]}


def build():
    nc = bass.Bass("TRN2", target_bir_lowering=False)
    kb = KB(nc)

    def din(name, shape, dt=F32):
        return nc.dram_tensor(name, shape, dt, kind="ExternalInput").ap()

    def dout(name, shape):
        return nc.dram_tensor(name, shape, F32, kind="ExternalOutput").ap()

    x = din("x", [2176, D])
    cache_k = din("cache_k", [NPHYS * 16, 2048])
    cache_v = din("cache_v", [NPHYS * 16, 2048])
    cache_ki = din("cache_ki", [NPHYS * 16, 512])
    state = din("state", [16, 30, 1024])
    ptab = din("ptab", [1, 256], I32)
    ln_g = din("ln_g", [D])
    w_in = din("w_in", [D, DIN])
    conv_w = din("conv_w", [31, 1024])
    conv_b = din("conv_b", [1024])
    cn_g = din("cn_g", [1024])
    cn_b = din("cn_b", [1024])
    w_pw = din("w_pw", [1024, 1024])
    b_pw = din("b_pw", [1024])
    w_ua = din("w_ua", [1024, D])
    w_uc = din("w_uc", [1024, D])
    w_out = din("w_out", [D, D])
    rel_bias = din("rel_bias", [32, 8])
    final_g = din("final_g", [1, D])
    cin = {k: din("c_" + k, v) for k, v in CONST_SHAPES.items()}

    y = dout("y", [2176, D])
    kk = dout("kk", [2176, 256])
    vv = dout("vv", [2176, 256])
    kio = dout("kio", [2176, 64])
    convp = dout("convp", [30, 1024])
    convs = dout("convs", [16, 30, 1024])
    dper = nc.dram_tensor("dper", [8, 129 * 384], F32, kind="Internal").ap()

    w_in_v = w_in.rearrange("(kc p) f -> p kc f", p=128)
    w_out_v = w_out.rearrange("(kc p) f -> p kc f", p=128)
    w_pw_v = w_pw.rearrange("(kc p) f -> p kc f", p=128)
    w_ua_v = w_ua.rearrange("(kc p) f -> p kc f", p=128)
    w_uc_v = w_uc.rearrange("(kc p) f -> p kc f", p=128)

    wsc = {}

    def wconv(key, src, kcn, ncols, col0=0, wtot=None):
        wtot = ncols if wtot is None else wtot
        if key not in wsc:
            t = nc.dram_tensor("wsc_" + key, [128, kcn * wtot], BF16, kind="Internal").ap()
            wsc[key] = (t, Res(), kb.slot("cv_" + key))
        t, r, sl = wsc[key]
        tv = t.rearrange("p (k c) -> p k c", k=kcn)
        kb.dma("pool", tv[:, :, col0:col0 + ncols], src, sl, writes=[r])

    def conv_first():
        for b2 in range(2):
            wconv(f"q{b2}", w_in_v[:, :, C_Q + b2 * 512: C_Q + (b2 + 1) * 512], 16, 512)
        wconv("kv", w_in_v[:, :, C_K:C_K + 512], 16, 512)
        for b2 in range(2):
            wconv(f"za{b2}", w_in_v[:, :, C_ZA + b2 * 512: C_ZA + (b2 + 1) * 512], 16, 512)
        wconv("qi", w_in_v[:, :, C_QI:C_QI + 512], 16, 512)
        wconv("ki", w_in_v[:, :, C_KI:C_KI + 72], 16, 72)

    def conv_rest(after):
        kb._deps(kb.eng["pool"], after, [])
        for b4 in range(4):
            wconv(f"ua{b4}", w_ua_v[:, :, b4 * 512:(b4 + 1) * 512], 8, 512)
        for b4 in range(4):
            wconv(f"ga{b4}", w_in_v[:, :, C_GA + b4 * 512: C_GA + (b4 + 1) * 512], 16, 512)
        for b2 in range(2):
            wconv(f"gg{b2}", w_in_v[:, :, C_GG + b2 * 512: C_GG + (b2 + 1) * 512], 16, 512)
        for b2 in range(2):
            wconv(f"gv{b2}", w_in_v[:, :, C_GV + b2 * 512: C_GV + (b2 + 1) * 512], 16, 512)
        for b2 in range(2):
            wconv(f"zc{b2}", w_in_v[:, :, C_ZC + b2 * 512: C_ZC + (b2 + 1) * 512], 16, 512)
        for b2 in range(2):
            wconv(f"pw{b2}", w_pw_v[:, :, b2 * 512:(b2 + 1) * 512], 8, 512)
        for b4 in range(4):
            wconv(f"uc{b4}", w_uc_v[:, :, b4 * 512:(b4 + 1) * 512], 8, 512)
            wconv(f"gc{b4}", w_in_v[:, :, C_GC + b4 * 512: C_GC + (b4 + 1) * 512], 16, 512)
        for b4 in range(4):
            wconv(f"wo{b4}", w_out_v[:, :, b4 * 512:(b4 + 1) * 512], 16, 512)


    def sb(name, shape, dt):
        return nc.alloc_sbuf_tensor(name, shape, dt)

    identf = sb("identf", [128, 128], F32)
    identb = sb("identb", [128, 128], BF16)
    onesb = sb("onesb", [128, 128], BF16)
    onesf = sb("onesf", [128, 128], F32)
    causneg = sb("causneg", [128, 128], F32)
    sampneg = sb("sampneg", [128, 128], F32)
    sampbm = sb("sampbm", [128, 128], F32)
    sampnb = sb("sampnb", [128, 128], F32)
    dq = sb("dq", [128, 64], F32)
    bsel = sb("bsel", [128, 16], F32)
    selbase = sb("selbase", [64, 248], F32)
    prmT = sb("prmT", [128, 48], F32)
    lngT = prmT[:, 0:16]
    cbT = prmT[:, 16:24]
    cngT = prmT[:, 24:32]
    cnbT = prmT[:, 32:40]
    bpwT = prmT[:, 40:48]
    cwT = sb("cwT", [128, 8, 31], F32)
    oh8T = sb("oh8T", [8, 128], F32)
    epsT = sb("epsT", [128, 1], F32)
    pidmat = sb("pidmat", [128, 32], F32)
    fvec = sb("fvec", [128, 32], F32)
    idxall = sb("idxall", [128, 32], I32)
    biasT = sb("biasT", [128, 4, 8, 128], BF16)
    xnT = sb("xnT", [128, 16, 512], BF16)
    mT = sb("mT", [128, 16, 512], BF16)
    wbuf = [sb(f"wbuf{i}", [128, 8192], BF16) for i in range(2)]
    KT = sb("KT", [128, 2, 2048], BF16)
    VA = sb("VA", [128, 16, 2, 128], BF16)
    KiT = sb("KiT", [64, 2048], BF16)
    uhalo = sb("uhalo", [128, 8, 30], BF16)
    SCR = 102 * 1024
    scr = sb("scr", [128, SCR], U8)
    psum_all = nc.alloc_psum_tensor("ps", [128, 4096], F32)

    class Arena:
        def __init__(self):
            self.off = 0

        def reset(self):
            self.off = 0

        def get(self, shape, dt):
            esz = 4 if dt in (F32, I32) else 2
            n = int(np.prod(shape[1:])) * esz
            n = (n + 63) // 64 * 64
            assert self.off + n <= SCR, (self.off, n)
            ap = scr[0:shape[0], self.off:self.off + n].bitcast(dt)
            self.off += n
            if len(shape) == 3:
                ap = ap[:, 0:shape[1] * shape[2]].rearrange("p (a b) -> p a b", a=shape[1])
            elif len(shape) == 4:
                ap = ap[:, 0:shape[1] * shape[2] * shape[3]].rearrange("p (a b c) -> p a b c", a=shape[1], b=shape[2])
            else:
                ap = ap[:, 0:shape[1]]
            return ap

    ar = Arena()

    banks = [psum_all[:, i * 512:(i + 1) * 512] for i in range(8)]
    bres = [Res() for _ in range(8)]
    pinned = set()
    bstate = {"i": 0}

    def bank():
        while True:
            i = bstate["i"] % 8
            bstate["i"] += 1
            if i not in pinned:
                return i, banks[i], bres[i]

    s_setup = kb.slot("setup")
    R_const = Res()
    ar.reset()
    ar.off = 56 * 1024
    prm = ar.get([48, 128], F32)
    cwst = ar.get([31, 1024], F32)
    ptJ = ar.get([8, 32], I32)
    ptJf = ar.get([8, 32], F32)
    with nc.allow_non_contiguous_dma(reason="tiny page table transpose"):
        for t_, a_ in [(identf, cin["identf"]), (causneg, cin["causneg"]), (sampneg, cin["sampneg"]),
                       (sampbm, cin["sampbm"]), (sampnb, cin["sampnb"]), (dq, cin["dq"]), (bsel, cin["bsel"]),
                       (selbase, cin["selbase"]), (pidmat, cin["pidmat"]), (fvec, cin["fvec"]), (oh8T, cin["oh8T"])]:
            kb.dma("sp", t_[:], a_, s_setup, writes=[R_const])
        kb.dma("sp", ptJ[:], ptab.rearrange("o (bh j) -> (o j) bh", j=8), s_setup, writes=[R_const])
        kb.dma("sp", prm[0:16, :], ln_g.rearrange("(kc p) -> kc p", p=128), s_setup, writes=[R_const])
        for k_, a_ in enumerate([conv_b, cn_g, cn_b, b_pw]):
            kb.dma("sp", prm[16 + 8 * k_:24 + 8 * k_, :], a_.rearrange("(c p) -> c p", p=128), s_setup, writes=[R_const])
        kb.dma("sp", cwst[:], conv_w, s_setup, writes=[R_const])
    bi, bk_, br = bank()
    kb.mm([lambda: nc.tensor.transpose(out=bk_[:, 0:48], in_=prm[0:48, :], identity=identf[0:48, 0:48])],
          reads=[R_const], writes=[br])
    kb.op("dve", lambda: nc.vector.tensor_copy(out=prmT[:], in_=bk_[:, 0:48]), reads=[br], writes=[R_const])
    bi, bk_, br = bank()
    kb.mm([(lambda c_=c_: nc.tensor.transpose(out=bk_[:, c_ * 31:(c_ + 1) * 31], in_=cwst[0:31, c_ * 128:(c_ + 1) * 128],
                                              identity=identf[0:31, 0:31])) for c_ in range(8)],
          reads=[R_const], writes=[br])
    kb.op("dve", lambda: nc.vector.tensor_copy(out=cwT[:].rearrange("p a b -> p (a b)"), in_=bk_[:, 0:248]),
          reads=[br], writes=[R_const])
    kb.op("dve", lambda: nc.vector.tensor_copy(out=ptJf[:], in_=ptJ[:]), reads=[R_const], writes=[R_const])
    bi, bk_, br = bank()
    kb.mm([lambda: nc.tensor.matmul(bk_[:, 0:32], oh8T[:, :], ptJf[:, :], start=True, stop=True)],
          reads=[R_const], writes=[br])
    kb.op("dve", lambda: nc.vector.scalar_tensor_tensor(out=idxall[:], in0=bk_[:, 0:32], scalar=16.0, in1=pidmat[:],
                                                        op0=ALU.mult, op1=ALU.add), reads=[br, R_const], writes=[R_const])
    kb.op("dve", lambda: nc.vector.tensor_copy(out=identb[:], in_=identf[:]), reads=[R_const], writes=[R_const])
    kb.op("dve", lambda: nc.vector.memset(onesb[:], 1.0), writes=[R_const])
    kb.op("dve", lambda: nc.vector.memset(onesf[:], 1.0 / 1024.0), writes=[R_const])
    kb.op("dve", lambda: nc.vector.memset(epsT[:], EPS), writes=[R_const])
    kb.op("dve", lambda: nc.vector.memset(uhalo[:], 0.0), writes=[R_const])

    LS = {}

    rb33 = ar.get([33, 8], F32)
    oh33 = ar.get([33, 384], F32)
    dx = ar.get([8, 384], F32)
    dx2 = ar.get([8, 384], F32)
    R_b0 = Res()
    s_b = kb.slot("btab", scratch=True)
    kb.op("dve", lambda: nc.vector.memset(rb33[0:33, :], NEGB), writes=[R_b0])
    kb.dma("sp", rb33[0:32, :], rel_bias, s_b, writes=[R_b0])
    kb.dma("sp", oh33[:], cin["oh33"], s_b, writes=[R_b0])
    bi, bk_, br = bank()
    kb.mm([lambda: nc.tensor.matmul(bk_[0:8, 0:384], rb33[:, :], oh33[:, :], start=True, stop=True)],
          reads=[R_b0], writes=[br])
    kb.op("dve", lambda: nc.vector.tensor_copy(out=dx[:], in_=bk_[0:8, 0:384]), reads=[br], writes=[R_b0])
    kb.op("dve", lambda: nc.vector.tensor_scalar(out=dx2[:], in0=dx[:], scalar1=dx[:, 383:384], scalar2=SQ128,
                                                 op0=ALU.subtract, op1=ALU.mult), reads=[R_b0], writes=[R_b0])
    R_dper = Res()
    kb.dma("sp", dper.rearrange("h (r m) -> h r m", m=384), dx2[:].unsqueeze(1).broadcast_to([8, 129, 384]),
           s_b, reads=[R_b0], writes=[R_dper])

    def late_setup():
        ar.off = 72 * 1024
        tst = ar.get([128, 8, 128], F32)
        thf = ar.get([128, 8, 128], F32)
        R_b = Res()
        R_bias = Res()

        def mk_table(cofs, dst_hi, dst_lo):
            src = dper[:, cofs:cofs + 383 * 128].rearrange("h (s x) -> s h x", x=383)[:, :, 0:128]
            kb.dma("sp", tst[:], src, s_b, reads=[R_dper], writes=[R_b])
            kb.op("dve", lambda: nc.vector.tensor_copy(out=dst_hi, in_=tst[:]), reads=[R_b], writes=[R_bias])
            kb.op("dve", lambda: nc.vector.tensor_copy(out=thf[:], in_=dst_hi), reads=[R_bias], writes=[R_b])
            kb.op("dve", lambda: nc.vector.tensor_tensor(out=dst_lo, in0=tst[:], in1=thf[:], op=ALU.subtract),
                  reads=[R_b], writes=[R_bias])

        with nc.allow_non_contiguous_dma(reason="toeplitz"):
            mk_table(128, biasT[:, 0], biasT[:, 1])
            mk_table(256, biasT[:, 2], biasT[:, 3])
        dsc = [nc.dram_tensor(f"dsc{c}", [128, 31 * 128], BF16, kind="Internal").ap() for c in range(8)]
        R_dsc = [Res() for _ in range(8)]
        dgt = [ar.get([128, 31, 128], BF16) for _ in range(2)]
        R_dgt = [Res(), Res()]
        s_dg = [kb.slot("dg0", scratch=True), kb.slot("dg1", scratch=True)]
        for c in range(8):
            di = c % 2
            kb.op("dve", lambda: nc.vector.tensor_tensor(out=dgt[di][:], in0=identf[:].unsqueeze(1).broadcast_to([128, 31, 128]),
                                                          in1=cwT[:, c, :].unsqueeze(2).broadcast_to([128, 31, 128]), op=ALU.mult),
                  reads=[R_const], writes=[R_dgt[di]])
            kb.dma("sp", dsc[c], dgt[di][:].rearrange("p a b -> p (a b)"), s_dg[di], reads=[R_dgt[di]], writes=[R_dsc[c]])
        LS.update(dict(R_dper=R_dper, R_bias=R_bias, dsc=dsc, R_dsc=R_dsc, s_dg=s_dg))

    R_uh = Res()
    wres = [Res(), Res()]
    wsl = [kb.slot("w0"), kb.slot("w1")]
    wst = {"n": 0}

    def wload(key, parts, kcn, ncols):
        i = wst["n"] % 2
        wst["n"] += 1
        flat = wbuf[i][:, 0:kcn * ncols]
        view = flat.rearrange("p (k c) -> p k c", k=kcn)
        t, r, sl = wsc[key]
        kb.dma("sp", flat, t, wsl[i], reads=[r], writes=[wres[i]])
        return view, wres[i]

    prefetched = {}

    def run_jobs(jobs, next_first=None, hook=None):
        loaded = {}
        order = [k for k, j in enumerate(jobs) if j[0] is not None]

        def ensure(k):
            if k not in loaded:
                key = jobs[k][0][0]
                if key in prefetched:
                    loaded[k] = prefetched.pop(key)
                else:
                    loaded[k] = wload(*jobs[k][0])
        pos = 0
        for k, (ls, fn) in enumerate(jobs):
            if ls is not None:
                ensure(k)
                pos = order.index(k)
                if pos + 1 < len(order):
                    ensure(order[pos + 1])
                elif next_first is not None:
                    prefetched[next_first[0]] = wload(*next_first)
                fn(*loaded.pop(k))
            else:
                fn()
            if hook is not None and hook[0] == k:
                hook[1]()

    R_xn = Res()

    def p1_alloc(gj):
        nt_ = 1 if gj == 4 else 4
        pb = {}
        pb["xb"] = [ar.get([128, D], F32) for _ in range(2)]
        pb["xs"] = [ar.get([128, D], F32) for _ in range(nt_)]
        pb["st"] = ar.get([128, 4 * nt_], F32)
        pb["R_xb"] = [Res(), Res()]
        pb["R_xs"] = [Res() for _ in range(nt_)]
        pb["R_st"] = Res()
        pb["nt"] = nt_
        return pb

    def p1_norm(gj, pb):
        tok0_ = gj * 512
        s_x = [kb.slot(f"x_{i}", scratch=True) for i in range(2)]
        xb, xs, st = pb["xb"], pb["xs"], pb["st"]
        for t in range(pb["nt"]):
            i = t % 2
            c0 = 4 * t
            kb.dma("sp", xb[i][:], x[tok0_ + t * 128: tok0_ + (t + 1) * 128, :], s_x[i], writes=[pb["R_xb"][i]])
            kb.op("act", lambda: nc.scalar.activation(out=xs[t][:], in_=xb[i][:], func=AF.Square,
                                                      accum_out=st[:, c0:c0 + 1]), reads=[pb["R_xb"][i]],
                  writes=[pb["R_xs"][t], pb["R_st"]])
            kb.op("act", lambda: nc.scalar.activation(out=st[:, c0 + 1:c0 + 2], in_=st[:, c0:c0 + 1], func=AF.Sqrt,
                                                      bias=epsT[:, 0:1], scale=1.0 / D), reads=[pb["R_st"], R_const],
                  writes=[pb["R_st"]])
            kb.op("dve", lambda: nc.vector.reciprocal(out=st[:, c0 + 2:c0 + 3], in_=st[:, c0 + 1:c0 + 2]), reads=[pb["R_st"]],
                  writes=[pb["R_st"]])
            kb.op("act", lambda: nc.scalar.activation(out=xs[t][:], in_=xb[i][:], func=AF.Copy, scale=st[:, c0 + 2:c0 + 3]),
                  reads=[pb["R_xb"][i], pb["R_st"]], writes=[pb["R_xs"][t]])

    def p1_tr(gj, pb):
        NT_ = 128 if gj == 4 else 512
        xn_ = xnT[:, :, 0:NT_]
        xs = pb["xs"]
        for t in range(pb["nt"]):
            for q4 in range(4):
                bi, bk, br = bank()
                kb.mm([(lambda kc=q4 * 4 + k, k=k: nc.tensor.transpose(out=bk[:, k * 128:(k + 1) * 128],
                                                                       in_=xs[t][:, kc * 128:(kc + 1) * 128],
                                                                       identity=identf[:])) for k in range(4)],
                      reads=[pb["R_xs"][t], R_const], writes=[br])
                for k in range(4):
                    kc = q4 * 4 + k
                    if kb.ev() == "act":
                        kb.op("act", lambda: nc.scalar.activation(out=xn_[:, kc, t * 128:(t + 1) * 128],
                                                                  in_=bk[:, k * 128:(k + 1) * 128], func=AF.Copy,
                                                                  scale=lngT[:, kc:kc + 1]),
                              reads=[br, R_const], writes=[R_xn])
                    else:
                        kb.op("dve", lambda: nc.vector.tensor_scalar(out=xn_[:, kc, t * 128:(t + 1) * 128],
                                                                     in0=bk[:, k * 128:(k + 1) * 128],
                                                                     scalar1=lngT[:, kc:kc + 1], scalar2=None,
                                                                     op0=ALU.mult),
                              reads=[br, R_const], writes=[R_xn])

    def group(gi):
        samp = gi == 4
        NT = 128 if samp else 512
        ntile = NT // 128
        tok0 = gi * 512
        xn = xnT[:, :, 0:NT]
        m_T = mT[:, :, 0:NT]
        R_m = Res()

        if gi == 0:
            ar.reset()
            pb = p1_alloc(0)
            p1_norm(0, pb)
            kb._deps(kb.eng["pool"], [R_const, pb["R_xb"][0], pb["R_xb"][1]], [])
            with nc.allow_non_contiguous_dma(reason="narrow idx weight block"):
                conv_first()
            p1_tr(0, pb)
        if gi == 0:
            late_setup()
            conv_rest([LS["R_bias"]] + LS["R_dsc"])
        R_dper, R_bias, dsc, R_dsc, s_dg = LS["R_dper"], LS["R_bias"], LS["dsc"], LS["R_dsc"], LS["s_dg"]
        kb.barrier()

        ar.reset()
        qT = ar.get([128, 8, NT], BF16)
        zsT = ar.get([128, 8, NT], BF16)
        qiT = ar.get([64, 8, NT], BF16)
        gT = ar.get([128, 8, NT], BF16)
        kvst = [ar.get([128, 512], F32) for _ in range(2)]
        kist = [ar.get([128, 72], F32) for _ in range(2)]
        wab = ar.get([128, ntile, 16], F32)
        R_q, R_zs, R_qi, R_g, R_wab = Res(), Res(), Res(), Res(), Res()
        R_kvst = [Res(), Res()]
        R_kist = [Res(), Res()]
        s_kv = [kb.slot(f"kv_{i}", scratch=True) for i in range(2)]
        s_ki = [kb.slot(f"ki_{i}", scratch=True) for i in range(2)]
        R_KT, R_VA, R_KiT = Res(), Res(), Res()
        if samp:
            KTn = ar.get([128, 2, 128], BF16)
            VAn = ar.get([128, 2, 128], BF16)
            KiTn = ar.get([64, 128], BF16)

        def formF(wv, wr, nchunk, M, evac, kcn=16, src=None, R_src=None):
            src = xn if src is None else src
            R_src = R_xn if R_src is None else R_src
            for c in range(nchunk):
                bi, bk, br = bank()
                kb.mm([(lambda kc=kc: nc.tensor.matmul(bk[0:M, 0:NT], wv[:, kc, c * M:(c + 1) * M], src[:, kc, :],
                                                       start=(kc == 0), stop=(kc == kcn - 1))) for kc in range(kcn)],
                      reads=[wr, R_src], writes=[br])
                evac(c, bk[0:M, 0:NT], br)

        def formT(wv, wr, ncols, evac, c0=0, kcn=16):
            for t in range(ntile):
                bi, bk, br = bank()
                kb.mm([(lambda kc=kc: nc.tensor.matmul(bk[:, 0:ncols], xn[:, kc, t * 128:(t + 1) * 128],
                                                       wv[:, kc, c0:c0 + ncols],
                                                       start=(kc == 0), stop=(kc == kcn - 1))) for kc in range(kcn)],
                      reads=[wr, R_xn], writes=[br])
                evac(t, bk[:, 0:ncols], br)

        def copy_evac(dst_fn, R_dst, func=None):
            def f(c, ps, br):
                if func is not None:
                    kb.op("act", lambda: nc.scalar.activation(out=dst_fn(c), in_=ps, func=func), reads=[br], writes=[R_dst])
                elif kb.ev() == "act":
                    kb.op("act", lambda: nc.scalar.copy(out=dst_fn(c), in_=ps), reads=[br], writes=[R_dst])
                else:
                    kb.op("dve", lambda: nc.vector.tensor_copy(out=dst_fn(c), in_=ps), reads=[br], writes=[R_dst])
            return f

        jobs = []
        for b2 in range(2):
            jobs.append(((f"q{b2}", [(0, w_in_v[:, :, C_Q + b2 * 512: C_Q + (b2 + 1) * 512])], 16, 512),
                         lambda wv, wr, b2=b2: formF(wv, wr, 4, 128, copy_evac(lambda c: qT[:, b2 * 4 + c, :], R_q))))

        def kv_job(wv, wr):
            if samp:
                formF(wv, wr, 2, 128, copy_evac(lambda c: KTn[:, c, :], R_KT))
            else:
                formF(wv, wr, 2, 128, copy_evac(lambda c: KT[:, c, tok0:tok0 + NT], R_KT))

            def ev(t, ps, br):
                i = t % 2
                kb.op("act", lambda: nc.scalar.copy(out=kvst[i][:, :], in_=ps), reads=[br], writes=[R_kvst[i]])
                kb.dma("sp", kk[tok0 + t * 128: tok0 + (t + 1) * 128, :], kvst[i][:, 0:256], s_kv[i],
                       reads=[R_kvst[i]], is_out=True)
                kb.dma("sp", vv[tok0 + t * 128: tok0 + (t + 1) * 128, :], kvst[i][:, 256:512], s_kv[i],
                       reads=[R_kvst[i]], is_out=True)
                dst = VAn[:, :, :] if samp else VA[:, gi * 4 + t, :, :]
                kb.op("dve", lambda: nc.vector.tensor_copy(out=dst, in_=kvst[i][:, 256:512].rearrange("p (g d) -> p g d", g=2)),
                      reads=[R_kvst[i]], writes=[R_VA])
            formT(wv, wr, 512, ev)
        jobs.append((("kv", [(0, w_in_v[:, :, C_K:C_K + 512])], 16, 512), kv_job))
        for b2 in range(2):
            jobs.append(((f"za{b2}", [(0, w_in_v[:, :, C_ZA + b2 * 512: C_ZA + (b2 + 1) * 512])], 16, 512),
                         lambda wv, wr, b2=b2: formF(wv, wr, 4, 128, copy_evac(lambda c: zsT[:, b2 * 4 + c, :], R_zs, AF.Silu))))
        jobs.append((("qi", [(0, w_in_v[:, :, C_QI: C_QI + 512])], 16, 512),
                     lambda wv, wr: formF(wv, wr, 8, 64, copy_evac(lambda c: qiT[:, c, :], R_qi))))

        def idx_job(wv, wr):
            if samp:
                formF(wv, wr, 1, 64, copy_evac(lambda c: KiTn[:, :], R_KiT))
            else:
                formF(wv, wr, 1, 64, copy_evac(lambda c: KiT[:, tok0:tok0 + NT], R_KiT))

            def ev(t, ps, br):
                i = t % 2
                kb.op("act", lambda: nc.scalar.copy(out=kist[i][:, :], in_=ps), reads=[br], writes=[R_kist[i]])
                kb.dma("sp", kio[tok0 + t * 128: tok0 + (t + 1) * 128, :], kist[i][:, 0:64], s_ki[i],
                       reads=[R_kist[i]], is_out=True)
                kb.op("act", lambda: nc.scalar.activation(out=wab[:, t, 0:8], in_=kist[i][:, 64:72], func=AF.Abs),
                      reads=[R_kist[i]], writes=[R_wab])
                kb.op("act", lambda: nc.scalar.activation(out=wab[:, t, 8:16], in_=kist[i][:, 64:72], func=AF.Sign),
                      reads=[R_kist[i]], writes=[R_wab])
            formT(wv, wr, 72, ev)
        jobs.append((("ki", [(0, w_in_v[:, :, C_KI:C_KI + 72])], 16, 72), idx_job))
        with nc.allow_non_contiguous_dma(reason="narrow idx weight block"):
            run_jobs(jobs, None)
        if gi == 0:
            for E_ in kb.eng.values():
                kb._deps(E_, [R_bias] + R_dsc, [])

        def gen_ga():
            for b4 in range(4):
                wv, wr = wload(f"ga{b4}", [], 16, 512)
                for c in range(4):
                    cc = b4 * 4 + c
                    bi, bk, br = bank()
                    kb.mm([(lambda kc=kc: nc.tensor.matmul(bk[:, 0:NT], wv[:, kc, c * 128:(c + 1) * 128], xn[:, kc, :],
                                                           start=(kc == 0), stop=(kc == 15))) for kc in range(16)],
                          reads=[wr, R_xn], writes=[br])
                    if kb.ev() == "act":
                        kb.op("act", lambda: nc.scalar.copy(out=m_T[:, cc, :], in_=bk[:, 0:NT]), reads=[br], writes=[R_m])
                    else:
                        kb.op("dve", lambda: nc.vector.tensor_copy(out=m_T[:, cc, :], in_=bk[:, 0:NT]), reads=[br], writes=[R_m])
                    yield

        class SelBuf:
            pass
        nset = 1 if samp else 2
        SB = []
        for k_ in range(nset):
            S = SelBuf()
            S.score = ar.get([128, 2176], F32)
            S.junk = ar.get([128, 2176], BF16)
            S.m01 = ar.get([128, 2176], BF16)
            S.maskT = ar.get([128, 17, 128], BF16)
            S.bis = ar.get([128, 80], F32)
            S.R_score, S.R_junk, S.R_m01, S.R_maskT, S.R_bis = Res(), Res(), Res(), Res(), Res()
            SB.append(S)
        S0 = SB[0]
        score, maskT, bis = S0.score, S0.maskT, S0.bis
        R_score, R_maskT, R_bis = S0.R_score, S0.R_maskT, S0.R_bis
        rtmp = [ar.get([128, 512], F32) for _ in range(2)]
        et = [ar.get([128, 512], BF16) for _ in range(3)]
        pt = [ar.get([128, 512], BF16) for _ in range(2)]
        rd = ar.get([128, 512], F32)
        tmpf = ar.get([128, 512], F32)
        R_rd, R_tmpf = Res(), Res()
        R_rtmp = [Res(), Res()]
        R_et = [Res(), Res(), Res()]
        R_pt = [Res(), Res()]
        cnt = {"r": 0, "e": 0}
        NEGM = NEGB * SQ128

        def gen_select(S, L, nblk, additive):
            bis_ = S.bis
            Rb = S.R_bis
            kb.op("dve", lambda: nc.vector.tensor_reduce(out=bis_[:, 0:1], in_=S.score[:, 0:L], axis=AX.X, op=ALU.max),
                  reads=[S.R_score], writes=[Rb])
            yield
            kb.op("dve", lambda: nc.vector.tensor_sub(out=bis_[:, 2:3], in0=bis_[:, 0:1], in1=bis_[:, 1:2]),
                  reads=[Rb], writes=[Rb])
            for k in range(NBIS + 1):
                pass
            kb.op("dve", lambda: nc.vector.tensor_scalar(out=bis_[:, 40:40 + NBIS + 1],
                                                         in0=fvec[:, 0:NBIS + 1], scalar1=bis_[:, 2:3], scalar2=None,
                                                         op0=ALU.mult), reads=[Rb, R_const], writes=[Rb])
            kb.op("dve", lambda: nc.vector.tensor_tensor(out=bis_[:, 8:9], in0=bis_[:, 1:2], in1=bis_[:, 40:41], op=ALU.add),
                  reads=[Rb], writes=[Rb])
            yield
            for k in range(NBIS):
                mid = bis_[:, 8 + k:9 + k]
                kb.op("dve", lambda: nc.vector.tensor_scalar(out=S.junk[:, 0:L], in0=S.score[:, 0:L], scalar1=mid,
                                                             scalar2=0.0, op0=ALU.is_ge, op1=ALU.add,
                                                             accum_out=bis_[:, 4:5]),
                      reads=[Rb, S.R_score], writes=[S.R_junk, Rb])
                yield
                kb.op("dve", lambda: nc.vector.tensor_scalar(out=bis_[:, 5:6], in0=bis_[:, 4:5], scalar1=255.5, scalar2=-0.5,
                                                             op0=ALU.is_ge, op1=ALU.add), reads=[Rb], writes=[Rb])
                kb.op("dve", lambda: nc.vector.scalar_tensor_tensor(out=bis_[:, 9 + k:10 + k], in0=bis_[:, 5:6],
                                                                    scalar=bis_[:, 40 + k:41 + k], in1=mid,
                                                                    op0=ALU.mult, op1=ALU.add), reads=[Rb], writes=[Rb])
                yield
            kb.op("dve", lambda: nc.vector.tensor_sub(out=bis_[:, 6:7], in0=bis_[:, 8 + NBIS:9 + NBIS],
                                                      in1=bis_[:, 40 + NBIS:41 + NBIS]), reads=[Rb], writes=[Rb])
            if additive:
                kb.op("dve", lambda: nc.vector.tensor_scalar(out=S.m01[:, 0:L], in0=S.score[:, 0:L], scalar1=bis_[:, 6:7],
                                                             scalar2=NEGM, op0=ALU.is_lt, op1=ALU.mult),
                      reads=[Rb, S.R_score], writes=[S.R_m01])
            else:
                kb.op("dve", lambda: nc.vector.tensor_scalar(out=S.m01[:, 0:L], in0=S.score[:, 0:L], scalar1=bis_[:, 6:7],
                                                             scalar2=None, op0=ALU.is_ge),
                      reads=[Rb, S.R_score], writes=[S.R_m01])
            yield
            j = 0
            while j < nblk:
                n = min(8, nblk - j)
                bi, bk, br = bank()
                bkb = bk.bitcast(BF16)
                kb.mm([(lambda jj=jj: nc.tensor.transpose(out=bkb[:, jj * 128:(jj + 1) * 128],
                                                          in_=S.m01[:, (j + jj) * 128:(j + jj + 1) * 128],
                                                          identity=identb[:])) for jj in range(n)],
                      reads=[S.R_m01, R_const], writes=[br])
                kb.op("act", lambda: nc.scalar.copy(out=S.maskT[:, j:j + n, :],
                                                    in_=bkb[:, 0:n * 128].rearrange("p (a b) -> p a b", a=n)),
                      reads=[br], writes=[S.R_maskT])
                j += n
                yield

        def select_mask(L, nblk):
            for _ in gen_select(S0, L, nblk, False):
                pass

        def finalize(g, bo, ro, bd, rdn, cols):
            kb.op("dve", lambda: nc.vector.reciprocal(out=rd[:], in_=bd[:, :]), reads=[rdn], writes=[R_rd])
            kb.op("dve", lambda: nc.vector.tensor_tensor(out=tmpf[:], in0=bo[:, :], in1=rd[:], op=ALU.mult),
                  reads=[ro, R_rd], writes=[R_tmpf])
            kb.op("dve", lambda: nc.vector.tensor_tensor(out=gT[:, 4 * g:4 * g + 4, cols],
                                                         in0=tmpf[:].rearrange("p (h q) -> p h q", h=4),
                                                         in1=zsT[:, 4 * g:4 * g + 4, cols], op=ALU.mult),
                  reads=[R_tmpf, R_zs], writes=[R_g])

        def attn_block(g, lhsK, cols, near, tabs, mask_ap, bo, ro, bd, rdn, lhsV, first, rK, rV, addmask=None, R_mk=None):
            bi, bk, br = bank()
            nmm = 1 + (2 if near else 0) + (1 if addmask is not None else 0)
            fns = [lambda: nc.tensor.matmul(bk[:, :], lhsK, qT[:, 4 * g:4 * g + 4, cols], start=True, stop=(nmm == 1),
                                            skip_group_check=True)]
            if near:
                fns.append(lambda: nc.tensor.matmul(bk[:, :], identb[:], tabs[0][:, 4 * g:4 * g + 4, :], start=False,
                                                    stop=False, skip_group_check=True))
                fns.append(lambda: nc.tensor.matmul(bk[:, :], identb[:], tabs[1][:, 4 * g:4 * g + 4, :], start=False,
                                                    stop=(addmask is None), skip_group_check=True))
            rds = [rK, R_q, R_bias, R_const]
            if addmask is not None:
                fns.append(lambda: nc.tensor.matmul(bk[:, :], identb[:], addmask.unsqueeze(1).broadcast_to([128, 4, 128]),
                                                    start=False, stop=True, skip_group_check=True))
                rds.append(R_mk)
            kb.mm(fns, reads=rds, writes=[br])
            ei = cnt["e"] % 3
            cnt["e"] += 1
            kb.op("act", lambda: nc.scalar.activation(out=et[ei][:], in_=bk[:, :], func=AF.Exp, scale=SCALE),
                  reads=[br], writes=[R_et[ei]])
            if mask_ap is not None:
                pi = ei % 2
                kb.op("dve", lambda: nc.vector.tensor_tensor(out=pt[pi][:].rearrange("p (h q) -> p h q", h=4),
                                                             in0=et[ei][:].rearrange("p (h q) -> p h q", h=4),
                                                             in1=mask_ap.unsqueeze(1).broadcast_to([128, 4, 128]),
                                                             op=ALU.mult),
                      reads=[R_et[ei], R_maskT], writes=[R_pt[pi]])
                src, rs = pt[pi], R_pt[pi]
            else:
                src, rs = et[ei], R_et[ei]
            kb.mm([lambda: nc.tensor.matmul(bo[:, :], lhsV, src[:], start=first, stop=False, skip_group_check=True),
                   lambda: nc.tensor.matmul(bd[:, :], onesb[:], src[:], start=first, stop=False, skip_group_check=True)],
                  reads=[rs, rV, R_const], writes=[ro, rdn])

        def pin():
            i, b, r = bank()
            pinned.add(i)
            return i, b, r

        if not samp:
            def gen_sel_tile(t):
                ti = gi * 4 + t
                S = SB[t % 2]
                L = 128 * (ti + 1)
                cols = slice(t * 128, (t + 1) * 128)
                if ti < 2:
                    return
                nch = (L + 511) // 512
                for h in range(8):
                    for ch in range(nch):
                        w = min(512, L - ch * 512)
                        bi, bk, br = bank()
                        kb.mm([lambda: nc.tensor.matmul(bk[:, 0:w], qiT[:, h, cols], KiT[:, ch * 512:ch * 512 + w],
                                                        start=True, stop=True)], reads=[R_qi, R_KiT], writes=[br])
                        ri = cnt["r"] % 2
                        cnt["r"] += 1
                        kb.op("act", lambda: nc.scalar.activation(out=rtmp[ri][:, 0:w], in_=bk[:, 0:w], func=AF.Relu,
                                                                  scale=wab[:, t, h:h + 1]),
                              reads=[br, R_wab], writes=[R_rtmp[ri]])
                        sc = S.score[:, ch * 512:ch * 512 + w]
                        if h == 0:
                            kb.op("dve", lambda: nc.vector.tensor_scalar(out=sc, in0=rtmp[ri][:, 0:w],
                                                                         scalar1=wab[:, t, 8 + h:9 + h], scalar2=None,
                                                                         op0=ALU.mult),
                                  reads=[R_rtmp[ri], R_wab], writes=[S.R_score])
                        else:
                            kb.op("dve", lambda: nc.vector.scalar_tensor_tensor(out=sc, in0=rtmp[ri][:, 0:w],
                                                                                scalar=wab[:, t, 8 + h:9 + h], in1=sc,
                                                                                op0=ALU.mult, op1=ALU.add),
                                  reads=[R_rtmp[ri], R_wab], writes=[S.R_score])
                        yield
                kb.op("dve", lambda: nc.vector.tensor_reduce(out=S.bis[:, 1:2], in_=S.score[:, 0:L], axis=AX.X, op=ALU.min),
                      reads=[S.R_score], writes=[S.R_bis])
                kb.op("dve", lambda: nc.vector.tensor_tensor(out=S.score[:, L - 128:L], in0=S.score[:, L - 128:L],
                                                             in1=causneg[:], op=ALU.add),
                      reads=[R_const], writes=[S.R_score])
                yield
                for _ in gen_select(S, L, ti + 1, True):
                    yield

            def gen_attn_tile(t):
                ti = gi * 4 + t
                S = SB[t % 2]
                cols = slice(t * 128, (t + 1) * 128)
                for g in range(2):
                    io, bo, ro = pin()
                    idn, bd, rdn = pin()
                    for j in range(ti + 1):
                        near = ti - j <= 1
                        tabs = (biasT[:, 2 * (ti - j)], biasT[:, 2 * (ti - j) + 1]) if near else None
                        attn_block(g, KT[:, g, j * 128:(j + 1) * 128], cols, near, tabs, None, bo, ro, bd, rdn,
                                   VA[:, j, g, :], j == 0, R_KT, R_VA,
                                   addmask=(S.maskT[:, j, :] if ti >= 2 else None), R_mk=S.R_maskT)
                        yield
                    finalize(g, bo, ro, bd, rdn, cols)
                    pinned.discard(io)
                    pinned.discard(idn)
                    yield

            def nsteps_sel(t):
                ti = gi * 4 + t
                if ti < 2:
                    return 0
                L = 128 * (ti + 1)
                return 8 * ((L + 511) // 512) + 3 + 2 * NBIS + 1 + (ti + 8) // 8

            def zipper(ga, na, gb, nb):
                da = db = 0
                alive_a = alive_b = True
                while alive_a or alive_b:
                    fa = da / max(na, 1) if alive_a else 2.0
                    fb = db / max(nb, 1) if alive_b else 2.0
                    if fa <= fb:
                        try:
                            next(ga)
                            da += 1
                        except StopIteration:
                            alive_a = False
                    else:
                        try:
                            next(gb)
                            db += 1
                        except StopIteration:
                            alive_b = False

            for _ in gen_sel_tile(0):
                pass
            for t in range(4):
                ga = gen_attn_tile(t)
                na = 2 * (gi * 4 + t + 2)
                if t + 1 < 4:
                    zipper(ga, na, gen_sel_tile(t + 1), nsteps_sel(t + 1))
                else:
                    zipper(ga, na, gen_ga(), 16)
        else:
            cols = slice(0, 128)
            ts = ar.get([128, 2, 8, 128], BF16)
            tab16 = ar.get([128, 8, 2, 64], BF16)
            ar_mark = ar.off
            tst2 = ar.get([128, 8, 128], F32)
            thf2 = ar.get([128, 8, 128], F32)
            stg = ar.get([128, 8, 64], F32)
            stgh = ar.get([128, 8, 64], F32)
            R_ts = Res()
            s_ts = kb.slot("ts", scratch=True)
            with nc.allow_non_contiguous_dma(reason="toeplitz"):
                src_ = dper[:, 128:128 + 383 * 128].rearrange("h (s x) -> s h x", x=383)[:, :, 0:128]
                kb.dma("sp", tst2[:], src_, s_ts, reads=[R_dper], writes=[R_ts])
                kb.op("dve", lambda: nc.vector.memset(stg[:], 0.0), writes=[R_ts])
                for pl in range(8):
                    off = 256 - pl
                    srcp = dper[:, off:off + 376 * 16].rearrange("h (pp x) -> pp h x", x=376)[:, :, 0:8]
                    kb.dma("sp", stg[112:128, pl, :].rearrange("p (h q) -> p h q", h=8), srcp, s_ts, reads=[R_dper], writes=[R_ts])
            kb.op("dve", lambda: nc.vector.tensor_tensor(out=tst2[:], in0=tst2[:],
                                                         in1=sampbm[:].unsqueeze(1).broadcast_to([128, 8, 128]),
                                                         op=ALU.mult), reads=[R_const], writes=[R_ts])
            kb.op("dve", lambda: nc.vector.tensor_tensor(out=tst2[:], in0=tst2[:],
                                                         in1=sampnb[:].unsqueeze(1).broadcast_to([128, 8, 128]),
                                                         op=ALU.add), reads=[R_const], writes=[R_ts])
            kb.op("dve", lambda: nc.vector.tensor_copy(out=ts[:, 0], in_=tst2[:]), writes=[R_ts])
            kb.op("dve", lambda: nc.vector.tensor_copy(out=thf2[:], in_=ts[:, 0]), writes=[R_ts])
            kb.op("dve", lambda: nc.vector.tensor_tensor(out=ts[:, 1], in0=tst2[:], in1=thf2[:], op=ALU.subtract),
                  writes=[R_ts])
            kb.op("dve", lambda: nc.vector.tensor_copy(out=tab16[:, :, 0, :], in_=stg[:]), writes=[R_ts])
            kb.op("dve", lambda: nc.vector.tensor_copy(out=stgh[:], in_=tab16[:, :, 0, :]), writes=[R_ts])
            kb.op("dve", lambda: nc.vector.tensor_tensor(out=tab16[:, :, 1, :], in0=stg[:], in1=stgh[:], op=ALU.subtract),
                  writes=[R_ts, R_bias])
            kb.barrier()
            ar.off = ar_mark
            wcol = ar.get([64, 16], F32)
            amat = ar.get([128, 64], F32)
            lq = [ar.get([64, 64], BF16) for _ in range(2)]
            kic = [ar.get([128, 2, 8, 64], BF16) for _ in range(2)]
            kitb = [ar.get([64, 2048], BF16) for _ in range(2)]
            rwt = [ar.get([64, 512], F32) for _ in range(4)]
            rwh = [ar.get([64, 512], BF16) for _ in range(4)]
            rwl = [ar.get([64, 512], BF16) for _ in range(4)]
            selb = ar.get([64, 248], BF16)
            R_rwh = [Res() for _ in range(4)]
            R_rwl = [Res() for _ in range(4)]
            kb.op("dve", lambda: nc.vector.tensor_copy(out=selb[:], in_=selbase[:]), reads=[R_const], writes=[R_const])
            R_wcol = Res()
            R_lq = [Res(), Res()]
            R_kic = [Res(), Res()]
            R_kitb = [Res(), Res()]
            R_rwt = [Res() for _ in range(4)]
            s_kic = [kb.slot(f"kic{i}", scratch=True) for i in range(2)]
            kb.op("dve", lambda: nc.vector.tensor_tensor(out=amat[:].rearrange("p (h q) -> p h q", h=8),
                                                         in0=dq[:].rearrange("p (h q) -> p h q", h=8),
                                                         in1=kist[0][:, 64:72].unsqueeze(2).broadcast_to([128, 8, 8]),
                                                         op=ALU.mult), reads=[R_kist[0], R_const], writes=[R_wcol])
            bi, bk, br = bank()
            kb.mm([lambda: nc.tensor.matmul(bk[0:64, 0:16], amat[:, :], bsel[:, :], start=True, stop=True)],
                  reads=[R_wcol, R_const], writes=[br])
            kb.op("dve", lambda: nc.vector.tensor_copy(out=wcol[:], in_=bk[0:64, 0:16]), reads=[br], writes=[R_wcol])
            sbk = [pin() for _ in range(5)]
            prepared = set()

            def prep(b):
                i = b % 2
                for hf in range(2):
                    kb.idma(kic[i][:, hf].rearrange("p a e -> p (a e)"), cache_ki, idxall[:, 2 * b + hf:2 * b + hf + 1],
                            s_kic[i], reads=[R_const], writes=[R_kic[i]])
                for hf in range(2):
                    bi, bk, br = bank()
                    bkb = bk.bitcast(BF16)
                    kb.mm([(lambda pl=pl: nc.tensor.transpose(out=bkb[0:64, pl * 128:(pl + 1) * 128], in_=kic[i][:, hf, pl, :],
                                                              identity=identb[:])) for pl in range(8)],
                          reads=[R_kic[i], R_const], writes=[br])
                    if kb.ev() == "act":
                        kb.op("act", lambda: nc.scalar.copy(out=kitb[i][:, hf * 1024:(hf + 1) * 1024], in_=bkb[0:64, :]),
                              reads=[br], writes=[R_kitb[i]])
                    else:
                        kb.op("dve", lambda: nc.vector.tensor_copy(out=kitb[i][:, hf * 1024:(hf + 1) * 1024], in_=bkb[0:64, :]),
                              reads=[br], writes=[R_kitb[i]])
                kb.op("dve", lambda: nc.vector.tensor_copy(out=lq[i][:].rearrange("p (h q) -> p h q", h=8),
                                                           in_=qiT[:, :, 8 * b:8 * b + 8]), reads=[R_qi], writes=[R_lq[i]])

            items = [(b, ch) for b in range(16) for ch in range(5)]

            def stageA(k):
                b, ch = items[k]
                i = b % 2
                if b not in prepared:
                    prepared.add(b)
                    prep(b)
                w = 512 if ch < 4 else 128
                rhs = kitb[i][:, ch * 512:(ch + 1) * 512] if ch < 4 else KiTn[:, :]
                bi, bk, br = pin()
                kb.mm([lambda: nc.tensor.matmul(bk[0:64, 0:w], lq[i][:, :], rhs, start=True, stop=True)],
                      reads=[R_lq[i], R_kitb[i], R_KiT], writes=[br])
                return bk, br, bi
            pend = {}
            LOOK = 1
            for k in range(min(LOOK, len(items))):
                pend[k] = stageA(k)
            for k in range(len(items)):
                if k + LOOK < len(items):
                    pend[k + LOOK] = stageA(k + LOOK)
                b, ch = items[k]
                w = 512 if ch < 4 else 128
                bk, br, bi_ = pend.pop(k)
                ri = k % 4
                kb.op("dve", lambda: nc.vector.tensor_scalar(out=rwt[ri][:, 0:w], in0=bk[0:64, 0:w], scalar1=0.0,
                                                             scalar2=wcol[:, b:b + 1], op0=ALU.max, op1=ALU.mult),
                      reads=[br, R_wcol], writes=[R_rwt[ri]])
                pinned.discard(bi_)
                kb.op("act", lambda: nc.scalar.copy(out=rwh[ri][:, 0:w], in_=rwt[ri][:, 0:w]), reads=[R_rwt[ri]],
                      writes=[R_rwh[ri]])
                kb.op("dve", lambda: nc.vector.tensor_tensor(out=rwl[ri][:, 0:w], in0=rwt[ri][:, 0:w], in1=rwh[ri][:, 0:w],
                                                             op=ALU.subtract), reads=[R_rwt[ri], R_rwh[ri]], writes=[R_rwl[ri]])
                sbi, sbk_, sbr = sbk[ch]
                kb.mm([lambda: nc.tensor.matmul(sbk_[:, 0:w], selb[:, 120 - 8 * b:248 - 8 * b], rwh[ri][:, 0:w],
                                                start=(b == 0), stop=False, skip_group_check=True),
                       lambda: nc.tensor.matmul(sbk_[:, 0:w], selb[:, 120 - 8 * b:248 - 8 * b], rwl[ri][:, 0:w],
                                                start=False, stop=(b == 15), skip_group_check=True)],
                      reads=[R_rwh[ri], R_rwl[ri], R_const], writes=[sbr])
            for ch in range(5):
                w = 512 if ch < 4 else 128
                sbi, sbk_, sbr = sbk[ch]
                kb.op("act", lambda: nc.scalar.copy(out=score[:, ch * 512:ch * 512 + w], in_=sbk_[:, 0:w]),
                      reads=[sbr], writes=[R_score])
                pinned.discard(sbi)
            L = 2176
            kb.op("dve", lambda: nc.vector.tensor_reduce(out=bis[:, 1:2], in_=score[:, 0:L], axis=AX.X, op=ALU.min),
                  reads=[R_score], writes=[R_bis])
            kb.op("dve", lambda: nc.vector.tensor_tensor(out=score[:, 2048:2176], in0=score[:, 2048:2176],
                                                         in1=sampneg[:], op=ALU.add), reads=[R_const], writes=[R_score])
            select_mask(L, 17)
            kb.barrier()
            ar.off = ar_mark
            acc = [(pin(), pin()) for _ in range(2)]
            for g in range(2):
                (io, bo, ro), (idn, bd, rdn) = acc[g]
                attn_block(g, KTn[:, g, :], cols, True, (ts[:, 0], ts[:, 1]), maskT[:, 16, :], bo, ro, bd, rdn,
                           VAn[:, g, :], True, R_KT, R_VA)
            kcb = [ar.get([128, 2, 8, 256], BF16) for _ in range(2)]
            vb = [ar.get([128, 2, 8, 256], BF16) for _ in range(2)]
            ktb = ar.get([128, 2, 2048], BF16)
            es = ar.get([128, 16, 64], BF16)
            ps_ = ar.get([128, 16, 64], BF16)
            R_kc = [Res(), Res()]
            R_vb = [Res(), Res()]
            R_ktb, R_es, R_ps = Res(), Res(), Res()
            s_kc = [kb.slot(f"kc{i}", scratch=True) for i in range(2)]
            s_vc = [kb.slot(f"vc{i}", scratch=True) for i in range(2)]
            il0, bl0, rl0 = pin()
            il1, bl1, rl1 = pin()
            lgb = [(bl0, rl0), (bl1, rl1)]

            def gather(b):
                i = b % 2
                for hf in range(2):
                    ix = idxall[:, 2 * b + hf:2 * b + hf + 1]
                    kb.idma(kcb[i][:, hf].rearrange("p a e -> p (a e)"), cache_k, ix, s_kc[i], reads=[R_const], writes=[R_kc[i]])
                    kb.idma(vb[i][:, hf].rearrange("p a e -> p (a e)"), cache_v, ix, s_vc[i], reads=[R_const], writes=[R_vb[i]])
            gather(0)
            for b in range(16):
                i = b % 2
                if b + 1 < 16:
                    gather(b + 1)
                for hf in range(2):
                    for g in range(2):
                        bi, bk, br = bank()
                        bkb = bk.bitcast(BF16)
                        kb.mm([(lambda pl=pl: nc.tensor.transpose(out=bkb[:, pl * 128:(pl + 1) * 128],
                                                                  in_=kcb[i][:, hf, pl, g * 128:(g + 1) * 128],
                                                                  identity=identb[:])) for pl in range(8)],
                              reads=[R_kc[i], R_const], writes=[br])
                        if kb.ev() == "act":
                            kb.op("act", lambda: nc.scalar.copy(out=ktb[:, g, hf * 1024:(hf + 1) * 1024], in_=bkb[:, :]),
                                  reads=[br], writes=[R_ktb])
                        else:
                            kb.op("dve", lambda: nc.vector.tensor_copy(out=ktb[:, g, hf * 1024:(hf + 1) * 1024], in_=bkb[:, :]),
                                  reads=[br], writes=[R_ktb])
                for hb in range(2):
                    bl, rl = lgb[hb]
                    blv = bl[:, :].rearrange("p (j c) -> p j c", j=8)
                    fns = []
                    for jj in range(8):
                        j = hb * 8 + jj
                        for g in range(2):
                            fns.append(lambda j=j, jj=jj, g=g, blv=blv: nc.tensor.matmul(
                                blv[:, jj, g * 32:(g + 1) * 32], ktb[:, g, j * 128:(j + 1) * 128],
                                qT[:, 4 * g:4 * g + 4, 8 * b:8 * b + 8], start=True, stop=(hb == 0), skip_group_check=True))
                            if hb == 1:
                                for hl in range(2):
                                    fns.append(lambda jj=jj, g=g, hl=hl, blv=blv: nc.tensor.matmul(
                                        blv[:, jj, g * 32:(g + 1) * 32], identb[:],
                                        tab16[:, jj, hl, :].rearrange("p (h q) -> p h q", h=8)[:, 4 * g:4 * g + 4, :],
                                        start=False, stop=(hl == 1), skip_group_check=True))
                    kb.mm(fns, reads=[R_ktb, R_q, R_bias, R_ts, R_const], writes=[rl])
                    kb.op("act", lambda: nc.scalar.activation(out=es[:, hb * 8:(hb + 1) * 8, :], in_=blv, func=AF.Exp,
                                                              scale=SCALE), reads=[rl], writes=[R_es])
                kb.op("dve", lambda: nc.vector.tensor_tensor(
                    out=ps_[:].rearrange("p j (h q) -> p j h q", h=8), in0=es[:].rearrange("p j (h q) -> p j h q", h=8),
                    in1=maskT[:, 0:16, 8 * b:8 * b + 8].unsqueeze(2).broadcast_to([128, 16, 8, 8]), op=ALU.mult),
                    reads=[R_es, R_maskT], writes=[R_ps])
                for g in range(2):
                    (io, bo, ro), (idn, bd, rdn) = acc[g]
                    bov = bo[:, :].rearrange("p (h q) -> p h q", h=4)[:, :, 8 * b:8 * b + 8]
                    bdv = bd[:, :].rearrange("p (h q) -> p h q", h=4)[:, :, 8 * b:8 * b + 8]
                    fns = []
                    for j in range(16):
                        fns.append(lambda j=j, g=g, bov=bov: nc.tensor.matmul(
                            bov, vb[i][:, j // 8, j % 8, g * 128:(g + 1) * 128], ps_[:, j, g * 32:(g + 1) * 32],
                            start=False, stop=False, skip_group_check=True))
                        fns.append(lambda j=j, g=g, bdv=bdv: nc.tensor.matmul(bdv, onesb[:], ps_[:, j, g * 32:(g + 1) * 32],
                                                                            start=False, stop=False, skip_group_check=True))
                    kb.mm(fns, reads=[R_ps, R_vb[i], R_const], writes=[ro, rdn])
            pinned.discard(il0)
            pinned.discard(il1)
            for g in range(2):
                (io, bo, ro), (idn, bd, rdn) = acc[g]
                finalize(g, bo, ro, bd, rdn, cols)
                pinned.discard(io)
                pinned.discard(idn)

        if samp:
            for _ in gen_ga():
                pass
        for cc in range(16):
            kb.op("act", lambda: nc.scalar.activation(out=m_T[:, cc, :], in_=m_T[:, cc, :], func=AF.Sigmoid),
                  reads=[R_m], writes=[R_m])
        jobs = []
        for b4 in range(4):
            def ua_job(wv, wr, b4=b4):
                def ev(c, ps, br):
                    cc = b4 * 4 + c
                    kb.op("dve", lambda: nc.vector.tensor_tensor(out=m_T[:, cc, :], in0=ps, in1=m_T[:, cc, :], op=ALU.mult),
                          reads=[br], writes=[R_m])
                formF(wv, wr, 4, 128, ev, kcn=8, src=gT, R_src=R_g)
            jobs.append(((f"ua{b4}", [(0, w_ua_v[:, :, b4 * 512:(b4 + 1) * 512])], 8, 512), ua_job))
        sgc = {"n": 0}
        run_jobs(jobs, ("gg0", [], 16, 512))
        kb.barrier()

        ar.reset()
        NB_ = 16 if samp else 1
        TW = 38 if samp else NT + 30
        uT = ar.get([128, 8, NB_ * TW], BF16)
        zcT = ar.get([128, 8, NT], BF16)
        dwf = ar.get([128, 8, NT], F32)
        aT = ar.get([128, 8, NT], BF16)
        cT = ar.get([128, 8, NT], BF16)
        diag = [ar.get([128, 31, 128], BF16) for _ in range(2)]
        sqt = [ar.get([128, NT], F32) for _ in range(2)]
        lnm = ar.get([128, NT], F32)
        lnv = ar.get([128, NT], F32)
        lnr = ar.get([128, NT], F32)
        t1 = [ar.get([128, NT], F32) for _ in range(2)]
        sgt2 = [ar.get([128, NT], F32) for _ in range(2)]
        tt2 = [ar.get([128, NT], F32) for _ in range(2)]
        R_u, R_zc, R_dw, R_a, R_c, R_ln = Res(), Res(), Res(), Res(), Res(), Res()
        R_diag = [Res(), Res()]
        R_sq = [Res(), Res()]
        R_t1 = [Res(), Res()]
        R_sg2 = [Res(), Res()]
        R_tt2 = [Res(), Res()]
        need_tok = samp or gi == 3
        if need_tok:
            sgtok = ar.get([128, 1024], F32)
            utok = ar.get([128, 1024], F32)
            R_sgtok, R_utok = Res(), Res()
            s_ut = kb.slot(f"ut", scratch=True)
        if samp:
            uv = uT[:].rearrange("p c (b w) -> p c b w", b=16)

            def ucols(c):
                return uv[:, c, :, 30:38]
            stt = ar.get([120, 1024], F32)
            R_stt = Res()
            s_stt = kb.slot("stt", scratch=True)
            for q4 in range(4):
                kb.dma("sp", stt[:], state[q4 * 4:(q4 + 1) * 4].rearrange("b r c -> (b r) c"), s_stt, writes=[R_stt])
                for c2 in range(2):
                    bi, bk, br = bank()
                    kb.mm([(lambda k=k: nc.tensor.transpose(out=bk[:, k * 120:(k + 1) * 120],
                                                            in_=stt[:, (c2 * 4 + k) * 128:(c2 * 4 + k + 1) * 128],
                                                            identity=identf[0:120, 0:120])) for k in range(4)],
                          reads=[R_stt, R_const], writes=[br])
                    kb.op("dve", lambda: nc.vector.tensor_copy(
                        out=uv[:, c2 * 4:(c2 + 1) * 4, q4 * 4:(q4 + 1) * 4, 0:30],
                        in_=bk[:, 0:480].rearrange("p (k b r) -> p k b r", k=4, b=4)), reads=[br], writes=[R_u])
            kb.dma("sp", convs[:, 0:22, :], state[:, 8:30, :], s_stt, is_out=True)
        else:
            def ucols(c):
                return uT[:, c, 30:30 + NT]
            kb.op("dve", lambda: nc.vector.tensor_copy(out=uT[:, :, 0:30], in_=uhalo[:]), reads=[R_const, R_uh], writes=[R_u])

        jobs = []
        for b2 in range(2):
            def gg_job(wv, wr, b2=b2):
                formF(wv, wr, 4, 128, copy_evac(lambda c: ucols(b2 * 4 + c), R_u, AF.Sigmoid))
                if need_tok:
                    t = ntile - 1
                    bi, bk, br = bank()
                    kb.mm([(lambda kc=kc: nc.tensor.matmul(bk[:, 0:512], xn[:, kc, t * 128:(t + 1) * 128], wv[:, kc, :],
                                                           start=(kc == 0), stop=(kc == 15))) for kc in range(16)],
                          reads=[wr, R_xn], writes=[br])
                    kb.op("act", lambda: nc.scalar.activation(out=sgtok[:, b2 * 512:(b2 + 1) * 512], in_=bk[:, 0:512],
                                                              func=AF.Sigmoid), reads=[br], writes=[R_sgtok])
            jobs.append(((f"gg{b2}", [(0, w_in_v[:, :, C_GG + b2 * 512: C_GG + (b2 + 1) * 512])], 16, 512), gg_job))
        for b2 in range(2):
            def gv_job(wv, wr, b2=b2):
                def ev(c, ps, br):
                    kb.op("dve", lambda: nc.vector.tensor_tensor(out=ucols(b2 * 4 + c), in0=ps if not samp else
                                                                 ps.rearrange("p (b t) -> p b t", b=16),
                                                                 in1=ucols(b2 * 4 + c), op=ALU.mult), reads=[br], writes=[R_u])
                formF(wv, wr, 4, 128, ev)
                if need_tok:
                    t = ntile - 1
                    bi, bk, br = bank()
                    kb.mm([(lambda kc=kc: nc.tensor.matmul(bk[:, 0:512], xn[:, kc, t * 128:(t + 1) * 128], wv[:, kc, :],
                                                           start=(kc == 0), stop=(kc == 15))) for kc in range(16)],
                          reads=[wr, R_xn], writes=[br])
                    kb.op("dve", lambda: nc.vector.tensor_tensor(out=utok[:, b2 * 512:(b2 + 1) * 512], in0=bk[:, 0:512],
                                                                 in1=sgtok[:, b2 * 512:(b2 + 1) * 512], op=ALU.mult),
                          reads=[br, R_sgtok], writes=[R_utok])
            jobs.append(((f"gv{b2}", [(0, w_in_v[:, :, C_GV + b2 * 512: C_GV + (b2 + 1) * 512])], 16, 512), gv_job))
        def conv_job():
            if need_tok:
                if samp:
                    for b in range(16):
                        kb.dma("sp", convs[b, 22:30, :], utok[8 * b:8 * b + 8, :], s_ut, reads=[R_utok], is_out=True)
                else:
                    kb.dma("sp", convp[:, :], utok[98:128, :], s_ut, reads=[R_utok], is_out=True)
            if not samp:
                kb.op("dve", lambda: nc.vector.tensor_copy(out=uhalo[:], in_=uT[:, :, NT:NT + 30]), reads=[R_u], writes=[R_uh])
            for c in range(8):
                di = c % 2
                kb.dma("sp", diag[di][:].rearrange("p a b -> p (a b)"), dsc[c], s_dg[di], reads=[R_dsc[c]], writes=[R_diag[di]])
                bi, bk, br = bank()
                if samp:
                    rhsf = lambda j: uv[:, c, :, j:j + 8]
                else:
                    rhsf = lambda j: uT[:, c, j:j + NT]
                kb.mm([(lambda j=j: nc.tensor.matmul(bk[:, 0:NT], diag[di][:, j, :], rhsf(j), start=(j == 0), stop=(j == 30)))
                       for j in range(31)], reads=[R_diag[di], R_u], writes=[br])
                kb.op("act", lambda: nc.scalar.activation(out=dwf[:, c, :], in_=bk[:, 0:NT], func=AF.Identity,
                                                          bias=cbT[:, c:c + 1], scale=1.0), reads=[br, R_const], writes=[R_dw])
            im, bm, rm = pin()
            iq, bq, rq = pin()
            for c in range(8):
                si = c % 2
                kb.op("act", lambda: nc.scalar.activation(out=sqt[si][:], in_=dwf[:, c, :], func=AF.Square),
                      reads=[R_dw], writes=[R_sq[si]])
                kb.mm([lambda: nc.tensor.matmul(bm[:, 0:NT], onesf[:], dwf[:, c, :], start=(c == 0), stop=(c == 7),
                                                skip_group_check=True)], reads=[R_dw, R_const], writes=[rm])
                kb.mm([lambda: nc.tensor.matmul(bq[:, 0:NT], onesf[:], sqt[si][:], start=(c == 0), stop=(c == 7),
                                                skip_group_check=True)], reads=[R_sq[si], R_const], writes=[rq])
            kb.op("dve", lambda: nc.vector.tensor_copy(out=lnm[:], in_=bm[:, 0:NT]), reads=[rm], writes=[R_ln])
            kb.op("dve", lambda: nc.vector.scalar_tensor_tensor(out=lnv[:], in0=lnm[:], scalar=-1.0, in1=lnm[:],
                                                                op0=ALU.mult, op1=ALU.mult), reads=[R_ln], writes=[R_ln])
            kb.op("dve", lambda: nc.vector.tensor_tensor(out=lnv[:], in0=bq[:, 0:NT], in1=lnv[:], op=ALU.add),
                  reads=[rq, R_ln], writes=[R_ln])
            kb.op("act", lambda: nc.scalar.activation(out=lnv[:], in_=lnv[:], func=AF.Sqrt, bias=epsT[:, 0:1], scale=1.0),
                  reads=[R_ln, R_const], writes=[R_ln])
            kb.op("dve", lambda: nc.vector.reciprocal(out=lnr[:], in_=lnv[:]), reads=[R_ln], writes=[R_ln])
            pinned.discard(im)
            pinned.discard(iq)
            for c in range(8):
                i = c % 2
                kb.op("dve", lambda: nc.vector.tensor_tensor(out=t1[i][:], in0=dwf[:, c, :], in1=lnm[:], op=ALU.subtract),
                      reads=[R_dw, R_ln], writes=[R_t1[i]])
                kb.op("dve", lambda: nc.vector.tensor_tensor(out=t1[i][:], in0=t1[i][:], in1=lnr[:], op=ALU.mult),
                      reads=[R_ln], writes=[R_t1[i]])
                kb.op("act", lambda: nc.scalar.activation(out=aT[:, c, :], in_=t1[i][:], func=AF.Silu,
                                                          bias=cnbT[:, c:c + 1], scale=cngT[:, c:c + 1]),
                      reads=[R_t1[i], R_const], writes=[R_a])
        jobs.append((None, conv_job))
        for b2 in range(2):
            jobs.append(((f"zc{b2}", [(0, w_in_v[:, :, C_ZC + b2 * 512: C_ZC + (b2 + 1) * 512])], 16, 512),
                         lambda wv, wr, b2=b2: formF(wv, wr, 4, 128, copy_evac(lambda c: zcT[:, b2 * 4 + c, :], R_zc, AF.Silu))))

        for b2 in range(2):
            def pw_job(wv, wr, b2=b2):
                def ev(c, ps, br):
                    cc = b2 * 4 + c
                    kb.op("dve", lambda: nc.vector.scalar_tensor_tensor(out=cT[:, cc, :], in0=ps, scalar=bpwT[:, cc:cc + 1],
                                                                        in1=zcT[:, cc, :], op0=ALU.add, op1=ALU.mult),
                          reads=[br, R_zc, R_const], writes=[R_c])
                formF(wv, wr, 4, 128, ev, kcn=8, src=aT, R_src=R_a)
            jobs.append(((f"pw{b2}", [(0, w_pw_v[:, :, b2 * 512:(b2 + 1) * 512])], 8, 512), pw_job))
        bcb = {}
        for b4 in range(4):
            def uc_job(wv, wr, b4=b4):
                for c in range(4):
                    bi, bk, br = pin()
                    kb.mm([(lambda kc=kc: nc.tensor.matmul(bk[:, 0:NT], wv[:, kc, c * 128:(c + 1) * 128], cT[:, kc, :],
                                                           start=(kc == 0), stop=(kc == 7))) for kc in range(8)],
                          reads=[wr, R_c], writes=[br])
                    bcb[(b4, c)] = (bi, bk, br)
            jobs.append(((f"uc{b4}", [(0, w_uc_v[:, :, b4 * 512:(b4 + 1) * 512])], 8, 512), uc_job))

            def gc_job(wv, wr, b4=b4):
                def ev(c, ps, br):
                    i = sgc["n"] % 2
                    sgc["n"] += 1
                    bi2, bk2, br2 = bcb.pop((b4, c))
                    kb.op("act", lambda: nc.scalar.activation(out=sgt2[i][:], in_=ps, func=AF.Sigmoid), reads=[br], writes=[R_sg2[i]])
                    kb.op("dve", lambda: nc.vector.tensor_tensor(out=tt2[i][:], in0=bk2[:, 0:NT], in1=sgt2[i][:], op=ALU.mult),
                          reads=[br2, R_sg2[i]], writes=[R_tt2[i]])
                    pinned.discard(bi2)
                    kb.op("dve", lambda: nc.vector.tensor_tensor(out=m_T[:, b4 * 4 + c, :], in0=m_T[:, b4 * 4 + c, :],
                                                                 in1=tt2[i][:], op=ALU.add), reads=[R_tt2[i]], writes=[R_m])
                formF(wv, wr, 4, 128, ev)
            jobs.append(((f"gc{b4}", [(0, w_in_v[:, :, C_GC + b4 * 512: C_GC + (b4 + 1) * 512])], 16, 512), gc_job))
        run_jobs(jobs, ("wo0", [], 16, 512))
        kb.barrier()

        ar.reset()
        yb = [ar.get([128, D], F32) for _ in range(ntile)]
        gbc = ar.get([128, D], F32)
        jk = ar.get([128, D], F32)
        st2 = ar.get([128, 16], F32)
        R_yb = [Res() for _ in range(ntile)]
        R_gbc, R_jk = Res(), Res()
        R_st2 = [Res() for _ in range(ntile)]
        s_y = [kb.slot(f"y_{i}", scratch=True) for i in range(ntile)]
        s_g = kb.slot(f"g", scratch=True)
        for t in range(ntile):
            kb.dma("sp", yb[t][:], x[tok0 + t * 128: tok0 + (t + 1) * 128, :], s_y[t], writes=[R_yb[t]])
        kb.dma("sp", gbc[:], final_g.broadcast_to([128, D]), s_g, writes=[R_gbc])

        def final_norm(t):
            c0 = 4 * t
            kb.op("act", lambda: nc.scalar.activation(out=jk[:], in_=yb[t][:], func=AF.Square, accum_out=st2[:, c0:c0 + 1]),
                  reads=[R_yb[t]], writes=[R_jk, R_st2[t]])
            kb.op("act", lambda: nc.scalar.activation(out=st2[:, c0 + 1:c0 + 2], in_=st2[:, c0:c0 + 1], func=AF.Sqrt,
                                                      bias=epsT[:, 0:1], scale=1.0 / D), reads=[R_st2[t], R_const],
                  writes=[R_st2[t]])
            kb.op("dve", lambda: nc.vector.reciprocal(out=st2[:, c0 + 2:c0 + 3], in_=st2[:, c0 + 1:c0 + 2]), reads=[R_st2[t]],
                  writes=[R_st2[t]])
            kb.op("dve", lambda: nc.vector.scalar_tensor_tensor(out=yb[t][:], in0=yb[t][:], scalar=st2[:, c0 + 2:c0 + 3],
                                                                in1=gbc[:], op0=ALU.mult, op1=ALU.mult),
                  reads=[R_st2[t], R_gbc], writes=[R_yb[t]])
            kb.dma("sp", y[tok0 + t * 128: tok0 + (t + 1) * 128, :], yb[t][:], s_y[t], reads=[R_yb[t]], is_out=True)

        jobs = []
        for b4 in range(4):
            def wo_job(wv, wr, b4=b4):
                for t in range(ntile):
                    bi, bk, br = bank()
                    kb.mm([(lambda kc=kc: nc.tensor.matmul(bk[:, 0:512], m_T[:, kc, t * 128:(t + 1) * 128], wv[:, kc, :],
                                                           start=(kc == 0), stop=(kc == 15))) for kc in range(16)],
                          reads=[wr, R_m], writes=[br])
                    ysl = yb[t][:, b4 * 512:(b4 + 1) * 512]
                    kb.op("dve", lambda: nc.vector.tensor_tensor(out=ysl, in0=bk[:, 0:512], in1=ysl, op=ALU.add),
                          reads=[br], writes=[R_yb[t]])
                    if b4 == 3:
                        final_norm(t)
            jobs.append(((f"wo{b4}", [(0, w_out_v[:, :, b4 * 512:(b4 + 1) * 512])], 16, 512), wo_job))
        if gi < 4:
            pbn = p1_alloc(gi + 1)
            run_jobs(jobs, ("q0", [], 16, 512), hook=(1, lambda: p1_norm(gi + 1, pbn)))
            p1_tr(gi + 1, pbn)
        else:
            run_jobs(jobs, None)
        kb.barrier()

    for gi in range(5):
        group(gi)
    last = {}
    for sem, val in kb.out_toks:
        last[id(sem)] = (sem, max(val, last.get(id(sem), (None, 0))[1]))
    for tok in last.values():
        kb.eng["sp"].wait(tok)
    return nc


_NC = None


def kernel(x_prompt, x_sample, cache_k, cache_v, cache_kidx, state_conv, page_table, ln_g, w_in, conv_w, conv_b,
           conv_norm_g, conv_norm_b, w_pw, b_pw, w_up_attn, w_up_conv, w_out, rel_bias, final_g):
    global _NC
    if _NC is None:
        _NC = build()
    nc = _NC
    f = lambda a: np.ascontiguousarray(np.asarray(a, dtype=np.float32))
    consts = host_consts()
    ck = f(cache_k)[0].reshape(NPHYS * 16, 2048)
    cv = f(cache_v)[0].reshape(NPHYS * 16, 2048)
    cki = f(cache_kidx)[0].reshape(NPHYS * 16, 512)
    shared = {
        "cache_k": ck, "cache_v": cv, "cache_ki": cki, "ln_g": f(ln_g)[0], "w_in": f(w_in)[0], "conv_w": f(conv_w)[0],
        "conv_b": f(conv_b)[0], "cn_g": f(conv_norm_g)[0], "cn_b": f(conv_norm_b)[0], "w_pw": f(w_pw)[0],
        "b_pw": f(b_pw)[0], "w_ua": f(w_up_attn)[0], "w_uc": f(w_up_conv)[0], "w_out": f(w_out)[0],
        "rel_bias": f(rel_bias), "final_g": f(final_g).reshape(1, D),
    }
    for k, v in consts.items():
        shared["c_" + k] = v
    xp = f(x_prompt)
    xs = f(x_sample)
    st = f(state_conv)[0]
    pt = np.ascontiguousarray(np.asarray(page_table, dtype=np.int32))
    in_maps = []
    for c in range(8):
        m = dict(shared)
        m["x"] = np.ascontiguousarray(np.concatenate([xp[c], xs[16 * c:16 * c + 16].reshape(128, D)], axis=0))
        m["state"] = np.ascontiguousarray(st[16 * c:16 * c + 16])
        m["ptab"] = np.ascontiguousarray(pt[16 * c:16 * c + 16].reshape(1, 256))
        in_maps.append(m)
    res = run_bass_kernel_spmd(nc, in_maps, core_ids=list(range(8)))
    R = res.results
    y_p = np.stack([R[c]["y"][0:2048] for c in range(8)])
    y_s = np.concatenate([R[c]["y"][2048:2176].reshape(16, 8, D) for c in range(8)], axis=0)
    k_p = np.stack([R[c]["kk"][0:2048].reshape(2048, 2, 128) for c in range(8)])[None]
    v_p = np.stack([R[c]["vv"][0:2048].reshape(2048, 2, 128) for c in range(8)])[None]
    ki_p = np.stack([R[c]["kio"][0:2048] for c in range(8)])[None]
    cp = np.stack([R[c]["convp"] for c in range(8)])[None]
    k_s = np.concatenate([R[c]["kk"][2048:2176].reshape(16, 8, 2, 128) for c in range(8)], axis=0)[None]
    v_s = np.concatenate([R[c]["vv"][2048:2176].reshape(16, 8, 2, 128) for c in range(8)], axis=0)[None]
    ki_s = np.concatenate([R[c]["kio"][2048:2176].reshape(16, 8, 64) for c in range(8)], axis=0)[None]
    cs = np.concatenate([R[c]["convs"] for c in range(8)], axis=0)[None]
    return (y_p.astype(np.float32), y_s.astype(np.float32), k_p.astype(np.float32), v_p.astype(np.float32),
            ki_p.astype(np.float32), cp.astype(np.float32), k_s.astype(np.float32), v_s.astype(np.float32),
            ki_s.astype(np.float32), cs.astype(np.float32))
```

```python
import math
import numpy as np
import concourse.bass as bass
import concourse.mybir as mybir
from concourse.bass_utils import run_bass_kernel_spmd

F32 = mybir.dt.float32
BF16 = mybir.dt.bfloat16
I32 = mybir.dt.int32
U8 = mybir.dt.uint8
AF = mybir.ActivationFunctionType
ALU = mybir.AluOpType
AX = mybir.AxisListType
ET = mybir.EngineType

D = 2048
DIN = 10312
C_Q, C_K, C_V, C_ZA, C_QI, C_KI, C_WI, C_GV, C_GG, C_ZC, C_GA, C_GC = (
    0, 1024, 1280, 1536, 2560, 3072, 3136, 3144, 4168, 5192, 6216, 8264)
NPHYS = 2560
NEGB = -30000.0
SCALE = 128.0 ** -0.5
SQ128 = 128.0 ** 0.5
NBIS = 18
EPS = 1e-6


class Res:
    __slots__ = ("w", "r")

    def __init__(self):
        self.w = None
        self.r = {}


class Eng:
    def __init__(self, nc, e, name):
        self.e = e
        self.sem = nc.alloc_semaphore("p_" + name)
        self.n = 0
        self.seen = {}

    def wait(self, tok):
        if tok is None:
            return
        sem, val = tok
        k = id(sem)
        if self.seen.get(k, 0) >= val:
            return
        self.e.wait_ge(sem, val)
        self.seen[k] = val


class Slot:
    def __init__(self, nc, name, scratch=False):
        self.sem = nc.alloc_semaphore("d_" + name)
        self.n = 0
        self.scratch = scratch


class KB:
    def __init__(self, nc):
        self.nc = nc
        self.eng = {
            "pe": Eng(nc, nc.tensor, "pe"), "act": Eng(nc, nc.scalar, "act"),
            "dve": Eng(nc, nc.vector, "dve"), "pool": Eng(nc, nc.gpsimd, "pool"),
            "sp": Eng(nc, nc.sync, "sp"),
        }
        self.slots = []
        self.slotd = {}
        self.out_toks = []
        self.flip = 0

    def slot(self, name, scratch=False):
        if name in self.slotd:
            return self.slotd[name]
        s = Slot(self.nc, name, scratch)
        self.slots.append(s)
        self.slotd[name] = s
        return s

    def _deps(self, E, reads, writes):
        for r in reads:
            if r.w:
                E.wait(r.w)
        for w in writes:
            if w.w:
                E.wait(w.w)
            for t in w.r.values():
                E.wait(t)

    def _mark(self, tok, reads, writes):
        for r in reads:
            r.r[id(tok[0])] = tok
        for w in writes:
            w.w = tok
            w.r = {}

    def op(self, en, fn, reads=(), writes=()):
        E = self.eng[en]
        self._deps(E, reads, writes)
        ins = fn()
        E.n += 1
        ins.then_inc(E.sem, 1)
        tok = (E.sem, E.n)
        self._mark(tok, reads, writes)
        return tok

    def mm(self, fns, reads=(), writes=()):
        E = self.eng["pe"]
        self._deps(E, reads, writes)
        ins = None
        for f in fns:
            ins = f()
        E.n += 1
        ins.then_inc(E.sem, 1)
        tok = (E.sem, E.n)
        self._mark(tok, reads, writes)
        return tok

    def dma(self, q, out, in_, slot, reads=(), writes=(), is_out=False, **kw):
        E = self.eng[q]
        self._deps(E, reads, writes)
        ins = E.e.dma_start(out=out, in_=in_, **kw)
        slot.n += 16
        ins.then_inc(slot.sem, 16)
        tok = (slot.sem, slot.n)
        self._mark(tok, reads, writes)
        if is_out:
            self.out_toks.append(tok)
        return tok

    def idma(self, out, in_, idx_ap, slot, reads=(), writes=()):
        E = self.eng["pool"]
        self._deps(E, reads, writes)
        ins = self.nc.gpsimd.indirect_dma_start(out=out, out_offset=None, in_=in_,
                                                in_offset=bass.IndirectOffsetOnAxis(ap=idx_ap, axis=0))
        slot.n += 16
        ins.then_inc(slot.sem, 16)
        tok = (slot.sem, slot.n)
        self._mark(tok, reads, writes)
        return tok

    def barrier(self):
        toks = [(E.sem, E.n) for k, E in self.eng.items() if E.n > 0 and k != "sp"]
        toks += [(s.sem, s.n) for s in self.slots if s.scratch and s.n > 0]
        for E in self.eng.values():
            for t in toks:
                E.wait(t)

    def ev(self):
        self.flip ^= 1
        return "act" if self.flip else "dve"


def t5_bucket_np(n):
    n = np.maximum(n, 0)
    max_exact = 16
    ratio = np.log(np.maximum(n, 1).astype(np.float32) / np.float32(max_exact)) / np.float32(math.log(128 / max_exact))
    large = np.minimum(max_exact + (ratio.astype(np.float32) * np.float32(16)).astype(np.int32), 31)
    return np.where(n < max_exact, n, large)


def host_consts():
    c = {}
    c["identf"] = np.eye(128, dtype=np.float32)
    m = np.arange(384)
    n = m - 128
    oh = np.zeros((33, 384), np.float32)
    bk = t5_bucket_np(n)
    for i in range(384):
        if n[i] >= 0:
            oh[bk[i], i] = 1.0
        else:
            oh[32, i] = 1.0
    c["oh33"] = oh
    q = np.arange(128)[:, None]
    s = np.arange(128)[None, :]
    c["causneg"] = np.where(s <= q, 0.0, -1e30).astype(np.float32)
    same = (q // 8) == (s // 8)
    c["sampneg"] = np.where(same & ((s % 8) <= (q % 8)), 0.0, -1e30).astype(np.float32)
    c["sampbm"] = same.astype(np.float32)
    c["sampnb"] = np.where(same, 0.0, NEGB * SQ128).astype(np.float32)
    tok = np.arange(128)
    dq = np.zeros((128, 8, 8), np.float32)
    for t in range(128):
        dq[t, :, t % 8] = 1.0
    c["dq"] = dq.reshape(128, 64)
    bs = np.zeros((128, 16), np.float32)
    bs[tok, tok // 8] = 1.0
    c["bsel"] = bs
    sb = np.zeros((64, 248), np.float32)
    for hq in range(64):
        sb[hq, 120 + hq % 8] = 1.0
    c["selbase"] = sb
    c["oh8T"] = (np.arange(128)[None, :] // 16 == np.arange(8)[:, None]).astype(np.float32)
    c["fvec"] = np.tile((2.0 ** -(np.arange(32, dtype=np.float64) + 1)).astype(np.float32)[None, :], (128, 1))
    c["pidmat"] = np.tile((np.arange(128, dtype=np.float32) % 16)[:, None], (1, 32))
    return c


CONST_SHAPES = {"identf": [128, 128], "oh33": [33, 384], "causneg": [128, 128], "sampneg": [128, 128],
                "sampbm": [128, 128], "sampnb": [128, 128], "dq": [128, 64], "bsel": [128, 16],
                "selbase": [64, 248], "pidmat": [128, 32], "fvec": [128, 32], "oh8T": [8, 128]}


def build():
    nc = bass.Bass("TRN2", target_bir_lowering=False)
    kb = KB(nc)

    def din(name, shape, dt=F32):
        return nc.dram_tensor(name, shape, dt, kind="ExternalInput").ap()

    def dout(name, shape):
        return nc.dram_tensor(name, shape, F32, kind="ExternalOutput").ap()

    x = din("x", [2176, D])
    cache_k = din("cache_k", [NPHYS * 16, 2048])
    cache_v = din("cache_v", [NPHYS * 16, 2048])
    cache_ki = din("cache_ki", [NPHYS * 16, 512])
    state = din("state", [16, 30, 1024])
    ptab = din("ptab", [1, 256], I32)
    ln_g = din("ln_g", [D])
    w_in = din("w_in", [D, DIN])
    conv_w = din("conv_w", [31, 1024])
    conv_b = din("conv_b", [1024])
    cn_g = din("cn_g", [1024])
    cn_b = din("cn_b", [1024])
    w_pw = din("w_pw", [1024, 1024])
    b_pw = din("b_pw", [1024])
    w_ua = din("w_ua", [1024, D])
    w_uc = din("w_uc", [1024, D])
    w_out = din("w_out", [D, D])
    rel_bias = din("rel_bias", [32, 8])
    final_g = din("final_g", [1, D])
    cin = {k: din("c_" + k, v) for k, v in CONST_SHAPES.items()}

    y = dout("y", [2176, D])
    kk = dout("kk", [2176, 256])
    vv = dout("vv", [2176, 256])
    kio = dout("kio", [2176, 64])
    convp = dout("convp", [30, 1024])
    convs = dout("convs", [16, 30, 1024])
    dper = nc.dram_tensor("dper", [8, 129 * 384], F32, kind="Internal").ap()

    w_in_v = w_in.rearrange("(kc p) f -> p kc f", p=128)
    w_out_v = w_out.rearrange("(kc p) f -> p kc f", p=128)
    w_pw_v = w_pw.rearrange("(kc p) f -> p kc f", p=128)
    w_ua_v = w_ua.rearrange("(kc p) f -> p kc f", p=128)
    w_uc_v = w_uc.rearrange("(kc p) f -> p kc f", p=128)

    wsc = {}

    def wconv(key, src, kcn, ncols, col0=0, wtot=None):
        wtot = ncols if wtot is None else wtot
        if key not in wsc:
            t = nc.dram_tensor("wsc_" + key, [128, kcn * wtot], BF16, kind="Internal").ap()
            wsc[key] = (t, Res(), kb.slot("cv_" + key))
        t, r, sl = wsc[key]
        tv = t.rearrange("p (k c) -> p k c", k=kcn)
        kb.dma("pool", tv[:, :, col0:col0 + ncols], src, sl, writes=[r])

    def conv_first():
        wconv("kv", w_in_v[:, :, C_K:C_K + 512], 16, 512)
        wconv("qi", w_in_v[:, :, C_QI:C_QI + 512], 16, 512)
        wconv("ki", w_in_v[:, :, C_KI:C_KI + 72], 16, 72)
        for b2 in range(2):
            wconv(f"q{b2}", w_in_v[:, :, C_Q + b2 * 512: C_Q + (b2 + 1) * 512], 16, 512)
        for b2 in range(2):
            wconv(f"za{b2}", w_in_v[:, :, C_ZA + b2 * 512: C_ZA + (b2 + 1) * 512], 16, 512)

    def conv_rest(after):
        kb._deps(kb.eng["pool"], after, [])
        for b4 in range(4):
            wconv(f"ua{b4}", w_ua_v[:, :, b4 * 512:(b4 + 1) * 512], 8, 512)
        for b4 in range(4):
            wconv(f"ga{b4}", w_in_v[:, :, C_GA + b4 * 512: C_GA + (b4 + 1) * 512], 16, 512)
        for b2 in range(2):
            wconv(f"gg{b2}", w_in_v[:, :, C_GG + b2 * 512: C_GG + (b2 + 1) * 512], 16, 512)
        for b2 in range(2):
            wconv(f"gv{b2}", w_in_v[:, :, C_GV + b2 * 512: C_GV + (b2 + 1) * 512], 16, 512)
        for b2 in range(2):
            wconv(f"zc{b2}", w_in_v[:, :, C_ZC + b2 * 512: C_ZC + (b2 + 1) * 512], 16, 512)
        for b2 in range(2):
            wconv(f"pw{b2}", w_pw_v[:, :, b2 * 512:(b2 + 1) * 512], 8, 512)
        for b4 in range(4):
            wconv(f"uc{b4}", w_uc_v[:, :, b4 * 512:(b4 + 1) * 512], 8, 512)
            wconv(f"gc{b4}", w_in_v[:, :, C_GC + b4 * 512: C_GC + (b4 + 1) * 512], 16, 512)
        for b4 in range(4):
            wconv(f"wo{b4}", w_out_v[:, :, b4 * 512:(b4 + 1) * 512], 16, 512)


    def sb(name, shape, dt):
        return nc.alloc_sbuf_tensor(name, shape, dt)

    identf = sb("identf", [128, 128], F32)
    identb = sb("identb", [128, 128], BF16)
    onesb = sb("onesb", [128, 128], BF16)
    onesf = sb("onesf", [128, 128], F32)
    causneg = sb("causneg", [128, 128], F32)
    sampneg = sb("sampneg", [128, 128], F32)
    sampbm = sb("sampbm", [128, 128], F32)
    sampnb = sb("sampnb", [128, 128], F32)
    dq = sb("dq", [128, 64], F32)
    bsel = sb("bsel", [128, 16], F32)
    selbase = sb("selbase", [64, 248], F32)
    prmT = sb("prmT", [128, 48], F32)
    lngT = prmT[:, 0:16]
    cbT = prmT[:, 16:24]
    cngT = prmT[:, 24:32]
    cnbT = prmT[:, 32:40]
    bpwT = prmT[:, 40:48]
    cwT = sb("cwT", [128, 8, 31], F32)
    oh8T = sb("oh8T", [8, 128], F32)
    epsT = sb("epsT", [128, 1], F32)
    pidmat = sb("pidmat", [128, 32], F32)
    fvec = sb("fvec", [128, 32], F32)
    idxall = sb("idxall", [128, 32], I32)
    biasT = sb("biasT", [128, 4, 8, 128], BF16)
    xnT = sb("xnT", [128, 16, 512], BF16)
    mT = sb("mT", [128, 16, 512], BF16)
    wbuf = [sb(f"wbuf{i}", [128, 8192], BF16) for i in range(2)]
    KT = sb("KT", [128, 2, 2048], BF16)
    VA = sb("VA", [128, 16, 2, 128], BF16)
    KiT = sb("KiT", [64, 2048], BF16)
    uhalo = sb("uhalo", [128, 8, 30], BF16)
    SCR = 102 * 1024
    scr = sb("scr", [128, SCR], U8)
    psum_all = nc.alloc_psum_tensor("ps", [128, 4096], F32)

    class Arena:
        def __init__(self):
            self.off = 0

        def reset(self):
            self.off = 0

        def get(self, shape, dt):
            esz = 4 if dt in (F32, I32) else 2
            n = int(np.prod(shape[1:])) * esz
            n = (n + 63) // 64 * 64
            assert self.off + n <= SCR, (self.off, n)
            ap = scr[0:shape[0], self.off:self.off + n].bitcast(dt)
            self.off += n
            if len(shape) == 3:
                ap = ap[:, 0:shape[1] * shape[2]].rearrange("p (a b) -> p a b", a=shape[1])
            elif len(shape) == 4:
                ap = ap[:, 0:shape[1] * shape[2] * shape[3]].rearrange("p (a b c) -> p a b c", a=shape[1], b=shape[2])
            else:
                ap = ap[:, 0:shape[1]]
            return ap

    ar = Arena()

    banks = [psum_all[:, i * 512:(i + 1) * 512] for i in range(8)]
    bres = [Res() for _ in range(8)]
    pinned = set()
    bstate = {"i": 0}

    def bank():
        while True:
            i = bstate["i"] % 8
            bstate["i"] += 1
            if i not in pinned:
                return i, banks[i], bres[i]

    s_setup = kb.slot("setup")
    R_const = Res()
    ar.reset()
    ar.off = 56 * 1024
    prm = ar.get([48, 128], F32)
    cwst = ar.get([31, 1024], F32)
    ptJ = ar.get([8, 32], I32)
    ptJf = ar.get([8, 32], F32)
    with nc.allow_non_contiguous_dma(reason="tiny page table transpose"):
        for t_, a_ in [(identf, cin["identf"]), (causneg, cin["causneg"]), (sampneg, cin["sampneg"]),
                       (sampbm, cin["sampbm"]), (sampnb, cin["sampnb"]), (dq, cin["dq"]), (bsel, cin["bsel"]),
                       (selbase, cin["selbase"]), (pidmat, cin["pidmat"]), (fvec, cin["fvec"]), (oh8T, cin["oh8T"])]:
            kb.dma("sp", t_[:], a_, s_setup, writes=[R_const])
        kb.dma("sp", ptJ[:], ptab.rearrange("o (bh j) -> (o j) bh", j=8), s_setup, writes=[R_const])
        kb.dma("sp", prm[0:16, :], ln_g.rearrange("(kc p) -> kc p", p=128), s_setup, writes=[R_const])
        for k_, a_ in enumerate([conv_b, cn_g, cn_b, b_pw]):
            kb.dma("sp", prm[16 + 8 * k_:24 + 8 * k_, :], a_.rearrange("(c p) -> c p", p=128), s_setup, writes=[R_const])
        kb.dma("sp", cwst[:], conv_w, s_setup, writes=[R_const])
    bi, bk_, br = bank()
    kb.mm([lambda: nc.tensor.transpose(out=bk_[:, 0:48], in_=prm[0:48, :], identity=identf[0:48, 0:48])],
          reads=[R_const], writes=[br])
    kb.op("dve", lambda: nc.vector.tensor_copy(out=prmT[:], in_=bk_[:, 0:48]), reads=[br], writes=[R_const])
    bi, bk_, br = bank()
    kb.mm([(lambda c_=c_: nc.tensor.transpose(out=bk_[:, c_ * 31:(c_ + 1) * 31], in_=cwst[0:31, c_ * 128:(c_ + 1) * 128],
                                              identity=identf[0:31, 0:31])) for c_ in range(8)],
          reads=[R_const], writes=[br])
    kb.op("dve", lambda: nc.vector.tensor_copy(out=cwT[:].rearrange("p a b -> p (a b)"), in_=bk_[:, 0:248]),
          reads=[br], writes=[R_const])
    kb.op("dve", lambda: nc.vector.tensor_copy(out=ptJf[:], in_=ptJ[:]), reads=[R_const], writes=[R_const])
    bi, bk_, br = bank()
    kb.mm([lambda: nc.tensor.matmul(bk_[:, 0:32], oh8T[:, :], ptJf[:, :], start=True, stop=True)],
          reads=[R_const], writes=[br])
    kb.op("dve", lambda: nc.vector.scalar_tensor_tensor(out=idxall[:], in0=bk_[:, 0:32], scalar=16.0, in1=pidmat[:],
                                                        op0=ALU.mult, op1=ALU.add), reads=[br, R_const], writes=[R_const])
    kb.op("dve", lambda: nc.vector.tensor_copy(out=identb[:], in_=identf[:]), reads=[R_const], writes=[R_const])
    kb.op("dve", lambda: nc.vector.memset(onesb[:], 1.0), writes=[R_const])
    kb.op("dve", lambda: nc.vector.memset(onesf[:], 1.0 / 1024.0), writes=[R_const])
    kb.op("dve", lambda: nc.vector.memset(epsT[:], EPS), writes=[R_const])
    kb.op("dve", lambda: nc.vector.memset(uhalo[:], 0.0), writes=[R_const])

    LS = {}

    rb33 = ar.get([33, 8], F32)
    oh33 = ar.get([33, 384], F32)
    dx = ar.get([8, 384], F32)
    dx2 = ar.get([8, 384], F32)
    R_b0 = Res()
    s_b = kb.slot("btab", scratch=True)
    kb.op("dve", lambda: nc.vector.memset(rb33[0:33, :], NEGB), writes=[R_b0])
    kb.dma("sp", rb33[0:32, :], rel_bias, s_b, writes=[R_b0])
    kb.dma("sp", oh33[:], cin["oh33"], s_b, writes=[R_b0])
    bi, bk_, br = bank()
    kb.mm([lambda: nc.tensor.matmul(bk_[0:8, 0:384], rb33[:, :], oh33[:, :], start=True, stop=True)],
          reads=[R_b0], writes=[br])
    kb.op("dve", lambda: nc.vector.tensor_copy(out=dx[:], in_=bk_[0:8, 0:384]), reads=[br], writes=[R_b0])
    kb.op("dve", lambda: nc.vector.tensor_scalar(out=dx2[:], in0=dx[:], scalar1=dx[:, 383:384], scalar2=SQ128,
                                                 op0=ALU.subtract, op1=ALU.mult), reads=[R_b0], writes=[R_b0])
    R_dper = Res()
    kb.dma("sp", dper.rearrange("h (r m) -> h r m", m=384), dx2[:].unsqueeze(1).broadcast_to([8, 129, 384]),
           s_b, reads=[R_b0], writes=[R_dper])

    def late_setup():
        ar.off = 72 * 1024
        tst = ar.get([128, 8, 128], F32)
        thf = ar.get([128, 8, 128], F32)
        R_b = Res()
        R_bias = Res()

        def mk_table(cofs, dst_hi, dst_lo):
            src = dper[:, cofs:cofs + 383 * 128].rearrange("h (s x) -> s h x", x=383)[:, :, 0:128]
            kb.dma("sp", tst[:], src, s_b, reads=[R_dper], writes=[R_b])
            kb.op("dve", lambda: nc.vector.tensor_copy(out=dst_hi, in_=tst[:]), reads=[R_b], writes=[R_bias])
            kb.op("dve", lambda: nc.vector.tensor_copy(out=thf[:], in_=dst_hi), reads=[R_bias], writes=[R_b])
            kb.op("dve", lambda: nc.vector.tensor_tensor(out=dst_lo, in0=tst[:], in1=thf[:], op=ALU.subtract),
                  reads=[R_b], writes=[R_bias])

        with nc.allow_non_contiguous_dma(reason="toeplitz"):
            mk_table(128, biasT[:, 0], biasT[:, 1])
            mk_table(256, biasT[:, 2], biasT[:, 3])
        dsc = [nc.dram_tensor(f"dsc{c}", [128, 31 * 128], BF16, kind="Internal").ap() for c in range(8)]
        R_dsc = [Res() for _ in range(8)]
        dgt = [ar.get([128, 31, 128], BF16) for _ in range(2)]
        R_dgt = [Res(), Res()]
        s_dg = [kb.slot("dg0", scratch=True), kb.slot("dg1", scratch=True)]
        for c in range(8):
            di = c % 2
            kb.op("dve", lambda: nc.vector.tensor_tensor(out=dgt[di][:], in0=identf[:].unsqueeze(1).broadcast_to([128, 31, 128]),
                                                          in1=cwT[:, c, :].unsqueeze(2).broadcast_to([128, 31, 128]), op=ALU.mult),
                  reads=[R_const], writes=[R_dgt[di]])
            kb.dma("sp", dsc[c], dgt[di][:].rearrange("p a b -> p (a b)"), s_dg[di], reads=[R_dgt[di]], writes=[R_dsc[c]])
        LS.update(dict(R_dper=R_dper, R_bias=R_bias, dsc=dsc, R_dsc=R_dsc, s_dg=s_dg))

    R_uh = Res()
    wres = [Res(), Res()]
    wsl = [kb.slot("w0"), kb.slot("w1")]
    wst = {"n": 0}

    def wload(key, parts, kcn, ncols):
        i = wst["n"] % 2
        wst["n"] += 1
        flat = wbuf[i][:, 0:kcn * ncols]
        view = flat.rearrange("p (k c) -> p k c", k=kcn)
        t, r, sl = wsc[key]
        kb.dma("sp", flat, t, wsl[i], reads=[r], writes=[wres[i]])
        return view, wres[i]

    prefetched = {}

    def run_jobs(jobs, next_first=None, hook=None):
        loaded = {}
        order = [k for k, j in enumerate(jobs) if j[0] is not None]

        def ensure(k):
            if k not in loaded:
                key = jobs[k][0][0]
                if key in prefetched:
                    loaded[k] = prefetched.pop(key)
                else:
                    loaded[k] = wload(*jobs[k][0])
        pos = 0
        for k, (ls, fn) in enumerate(jobs):
            if ls is not None:
                ensure(k)
                pos = order.index(k)
                if pos + 1 < len(order):
                    ensure(order[pos + 1])
                elif next_first is not None:
                    prefetched[next_first[0]] = wload(*next_first)
                fn(*loaded.pop(k))
            else:
                fn()
            if hook is not None and hook[0] == k:
                hook[1]()

    R_xn = Res()

    def p1_alloc(gj):
        nt_ = 1 if gj == 4 else 4
        pb = {}
        pb["xb"] = [ar.get([128, D], F32) for _ in range(2)]
        pb["xs"] = [ar.get([128, D], F32) for _ in range(nt_)]
        pb["st"] = ar.get([128, 4 * nt_], F32)
        pb["R_xb"] = [Res(), Res()]
        pb["R_xs"] = [Res() for _ in range(nt_)]
        pb["R_st"] = Res()
        pb["nt"] = nt_
        return pb

    def p1_norm(gj, pb):
        tok0_ = gj * 512
        s_x = [kb.slot(f"x_{i}", scratch=True) for i in range(2)]
        xb, xs, st = pb["xb"], pb["xs"], pb["st"]
        for t in range(pb["nt"]):
            i = t % 2
            c0 = 4 * t
            kb.dma("sp", xb[i][:], x[tok0_ + t * 128: tok0_ + (t + 1) * 128, :], s_x[i], writes=[pb["R_xb"][i]])
            kb.op("act", lambda: nc.scalar.activation(out=xs[t][:], in_=xb[i][:], func=AF.Square,
                                                      accum_out=st[:, c0:c0 + 1]), reads=[pb["R_xb"][i]],
                  writes=[pb["R_xs"][t], pb["R_st"]])
            kb.op("act", lambda: nc.scalar.activation(out=st[:, c0 + 1:c0 + 2], in_=st[:, c0:c0 + 1], func=AF.Sqrt,
                                                      bias=epsT[:, 0:1], scale=1.0 / D), reads=[pb["R_st"], R_const],
                  writes=[pb["R_st"]])
            kb.op("dve", lambda: nc.vector.reciprocal(out=st[:, c0 + 2:c0 + 3], in_=st[:, c0 + 1:c0 + 2]), reads=[pb["R_st"]],
                  writes=[pb["R_st"]])
            kb.op("act", lambda: nc.scalar.activation(out=xs[t][:], in_=xb[i][:], func=AF.Copy, scale=st[:, c0 + 2:c0 + 3]),
                  reads=[pb["R_xb"][i], pb["R_st"]], writes=[pb["R_xs"][t]])

    def p1_tr(gj, pb):
        NT_ = 128 if gj == 4 else 512
        xn_ = xnT[:, :, 0:NT_]
        xs = pb["xs"]
        for t in range(pb["nt"]):
            for q4 in range(4):
                bi, bk, br = bank()
                kb.mm([(lambda kc=q4 * 4 + k, k=k: nc.tensor.transpose(out=bk[:, k * 128:(k + 1) * 128],
                                                                       in_=xs[t][:, kc * 128:(kc + 1) * 128],
                                                                       identity=identf[:])) for k in range(4)],
                      reads=[pb["R_xs"][t], R_const], writes=[br])
                for k in range(4):
                    kc = q4 * 4 + k
                    if kb.ev() == "act":
                        kb.op("act", lambda: nc.scalar.activation(out=xn_[:, kc, t * 128:(t + 1) * 128],
                                                                  in_=bk[:, k * 128:(k + 1) * 128], func=AF.Copy,
                                                                  scale=lngT[:, kc:kc + 1]),
                              reads=[br, R_const], writes=[R_xn])
                    else:
                        kb.op("dve", lambda: nc.vector.tensor_scalar(out=xn_[:, kc, t * 128:(t + 1) * 128],
                                                                     in0=bk[:, k * 128:(k + 1) * 128],
                                                                     scalar1=lngT[:, kc:kc + 1], scalar2=None,
                                                                     op0=ALU.mult),
                              reads=[br, R_const], writes=[R_xn])

    def group(gi):
        samp = gi == 4
        NT = 128 if samp else 512
        ntile = NT // 128
        tok0 = gi * 512
        xn = xnT[:, :, 0:NT]
        m_T = mT[:, :, 0:NT]
        R_m = Res()

        if gi == 0:
            ar.reset()
            pb = p1_alloc(0)
            p1_norm(0, pb)
            kb._deps(kb.eng["pool"], [R_const, pb["R_xb"][0], pb["R_xb"][1]], [])
            with nc.allow_non_contiguous_dma(reason="narrow idx weight block"):
                conv_first()
            p1_tr(0, pb)
        if gi == 0:
            late_setup()
            conv_rest([LS["R_bias"]] + LS["R_dsc"])
        R_dper, R_bias, dsc, R_dsc, s_dg = LS["R_dper"], LS["R_bias"], LS["dsc"], LS["R_dsc"], LS["s_dg"]
        kb.barrier()

        ar.reset()
        qT = ar.get([128, 8, NT], BF16)
        zsT = ar.get([128, 8, NT], BF16)
        qiT = ar.get([64, 8, NT], BF16)
        gT = ar.get([128, 8, NT], BF16)
        kvst = [ar.get([128, 512], F32) for _ in range(2)]
        kist = [ar.get([128, 72], F32) for _ in range(2)]
        wab = ar.get([128, ntile, 16], F32)
        R_q, R_zs, R_qi, R_g, R_wab = Res(), Res(), Res(), Res(), Res()
        R_kvst = [Res(), Res()]
        R_kist = [Res(), Res()]
        s_kv = [kb.slot(f"kv_{i}", scratch=True) for i in range(2)]
        s_ki = [kb.slot(f"ki_{i}", scratch=True) for i in range(2)]
        R_KT, R_VA, R_KiT = Res(), Res(), Res()
        if samp:
            KTn = ar.get([128, 2, 128], BF16)
            VAn = ar.get([128, 2, 128], BF16)
            KiTn = ar.get([64, 128], BF16)

        def formF(wv, wr, nchunk, M, evac, kcn=16, src=None, R_src=None):
            src = xn if src is None else src
            R_src = R_xn if R_src is None else R_src
            for c in range(nchunk):
                bi, bk, br = bank()
                kb.mm([(lambda kc=kc: nc.tensor.matmul(bk[0:M, 0:NT], wv[:, kc, c * M:(c + 1) * M], src[:, kc, :],
                                                       start=(kc == 0), stop=(kc == kcn - 1))) for kc in range(kcn)],
                      reads=[wr, R_src], writes=[br])
                evac(c, bk[0:M, 0:NT], br)

        def formT(wv, wr, ncols, evac, c0=0, kcn=16):
            for t in range(ntile):
                bi, bk, br = bank()
                kb.mm([(lambda kc=kc: nc.tensor.matmul(bk[:, 0:ncols], xn[:, kc, t * 128:(t + 1) * 128],
                                                       wv[:, kc, c0:c0 + ncols],
                                                       start=(kc == 0), stop=(kc == kcn - 1))) for kc in range(kcn)],
                      reads=[wr, R_xn], writes=[br])
                evac(t, bk[:, 0:ncols], br)

        def copy_evac(dst_fn, R_dst, func=None):
            def f(c, ps, br):
                if func is not None:
                    kb.op("act", lambda: nc.scalar.activation(out=dst_fn(c), in_=ps, func=func), reads=[br], writes=[R_dst])
                elif kb.ev() == "act":
                    kb.op("act", lambda: nc.scalar.copy(out=dst_fn(c), in_=ps), reads=[br], writes=[R_dst])
                else:
                    kb.op("dve", lambda: nc.vector.tensor_copy(out=dst_fn(c), in_=ps), reads=[br], writes=[R_dst])
            return f

        jobs = []

        def kv_job(wv, wr):
            if samp:
                formF(wv, wr, 2, 128, copy_evac(lambda c: KTn[:, c, :], R_KT))
            else:
                formF(wv, wr, 2, 128, copy_evac(lambda c: KT[:, c, tok0:tok0 + NT], R_KT))

            def ev(t, ps, br):
                i = t % 2
                kb.op("act", lambda: nc.scalar.copy(out=kvst[i][:, :], in_=ps), reads=[br], writes=[R_kvst[i]])
                kb.dma("sp", kk[tok0 + t * 128: tok0 + (t + 1) * 128, :], kvst[i][:, 0:256], s_kv[i],
                       reads=[R_kvst[i]], is_out=True)
                kb.dma("sp", vv[tok0 + t * 128: tok0 + (t + 1) * 128, :], kvst[i][:, 256:512], s_kv[i],
                       reads=[R_kvst[i]], is_out=True)
                dst = VAn[:, :, :] if samp else VA[:, gi * 4 + t, :, :]
                kb.op("dve", lambda: nc.vector.tensor_copy(out=dst, in_=kvst[i][:, 256:512].rearrange("p (g d) -> p g d", g=2)),
                      reads=[R_kvst[i]], writes=[R_VA])
            formT(wv, wr, 512, ev)
        jobs.append((("kv", [(0, w_in_v[:, :, C_K:C_K + 512])], 16, 512), kv_job))
        jobs.append((("qi", [(0, w_in_v[:, :, C_QI: C_QI + 512])], 16, 512),
                     lambda wv, wr: formF(wv, wr, 8, 64, copy_evac(lambda c: qiT[:, c, :], R_qi))))

        def idx_job(wv, wr):
            if samp:
                formF(wv, wr, 1, 64, copy_evac(lambda c: KiTn[:, :], R_KiT))
            else:
                formF(wv, wr, 1, 64, copy_evac(lambda c: KiT[:, tok0:tok0 + NT], R_KiT))

            def ev(t, ps, br):
                i = t % 2
                kb.op("act", lambda: nc.scalar.copy(out=kist[i][:, :], in_=ps), reads=[br], writes=[R_kist[i]])
                kb.dma("sp", kio[tok0 + t * 128: tok0 + (t + 1) * 128, :], kist[i][:, 0:64], s_ki[i],
                       reads=[R_kist[i]], is_out=True)
                kb.op("act", lambda: nc.scalar.activation(out=wab[:, t, 0:8], in_=kist[i][:, 64:72], func=AF.Abs),
                      reads=[R_kist[i]], writes=[R_wab])
                kb.op("act", lambda: nc.scalar.activation(out=wab[:, t, 8:16], in_=kist[i][:, 64:72], func=AF.Sign),
                      reads=[R_kist[i]], writes=[R_wab])
            formT(wv, wr, 72, ev)
        jobs.append((("ki", [(0, w_in_v[:, :, C_KI:C_KI + 72])], 16, 72), idx_job))
        with nc.allow_non_contiguous_dma(reason="narrow idx weight block"):
            run_jobs(jobs, None)
        if gi == 0:
            for E_ in kb.eng.values():
                kb._deps(E_, [R_bias] + R_dsc, [])

        def gen_X2():
            for key, dst, Rd, fn_ in [("q0", lambda c: qT[:, c, :], R_q, None), ("q1", lambda c: qT[:, 4 + c, :], R_q, None),
                                      ("za0", lambda c: zsT[:, c, :], R_zs, AF.Silu),
                                      ("za1", lambda c: zsT[:, 4 + c, :], R_zs, AF.Silu)]:
                wv, wr = wload(key, [], 16, 512)
                ev_ = copy_evac(dst, Rd, fn_)
                for c in range(4):
                    bi, bk, br = bank()
                    kb.mm([(lambda kc=kc: nc.tensor.matmul(bk[:, 0:NT], wv[:, kc, c * 128:(c + 1) * 128], xn[:, kc, :],
                                                           start=(kc == 0), stop=(kc == 15))) for kc in range(16)],
                          reads=[wr, R_xn], writes=[br])
                    ev_(c, bk[:, 0:NT], br)
                    yield

        def gen_ga():
            for b4 in range(4):
                wv, wr = wload(f"ga{b4}", [], 16, 512)
                for c in range(4):
                    cc = b4 * 4 + c
                    bi, bk, br = bank()
                    kb.mm([(lambda kc=kc: nc.tensor.matmul(bk[:, 0:NT], wv[:, kc, c * 128:(c + 1) * 128], xn[:, kc, :],
                                                           start=(kc == 0), stop=(kc == 15))) for kc in range(16)],
                          reads=[wr, R_xn], writes=[br])
                    if kb.ev() == "act":
                        kb.op("act", lambda: nc.scalar.copy(out=m_T[:, cc, :], in_=bk[:, 0:NT]), reads=[br], writes=[R_m])
                    else:
                        kb.op("dve", lambda: nc.vector.tensor_copy(out=m_T[:, cc, :], in_=bk[:, 0:NT]), reads=[br], writes=[R_m])
                    yield

        class SelBuf:
            pass
        nset = 1 if samp else 2
        SB = []
        for k_ in range(nset):
            S = SelBuf()
            S.score = ar.get([128, 2176], F32)
            S.junk = ar.get([128, 2176], BF16)
            S.m01 = ar.get([128, 2176], BF16)
            S.maskT = ar.get([128, 17, 128], BF16)
            S.bis = ar.get([128, 80], F32)
            S.R_score, S.R_junk, S.R_m01, S.R_maskT, S.R_bis = Res(), Res(), Res(), Res(), Res()
            SB.append(S)
        S0 = SB[0]
        score, maskT, bis = S0.score, S0.maskT, S0.bis
        R_score, R_maskT, R_bis = S0.R_score, S0.R_maskT, S0.R_bis
        rtmp = [ar.get([128, 512], F32) for _ in range(2)]
        et = [ar.get([128, 512], BF16) for _ in range(3)]
        pt = [ar.get([128, 512], BF16) for _ in range(2)]
        rd = ar.get([128, 512], F32)
        tmpf = ar.get([128, 512], F32)
        R_rd, R_tmpf = Res(), Res()
        R_rtmp = [Res(), Res()]
        R_et = [Res(), Res(), Res()]
        R_pt = [Res(), Res()]
        cnt = {"r": 0, "e": 0}
        NEGM = NEGB * SQ128

        def gen_select(S, L, nblk, additive):
            bis_ = S.bis
            Rb = S.R_bis
            kb.op("dve", lambda: nc.vector.tensor_reduce(out=bis_[:, 0:1], in_=S.score[:, 0:L], axis=AX.X, op=ALU.max),
                  reads=[S.R_score], writes=[Rb])
            yield
            kb.op("dve", lambda: nc.vector.tensor_sub(out=bis_[:, 2:3], in0=bis_[:, 0:1], in1=bis_[:, 1:2]),
                  reads=[Rb], writes=[Rb])
            for k in range(NBIS + 1):
                pass
            kb.op("dve", lambda: nc.vector.tensor_scalar(out=bis_[:, 40:40 + NBIS + 1],
                                                         in0=fvec[:, 0:NBIS + 1], scalar1=bis_[:, 2:3], scalar2=None,
                                                         op0=ALU.mult), reads=[Rb, R_const], writes=[Rb])
            kb.op("dve", lambda: nc.vector.tensor_tensor(out=bis_[:, 8:9], in0=bis_[:, 1:2], in1=bis_[:, 40:41], op=ALU.add),
                  reads=[Rb], writes=[Rb])
            yield
            for k in range(NBIS):
                mid = bis_[:, 8 + k:9 + k]
                kb.op("dve", lambda: nc.vector.tensor_scalar(out=S.junk[:, 0:L], in0=S.score[:, 0:L], scalar1=mid,
                                                             scalar2=0.0, op0=ALU.is_ge, op1=ALU.add,
                                                             accum_out=bis_[:, 4:5]),
                      reads=[Rb, S.R_score], writes=[S.R_junk, Rb])
                yield
                kb.op("dve", lambda: nc.vector.tensor_scalar(out=bis_[:, 5:6], in0=bis_[:, 4:5], scalar1=255.5, scalar2=-0.5,
                                                             op0=ALU.is_ge, op1=ALU.add), reads=[Rb], writes=[Rb])
                kb.op("dve", lambda: nc.vector.scalar_tensor_tensor(out=bis_[:, 9 + k:10 + k], in0=bis_[:, 5:6],
                                                                    scalar=bis_[:, 40 + k:41 + k], in1=mid,
                                                                    op0=ALU.mult, op1=ALU.add), reads=[Rb], writes=[Rb])
                yield
            kb.op("dve", lambda: nc.vector.tensor_sub(out=bis_[:, 6:7], in0=bis_[:, 8 + NBIS:9 + NBIS],
                                                      in1=bis_[:, 40 + NBIS:41 + NBIS]), reads=[Rb], writes=[Rb])
            if additive:
                kb.op("dve", lambda: nc.vector.tensor_scalar(out=S.m01[:, 0:L], in0=S.score[:, 0:L], scalar1=bis_[:, 6:7],
                                                             scalar2=NEGM, op0=ALU.is_lt, op1=ALU.mult),
                      reads=[Rb, S.R_score], writes=[S.R_m01])
            else:
                kb.op("dve", lambda: nc.vector.tensor_scalar(out=S.m01[:, 0:L], in0=S.score[:, 0:L], scalar1=bis_[:, 6:7],
                                                             scalar2=None, op0=ALU.is_ge),
                      reads=[Rb, S.R_score], writes=[S.R_m01])
            yield
            j = 0
            while j < nblk:
                n = min(8, nblk - j)
                bi, bk, br = bank()
                bkb = bk.bitcast(BF16)
                kb.mm([(lambda jj=jj: nc.tensor.transpose(out=bkb[:, jj * 128:(jj + 1) * 128],
                                                          in_=S.m01[:, (j + jj) * 128:(j + jj + 1) * 128],
                                                          identity=identb[:])) for jj in range(n)],
                      reads=[S.R_m01, R_const], writes=[br])
                kb.op("act", lambda: nc.scalar.copy(out=S.maskT[:, j:j + n, :],
                                                    in_=bkb[:, 0:n * 128].rearrange("p (a b) -> p a b", a=n)),
                      reads=[br], writes=[S.R_maskT])
                j += n
                yield

        def select_mask(L, nblk):
            for _ in gen_select(S0, L, nblk, False):
                pass

        def finalize(g, bo, ro, bd, rdn, cols):
            kb.op("dve", lambda: nc.vector.reciprocal(out=rd[:], in_=bd[:, :]), reads=[rdn], writes=[R_rd])
            kb.op("dve", lambda: nc.vector.tensor_tensor(out=tmpf[:], in0=bo[:, :], in1=rd[:], op=ALU.mult),
                  reads=[ro, R_rd], writes=[R_tmpf])
            kb.op("dve", lambda: nc.vector.tensor_tensor(out=gT[:, 4 * g:4 * g + 4, cols],
                                                         in0=tmpf[:].rearrange("p (h q) -> p h q", h=4),
                                                         in1=zsT[:, 4 * g:4 * g + 4, cols], op=ALU.mult),
                  reads=[R_tmpf, R_zs], writes=[R_g])

        def attn_block(g, lhsK, cols, near, tabs, mask_ap, bo, ro, bd, rdn, lhsV, first, rK, rV, addmask=None, R_mk=None):
            bi, bk, br = bank()
            nmm = 1 + (2 if near else 0) + (1 if addmask is not None else 0)
            fns = [lambda: nc.tensor.matmul(bk[:, :], lhsK, qT[:, 4 * g:4 * g + 4, cols], start=True, stop=(nmm == 1),
                                            skip_group_check=True)]
            if near:
                fns.append(lambda: nc.tensor.matmul(bk[:, :], identb[:], tabs[0][:, 4 * g:4 * g + 4, :], start=False,
                                                    stop=False, skip_group_check=True))
                fns.append(lambda: nc.tensor.matmul(bk[:, :], identb[:], tabs[1][:, 4 * g:4 * g + 4, :], start=False,
                                                    stop=(addmask is None), skip_group_check=True))
            rds = [rK, R_q, R_bias, R_const]
            if addmask is not None:
                fns.append(lambda: nc.tensor.matmul(bk[:, :], identb[:], addmask.unsqueeze(1).broadcast_to([128, 4, 128]),
                                                    start=False, stop=True, skip_group_check=True))
                rds.append(R_mk)
            kb.mm(fns, reads=rds, writes=[br])
            ei = cnt["e"] % 3
            cnt["e"] += 1
            kb.op("act", lambda: nc.scalar.activation(out=et[ei][:], in_=bk[:, :], func=AF.Exp, scale=SCALE),
                  reads=[br], writes=[R_et[ei]])
            if mask_ap is not None:
                pi = ei % 2
                kb.op("dve", lambda: nc.vector.tensor_tensor(out=pt[pi][:].rearrange("p (h q) -> p h q", h=4),
                                                             in0=et[ei][:].rearrange("p (h q) -> p h q", h=4),
                                                             in1=mask_ap.unsqueeze(1).broadcast_to([128, 4, 128]),
                                                             op=ALU.mult),
                      reads=[R_et[ei], R_maskT], writes=[R_pt[pi]])
                src, rs = pt[pi], R_pt[pi]
            else:
                src, rs = et[ei], R_et[ei]
            kb.mm([lambda: nc.tensor.matmul(bo[:, :], lhsV, src[:], start=first, stop=False, skip_group_check=True),
                   lambda: nc.tensor.matmul(bd[:, :], onesb[:], src[:], start=first, stop=False, skip_group_check=True)],
                  reads=[rs, rV, R_const], writes=[ro, rdn])

        def pin():
            i, b, r = bank()
            pinned.add(i)
            return i, b, r

        if not samp:
            def gen_sel_tile(t):
                ti = gi * 4 + t
                S = SB[t % 2]
                L = 128 * (ti + 1)
                cols = slice(t * 128, (t + 1) * 128)
                if ti < 2:
                    return
                nch = (L + 511) // 512
                for h in range(8):
                    for ch in range(nch):
                        w = min(512, L - ch * 512)
                        bi, bk, br = bank()
                        kb.mm([lambda: nc.tensor.matmul(bk[:, 0:w], qiT[:, h, cols], KiT[:, ch * 512:ch * 512 + w],
                                                        start=True, stop=True)], reads=[R_qi, R_KiT], writes=[br])
                        ri = cnt["r"] % 2
                        cnt["r"] += 1
                        kb.op("act", lambda: nc.scalar.activation(out=rtmp[ri][:, 0:w], in_=bk[:, 0:w], func=AF.Relu,
                                                                  scale=wab[:, t, h:h + 1]),
                              reads=[br, R_wab], writes=[R_rtmp[ri]])
                        sc = S.score[:, ch * 512:ch * 512 + w]
                        if h == 0:
                            kb.op("dve", lambda: nc.vector.tensor_scalar(out=sc, in0=rtmp[ri][:, 0:w],
                                                                         scalar1=wab[:, t, 8 + h:9 + h], scalar2=None,
                                                                         op0=ALU.mult),
                                  reads=[R_rtmp[ri], R_wab], writes=[S.R_score])
                        else:
                            kb.op("dve", lambda: nc.vector.scalar_tensor_tensor(out=sc, in0=rtmp[ri][:, 0:w],
                                                                                scalar=wab[:, t, 8 + h:9 + h], in1=sc,
                                                                                op0=ALU.mult, op1=ALU.add),
                                  reads=[R_rtmp[ri], R_wab], writes=[S.R_score])
                        yield
                kb.op("dve", lambda: nc.vector.tensor_reduce(out=S.bis[:, 1:2], in_=S.score[:, 0:L], axis=AX.X, op=ALU.min),
                      reads=[S.R_score], writes=[S.R_bis])
                kb.op("dve", lambda: nc.vector.tensor_tensor(out=S.score[:, L - 128:L], in0=S.score[:, L - 128:L],
                                                             in1=causneg[:], op=ALU.add),
                      reads=[R_const], writes=[S.R_score])
                yield
                for _ in gen_select(S, L, ti + 1, True):
                    yield

            def gen_attn_tile(t):
                ti = gi * 4 + t
                S = SB[t % 2]
                cols = slice(t * 128, (t + 1) * 128)
                for g in range(2):
                    io, bo, ro = pin()
                    idn, bd, rdn = pin()
                    for j in range(ti + 1):
                        near = ti - j <= 1
                        tabs = (biasT[:, 2 * (ti - j)], biasT[:, 2 * (ti - j) + 1]) if near else None
                        attn_block(g, KT[:, g, j * 128:(j + 1) * 128], cols, near, tabs, None, bo, ro, bd, rdn,
                                   VA[:, j, g, :], j == 0, R_KT, R_VA,
                                   addmask=(S.maskT[:, j, :] if ti >= 2 else None), R_mk=S.R_maskT)
                        yield
                    finalize(g, bo, ro, bd, rdn, cols)
                    pinned.discard(io)
                    pinned.discard(idn)
                    yield

            def nsteps_sel(t):
                ti = gi * 4 + t
                if ti < 2:
                    return 0
                L = 128 * (ti + 1)
                return 8 * ((L + 511) // 512) + 3 + 2 * NBIS + 1 + (ti + 8) // 8

            def zipper(ga, na, gb, nb):
                da = db = 0
                alive_a = alive_b = True
                while alive_a or alive_b:
                    fa = da / max(na, 1) if alive_a else 2.0
                    fb = db / max(nb, 1) if alive_b else 2.0
                    if fa <= fb:
                        try:
                            next(ga)
                            da += 1
                        except StopIteration:
                            alive_a = False
                    else:
                        try:
                            next(gb)
                            db += 1
                        except StopIteration:
                            alive_b = False

            zipper(gen_sel_tile(0), nsteps_sel(0), gen_X2(), 16)
            for t in range(4):
                ga = gen_attn_tile(t)
                na = 2 * (gi * 4 + t + 2)
                if t + 1 < 4:
                    zipper(ga, na, gen_sel_tile(t + 1), nsteps_sel(t + 1))
                else:
                    zipper(ga, na, gen_ga(), 16)
        else:
            for _ in gen_X2():
                pass
            cols = slice(0, 128)
            ts = ar.get([128, 2, 8, 128], BF16)
            tab16 = ar.get([128, 8, 2, 64], BF16)
            ar_mark = ar.off
            tst2 = ar.get([128, 8, 128], F32)
            thf2 = ar.get([128, 8, 128], F32)
            stg = ar.get([128, 8, 64], F32)
            stgh = ar.get([128, 8, 64], F32)
            R_ts = Res()
            s_ts = kb.slot("ts", scratch=True)
            with nc.allow_non_contiguous_dma(reason="toeplitz"):
                src_ = dper[:, 128:128 + 383 * 128].rearrange("h (s x) -> s h x", x=383)[:, :, 0:128]
                kb.dma("sp", tst2[:], src_, s_ts, reads=[R_dper], writes=[R_ts])
                kb.op("dve", lambda: nc.vector.memset(stg[:], 0.0), writes=[R_ts])
                for pl in range(8):
                    off = 256 - pl
                    srcp = dper[:, off:off + 376 * 16].rearrange("h (pp x) -> pp h x", x=376)[:, :, 0:8]
                    kb.dma("sp", stg[112:128, pl, :].rearrange("p (h q) -> p h q", h=8), srcp, s_ts, reads=[R_dper], writes=[R_ts])
            kb.op("dve", lambda: nc.vector.tensor_tensor(out=tst2[:], in0=tst2[:],
                                                         in1=sampbm[:].unsqueeze(1).broadcast_to([128, 8, 128]),
                                                         op=ALU.mult), reads=[R_const], writes=[R_ts])
            kb.op("dve", lambda: nc.vector.tensor_tensor(out=tst2[:], in0=tst2[:],
                                                         in1=sampnb[:].unsqueeze(1).broadcast_to([128, 8, 128]),
                                                         op=ALU.add), reads=[R_const], writes=[R_ts])
            kb.op("dve", lambda: nc.vector.tensor_copy(out=ts[:, 0], in_=tst2[:]), writes=[R_ts])
            kb.op("dve", lambda: nc.vector.tensor_copy(out=thf2[:], in_=ts[:, 0]), writes=[R_ts])
            kb.op("dve", lambda: nc.vector.tensor_tensor(out=ts[:, 1], in0=tst2[:], in1=thf2[:], op=ALU.subtract),
                  writes=[R_ts])
            kb.op("dve", lambda: nc.vector.tensor_copy(out=tab16[:, :, 0, :], in_=stg[:]), writes=[R_ts])
            kb.op("dve", lambda: nc.vector.tensor_copy(out=stgh[:], in_=tab16[:, :, 0, :]), writes=[R_ts])
            kb.op("dve", lambda: nc.vector.tensor_tensor(out=tab16[:, :, 1, :], in0=stg[:], in1=stgh[:], op=ALU.subtract),
                  writes=[R_ts, R_bias])
            kb.barrier()
            ar.off = ar_mark
            wcol = ar.get([64, 16], F32)
            amat = ar.get([128, 64], F32)
            lq = [ar.get([64, 64], BF16) for _ in range(2)]
            kic = [ar.get([128, 2, 8, 64], BF16) for _ in range(2)]
            kitb = [ar.get([64, 2048], BF16) for _ in range(2)]
            rwt = [ar.get([64, 512], F32) for _ in range(4)]
            rwh = [ar.get([64, 512], BF16) for _ in range(4)]
            rwl = [ar.get([64, 512], BF16) for _ in range(4)]
            selb = ar.get([64, 248], BF16)
            R_rwh = [Res() for _ in range(4)]
            R_rwl = [Res() for _ in range(4)]
            kb.op("dve", lambda: nc.vector.tensor_copy(out=selb[:], in_=selbase[:]), reads=[R_const], writes=[R_const])
            R_wcol = Res()
            R_lq = [Res(), Res()]
            R_kic = [Res(), Res()]
            R_kitb = [Res(), Res()]
            R_rwt = [Res() for _ in range(4)]
            s_kic = [kb.slot(f"kic{i}", scratch=True) for i in range(2)]
            kb.op("dve", lambda: nc.vector.tensor_tensor(out=amat[:].rearrange("p (h q) -> p h q", h=8),
                                                         in0=dq[:].rearrange("p (h q) -> p h q", h=8),
                                                         in1=kist[0][:, 64:72].unsqueeze(2).broadcast_to([128, 8, 8]),
                                                         op=ALU.mult), reads=[R_kist[0], R_const], writes=[R_wcol])
            bi, bk, br = bank()
            kb.mm([lambda: nc.tensor.matmul(bk[0:64, 0:16], amat[:, :], bsel[:, :], start=True, stop=True)],
                  reads=[R_wcol, R_const], writes=[br])
            kb.op("dve", lambda: nc.vector.tensor_copy(out=wcol[:], in_=bk[0:64, 0:16]), reads=[br], writes=[R_wcol])
            sbk = [pin() for _ in range(5)]
            prepared = set()

            def prep(b):
                i = b % 2
                for hf in range(2):
                    kb.idma(kic[i][:, hf].rearrange("p a e -> p (a e)"), cache_ki, idxall[:, 2 * b + hf:2 * b + hf + 1],
                            s_kic[i], reads=[R_const], writes=[R_kic[i]])
                for hf in range(2):
                    bi, bk, br = bank()
                    bkb = bk.bitcast(BF16)
                    kb.mm([(lambda pl=pl: nc.tensor.transpose(out=bkb[0:64, pl * 128:(pl + 1) * 128], in_=kic[i][:, hf, pl, :],
                                                              identity=identb[:])) for pl in range(8)],
                          reads=[R_kic[i], R_const], writes=[br])
                    if kb.ev() == "act":
                        kb.op("act", lambda: nc.scalar.copy(out=kitb[i][:, hf * 1024:(hf + 1) * 1024], in_=bkb[0:64, :]),
                              reads=[br], writes=[R_kitb[i]])
                    else:
                        kb.op("dve", lambda: nc.vector.tensor_copy(out=kitb[i][:, hf * 1024:(hf + 1) * 1024], in_=bkb[0:64, :]),
                              reads=[br], writes=[R_kitb[i]])
                kb.op("dve", lambda: nc.vector.tensor_copy(out=lq[i][:].rearrange("p (h q) -> p h q", h=8),
                                                           in_=qiT[:, :, 8 * b:8 * b + 8]), reads=[R_qi], writes=[R_lq[i]])

            items = [(b, ch) for b in range(16) for ch in range(5)]

            def stageA(k):
                b, ch = items[k]
                i = b % 2
                if b not in prepared:
                    prepared.add(b)
                    prep(b)
                w = 512 if ch < 4 else 128
                rhs = kitb[i][:, ch * 512:(ch + 1) * 512] if ch < 4 else KiTn[:, :]
                bi, bk, br = pin()
                kb.mm([lambda: nc.tensor.matmul(bk[0:64, 0:w], lq[i][:, :], rhs, start=True, stop=True)],
                      reads=[R_lq[i], R_kitb[i], R_KiT], writes=[br])
                return bk, br, bi
            pend = {}
            LOOK = 1
            for k in range(min(LOOK, len(items))):
                pend[k] = stageA(k)
            for k in range(len(items)):
                if k + LOOK < len(items):
                    pend[k + LOOK] = stageA(k + LOOK)
                b, ch = items[k]
                w = 512 if ch < 4 else 128
                bk, br, bi_ = pend.pop(k)
                ri = k % 4
                kb.op("dve", lambda: nc.vector.tensor_scalar(out=rwt[ri][:, 0:w], in0=bk[0:64, 0:w], scalar1=0.0,
                                                             scalar2=wcol[:, b:b + 1], op0=ALU.max, op1=ALU.mult),
                      reads=[br, R_wcol], writes=[R_rwt[ri]])
                pinned.discard(bi_)
                kb.op("act", lambda: nc.scalar.copy(out=rwh[ri][:, 0:w], in_=rwt[ri][:, 0:w]), reads=[R_rwt[ri]],
                      writes=[R_rwh[ri]])
                kb.op("dve", lambda: nc.vector.tensor_tensor(out=rwl[ri][:, 0:w], in0=rwt[ri][:, 0:w], in1=rwh[ri][:, 0:w],
                                                             op=ALU.subtract), reads=[R_rwt[ri], R_rwh[ri]], writes=[R_rwl[ri]])
                sbi, sbk_, sbr = sbk[ch]
                kb.mm([lambda: nc.tensor.matmul(sbk_[:, 0:w], selb[:, 120 - 8 * b:248 - 8 * b], rwh[ri][:, 0:w],
                                                start=(b == 0), stop=False, skip_group_check=True),
                       lambda: nc.tensor.matmul(sbk_[:, 0:w], selb[:, 120 - 8 * b:248 - 8 * b], rwl[ri][:, 0:w],
                                                start=False, stop=(b == 15), skip_group_check=True)],
                      reads=[R_rwh[ri], R_rwl[ri], R_const], writes=[sbr])
            for ch in range(5):
                w = 512 if ch < 4 else 128
                sbi, sbk_, sbr = sbk[ch]
                kb.op("act", lambda: nc.scalar.copy(out=score[:, ch * 512:ch * 512 + w], in_=sbk_[:, 0:w]),
                      reads=[sbr], writes=[R_score])
                pinned.discard(sbi)
            L = 2176
            kb.op("dve", lambda: nc.vector.tensor_reduce(out=bis[:, 1:2], in_=score[:, 0:L], axis=AX.X, op=ALU.min),
                  reads=[R_score], writes=[R_bis])
            kb.op("dve", lambda: nc.vector.tensor_tensor(out=score[:, 2048:2176], in0=score[:, 2048:2176],
                                                         in1=sampneg[:], op=ALU.add), reads=[R_const], writes=[R_score])
            select_mask(L, 17)
            kb.barrier()
            ar.off = ar_mark
            acc = [(pin(), pin()) for _ in range(2)]
            for g in range(2):
                (io, bo, ro), (idn, bd, rdn) = acc[g]
                attn_block(g, KTn[:, g, :], cols, True, (ts[:, 0], ts[:, 1]), maskT[:, 16, :], bo, ro, bd, rdn,
                           VAn[:, g, :], True, R_KT, R_VA)
            kcb = [ar.get([128, 2, 8, 256], BF16) for _ in range(2)]
            vb = [ar.get([128, 2, 8, 256], BF16) for _ in range(2)]
            ktb = ar.get([128, 2, 2048], BF16)
            es = ar.get([128, 16, 64], BF16)
            ps_ = ar.get([128, 16, 64], BF16)
            R_kc = [Res(), Res()]
            R_vb = [Res(), Res()]
            R_ktb, R_es, R_ps = Res(), Res(), Res()
            s_kc = [kb.slot(f"kc{i}", scratch=True) for i in range(2)]
            s_vc = [kb.slot(f"vc{i}", scratch=True) for i in range(2)]
            il0, bl0, rl0 = pin()
            il1, bl1, rl1 = pin()
            lgb = [(bl0, rl0), (bl1, rl1)]

            def gather(b):
                i = b % 2
                for hf in range(2):
                    ix = idxall[:, 2 * b + hf:2 * b + hf + 1]
                    kb.idma(kcb[i][:, hf].rearrange("p a e -> p (a e)"), cache_k, ix, s_kc[i], reads=[R_const], writes=[R_kc[i]])
                    kb.idma(vb[i][:, hf].rearrange("p a e -> p (a e)"), cache_v, ix, s_vc[i], reads=[R_const], writes=[R_vb[i]])
            gather(0)
            for b in range(16):
                i = b % 2
                if b + 1 < 16:
                    gather(b + 1)
                for hf in range(2):
                    for g in range(2):
                        bi, bk, br = bank()
                        bkb = bk.bitcast(BF16)
                        kb.mm([(lambda pl=pl: nc.tensor.transpose(out=bkb[:, pl * 128:(pl + 1) * 128],
                                                                  in_=kcb[i][:, hf, pl, g * 128:(g + 1) * 128],
                                                                  identity=identb[:])) for pl in range(8)],
                              reads=[R_kc[i], R_const], writes=[br])
                        if kb.ev() == "act":
                            kb.op("act", lambda: nc.scalar.copy(out=ktb[:, g, hf * 1024:(hf + 1) * 1024], in_=bkb[:, :]),
                                  reads=[br], writes=[R_ktb])
                        else:
                            kb.op("dve", lambda: nc.vector.tensor_copy(out=ktb[:, g, hf * 1024:(hf + 1) * 1024], in_=bkb[:, :]),
                                  reads=[br], writes=[R_ktb])
                for hb in range(2):
                    bl, rl = lgb[hb]
                    blv = bl[:, :].rearrange("p (j c) -> p j c", j=8)
                    fns = []
                    for jj in range(8):
                        j = hb * 8 + jj
                        for g in range(2):
                            fns.append(lambda j=j, jj=jj, g=g, blv=blv: nc.tensor.matmul(
                                blv[:, jj, g * 32:(g + 1) * 32], ktb[:, g, j * 128:(j + 1) * 128],
                                qT[:, 4 * g:4 * g + 4, 8 * b:8 * b + 8], start=True, stop=(hb == 0), skip_group_check=True))
                            if hb == 1:
                                for hl in range(2):
                                    fns.append(lambda jj=jj, g=g, hl=hl, blv=blv: nc.tensor.matmul(
                                        blv[:, jj, g * 32:(g + 1) * 32], identb[:],
                                        tab16[:, jj, hl, :].rearrange("p (h q) -> p h q", h=8)[:, 4 * g:4 * g + 4, :],
                                        start=False, stop=(hl == 1), skip_group_check=True))
                    kb.mm(fns, reads=[R_ktb, R_q, R_bias, R_ts, R_const], writes=[rl])
                    kb.op("act", lambda: nc.scalar.activation(out=es[:, hb * 8:(hb + 1) * 8, :], in_=blv, func=AF.Exp,
                                                              scale=SCALE), reads=[rl], writes=[R_es])
                kb.op("dve", lambda: nc.vector.tensor_tensor(
                    out=ps_[:].rearrange("p j (h q) -> p j h q", h=8), in0=es[:].rearrange("p j (h q) -> p j h q", h=8),
                    in1=maskT[:, 0:16, 8 * b:8 * b + 8].unsqueeze(2).broadcast_to([128, 16, 8, 8]), op=ALU.mult),
                    reads=[R_es, R_maskT], writes=[R_ps])
                for g in range(2):
                    (io, bo, ro), (idn, bd, rdn) = acc[g]
                    bov = bo[:, :].rearrange("p (h q) -> p h q", h=4)[:, :, 8 * b:8 * b + 8]
                    bdv = bd[:, :].rearrange("p (h q) -> p h q", h=4)[:, :, 8 * b:8 * b + 8]
                    fns = []
                    for j in range(16):
                        fns.append(lambda j=j, g=g, bov=bov: nc.tensor.matmul(
                            bov, vb[i][:, j // 8, j % 8, g * 128:(g + 1) * 128], ps_[:, j, g * 32:(g + 1) * 32],
                            start=False, stop=False, skip_group_check=True))
                        fns.append(lambda j=j, g=g, bdv=bdv: nc.tensor.matmul(bdv, onesb[:], ps_[:, j, g * 32:(g + 1) * 32],
                                                                            start=False, stop=False, skip_group_check=True))
                    kb.mm(fns, reads=[R_ps, R_vb[i], R_const], writes=[ro, rdn])
            pinned.discard(il0)
            pinned.discard(il1)
            for g in range(2):
                (io, bo, ro), (idn, bd, rdn) = acc[g]
                finalize(g, bo, ro, bd, rdn, cols)
                pinned.discard(io)
                pinned.discard(idn)

        if samp:
            for _ in gen_ga():
                pass
        for cc in range(16):
            kb.op("act", lambda: nc.scalar.activation(out=m_T[:, cc, :], in_=m_T[:, cc, :], func=AF.Sigmoid),
                  reads=[R_m], writes=[R_m])
        jobs = []
        for b4 in range(4):
            def ua_job(wv, wr, b4=b4):
                def ev(c, ps, br):
                    cc = b4 * 4 + c
                    kb.op("dve", lambda: nc.vector.tensor_tensor(out=m_T[:, cc, :], in0=ps, in1=m_T[:, cc, :], op=ALU.mult),
                          reads=[br], writes=[R_m])
                formF(wv, wr, 4, 128, ev, kcn=8, src=gT, R_src=R_g)
            jobs.append(((f"ua{b4}", [(0, w_ua_v[:, :, b4 * 512:(b4 + 1) * 512])], 8, 512), ua_job))
        sgc = {"n": 0}
        run_jobs(jobs, ("gg0", [], 16, 512))
        kb.barrier()

        ar.reset()
        NB_ = 16 if samp else 1
        TW = 38 if samp else NT + 30
        uT = ar.get([128, 8, NB_ * TW], BF16)
        zcT = ar.get([128, 8, NT], BF16)
        dwf = ar.get([128, 8, NT], F32)
        aT = ar.get([128, 8, NT], BF16)
        cT = ar.get([128, 8, NT], BF16)
        diag = [ar.get([128, 31, 128], BF16) for _ in range(2)]
        sqt = [ar.get([128, NT], F32) for _ in range(2)]
        lnm = ar.get([128, NT], F32)
        lnv = ar.get([128, NT], F32)
        lnr = ar.get([128, NT], F32)
        t1 = [ar.get([128, NT], F32) for _ in range(2)]
        sgt2 = [ar.get([128, NT], F32) for _ in range(2)]
        tt2 = [ar.get([128, NT], F32) for _ in range(2)]
        R_u, R_zc, R_dw, R_a, R_c, R_ln = Res(), Res(), Res(), Res(), Res(), Res()
        R_diag = [Res(), Res()]
        R_sq = [Res(), Res()]
        R_t1 = [Res(), Res()]
        R_sg2 = [Res(), Res()]
        R_tt2 = [Res(), Res()]
        need_tok = samp or gi == 3
        if need_tok:
            sgtok = ar.get([128, 1024], F32)
            utok = ar.get([128, 1024], F32)
            R_sgtok, R_utok = Res(), Res()
            s_ut = kb.slot(f"ut", scratch=True)
        if samp:
            uv = uT[:].rearrange("p c (b w) -> p c b w", b=16)

            def ucols(c):
                return uv[:, c, :, 30:38]
            stt = ar.get([120, 1024], F32)
            R_stt = Res()
            s_stt = kb.slot("stt", scratch=True)
            for q4 in range(4):
                kb.dma("sp", stt[:], state[q4 * 4:(q4 + 1) * 4].rearrange("b r c -> (b r) c"), s_stt, writes=[R_stt])
                for c2 in range(2):
                    bi, bk, br = bank()
                    kb.mm([(lambda k=k: nc.tensor.transpose(out=bk[:, k * 120:(k + 1) * 120],
                                                            in_=stt[:, (c2 * 4 + k) * 128:(c2 * 4 + k + 1) * 128],
                                                            identity=identf[0:120, 0:120])) for k in range(4)],
                          reads=[R_stt, R_const], writes=[br])
                    kb.op("dve", lambda: nc.vector.tensor_copy(
                        out=uv[:, c2 * 4:(c2 + 1) * 4, q4 * 4:(q4 + 1) * 4, 0:30],
                        in_=bk[:, 0:480].rearrange("p (k b r) -> p k b r", k=4, b=4)), reads=[br], writes=[R_u])
            kb.dma("sp", convs[:, 0:22, :], state[:, 8:30, :], s_stt, is_out=True)
        else:
            def ucols(c):
                return uT[:, c, 30:30 + NT]
            kb.op("dve", lambda: nc.vector.tensor_copy(out=uT[:, :, 0:30], in_=uhalo[:]), reads=[R_const, R_uh], writes=[R_u])

        jobs = []
        for b2 in range(2):
            def gg_job(wv, wr, b2=b2):
                formF(wv, wr, 4, 128, copy_evac(lambda c: ucols(b2 * 4 + c), R_u, AF.Sigmoid))
                if need_tok:
                    t = ntile - 1
                    bi, bk, br = bank()
                    kb.mm([(lambda kc=kc: nc.tensor.matmul(bk[:, 0:512], xn[:, kc, t * 128:(t + 1) * 128], wv[:, kc, :],
                                                           start=(kc == 0), stop=(kc == 15))) for kc in range(16)],
                          reads=[wr, R_xn], writes=[br])
                    kb.op("act", lambda: nc.scalar.activation(out=sgtok[:, b2 * 512:(b2 + 1) * 512], in_=bk[:, 0:512],
                                                              func=AF.Sigmoid), reads=[br], writes=[R_sgtok])
            jobs.append(((f"gg{b2}", [(0, w_in_v[:, :, C_GG + b2 * 512: C_GG + (b2 + 1) * 512])], 16, 512), gg_job))
        for b2 in range(2):
            def gv_job(wv, wr, b2=b2):
                def ev(c, ps, br):
                    kb.op("dve", lambda: nc.vector.tensor_tensor(out=ucols(b2 * 4 + c), in0=ps if not samp else
                                                                 ps.rearrange("p (b t) -> p b t", b=16),
                                                                 in1=ucols(b2 * 4 + c), op=ALU.mult), reads=[br], writes=[R_u])
                formF(wv, wr, 4, 128, ev)
                if need_tok:
                    t = ntile - 1
                    bi, bk, br = bank()
                    kb.mm([(lambda kc=kc: nc.tensor.matmul(bk[:, 0:512], xn[:, kc, t * 128:(t + 1) * 128], wv[:, kc, :],
                                                           start=(kc == 0), stop=(kc == 15))) for kc in range(16)],
                          reads=[wr, R_xn], writes=[br])
                    kb.op("dve", lambda: nc.vector.tensor_tensor(out=utok[:, b2 * 512:(b2 + 1) * 512], in0=bk[:, 0:512],
                                                                 in1=sgtok[:, b2 * 512:(b2 + 1) * 512], op=ALU.mult),
                          reads=[br, R_sgtok], writes=[R_utok])
            jobs.append(((f"gv{b2}", [(0, w_in_v[:, :, C_GV + b2 * 512: C_GV + (b2 + 1) * 512])], 16, 512), gv_job))
        def conv_job():
            if need_tok:
                if samp:
                    for b in range(16):
                        kb.dma("sp", convs[b, 22:30, :], utok[8 * b:8 * b + 8, :], s_ut, reads=[R_utok], is_out=True)
                else:
                    kb.dma("sp", convp[:, :], utok[98:128, :], s_ut, reads=[R_utok], is_out=True)
            if not samp:
                kb.op("dve", lambda: nc.vector.tensor_copy(out=uhalo[:], in_=uT[:, :, NT:NT + 30]), reads=[R_u], writes=[R_uh])
            for c in range(8):
                di = c % 2
                kb.dma("sp", diag[di][:].rearrange("p a b -> p (a b)"), dsc[c], s_dg[di], reads=[R_dsc[c]], writes=[R_diag[di]])
                bi, bk, br = bank()
                if samp:
                    rhsf = lambda j: uv[:, c, :, j:j + 8]
                else:
                    rhsf = lambda j: uT[:, c, j:j + NT]
                kb.mm([(lambda j=j: nc.tensor.matmul(bk[:, 0:NT], diag[di][:, j, :], rhsf(j), start=(j == 0), stop=(j == 30)))
                       for j in range(31)], reads=[R_diag[di], R_u], writes=[br])
                kb.op("act", lambda: nc.scalar.activation(out=dwf[:, c, :], in_=bk[:, 0:NT], func=AF.Identity,
                                                          bias=cbT[:, c:c + 1], scale=1.0), reads=[br, R_const], writes=[R_dw])
            im, bm, rm = pin()
            iq, bq, rq = pin()
            for c in range(8):
                si = c % 2
                kb.op("act", lambda: nc.scalar.activation(out=sqt[si][:], in_=dwf[:, c, :], func=AF.Square),
                      reads=[R_dw], writes=[R_sq[si]])
                kb.mm([lambda: nc.tensor.matmul(bm[:, 0:NT], onesf[:], dwf[:, c, :], start=(c == 0), stop=(c == 7),
                                                skip_group_check=True)], reads=[R_dw, R_const], writes=[rm])
                kb.mm([lambda: nc.tensor.matmul(bq[:, 0:NT], onesf[:], sqt[si][:], start=(c == 0), stop=(c == 7),
                                                skip_group_check=True)], reads=[R_sq[si], R_const], writes=[rq])
            kb.op("dve", lambda: nc.vector.tensor_copy(out=lnm[:], in_=bm[:, 0:NT]), reads=[rm], writes=[R_ln])
            kb.op("dve", lambda: nc.vector.scalar_tensor_tensor(out=lnv[:], in0=lnm[:], scalar=-1.0, in1=lnm[:],
                                                                op0=ALU.mult, op1=ALU.mult), reads=[R_ln], writes=[R_ln])
            kb.op("dve", lambda: nc.vector.tensor_tensor(out=lnv[:], in0=bq[:, 0:NT], in1=lnv[:], op=ALU.add),
                  reads=[rq, R_ln], writes=[R_ln])
            kb.op("act", lambda: nc.scalar.activation(out=lnv[:], in_=lnv[:], func=AF.Sqrt, bias=epsT[:, 0:1], scale=1.0),
                  reads=[R_ln, R_const], writes=[R_ln])
            kb.op("dve", lambda: nc.vector.reciprocal(out=lnr[:], in_=lnv[:]), reads=[R_ln], writes=[R_ln])
            pinned.discard(im)
            pinned.discard(iq)
            for c in range(8):
                i = c % 2
                kb.op("dve", lambda: nc.vector.tensor_tensor(out=t1[i][:], in0=dwf[:, c, :], in1=lnm[:], op=ALU.subtract),
                      reads=[R_dw, R_ln], writes=[R_t1[i]])
                kb.op("dve", lambda: nc.vector.tensor_tensor(out=t1[i][:], in0=t1[i][:], in1=lnr[:], op=ALU.mult),
                      reads=[R_ln], writes=[R_t1[i]])
                kb.op("act", lambda: nc.scalar.activation(out=aT[:, c, :], in_=t1[i][:], func=AF.Silu,
                                                          bias=cnbT[:, c:c + 1], scale=cngT[:, c:c + 1]),
                      reads=[R_t1[i], R_const], writes=[R_a])
        jobs.append((None, conv_job))
        for b2 in range(2):
            jobs.append(((f"zc{b2}", [(0, w_in_v[:, :, C_ZC + b2 * 512: C_ZC + (b2 + 1) * 512])], 16, 512),
                         lambda wv, wr, b2=b2: formF(wv, wr, 4, 128, copy_evac(lambda c: zcT[:, b2 * 4 + c, :], R_zc, AF.Silu))))

        for b2 in range(2):
            def pw_job(wv, wr, b2=b2):
                def ev(c, ps, br):
                    cc = b2 * 4 + c
                    kb.op("dve", lambda: nc.vector.scalar_tensor_tensor(out=cT[:, cc, :], in0=ps, scalar=bpwT[:, cc:cc + 1],
                                                                        in1=zcT[:, cc, :], op0=ALU.add, op1=ALU.mult),
                          reads=[br, R_zc, R_const], writes=[R_c])
                formF(wv, wr, 4, 128, ev, kcn=8, src=aT, R_src=R_a)
            jobs.append(((f"pw{b2}", [(0, w_pw_v[:, :, b2 * 512:(b2 + 1) * 512])], 8, 512), pw_job))
        bcb = {}
        for b4 in range(4):
            def uc_job(wv, wr, b4=b4):
                for c in range(4):
                    bi, bk, br = pin()
                    kb.mm([(lambda kc=kc: nc.tensor.matmul(bk[:, 0:NT], wv[:, kc, c * 128:(c + 1) * 128], cT[:, kc, :],
                                                           start=(kc == 0), stop=(kc == 7))) for kc in range(8)],
                          reads=[wr, R_c], writes=[br])
                    bcb[(b4, c)] = (bi, bk, br)
            jobs.append(((f"uc{b4}", [(0, w_uc_v[:, :, b4 * 512:(b4 + 1) * 512])], 8, 512), uc_job))

            def gc_job(wv, wr, b4=b4):
                def ev(c, ps, br):
                    i = sgc["n"] % 2
                    sgc["n"] += 1
                    bi2, bk2, br2 = bcb.pop((b4, c))
                    kb.op("act", lambda: nc.scalar.activation(out=sgt2[i][:], in_=ps, func=AF.Sigmoid), reads=[br], writes=[R_sg2[i]])
                    kb.op("dve", lambda: nc.vector.tensor_tensor(out=tt2[i][:], in0=bk2[:, 0:NT], in1=sgt2[i][:], op=ALU.mult),
                          reads=[br2, R_sg2[i]], writes=[R_tt2[i]])
                    pinned.discard(bi2)
                    kb.op("dve", lambda: nc.vector.tensor_tensor(out=m_T[:, b4 * 4 + c, :], in0=m_T[:, b4 * 4 + c, :],
                                                                 in1=tt2[i][:], op=ALU.add), reads=[R_tt2[i]], writes=[R_m])
                formF(wv, wr, 4, 128, ev)
            jobs.append(((f"gc{b4}", [(0, w_in_v[:, :, C_GC + b4 * 512: C_GC + (b4 + 1) * 512])], 16, 512), gc_job))
        run_jobs(jobs, ("wo0", [], 16, 512))
        kb.barrier()

        ar.reset()
        yb = [ar.get([128, D], F32) for _ in range(ntile)]
        gbc = ar.get([128, D], F32)
        jk = ar.get([128, D], F32)
        st2 = ar.get([128, 16], F32)
        R_yb = [Res() for _ in range(ntile)]
        R_gbc, R_jk = Res(), Res()
        R_st2 = [Res() for _ in range(ntile)]
        s_y = [kb.slot(f"y_{i}", scratch=True) for i in range(ntile)]
        s_g = kb.slot(f"g", scratch=True)
        for t in range(ntile):
            kb.dma("sp", yb[t][:], x[tok0 + t * 128: tok0 + (t + 1) * 128, :], s_y[t], writes=[R_yb[t]])
        kb.dma("sp", gbc[:], final_g.broadcast_to([128, D]), s_g, writes=[R_gbc])

        def final_norm(t):
            c0 = 4 * t
            kb.op("act", lambda: nc.scalar.activation(out=jk[:], in_=yb[t][:], func=AF.Square, accum_out=st2[:, c0:c0 + 1]),
                  reads=[R_yb[t]], writes=[R_jk, R_st2[t]])
            kb.op("act", lambda: nc.scalar.activation(out=st2[:, c0 + 1:c0 + 2], in_=st2[:, c0:c0 + 1], func=AF.Sqrt,
                                                      bias=epsT[:, 0:1], scale=1.0 / D), reads=[R_st2[t], R_const],
                  writes=[R_st2[t]])
            kb.op("dve", lambda: nc.vector.reciprocal(out=st2[:, c0 + 2:c0 + 3], in_=st2[:, c0 + 1:c0 + 2]), reads=[R_st2[t]],
                  writes=[R_st2[t]])
            kb.op("dve", lambda: nc.vector.scalar_tensor_tensor(out=yb[t][:], in0=yb[t][:], scalar=st2[:, c0 + 2:c0 + 3],
                                                                in1=gbc[:], op0=ALU.mult, op1=ALU.mult),
                  reads=[R_st2[t], R_gbc], writes=[R_yb[t]])
            kb.dma("sp", y[tok0 + t * 128: tok0 + (t + 1) * 128, :], yb[t][:], s_y[t], reads=[R_yb[t]], is_out=True)

        jobs = []
        for b4 in range(4):
            def wo_job(wv, wr, b4=b4):
                for t in range(ntile):
                    bi, bk, br = bank()
                    kb.mm([(lambda kc=kc: nc.tensor.matmul(bk[:, 0:512], m_T[:, kc, t * 128:(t + 1) * 128], wv[:, kc, :],
                                                           start=(kc == 0), stop=(kc == 15))) for kc in range(16)],
                          reads=[wr, R_m], writes=[br])
                    ysl = yb[t][:, b4 * 512:(b4 + 1) * 512]
                    kb.op("dve", lambda: nc.vector.tensor_tensor(out=ysl, in0=bk[:, 0:512], in1=ysl, op=ALU.add),
                          reads=[br], writes=[R_yb[t]])
                    if b4 == 3:
                        final_norm(t)
            jobs.append(((f"wo{b4}", [(0, w_out_v[:, :, b4 * 512:(b4 + 1) * 512])], 16, 512), wo_job))
        if gi < 4:
            pbn = p1_alloc(gi + 1)
            run_jobs(jobs, ("kv", [], 16, 512), hook=(1, lambda: p1_norm(gi + 1, pbn)))
            p1_tr(gi + 1, pbn)
        else:
            run_jobs(jobs, None)
        kb.barrier()

    for gi in range(5):
        group(gi)
    last = {}
    for sem, val in kb.out_toks:
        last[id(sem)] = (sem, max(val, last.get(id(sem), (None, 0))[1]))
    for tok in last.values():
        kb.eng["sp"].wait(tok)
    return nc


_NC = None


def kernel(x_prompt, x_sample, cache_k, cache_v, cache_kidx, state_conv, page_table, ln_g, w_in, conv_w, conv_b,
           conv_norm_g, conv_norm_b, w_pw, b_pw, w_up_attn, w_up_conv, w_out, rel_bias, final_g):
    global _NC
    if _NC is None:
        _NC = build()
    nc = _NC
    f = lambda a: np.ascontiguousarray(np.asarray(a, dtype=np.float32))
    consts = host_consts()
    ck = f(cache_k)[0].reshape(NPHYS * 16, 2048)
    cv = f(cache_v)[0].reshape(NPHYS * 16, 2048)
    cki = f(cache_kidx)[0].reshape(NPHYS * 16, 512)
    shared = {
        "cache_k": ck, "cache_v": cv, "cache_ki": cki, "ln_g": f(ln_g)[0], "w_in": f(w_in)[0], "conv_w": f(conv_w)[0],
        "conv_b": f(conv_b)[0], "cn_g": f(conv_norm_g)[0], "cn_b": f(conv_norm_b)[0], "w_pw": f(w_pw)[0],
        "b_pw": f(b_pw)[0], "w_ua": f(w_up_attn)[0], "w_uc": f(w_up_conv)[0], "w_out": f(w_out)[0],
        "rel_bias": f(rel_bias), "final_g": f(final_g).reshape(1, D),
    }
    for k, v in consts.items():
        shared["c_" + k] = v
    xp = f(x_prompt)
    xs = f(x_sample)
    st = f(state_conv)[0]
    pt = np.ascontiguousarray(np.asarray(page_table, dtype=np.int32))
    in_maps = []
    for c in range(8):
        m = dict(shared)
        m["x"] = np.ascontiguousarray(np.concatenate([xp[c], xs[16 * c:16 * c + 16].reshape(128, D)], axis=0))
        m["state"] = np.ascontiguousarray(st[16 * c:16 * c + 16])
        m["ptab"] = np.ascontiguousarray(pt[16 * c:16 * c + 16].reshape(1, 256))
        in_maps.append(m)
    res = run_bass_kernel_spmd(nc, in_maps, core_ids=list(range(8)))
    R = res.results
    y_p = np.stack([R[c]["y"][0:2048] for c in range(8)])
    y_s = np.concatenate([R[c]["y"][2048:2176].reshape(16, 8, D) for c in range(8)], axis=0)
    k_p = np.stack([R[c]["kk"][0:2048].reshape(2048, 2, 128) for c in range(8)])[None]
    v_p = np.stack([R[c]["vv"][0:2048].reshape(2048, 2, 128) for c in range(8)])[None]
    ki_p = np.stack([R[c]["kio"][0:2048] for c in range(8)])[None]
    cp = np.stack([R[c]["convp"] for c in range(8)])[None]
    k_s = np.concatenate([R[c]["kk"][2048:2176].reshape(16, 8, 2, 128) for c in range(8)], axis=0)[None]
    v_s = np.concatenate([R[c]["vv"][2048:2176].reshape(16, 8, 2, 128) for c in range(8)], axis=0)[None]
    ki_s = np.concatenate([R[c]["kio"][2048:2176].reshape(16, 8, 64) for c in range(8)], axis=0)[None]
    cs = np.concatenate([R[c]["convs"] for c in range(8)], axis=0)[None]
    return (y_p.astype(np.float32), y_s.astype(np.float32), k_p.astype(np.float32), v_p.astype(np.float32),
            ki_p.astype(np.float32), cp.astype(np.float32), k_s.astype(np.float32), v_s.astype(np.float32),
            ki_s.astype(np.float32), cs.astype(np.float32))
```

```python
import math
import numpy as np
import concourse.bass as bass
import concourse.mybir as mybir
from concourse.bass_utils import run_bass_kernel_spmd

F32 = mybir.dt.float32
BF16 = mybir.dt.bfloat16
I32 = mybir.dt.int32
U8 = mybir.dt.uint8
AF = mybir.ActivationFunctionType
ALU = mybir.AluOpType
AX = mybir.AxisListType
ET = mybir.EngineType

D = 2048
DIN = 10312
C_Q, C_K, C_V, C_ZA, C_QI, C_KI, C_WI, C_GV, C_GG, C_ZC, C_GA, C_GC = (
    0, 1024, 1280, 1536, 2560, 3072, 3136, 3144, 4168, 5192, 6216, 8264)
NPHYS = 2560
NEGB = -30000.0
SCALE = 128.0 ** -0.5
SQ128 = 128.0 ** 0.5
NBIS = 18
EPS = 1e-6


class Res:
    __slots__ = ("w", "r")

    def __init__(self):
        self.w = None
        self.r = {}


class Eng:
    def __init__(self, nc, e, name):
        self.e = e
        self.sem = nc.alloc_semaphore("p_" + name)
        self.n = 0
        self.seen = {}

    def wait(self, tok):
        if tok is None:
            return
        sem, val = tok
        k = id(sem)
        if self.seen.get(k, 0) >= val:
            return
        self.e.wait_ge(sem, val)
        self.seen[k] = val


class Slot:
    def __init__(self, nc, name, scratch=False):
        self.sem = nc.alloc_semaphore("d_" + name)
        self.n = 0
        self.scratch = scratch


class KB:
    def __init__(self, nc):
        self.nc = nc
        self.eng = {
            "pe": Eng(nc, nc.tensor, "pe"), "act": Eng(nc, nc.scalar, "act"),
            "dve": Eng(nc, nc.vector, "dve"), "pool": Eng(nc, nc.gpsimd, "pool"),
            "sp": Eng(nc, nc.sync, "sp"),
        }
        self.slots = []
        self.slotd = {}
        self.out_toks = []
        self.flip = 0

    def slot(self, name, scratch=False):
        if name in self.slotd:
            return self.slotd[name]
        s = Slot(self.nc, name, scratch)
        self.slots.append(s)
        self.slotd[name] = s
        return s

    def _deps(self, E, reads, writes):
        for r in reads:
            if r.w:
                E.wait(r.w)
        for w in writes:
            if w.w:
                E.wait(w.w)
            for t in w.r.values():
                E.wait(t)

    def _mark(self, tok, reads, writes):
        for r in reads:
            r.r[id(tok[0])] = tok
        for w in writes:
            w.w = tok
            w.r = {}

    def op(self, en, fn, reads=(), writes=()):
        E = self.eng[en]
        self._deps(E, reads, writes)
        ins = fn()
        E.n += 1
        ins.then_inc(E.sem, 1)
        tok = (E.sem, E.n)
        self._mark(tok, reads, writes)
        return tok

    def mm(self, fns, reads=(), writes=()):
        E = self.eng["pe"]
        self._deps(E, reads, writes)
        ins = None
        for f in fns:
            ins = f()
        E.n += 1
        ins.then_inc(E.sem, 1)
        tok = (E.sem, E.n)
        self._mark(tok, reads, writes)
        return tok

    def dma(self, q, out, in_, slot, reads=(), writes=(), is_out=False, **kw):
        E = self.eng[q]
        self._deps(E, reads, writes)
        ins = E.e.dma_start(out=out, in_=in_, **kw)
        slot.n += 16
        ins.then_inc(slot.sem, 16)
        tok = (slot.sem, slot.n)
        self._mark(tok, reads, writes)
        if is_out:
            self.out_toks.append(tok)
        return tok

    def idma(self, out, in_, idx_ap, slot, reads=(), writes=()):
        E = self.eng["pool"]
        self._deps(E, reads, writes)
        ins = self.nc.gpsimd.indirect_dma_start(out=out, out_offset=None, in_=in_,
                                                in_offset=bass.IndirectOffsetOnAxis(ap=idx_ap, axis=0))
        slot.n += 16
        ins.then_inc(slot.sem, 16)
        tok = (slot.sem, slot.n)
        self._mark(tok, reads, writes)
        return tok

    def barrier(self):
        toks = [(E.sem, E.n) for k, E in self.eng.items() if E.n > 0 and k != "sp"]
        toks += [(s.sem, s.n) for s in self.slots if s.scratch and s.n > 0]
        for E in self.eng.values():
            for t in toks:
                E.wait(t)

    def ev(self):
        self.flip ^= 1
        return "act" if self.flip else "dve"


def t5_bucket_np(n):
    n = np.maximum(n, 0)
    max_exact = 16
    ratio = np.log(np.maximum(n, 1).astype(np.float32) / np.float32(max_exact)) / np.float32(math.log(128 / max_exact))
    large = np.minimum(max_exact + (ratio.astype(np.float32) * np.float32(16)).astype(np.int32), 31)
    return np.where(n < max_exact, n, large)


def host_consts():
    c = {}
    c["identf"] = np.eye(128, dtype=np.float32)
    m = np.arange(384)
    n = m - 128
    oh = np.zeros((33, 384), np.float32)
    bk = t5_bucket_np(n)
    for i in range(384):
        if n[i] >= 0:
            oh[bk[i], i] = 1.0
        else:
            oh[32, i] = 1.0
    c["oh33"] = oh
    q = np.arange(128)[:, None]
    s = np.arange(128)[None, :]
    c["causneg"] = np.where(s <= q, 0.0, -1e30).astype(np.float32)
    same = (q // 8) == (s // 8)
    c["sampneg"] = np.where(same & ((s % 8) <= (q % 8)), 0.0, -1e30).astype(np.float32)
    c["sampbm"] = same.astype(np.float32)
    c["sampnb"] = np.where(same, 0.0, NEGB * SQ128).astype(np.float32)
    tok = np.arange(128)
    dq = np.zeros((128, 8, 8), np.float32)
    for t in range(128):
        dq[t, :, t % 8] = 1.0
    c["dq"] = dq.reshape(128, 64)
    bs = np.zeros((128, 16), np.float32)
    bs[tok, tok // 8] = 1.0
    c["bsel"] = bs
    sb = np.zeros((64, 248), np.float32)
    for hq in range(64):
        sb[hq, 120 + hq % 8] = 1.0
    c["selbase"] = sb
    c["oh8T"] = (np.arange(128)[None, :] // 16 == np.arange(8)[:, None]).astype(np.float32)
    c["fvec"] = np.tile((2.0 ** -(np.arange(32, dtype=np.float64) + 1)).astype(np.float32)[None, :], (128, 1))
    c["pidmat"] = np.tile((np.arange(128, dtype=np.float32) % 16)[:, None], (1, 32))
    return c


CONST_SHAPES = {"identf": [128, 128], "oh33": [33, 384], "causneg": [128, 128], "sampneg": [128, 128],
                "sampbm": [128, 128], "sampnb": [128, 128], "dq": [128, 64], "bsel": [128, 16],
                "selbase": [64, 248], "pidmat": [128, 32], "fvec": [128, 32], "oh8T": [8, 128]}


def build():
    nc = bass.Bass("TRN2", target_bir_lowering=False)
    kb = KB(nc)

    def din(name, shape, dt=F32):
        return nc.dram_tensor(name, shape, dt, kind="ExternalInput").ap()

    def dout(name, shape):
        return nc.dram_tensor(name, shape, F32, kind="ExternalOutput").ap()

    x = din("x", [2176, D])
    cache_k = din("cache_k", [NPHYS * 16, 2048])
    cache_v = din("cache_v", [NPHYS * 16, 2048])
    cache_ki = din("cache_ki", [NPHYS * 16, 512])
    state = din("state", [16, 30, 1024])
    ptab = din("ptab", [1, 256], I32)
    ln_g = din("ln_g", [D])
    w_in = din("w_in", [D, DIN])
    conv_w = din("conv_w", [31, 1024])
    conv_b = din("conv_b", [1024])
    cn_g = din("cn_g", [1024])
    cn_b = din("cn_b", [1024])
    w_pw = din("w_pw", [1024, 1024])
    b_pw = din("b_pw", [1024])
    w_ua = din("w_ua", [1024, D])
    w_uc = din("w_uc", [1024, D])
    w_out = din("w_out", [D, D])
    rel_bias = din("rel_bias", [32, 8])
    final_g = din("final_g", [1, D])
    cin = {k: din("c_" + k, v) for k, v in CONST_SHAPES.items()}

    y = dout("y", [2176, D])
    kk = dout("kk", [2176, 256])
    vv = dout("vv", [2176, 256])
    kio = dout("kio", [2176, 64])
    convp = dout("convp", [30, 1024])
    convs = dout("convs", [16, 30, 1024])
    dper = nc.dram_tensor("dper", [8, 129 * 384], F32, kind="Internal").ap()

    w_in_v = w_in.rearrange("(kc p) f -> p kc f", p=128)
    w_out_v = w_out.rearrange("(kc p) f -> p kc f", p=128)
    w_pw_v = w_pw.rearrange("(kc p) f -> p kc f", p=128)
    w_ua_v = w_ua.rearrange("(kc p) f -> p kc f", p=128)
    w_uc_v = w_uc.rearrange("(kc p) f -> p kc f", p=128)

    wsc = {}

    def wconv(key, src, kcn, ncols, col0=0, wtot=None):
        wtot = ncols if wtot is None else wtot
        if key not in wsc:
            t = nc.dram_tensor("wsc_" + key, [128, kcn * wtot], BF16, kind="Internal").ap()
            wsc[key] = (t, Res(), kb.slot("cv_" + key))
        t, r, sl = wsc[key]
        tv = t.rearrange("p (k c) -> p k c", k=kcn)
        kb.dma("pool", tv[:, :, col0:col0 + ncols], src, sl, writes=[r])

    def conv_first():
        wconv("kv", w_in_v[:, :, C_K:C_K + 512], 16, 512)
        wconv("qi", w_in_v[:, :, C_QI:C_QI + 512], 16, 512)
        wconv("ki", w_in_v[:, :, C_KI:C_KI + 72], 16, 72)
        for b2 in range(2):
            wconv(f"q{b2}", w_in_v[:, :, C_Q + b2 * 512: C_Q + (b2 + 1) * 512], 16, 512)
        for b2 in range(2):
            wconv(f"za{b2}", w_in_v[:, :, C_ZA + b2 * 512: C_ZA + (b2 + 1) * 512], 16, 512)

    def conv_rest(after):
        kb._deps(kb.eng["pool"], after, [])
        for b4 in range(4):
            wconv(f"ua{b4}", w_ua_v[:, :, b4 * 512:(b4 + 1) * 512], 8, 512)
        for b4 in range(4):
            wconv(f"ga{b4}", w_in_v[:, :, C_GA + b4 * 512: C_GA + (b4 + 1) * 512], 16, 512)
        for b2 in range(2):
            wconv(f"gg{b2}", w_in_v[:, :, C_GG + b2 * 512: C_GG + (b2 + 1) * 512], 16, 512)
        for b2 in range(2):
            wconv(f"gv{b2}", w_in_v[:, :, C_GV + b2 * 512: C_GV + (b2 + 1) * 512], 16, 512)
        for b2 in range(2):
            wconv(f"zc{b2}", w_in_v[:, :, C_ZC + b2 * 512: C_ZC + (b2 + 1) * 512], 16, 512)
        for b2 in range(2):
            wconv(f"pw{b2}", w_pw_v[:, :, b2 * 512:(b2 + 1) * 512], 8, 512)
        for b4 in range(4):
            wconv(f"uc{b4}", w_uc_v[:, :, b4 * 512:(b4 + 1) * 512], 8, 512)
            wconv(f"gc{b4}", w_in_v[:, :, C_GC + b4 * 512: C_GC + (b4 + 1) * 512], 16, 512)
        for b4 in range(4):
            wconv(f"wo{b4}", w_out_v[:, :, b4 * 512:(b4 + 1) * 512], 16, 512)


    def sb(name, shape, dt):
        return nc.alloc_sbuf_tensor(name, shape, dt)

    identf = sb("identf", [128, 128], F32)
    identb = sb("identb", [128, 128], BF16)
    onesb = sb("onesb", [128, 128], BF16)
    onesf = sb("onesf", [128, 128], F32)
    causneg = sb("causneg", [128, 128], F32)
    sampneg = sb("sampneg", [128, 128], F32)
    sampbm = sb("sampbm", [128, 128], F32)
    sampnb = sb("sampnb", [128, 128], F32)
    dq = sb("dq", [128, 64], F32)
    bsel = sb("bsel", [128, 16], F32)
    selbase = sb("selbase", [64, 248], F32)
    prmT = sb("prmT", [128, 48], F32)
    lngT = prmT[:, 0:16]
    cbT = prmT[:, 16:24]
    cngT = prmT[:, 24:32]
    cnbT = prmT[:, 32:40]
    bpwT = prmT[:, 40:48]
    cwT = sb("cwT", [128, 8, 31], F32)
    oh8T = sb("oh8T", [8, 128], F32)
    epsT = sb("epsT", [128, 1], F32)
    pidmat = sb("pidmat", [128, 32], F32)
    fvec = sb("fvec", [128, 32], F32)
    idxall = sb("idxall", [128, 32], I32)
    biasT = sb("biasT", [128, 4, 8, 128], BF16)
    xnT = sb("xnT", [128, 16, 512], BF16)
    mT = sb("mT", [128, 16, 512], BF16)
    wbuf = [sb(f"wbuf{i}", [128, 8192], BF16) for i in range(2)]
    KT = sb("KT", [128, 2, 2048], BF16)
    VA = sb("VA", [128, 16, 2, 128], BF16)
    KiT = sb("KiT", [64, 2048], BF16)
    uhalo = sb("uhalo", [128, 8, 30], BF16)
    SCR = 102 * 1024
    scr = sb("scr", [128, SCR], U8)
    psum_all = nc.alloc_psum_tensor("ps", [128, 4096], F32)

    class Arena:
        def __init__(self):
            self.off = 0

        def reset(self):
            self.off = 0

        def get(self, shape, dt):
            esz = 4 if dt in (F32, I32) else 2
            n = int(np.prod(shape[1:])) * esz
            n = (n + 63) // 64 * 64
            assert self.off + n <= SCR, (self.off, n)
            ap = scr[0:shape[0], self.off:self.off + n].bitcast(dt)
            self.off += n
            if len(shape) == 3:
                ap = ap[:, 0:shape[1] * shape[2]].rearrange("p (a b) -> p a b", a=shape[1])
            elif len(shape) == 4:
                ap = ap[:, 0:shape[1] * shape[2] * shape[3]].rearrange("p (a b c) -> p a b c", a=shape[1], b=shape[2])
            else:
                ap = ap[:, 0:shape[1]]
            return ap

    ar = Arena()

    banks = [psum_all[:, i * 512:(i + 1) * 512] for i in range(8)]
    bres = [Res() for _ in range(8)]
    pinned = set()
    bstate = {"i": 0}

    def bank():
        while True:
            i = bstate["i"] % 8
            bstate["i"] += 1
            if i not in pinned:
                return i, banks[i], bres[i]

    s_setup = kb.slot("setup")
    R_const = Res()
    ar.reset()
    ar.off = 56 * 1024
    prm = ar.get([48, 128], F32)
    cwst = ar.get([31, 1024], F32)
    ptJ = ar.get([8, 32], I32)
    ptJf = ar.get([8, 32], F32)
    with nc.allow_non_contiguous_dma(reason="tiny page table transpose"):
        for t_, a_ in [(identf, cin["identf"]), (causneg, cin["causneg"]), (sampneg, cin["sampneg"]),
                       (sampbm, cin["sampbm"]), (sampnb, cin["sampnb"]), (dq, cin["dq"]), (bsel, cin["bsel"]),
                       (selbase, cin["selbase"]), (pidmat, cin["pidmat"]), (fvec, cin["fvec"]), (oh8T, cin["oh8T"])]:
            kb.dma("sp", t_[:], a_, s_setup, writes=[R_const])
        kb.dma("sp", ptJ[:], ptab.rearrange("o (bh j) -> (o j) bh", j=8), s_setup, writes=[R_const])
        kb.dma("sp", prm[0:16, :], ln_g.rearrange("(kc p) -> kc p", p=128), s_setup, writes=[R_const])
        for k_, a_ in enumerate([conv_b, cn_g, cn_b, b_pw]):
            kb.dma("sp", prm[16 + 8 * k_:24 + 8 * k_, :], a_.rearrange("(c p) -> c p", p=128), s_setup, writes=[R_const])
        kb.dma("sp", cwst[:], conv_w, s_setup, writes=[R_const])
    bi, bk_, br = bank()
    kb.mm([lambda: nc.tensor.transpose(out=bk_[:, 0:48], in_=prm[0:48, :], identity=identf[0:48, 0:48])],
          reads=[R_const], writes=[br])
    kb.op("dve", lambda: nc.vector.tensor_copy(out=prmT[:], in_=bk_[:, 0:48]), reads=[br], writes=[R_const])
    bi, bk_, br = bank()
    kb.mm([(lambda c_=c_: nc.tensor.transpose(out=bk_[:, c_ * 31:(c_ + 1) * 31], in_=cwst[0:31, c_ * 128:(c_ + 1) * 128],
                                              identity=identf[0:31, 0:31])) for c_ in range(8)],
          reads=[R_const], writes=[br])
    kb.op("dve", lambda: nc.vector.tensor_copy(out=cwT[:].rearrange("p a b -> p (a b)"), in_=bk_[:, 0:248]),
          reads=[br], writes=[R_const])
    kb.op("dve", lambda: nc.vector.tensor_copy(out=ptJf[:], in_=ptJ[:]), reads=[R_const], writes=[R_const])
    bi, bk_, br = bank()
    kb.mm([lambda: nc.tensor.matmul(bk_[:, 0:32], oh8T[:, :], ptJf[:, :], start=True, stop=True)],
          reads=[R_const], writes=[br])
    kb.op("dve", lambda: nc.vector.scalar_tensor_tensor(out=idxall[:], in0=bk_[:, 0:32], scalar=16.0, in1=pidmat[:],
                                                        op0=ALU.mult, op1=ALU.add), reads=[br, R_const], writes=[R_const])
    kb.op("dve", lambda: nc.vector.tensor_copy(out=identb[:], in_=identf[:]), reads=[R_const], writes=[R_const])
    kb.op("dve", lambda: nc.vector.memset(onesb[:], 1.0), writes=[R_const])
    kb.op("dve", lambda: nc.vector.memset(onesf[:], 1.0 / 1024.0), writes=[R_const])
    kb.op("dve", lambda: nc.vector.memset(epsT[:], EPS), writes=[R_const])
    kb.op("dve", lambda: nc.vector.memset(uhalo[:], 0.0), writes=[R_const])

    LS = {}

    rb33 = ar.get([33, 8], F32)
    oh33 = ar.get([33, 384], F32)
    dx = ar.get([8, 384], F32)
    dx2 = ar.get([8, 384], F32)
    R_b0 = Res()
    s_b = kb.slot("btab", scratch=True)
    kb.op("dve", lambda: nc.vector.memset(rb33[0:33, :], NEGB), writes=[R_b0])
    kb.dma("sp", rb33[0:32, :], rel_bias, s_b, writes=[R_b0])
    kb.dma("sp", oh33[:], cin["oh33"], s_b, writes=[R_b0])
    bi, bk_, br = bank()
    kb.mm([lambda: nc.tensor.matmul(bk_[0:8, 0:384], rb33[:, :], oh33[:, :], start=True, stop=True)],
          reads=[R_b0], writes=[br])
    kb.op("dve", lambda: nc.vector.tensor_copy(out=dx[:], in_=bk_[0:8, 0:384]), reads=[br], writes=[R_b0])
    kb.op("dve", lambda: nc.vector.tensor_scalar(out=dx2[:], in0=dx[:], scalar1=dx[:, 383:384], scalar2=SQ128,
                                                 op0=ALU.subtract, op1=ALU.mult), reads=[R_b0], writes=[R_b0])
    R_dper = Res()
    kb.dma("sp", dper.rearrange("h (r m) -> h r m", m=384), dx2[:].unsqueeze(1).broadcast_to([8, 129, 384]),
           s_b, reads=[R_b0], writes=[R_dper])

    def late_setup():
        ar.off = 72 * 1024
        tst = ar.get([128, 8, 128], F32)
        thf = ar.get([128, 8, 128], F32)
        R_b = Res()
        R_bias = Res()

        def mk_table(cofs, dst_hi, dst_lo):
            src = dper[:, cofs:cofs + 383 * 128].rearrange("h (s x) -> s h x", x=383)[:, :, 0:128]
            kb.dma("sp", tst[:], src, s_b, reads=[R_dper], writes=[R_b])
            kb.op("dve", lambda: nc.vector.tensor_copy(out=dst_hi, in_=tst[:]), reads=[R_b], writes=[R_bias])
            kb.op("dve", lambda: nc.vector.tensor_copy(out=thf[:], in_=dst_hi), reads=[R_bias], writes=[R_b])
            kb.op("dve", lambda: nc.vector.tensor_tensor(out=dst_lo, in0=tst[:], in1=thf[:], op=ALU.subtract),
                  reads=[R_b], writes=[R_bias])

        with nc.allow_non_contiguous_dma(reason="toeplitz"):
            mk_table(128, biasT[:, 0], biasT[:, 1])
            mk_table(256, biasT[:, 2], biasT[:, 3])
        dsc = [nc.dram_tensor(f"dsc{c}", [128, 31 * 128], BF16, kind="Internal").ap() for c in range(8)]
        R_dsc = [Res() for _ in range(8)]
        s_dg = [kb.slot("dg0", scratch=True), kb.slot("dg1", scratch=True)]
        LS.update(dict(R_dper=R_dper, R_bias=R_bias, dsc=dsc, R_dsc=R_dsc, s_dg=s_dg))

    R_uh = Res()
    wres = [Res(), Res()]
    wsl = [kb.slot("w0"), kb.slot("w1")]
    wst = {"n": 0}

    def wload(key, parts, kcn, ncols):
        i = wst["n"] % 2
        wst["n"] += 1
        flat = wbuf[i][:, 0:kcn * ncols]
        view = flat.rearrange("p (k c) -> p k c", k=kcn)
        t, r, sl = wsc[key]
        kb.dma("sp", flat, t, wsl[i], reads=[r], writes=[wres[i]])
        return view, wres[i]

    prefetched = {}

    def run_jobs(jobs, next_first=None, hook=None):
        loaded = {}
        order = [k for k, j in enumerate(jobs) if j[0] is not None]

        def ensure(k):
            if k not in loaded:
                key = jobs[k][0][0]
                if key in prefetched:
                    loaded[k] = prefetched.pop(key)
                else:
                    loaded[k] = wload(*jobs[k][0])
        pos = 0
        for k, (ls, fn) in enumerate(jobs):
            if ls is not None:
                ensure(k)
                pos = order.index(k)
                if pos + 1 < len(order):
                    ensure(order[pos + 1])
                elif next_first is not None:
                    prefetched[next_first[0]] = wload(*next_first)
                fn(*loaded.pop(k))
            else:
                fn()
            if hook is not None and hook[0] == k:
                hook[1]()

    R_xn = Res()

    def p1_alloc(gj):
        nt_ = 1 if gj == 4 else 4
        pb = {}
        pb["xb"] = [ar.get([128, D], F32) for _ in range(2)]
        pb["xs"] = [ar.get([128, D], F32) for _ in range(nt_)]
        pb["st"] = ar.get([128, 4 * nt_], F32)
        pb["R_xb"] = [Res(), Res()]
        pb["R_xs"] = [Res() for _ in range(nt_)]
        pb["R_st"] = Res()
        pb["nt"] = nt_
        return pb

    def p1_norm(gj, pb):
        tok0_ = gj * 512
        s_x = [kb.slot(f"x_{i}", scratch=True) for i in range(2)]
        xb, xs, st = pb["xb"], pb["xs"], pb["st"]
        for t in range(pb["nt"]):
            i = t % 2
            c0 = 4 * t
            kb.dma("sp", xb[i][:], x[tok0_ + t * 128: tok0_ + (t + 1) * 128, :], s_x[i], writes=[pb["R_xb"][i]])
            kb.op("act", lambda: nc.scalar.activation(out=xs[t][:], in_=xb[i][:], func=AF.Square,
                                                      accum_out=st[:, c0:c0 + 1]), reads=[pb["R_xb"][i]],
                  writes=[pb["R_xs"][t], pb["R_st"]])
            kb.op("act", lambda: nc.scalar.activation(out=st[:, c0 + 1:c0 + 2], in_=st[:, c0:c0 + 1], func=AF.Sqrt,
                                                      bias=epsT[:, 0:1], scale=1.0 / D), reads=[pb["R_st"], R_const],
                  writes=[pb["R_st"]])
            kb.op("dve", lambda: nc.vector.reciprocal(out=st[:, c0 + 2:c0 + 3], in_=st[:, c0 + 1:c0 + 2]), reads=[pb["R_st"]],
                  writes=[pb["R_st"]])
            kb.op("act", lambda: nc.scalar.activation(out=xs[t][:], in_=xb[i][:], func=AF.Copy, scale=st[:, c0 + 2:c0 + 3]),
                  reads=[pb["R_xb"][i], pb["R_st"]], writes=[pb["R_xs"][t]])

    def p1_tr(gj, pb):
        NT_ = 128 if gj == 4 else 512
        xn_ = xnT[:, :, 0:NT_]
        xs = pb["xs"]
        for t in range(pb["nt"]):
            for q4 in range(4):
                bi, bk, br = bank()
                kb.mm([(lambda kc=q4 * 4 + k, k=k: nc.tensor.transpose(out=bk[:, k * 128:(k + 1) * 128],
                                                                       in_=xs[t][:, kc * 128:(kc + 1) * 128],
                                                                       identity=identf[:])) for k in range(4)],
                      reads=[pb["R_xs"][t], R_const], writes=[br])
                for k in range(4):
                    kc = q4 * 4 + k
                    if kb.ev() == "act":
                        kb.op("act", lambda: nc.scalar.activation(out=xn_[:, kc, t * 128:(t + 1) * 128],
                                                                  in_=bk[:, k * 128:(k + 1) * 128], func=AF.Copy,
                                                                  scale=lngT[:, kc:kc + 1]),
                              reads=[br, R_const], writes=[R_xn])
                    else:
                        kb.op("dve", lambda: nc.vector.tensor_scalar(out=xn_[:, kc, t * 128:(t + 1) * 128],
                                                                     in0=bk[:, k * 128:(k + 1) * 128],
                                                                     scalar1=lngT[:, kc:kc + 1], scalar2=None,
                                                                     op0=ALU.mult),
                              reads=[br, R_const], writes=[R_xn])

    def group(gi):
        samp = gi == 4
        NT = 128 if samp else 512
        ntile = NT // 128
        tok0 = gi * 512
        xn = xnT[:, :, 0:NT]
        m_T = mT[:, :, 0:NT]
        R_m = Res()

        if gi == 0:
            ar.reset()
            pb = p1_alloc(0)
            p1_norm(0, pb)
            kb._deps(kb.eng["pool"], [R_const, pb["R_xb"][0], pb["R_xb"][1]], [])
            with nc.allow_non_contiguous_dma(reason="narrow idx weight block"):
                conv_first()
            p1_tr(0, pb)
        if gi == 0:
            late_setup()
            conv_rest([LS["R_bias"]] + LS["R_dsc"])
        R_dper, R_bias, dsc, R_dsc, s_dg = LS["R_dper"], LS["R_bias"], LS["dsc"], LS["R_dsc"], LS["s_dg"]
        kb.barrier()

        ar.reset()
        qT = ar.get([128, 8, NT], BF16)
        zsT = ar.get([128, 8, NT], BF16)
        qiT = ar.get([64, 8, NT], BF16)
        gT = ar.get([128, 8, NT], BF16)
        kvst = [ar.get([128, 512], F32) for _ in range(2)]
        kist = [ar.get([128, 72], F32) for _ in range(2)]
        wab = ar.get([128, ntile, 16], F32)
        R_q, R_zs, R_qi, R_g, R_wab = Res(), Res(), Res(), Res(), Res()
        R_kvst = [Res(), Res()]
        R_kist = [Res(), Res()]
        s_kv = [kb.slot(f"kv_{i}", scratch=True) for i in range(2)]
        s_ki = [kb.slot(f"ki_{i}", scratch=True) for i in range(2)]
        R_KT, R_VA, R_KiT = Res(), Res(), Res()
        if samp:
            KTn = ar.get([128, 2, 128], BF16)
            VAn = ar.get([128, 2, 128], BF16)
            KiTn = ar.get([64, 128], BF16)

        def formF(wv, wr, nchunk, M, evac, kcn=16, src=None, R_src=None):
            src = xn if src is None else src
            R_src = R_xn if R_src is None else R_src
            for c in range(nchunk):
                bi, bk, br = bank()
                kb.mm([(lambda kc=kc: nc.tensor.matmul(bk[0:M, 0:NT], wv[:, kc, c * M:(c + 1) * M], src[:, kc, :],
                                                       start=(kc == 0), stop=(kc == kcn - 1))) for kc in range(kcn)],
                      reads=[wr, R_src], writes=[br])
                evac(c, bk[0:M, 0:NT], br)

        def formT(wv, wr, ncols, evac, c0=0, kcn=16):
            for t in range(ntile):
                bi, bk, br = bank()
                kb.mm([(lambda kc=kc: nc.tensor.matmul(bk[:, 0:ncols], xn[:, kc, t * 128:(t + 1) * 128],
                                                       wv[:, kc, c0:c0 + ncols],
                                                       start=(kc == 0), stop=(kc == kcn - 1))) for kc in range(kcn)],
                      reads=[wr, R_xn], writes=[br])
                evac(t, bk[:, 0:ncols], br)

        def copy_evac(dst_fn, R_dst, func=None):
            def f(c, ps, br):
                if func is not None:
                    kb.op("act", lambda: nc.scalar.activation(out=dst_fn(c), in_=ps, func=func), reads=[br], writes=[R_dst])
                elif kb.ev() == "act":
                    kb.op("act", lambda: nc.scalar.copy(out=dst_fn(c), in_=ps), reads=[br], writes=[R_dst])
                else:
                    kb.op("dve", lambda: nc.vector.tensor_copy(out=dst_fn(c), in_=ps), reads=[br], writes=[R_dst])
            return f

        jobs = []

        def kv_job(wv, wr):
            if samp:
                formF(wv, wr, 2, 128, copy_evac(lambda c: KTn[:, c, :], R_KT))
            else:
                formF(wv, wr, 2, 128, copy_evac(lambda c: KT[:, c, tok0:tok0 + NT], R_KT))

            def ev(t, ps, br):
                i = t % 2
                kb.op("act", lambda: nc.scalar.copy(out=kvst[i][:, :], in_=ps), reads=[br], writes=[R_kvst[i]])
                kb.dma("sp", kk[tok0 + t * 128: tok0 + (t + 1) * 128, :], kvst[i][:, 0:256], s_kv[i],
                       reads=[R_kvst[i]], is_out=True)
                kb.dma("sp", vv[tok0 + t * 128: tok0 + (t + 1) * 128, :], kvst[i][:, 256:512], s_kv[i],
                       reads=[R_kvst[i]], is_out=True)
                dst = VAn[:, :, :] if samp else VA[:, gi * 4 + t, :, :]
                kb.op("dve", lambda: nc.vector.tensor_copy(out=dst, in_=kvst[i][:, 256:512].rearrange("p (g d) -> p g d", g=2)),
                      reads=[R_kvst[i]], writes=[R_VA])
            formT(wv, wr, 512, ev)
        jobs.append((("kv", [(0, w_in_v[:, :, C_K:C_K + 512])], 16, 512), kv_job))
        jobs.append((("qi", [(0, w_in_v[:, :, C_QI: C_QI + 512])], 16, 512),
                     lambda wv, wr: formF(wv, wr, 8, 64, copy_evac(lambda c: qiT[:, c, :], R_qi))))

        def idx_job(wv, wr):
            if samp:
                formF(wv, wr, 1, 64, copy_evac(lambda c: KiTn[:, :], R_KiT))
            else:
                formF(wv, wr, 1, 64, copy_evac(lambda c: KiT[:, tok0:tok0 + NT], R_KiT))

            def ev(t, ps, br):
                i = t % 2
                kb.op("act", lambda: nc.scalar.copy(out=kist[i][:, :], in_=ps), reads=[br], writes=[R_kist[i]])
                kb.dma("sp", kio[tok0 + t * 128: tok0 + (t + 1) * 128, :], kist[i][:, 0:64], s_ki[i],
                       reads=[R_kist[i]], is_out=True)
                kb.op("act", lambda: nc.scalar.activation(out=wab[:, t, 0:8], in_=kist[i][:, 64:72], func=AF.Abs),
                      reads=[R_kist[i]], writes=[R_wab])
                kb.op("act", lambda: nc.scalar.activation(out=wab[:, t, 8:16], in_=kist[i][:, 64:72], func=AF.Sign),
                      reads=[R_kist[i]], writes=[R_wab])
            formT(wv, wr, 72, ev)
        jobs.append((("ki", [(0, w_in_v[:, :, C_KI:C_KI + 72])], 16, 72), idx_job))
        with nc.allow_non_contiguous_dma(reason="narrow idx weight block"):
            run_jobs(jobs, None)
        if gi == 0:
            for E_ in kb.eng.values():
                kb._deps(E_, [R_bias] + R_dsc, [])

        def gen_X2():
            for key, dst, Rd, fn_ in [("q0", lambda c: qT[:, c, :], R_q, None), ("q1", lambda c: qT[:, 4 + c, :], R_q, None),
                                      ("za0", lambda c: zsT[:, c, :], R_zs, AF.Silu),
                                      ("za1", lambda c: zsT[:, 4 + c, :], R_zs, AF.Silu)]:
                wv, wr = wload(key, [], 16, 512)
                ev_ = copy_evac(dst, Rd, fn_)
                for c in range(4):
                    bi, bk, br = bank()
                    kb.mm([(lambda kc=kc: nc.tensor.matmul(bk[:, 0:NT], wv[:, kc, c * 128:(c + 1) * 128], xn[:, kc, :],
                                                           start=(kc == 0), stop=(kc == 15))) for kc in range(16)],
                          reads=[wr, R_xn], writes=[br])
                    ev_(c, bk[:, 0:NT], br)
                    yield

        def gen_ga():
            for b4 in range(4):
                wv, wr = wload(f"ga{b4}", [], 16, 512)
                for c in range(4):
                    cc = b4 * 4 + c
                    bi, bk, br = bank()
                    kb.mm([(lambda kc=kc: nc.tensor.matmul(bk[:, 0:NT], wv[:, kc, c * 128:(c + 1) * 128], xn[:, kc, :],
                                                           start=(kc == 0), stop=(kc == 15))) for kc in range(16)],
                          reads=[wr, R_xn], writes=[br])
                    kb.op("act", lambda: nc.scalar.copy(out=m_T[:, cc, :], in_=bk[:, 0:NT]), reads=[br], writes=[R_m])
                    yield

        class SelBuf:
            pass
        nset = 1 if samp else 2
        SB = []
        for k_ in range(nset):
            S = SelBuf()
            S.score = ar.get([128, 2176], F32)
            S.junk = ar.get([128, 2176], BF16)
            S.m01 = ar.get([128, 2176], BF16)
            S.maskT = ar.get([128, 17, 128], BF16)
            S.bis = ar.get([128, 80], F32)
            S.R_score, S.R_junk, S.R_m01, S.R_maskT, S.R_bis = Res(), Res(), Res(), Res(), Res()
            SB.append(S)
        S0 = SB[0]
        score, maskT, bis = S0.score, S0.maskT, S0.bis
        R_score, R_maskT, R_bis = S0.R_score, S0.R_maskT, S0.R_bis
        rtmp = [ar.get([128, 512], F32) for _ in range(2)]
        et = [ar.get([128, 512], BF16) for _ in range(3)]
        pt = [ar.get([128, 512], BF16) for _ in range(2)]
        rd = ar.get([128, 512], F32)
        tmpf = ar.get([128, 512], F32)
        R_rd, R_tmpf = Res(), Res()
        R_rtmp = [Res(), Res()]
        R_et = [Res(), Res(), Res()]
        R_pt = [Res(), Res()]
        cnt = {"r": 0, "e": 0}
        NEGM = NEGB * SQ128

        def gen_select(S, L, nblk, additive):
            bis_ = S.bis
            Rb = S.R_bis
            kb.op("dve", lambda: nc.vector.tensor_reduce(out=bis_[:, 0:1], in_=S.score[:, 0:L], axis=AX.X, op=ALU.max),
                  reads=[S.R_score], writes=[Rb])
            yield
            kb.op("dve", lambda: nc.vector.tensor_sub(out=bis_[:, 2:3], in0=bis_[:, 0:1], in1=bis_[:, 1:2]),
                  reads=[Rb], writes=[Rb])
            for k in range(NBIS + 1):
                pass
            kb.op("dve", lambda: nc.vector.tensor_scalar(out=bis_[:, 40:40 + NBIS + 1],
                                                         in0=fvec[:, 0:NBIS + 1], scalar1=bis_[:, 2:3], scalar2=None,
                                                         op0=ALU.mult), reads=[Rb, R_const], writes=[Rb])
            kb.op("dve", lambda: nc.vector.tensor_tensor(out=bis_[:, 8:9], in0=bis_[:, 1:2], in1=bis_[:, 40:41], op=ALU.add),
                  reads=[Rb], writes=[Rb])
            yield
            for k in range(NBIS):
                mid = bis_[:, 8 + k:9 + k]
                kb.op("dve", lambda: nc.vector.tensor_scalar(out=S.junk[:, 0:L], in0=S.score[:, 0:L], scalar1=mid,
                                                             scalar2=0.0, op0=ALU.is_ge, op1=ALU.add,
                                                             accum_out=bis_[:, 4:5]),
                      reads=[Rb, S.R_score], writes=[S.R_junk, Rb])
                yield
                kb.op("dve", lambda: nc.vector.tensor_scalar(out=bis_[:, 5:6], in0=bis_[:, 4:5], scalar1=255.5, scalar2=-0.5,
                                                             op0=ALU.is_ge, op1=ALU.add), reads=[Rb], writes=[Rb])
                kb.op("dve", lambda: nc.vector.scalar_tensor_tensor(out=bis_[:, 9 + k:10 + k], in0=bis_[:, 5:6],
                                                                    scalar=bis_[:, 40 + k:41 + k], in1=mid,
                                                                    op0=ALU.mult, op1=ALU.add), reads=[Rb], writes=[Rb])
                yield
            kb.op("dve", lambda: nc.vector.tensor_sub(out=bis_[:, 6:7], in0=bis_[:, 8 + NBIS:9 + NBIS],
                                                      in1=bis_[:, 40 + NBIS:41 + NBIS]), reads=[Rb], writes=[Rb])
            if additive:
                kb.op("dve", lambda: nc.vector.tensor_scalar(out=S.m01[:, 0:L], in0=S.score[:, 0:L], scalar1=bis_[:, 6:7],
                                                             scalar2=NEGM, op0=ALU.is_lt, op1=ALU.mult),
                      reads=[Rb, S.R_score], writes=[S.R_m01])
            else:
                kb.op("dve", lambda: nc.vector.tensor_scalar(out=S.m01[:, 0:L], in0=S.score[:, 0:L], scalar1=bis_[:, 6:7],
                                                             scalar2=None, op0=ALU.is_ge),
                      reads=[Rb, S.R_score], writes=[S.R_m01])
            yield
            j = 0
            while j < nblk:
                n = min(8, nblk - j)
                bi, bk, br = bank()
                bkb = bk.bitcast(BF16)
                kb.mm([(lambda jj=jj: nc.tensor.transpose(out=bkb[:, jj * 128:(jj + 1) * 128],
                                                          in_=S.m01[:, (j + jj) * 128:(j + jj + 1) * 128],
                                                          identity=identb[:])) for jj in range(n)],
                      reads=[S.R_m01, R_const], writes=[br])
                kb.op("act", lambda: nc.scalar.copy(out=S.maskT[:, j:j + n, :],
                                                    in_=bkb[:, 0:n * 128].rearrange("p (a b) -> p a b", a=n)),
                      reads=[br], writes=[S.R_maskT])
                j += n
                yield

        def select_mask(L, nblk):
            for _ in gen_select(S0, L, nblk, False):
                pass

        def finalize(g, bo, ro, bd, rdn, cols):
            kb.op("dve", lambda: nc.vector.reciprocal(out=rd[:], in_=bd[:, :]), reads=[rdn], writes=[R_rd])
            kb.op("dve", lambda: nc.vector.tensor_tensor(out=tmpf[:], in0=bo[:, :], in1=rd[:], op=ALU.mult),
                  reads=[ro, R_rd], writes=[R_tmpf])
            kb.op("dve", lambda: nc.vector.tensor_tensor(out=gT[:, 4 * g:4 * g + 4, cols],
                                                         in0=tmpf[:].rearrange("p (h q) -> p h q", h=4),
                                                         in1=zsT[:, 4 * g:4 * g + 4, cols], op=ALU.mult),
                  reads=[R_tmpf, R_zs], writes=[R_g])

        def attn_block(g, lhsK, cols, near, tabs, mask_ap, bo, ro, bd, rdn, lhsV, first, rK, rV, addmask=None, R_mk=None):
            bi, bk, br = bank()
            nmm = 1 + (2 if near else 0) + (1 if addmask is not None else 0)
            fns = [lambda: nc.tensor.matmul(bk[:, :], lhsK, qT[:, 4 * g:4 * g + 4, cols], start=True, stop=(nmm == 1),
                                            skip_group_check=True)]
            if near:
                fns.append(lambda: nc.tensor.matmul(bk[:, :], identb[:], tabs[0][:, 4 * g:4 * g + 4, :], start=False,
                                                    stop=False, skip_group_check=True))
                fns.append(lambda: nc.tensor.matmul(bk[:, :], identb[:], tabs[1][:, 4 * g:4 * g + 4, :], start=False,
                                                    stop=(addmask is None), skip_group_check=True))
            rds = [rK, R_q, R_bias, R_const]
            if addmask is not None:
                fns.append(lambda: nc.tensor.matmul(bk[:, :], identb[:], addmask.unsqueeze(1).broadcast_to([128, 4, 128]),
                                                    start=False, stop=True, skip_group_check=True))
                rds.append(R_mk)
            kb.mm(fns, reads=rds, writes=[br])
            ei = cnt["e"] % 3
            cnt["e"] += 1
            kb.op("act", lambda: nc.scalar.activation(out=et[ei][:], in_=bk[:, :], func=AF.Exp, scale=SCALE),
                  reads=[br], writes=[R_et[ei]])
            if mask_ap is not None:
                pi = ei % 2
                kb.op("dve", lambda: nc.vector.tensor_tensor(out=pt[pi][:].rearrange("p (h q) -> p h q", h=4),
                                                             in0=et[ei][:].rearrange("p (h q) -> p h q", h=4),
                                                             in1=mask_ap.unsqueeze(1).broadcast_to([128, 4, 128]),
                                                             op=ALU.mult),
                      reads=[R_et[ei], R_maskT], writes=[R_pt[pi]])
                src, rs = pt[pi], R_pt[pi]
            else:
                src, rs = et[ei], R_et[ei]
            kb.mm([lambda: nc.tensor.matmul(bo[:, :], lhsV, src[:], start=first, stop=False, skip_group_check=True),
                   lambda: nc.tensor.matmul(bd[:, :], onesb[:], src[:], start=first, stop=False, skip_group_check=True)],
                  reads=[rs, rV, R_const], writes=[ro, rdn])

        def pin():
            i, b, r = bank()
            pinned.add(i)
            return i, b, r

        if not samp:
            def gen_sel_tile(t):
                ti = gi * 4 + t
                S = SB[t % 2]
                L = 128 * (ti + 1)
                cols = slice(t * 128, (t + 1) * 128)
                if ti < 2:
                    return
                nch = (L + 511) // 512
                for h in range(8):
                    for ch in range(nch):
                        w = min(512, L - ch * 512)
                        bi, bk, br = bank()
                        kb.mm([lambda: nc.tensor.matmul(bk[:, 0:w], qiT[:, h, cols], KiT[:, ch * 512:ch * 512 + w],
                                                        start=True, stop=True)], reads=[R_qi, R_KiT], writes=[br])
                        ri = cnt["r"] % 2
                        cnt["r"] += 1
                        kb.op("act", lambda: nc.scalar.activation(out=rtmp[ri][:, 0:w], in_=bk[:, 0:w], func=AF.Relu,
                                                                  scale=wab[:, t, h:h + 1]),
                              reads=[br, R_wab], writes=[R_rtmp[ri]])
                        sc = S.score[:, ch * 512:ch * 512 + w]
                        if h == 0:
                            kb.op("dve", lambda: nc.vector.tensor_scalar(out=sc, in0=rtmp[ri][:, 0:w],
                                                                         scalar1=wab[:, t, 8 + h:9 + h], scalar2=None,
                                                                         op0=ALU.mult),
                                  reads=[R_rtmp[ri], R_wab], writes=[S.R_score])
                        else:
                            kb.op("dve", lambda: nc.vector.scalar_tensor_tensor(out=sc, in0=rtmp[ri][:, 0:w],
                                                                                scalar=wab[:, t, 8 + h:9 + h], in1=sc,
                                                                                op0=ALU.mult, op1=ALU.add),
                                  reads=[R_rtmp[ri], R_wab], writes=[S.R_score])
                        yield
                kb.op("dve", lambda: nc.vector.tensor_reduce(out=S.bis[:, 1:2], in_=S.score[:, 0:L], axis=AX.X, op=ALU.min),
                      reads=[S.R_score], writes=[S.R_bis])
                kb.op("dve", lambda: nc.vector.tensor_tensor(out=S.score[:, L - 128:L], in0=S.score[:, L - 128:L],
                                                             in1=causneg[:], op=ALU.add),
                      reads=[R_const], writes=[S.R_score])
                yield
                for _ in gen_select(S, L, ti + 1, True):
                    yield

            def gen_attn_tile(t):
                ti = gi * 4 + t
                S = SB[t % 2]
                cols = slice(t * 128, (t + 1) * 128)
                for g in range(2):
                    io, bo, ro = pin()
                    idn, bd, rdn = pin()
                    for j in range(ti + 1):
                        near = ti - j <= 1
                        tabs = (biasT[:, 2 * (ti - j)], biasT[:, 2 * (ti - j) + 1]) if near else None
                        attn_block(g, KT[:, g, j * 128:(j + 1) * 128], cols, near, tabs, None, bo, ro, bd, rdn,
                                   VA[:, j, g, :], j == 0, R_KT, R_VA,
                                   addmask=(S.maskT[:, j, :] if ti >= 2 else None), R_mk=S.R_maskT)
                        yield
                    finalize(g, bo, ro, bd, rdn, cols)
                    pinned.discard(io)
                    pinned.discard(idn)
                    yield

            def nsteps_sel(t):
                ti = gi * 4 + t
                if ti < 2:
                    return 0
                L = 128 * (ti + 1)
                return 8 * ((L + 511) // 512) + 3 + 2 * NBIS + 1 + (ti + 8) // 8

            def zipper(ga, na, gb, nb):
                da = db = 0
                alive_a = alive_b = True
                while alive_a or alive_b:
                    fa = da / max(na, 1) if alive_a else 2.0
                    fb = db / max(nb, 1) if alive_b else 2.0
                    if fa <= fb:
                        try:
                            next(ga)
                            da += 1
                        except StopIteration:
                            alive_a = False
                    else:
                        try:
                            next(gb)
                            db += 1
                        except StopIteration:
                            alive_b = False

            zipper(gen_sel_tile(0), nsteps_sel(0), gen_X2(), 16)
            def with_side(main, side, every):
                n_ = 0
                for _ in main:
                    n_ += 1
                    if n_ % every == 0:
                        try:
                            next(side)
                        except StopIteration:
                            pass
                    yield

            gga = gen_ga()
            for t in range(4):
                na = 2 * (gi * 4 + t + 2)
                ga = with_side(gen_attn_tile(t), gga, max(1, na // 4))
                if t + 1 < 4:
                    zipper(ga, na, gen_sel_tile(t + 1), nsteps_sel(t + 1))
                else:
                    for _ in ga:
                        pass
            for _ in gga:
                pass
        else:
            for _ in gen_X2():
                pass
            cols = slice(0, 128)
            ts = ar.get([128, 2, 8, 128], BF16)
            tab16 = ar.get([128, 8, 2, 64], BF16)
            ar_mark = ar.off
            tst2 = ar.get([128, 8, 128], F32)
            thf2 = ar.get([128, 8, 128], F32)
            stg = ar.get([128, 8, 64], F32)
            stgh = ar.get([128, 8, 64], F32)
            R_ts = Res()
            s_ts = kb.slot("ts", scratch=True)
            with nc.allow_non_contiguous_dma(reason="toeplitz"):
                src_ = dper[:, 128:128 + 383 * 128].rearrange("h (s x) -> s h x", x=383)[:, :, 0:128]
                kb.dma("sp", tst2[:], src_, s_ts, reads=[R_dper], writes=[R_ts])
                kb.op("dve", lambda: nc.vector.memset(stg[:], 0.0), writes=[R_ts])
                for pl in range(8):
                    off = 256 - pl
                    srcp = dper[:, off:off + 376 * 16].rearrange("h (pp x) -> pp h x", x=376)[:, :, 0:8]
                    kb.dma("sp", stg[112:128, pl, :].rearrange("p (h q) -> p h q", h=8), srcp, s_ts, reads=[R_dper], writes=[R_ts])
            kb.op("dve", lambda: nc.vector.tensor_tensor(out=tst2[:], in0=tst2[:],
                                                         in1=sampbm[:].unsqueeze(1).broadcast_to([128, 8, 128]),
                                                         op=ALU.mult), reads=[R_const], writes=[R_ts])
            kb.op("dve", lambda: nc.vector.tensor_tensor(out=tst2[:], in0=tst2[:],
                                                         in1=sampnb[:].unsqueeze(1).broadcast_to([128, 8, 128]),
                                                         op=ALU.add), reads=[R_const], writes=[R_ts])
            kb.op("dve", lambda: nc.vector.tensor_copy(out=ts[:, 0], in_=tst2[:]), writes=[R_ts])
            kb.op("dve", lambda: nc.vector.tensor_copy(out=thf2[:], in_=ts[:, 0]), writes=[R_ts])
            kb.op("dve", lambda: nc.vector.tensor_tensor(out=ts[:, 1], in0=tst2[:], in1=thf2[:], op=ALU.subtract),
                  writes=[R_ts])
            kb.op("dve", lambda: nc.vector.tensor_copy(out=tab16[:, :, 0, :], in_=stg[:]), writes=[R_ts])
            kb.op("dve", lambda: nc.vector.tensor_copy(out=stgh[:], in_=tab16[:, :, 0, :]), writes=[R_ts])
            kb.op("dve", lambda: nc.vector.tensor_tensor(out=tab16[:, :, 1, :], in0=stg[:], in1=stgh[:], op=ALU.subtract),
                  writes=[R_ts, R_bias])
            kb.barrier()
            ar.off = ar_mark
            wcol = ar.get([64, 16], F32)
            amat = ar.get([128, 64], F32)
            lq = [ar.get([64, 64], BF16) for _ in range(2)]
            kic = [ar.get([128, 2, 8, 64], BF16) for _ in range(2)]
            kitb = [ar.get([64, 2048], BF16) for _ in range(2)]
            rwt = [ar.get([64, 512], F32) for _ in range(4)]
            rwh = [ar.get([64, 512], BF16) for _ in range(4)]
            rwl = [ar.get([64, 512], BF16) for _ in range(4)]
            selb = ar.get([64, 248], BF16)
            R_rwh = [Res() for _ in range(4)]
            R_rwl = [Res() for _ in range(4)]
            kb.op("dve", lambda: nc.vector.tensor_copy(out=selb[:], in_=selbase[:]), reads=[R_const], writes=[R_const])
            R_wcol = Res()
            R_lq = [Res(), Res()]
            R_kic = [Res(), Res()]
            R_kitb = [Res(), Res()]
            R_rwt = [Res() for _ in range(4)]
            s_kic = [kb.slot(f"kic{i}", scratch=True) for i in range(2)]
            kb.op("dve", lambda: nc.vector.tensor_tensor(out=amat[:].rearrange("p (h q) -> p h q", h=8),
                                                         in0=dq[:].rearrange("p (h q) -> p h q", h=8),
                                                         in1=kist[0][:, 64:72].unsqueeze(2).broadcast_to([128, 8, 8]),
                                                         op=ALU.mult), reads=[R_kist[0], R_const], writes=[R_wcol])
            bi, bk, br = bank()
            kb.mm([lambda: nc.tensor.matmul(bk[0:64, 0:16], amat[:, :], bsel[:, :], start=True, stop=True)],
                  reads=[R_wcol, R_const], writes=[br])
            kb.op("dve", lambda: nc.vector.tensor_copy(out=wcol[:], in_=bk[0:64, 0:16]), reads=[br], writes=[R_wcol])
            sbk = [pin() for _ in range(5)]
            prepared = set()

            def prep(b):
                i = b % 2
                for hf in range(2):
                    kb.idma(kic[i][:, hf].rearrange("p a e -> p (a e)"), cache_ki, idxall[:, 2 * b + hf:2 * b + hf + 1],
                            s_kic[i], reads=[R_const], writes=[R_kic[i]])
                for hf in range(2):
                    bi, bk, br = bank()
                    bkb = bk.bitcast(BF16)
                    kb.mm([(lambda pl=pl: nc.tensor.transpose(out=bkb[0:64, pl * 128:(pl + 1) * 128], in_=kic[i][:, hf, pl, :],
                                                              identity=identb[:])) for pl in range(8)],
                          reads=[R_kic[i], R_const], writes=[br])
                    if kb.ev() == "act":
                        kb.op("act", lambda: nc.scalar.copy(out=kitb[i][:, hf * 1024:(hf + 1) * 1024], in_=bkb[0:64, :]),
                              reads=[br], writes=[R_kitb[i]])
                    else:
                        kb.op("dve", lambda: nc.vector.tensor_copy(out=kitb[i][:, hf * 1024:(hf + 1) * 1024], in_=bkb[0:64, :]),
                              reads=[br], writes=[R_kitb[i]])
                kb.op("dve", lambda: nc.vector.tensor_copy(out=lq[i][:].rearrange("p (h q) -> p h q", h=8),
                                                           in_=qiT[:, :, 8 * b:8 * b + 8]), reads=[R_qi], writes=[R_lq[i]])

            items = [(b, ch) for b in range(16) for ch in range(5)]

            def stageA(k):
                b, ch = items[k]
                i = b % 2
                if b not in prepared:
                    prepared.add(b)
                    prep(b)
                w = 512 if ch < 4 else 128
                rhs = kitb[i][:, ch * 512:(ch + 1) * 512] if ch < 4 else KiTn[:, :]
                bi, bk, br = pin()
                kb.mm([lambda: nc.tensor.matmul(bk[0:64, 0:w], lq[i][:, :], rhs, start=True, stop=True)],
                      reads=[R_lq[i], R_kitb[i], R_KiT], writes=[br])
                return bk, br, bi
            pend = {}
            LOOK = 1
            for k in range(min(LOOK, len(items))):
                pend[k] = stageA(k)
            for k in range(len(items)):
                if k + LOOK < len(items):
                    pend[k + LOOK] = stageA(k + LOOK)
                b, ch = items[k]
                w = 512 if ch < 4 else 128
                bk, br, bi_ = pend.pop(k)
                ri = k % 4
                kb.op("dve", lambda: nc.vector.tensor_scalar(out=rwt[ri][:, 0:w], in0=bk[0:64, 0:w], scalar1=0.0,
                                                             scalar2=wcol[:, b:b + 1], op0=ALU.max, op1=ALU.mult),
                      reads=[br, R_wcol], writes=[R_rwt[ri]])
                pinned.discard(bi_)
                kb.op("act", lambda: nc.scalar.copy(out=rwh[ri][:, 0:w], in_=rwt[ri][:, 0:w]), reads=[R_rwt[ri]],
                      writes=[R_rwh[ri]])
                kb.op("dve", lambda: nc.vector.tensor_tensor(out=rwl[ri][:, 0:w], in0=rwt[ri][:, 0:w], in1=rwh[ri][:, 0:w],
                                                             op=ALU.subtract), reads=[R_rwt[ri], R_rwh[ri]], writes=[R_rwl[ri]])
                sbi, sbk_, sbr = sbk[ch]
                kb.mm([lambda: nc.tensor.matmul(sbk_[:, 0:w], selb[:, 120 - 8 * b:248 - 8 * b], rwh[ri][:, 0:w],
                                                start=(b == 0), stop=False, skip_group_check=True),
                       lambda: nc.tensor.matmul(sbk_[:, 0:w], selb[:, 120 - 8 * b:248 - 8 * b], rwl[ri][:, 0:w],
                                                start=False, stop=(b == 15), skip_group_check=True)],
                      reads=[R_rwh[ri], R_rwl[ri], R_const], writes=[sbr])
            for ch in range(5):
                w = 512 if ch < 4 else 128
                sbi, sbk_, sbr = sbk[ch]
                kb.op("act", lambda: nc.scalar.copy(out=score[:, ch * 512:ch * 512 + w], in_=sbk_[:, 0:w]),
                      reads=[sbr], writes=[R_score])
                pinned.discard(sbi)
            L = 2176
            kb.op("dve", lambda: nc.vector.tensor_reduce(out=bis[:, 1:2], in_=score[:, 0:L], axis=AX.X, op=ALU.min),
                  reads=[R_score], writes=[R_bis])
            kb.op("dve", lambda: nc.vector.tensor_tensor(out=score[:, 2048:2176], in0=score[:, 2048:2176],
                                                         in1=sampneg[:], op=ALU.add), reads=[R_const], writes=[R_score])
            select_mask(L, 17)
            kb.barrier()
            ar.off = ar_mark
            acc = [(pin(), pin()) for _ in range(2)]
            for g in range(2):
                (io, bo, ro), (idn, bd, rdn) = acc[g]
                attn_block(g, KTn[:, g, :], cols, True, (ts[:, 0], ts[:, 1]), maskT[:, 16, :], bo, ro, bd, rdn,
                           VAn[:, g, :], True, R_KT, R_VA)
            kcb = [ar.get([128, 2, 8, 256], BF16) for _ in range(2)]
            vb = [ar.get([128, 2, 8, 256], BF16) for _ in range(2)]
            ktb = ar.get([128, 2, 2048], BF16)
            es = ar.get([128, 16, 64], BF16)
            ps_ = ar.get([128, 16, 64], BF16)
            R_kc = [Res(), Res()]
            R_vb = [Res(), Res()]
            R_ktb, R_es, R_ps = Res(), Res(), Res()
            s_kc = [kb.slot(f"kc{i}", scratch=True) for i in range(2)]
            s_vc = [kb.slot(f"vc{i}", scratch=True) for i in range(2)]
            il0, bl0, rl0 = pin()
            il1, bl1, rl1 = pin()
            lgb = [(bl0, rl0), (bl1, rl1)]

            def gather(b):
                i = b % 2
                for hf in range(2):
                    ix = idxall[:, 2 * b + hf:2 * b + hf + 1]
                    kb.idma(kcb[i][:, hf].rearrange("p a e -> p (a e)"), cache_k, ix, s_kc[i], reads=[R_const], writes=[R_kc[i]])
                    kb.idma(vb[i][:, hf].rearrange("p a e -> p (a e)"), cache_v, ix, s_vc[i], reads=[R_const], writes=[R_vb[i]])
            gather(0)
            for b in range(16):
                i = b % 2
                if b + 1 < 16:
                    gather(b + 1)
                for hf in range(2):
                    for g in range(2):
                        bi, bk, br = bank()
                        bkb = bk.bitcast(BF16)
                        kb.mm([(lambda pl=pl: nc.tensor.transpose(out=bkb[:, pl * 128:(pl + 1) * 128],
                                                                  in_=kcb[i][:, hf, pl, g * 128:(g + 1) * 128],
                                                                  identity=identb[:])) for pl in range(8)],
                              reads=[R_kc[i], R_const], writes=[br])
                        if kb.ev() == "act":
                            kb.op("act", lambda: nc.scalar.copy(out=ktb[:, g, hf * 1024:(hf + 1) * 1024], in_=bkb[:, :]),
                                  reads=[br], writes=[R_ktb])
                        else:
                            kb.op("dve", lambda: nc.vector.tensor_copy(out=ktb[:, g, hf * 1024:(hf + 1) * 1024], in_=bkb[:, :]),
                                  reads=[br], writes=[R_ktb])
                for hb in range(2):
                    bl, rl = lgb[hb]
                    blv = bl[:, :].rearrange("p (j c) -> p j c", j=8)
                    fns = []
                    for jj in range(8):
                        j = hb * 8 + jj
                        for g in range(2):
                            fns.append(lambda j=j, jj=jj, g=g, blv=blv: nc.tensor.matmul(
                                blv[:, jj, g * 32:(g + 1) * 32], ktb[:, g, j * 128:(j + 1) * 128],
                                qT[:, 4 * g:4 * g + 4, 8 * b:8 * b + 8], start=True, stop=(hb == 0), skip_group_check=True))
                            if hb == 1:
                                for hl in range(2):
                                    fns.append(lambda jj=jj, g=g, hl=hl, blv=blv: nc.tensor.matmul(
                                        blv[:, jj, g * 32:(g + 1) * 32], identb[:],
                                        tab16[:, jj, hl, :].rearrange("p (h q) -> p h q", h=8)[:, 4 * g:4 * g + 4, :],
                                        start=False, stop=(hl == 1), skip_group_check=True))
                    kb.mm(fns, reads=[R_ktb, R_q, R_bias, R_ts, R_const], writes=[rl])
                    kb.op("act", lambda: nc.scalar.activation(out=es[:, hb * 8:(hb + 1) * 8, :], in_=blv, func=AF.Exp,
                                                              scale=SCALE), reads=[rl], writes=[R_es])
                kb.op("dve", lambda: nc.vector.tensor_tensor(
                    out=ps_[:].rearrange("p j (h q) -> p j h q", h=8), in0=es[:].rearrange("p j (h q) -> p j h q", h=8),
                    in1=maskT[:, 0:16, 8 * b:8 * b + 8].unsqueeze(2).broadcast_to([128, 16, 8, 8]), op=ALU.mult),
                    reads=[R_es, R_maskT], writes=[R_ps])
                for g in range(2):
                    (io, bo, ro), (idn, bd, rdn) = acc[g]
                    bov = bo[:, :].rearrange("p (h q) -> p h q", h=4)[:, :, 8 * b:8 * b + 8]
                    bdv = bd[:, :].rearrange("p (h q) -> p h q", h=4)[:, :, 8 * b:8 * b + 8]
                    fns = []
                    for j in range(16):
                        fns.append(lambda j=j, g=g, bov=bov: nc.tensor.matmul(
                            bov, vb[i][:, j // 8, j % 8, g * 128:(g + 1) * 128], ps_[:, j, g * 32:(g + 1) * 32],
                            start=False, stop=False, skip_group_check=True))
                        fns.append(lambda j=j, g=g, bdv=bdv: nc.tensor.matmul(bdv, onesb[:], ps_[:, j, g * 32:(g + 1) * 32],
                                                                            start=False, stop=False, skip_group_check=True))
                    kb.mm(fns, reads=[R_ps, R_vb[i], R_const], writes=[ro, rdn])
            pinned.discard(il0)
            pinned.discard(il1)
            for g in range(2):
                (io, bo, ro), (idn, bd, rdn) = acc[g]
                finalize(g, bo, ro, bd, rdn, cols)
                pinned.discard(io)
                pinned.discard(idn)

        if samp:
            for _ in gen_ga():
                pass
        for cc in range(16):
            kb.op("act", lambda: nc.scalar.activation(out=m_T[:, cc, :], in_=m_T[:, cc, :], func=AF.Sigmoid),
                  reads=[R_m], writes=[R_m])
        jobs = []
        for b4 in range(4):
            def ua_job(wv, wr, b4=b4):
                def ev(c, ps, br):
                    cc = b4 * 4 + c
                    kb.op("dve", lambda: nc.vector.tensor_tensor(out=m_T[:, cc, :], in0=ps, in1=m_T[:, cc, :], op=ALU.mult),
                          reads=[br], writes=[R_m])
                formF(wv, wr, 4, 128, ev, kcn=8, src=gT, R_src=R_g)
            jobs.append(((f"ua{b4}", [(0, w_ua_v[:, :, b4 * 512:(b4 + 1) * 512])], 8, 512), ua_job))
        sgc = {"n": 0}
        run_jobs(jobs, ("gg0", [], 16, 512))
        kb.barrier()

        ar.reset()
        NB_ = 16 if samp else 1
        TW = 38 if samp else NT + 30
        uT = ar.get([128, 8, NB_ * TW], BF16)
        zcT = ar.get([128, 8, NT], BF16)
        dwf = ar.get([128, 8, NT], F32)
        aT = ar.get([128, 8, NT], BF16)
        cT = ar.get([128, 8, NT], BF16)
        diag = [ar.get([128, 31, 128], BF16) for _ in range(2)]
        sqt = [ar.get([128, NT], F32) for _ in range(2)]
        lnm = ar.get([128, NT], F32)
        lnv = ar.get([128, NT], F32)
        lnr = ar.get([128, NT], F32)
        t1 = [ar.get([128, NT], F32) for _ in range(2)]
        sgt2 = [ar.get([128, NT], F32) for _ in range(2)]
        tt2 = [ar.get([128, NT], F32) for _ in range(2)]
        R_u, R_zc, R_dw, R_a, R_c, R_ln = Res(), Res(), Res(), Res(), Res(), Res()
        R_diag = [Res(), Res()]
        R_sq = [Res(), Res()]
        R_t1 = [Res(), Res()]
        R_sg2 = [Res(), Res()]
        R_tt2 = [Res(), Res()]
        need_tok = samp or gi == 3
        if need_tok:
            sgtok = ar.get([128, 1024], F32)
            utok = ar.get([128, 1024], F32)
            R_sgtok, R_utok = Res(), Res()
            s_ut = kb.slot(f"ut", scratch=True)
        if samp:
            uv = uT[:].rearrange("p c (b w) -> p c b w", b=16)

            def ucols(c):
                return uv[:, c, :, 30:38]
            stt = ar.get([120, 1024], F32)
            R_stt = Res()
            s_stt = kb.slot("stt", scratch=True)
            for q4 in range(4):
                kb.dma("sp", stt[:], state[q4 * 4:(q4 + 1) * 4].rearrange("b r c -> (b r) c"), s_stt, writes=[R_stt])
                for c2 in range(2):
                    bi, bk, br = bank()
                    kb.mm([(lambda k=k: nc.tensor.transpose(out=bk[:, k * 120:(k + 1) * 120],
                                                            in_=stt[:, (c2 * 4 + k) * 128:(c2 * 4 + k + 1) * 128],
                                                            identity=identf[0:120, 0:120])) for k in range(4)],
                          reads=[R_stt, R_const], writes=[br])
                    kb.op("dve", lambda: nc.vector.tensor_copy(
                        out=uv[:, c2 * 4:(c2 + 1) * 4, q4 * 4:(q4 + 1) * 4, 0:30],
                        in_=bk[:, 0:480].rearrange("p (k b r) -> p k b r", k=4, b=4)), reads=[br], writes=[R_u])
            kb.dma("sp", convs[:, 0:22, :], state[:, 8:30, :], s_stt, is_out=True)
        else:
            def ucols(c):
                return uT[:, c, 30:30 + NT]
            kb.op("dve", lambda: nc.vector.tensor_copy(out=uT[:, :, 0:30], in_=uhalo[:]), reads=[R_const, R_uh], writes=[R_u])

        jobs = []
        for b2 in range(2):
            def gg_job(wv, wr, b2=b2):
                formF(wv, wr, 4, 128, copy_evac(lambda c: ucols(b2 * 4 + c), R_u, AF.Sigmoid))
                if need_tok:
                    t = ntile - 1
                    bi, bk, br = bank()
                    kb.mm([(lambda kc=kc: nc.tensor.matmul(bk[:, 0:512], xn[:, kc, t * 128:(t + 1) * 128], wv[:, kc, :],
                                                           start=(kc == 0), stop=(kc == 15))) for kc in range(16)],
                          reads=[wr, R_xn], writes=[br])
                    kb.op("act", lambda: nc.scalar.activation(out=sgtok[:, b2 * 512:(b2 + 1) * 512], in_=bk[:, 0:512],
                                                              func=AF.Sigmoid), reads=[br], writes=[R_sgtok])
            jobs.append(((f"gg{b2}", [(0, w_in_v[:, :, C_GG + b2 * 512: C_GG + (b2 + 1) * 512])], 16, 512), gg_job))
        for b2 in range(2):
            def gv_job(wv, wr, b2=b2):
                def ev(c, ps, br):
                    kb.op("dve", lambda: nc.vector.tensor_tensor(out=ucols(b2 * 4 + c), in0=ps if not samp else
                                                                 ps.rearrange("p (b t) -> p b t", b=16),
                                                                 in1=ucols(b2 * 4 + c), op=ALU.mult), reads=[br], writes=[R_u])
                formF(wv, wr, 4, 128, ev)
                if need_tok:
                    t = ntile - 1
                    bi, bk, br = bank()
                    kb.mm([(lambda kc=kc: nc.tensor.matmul(bk[:, 0:512], xn[:, kc, t * 128:(t + 1) * 128], wv[:, kc, :],
                                                           start=(kc == 0), stop=(kc == 15))) for kc in range(16)],
                          reads=[wr, R_xn], writes=[br])
                    kb.op("dve", lambda: nc.vector.tensor_tensor(out=utok[:, b2 * 512:(b2 + 1) * 512], in0=bk[:, 0:512],
                                                                 in1=sgtok[:, b2 * 512:(b2 + 1) * 512], op=ALU.mult),
                          reads=[br, R_sgtok], writes=[R_utok])
            jobs.append(((f"gv{b2}", [(0, w_in_v[:, :, C_GV + b2 * 512: C_GV + (b2 + 1) * 512])], 16, 512), gv_job))
        def conv_job():
            if need_tok:
                if samp:
                    for b in range(16):
                        kb.dma("sp", convs[b, 22:30, :], utok[8 * b:8 * b + 8, :], s_ut, reads=[R_utok], is_out=True)
                else:
                    kb.dma("sp", convp[:, :], utok[98:128, :], s_ut, reads=[R_utok], is_out=True)
            if not samp:
                kb.op("dve", lambda: nc.vector.tensor_copy(out=uhalo[:], in_=uT[:, :, NT:NT + 30]), reads=[R_u], writes=[R_uh])
            for c in range(8):
                di = c % 2
                if gi == 0:
                    kb.op("dve", lambda: nc.vector.tensor_tensor(out=diag[di][:],
                                                                 in0=identf[:].unsqueeze(1).broadcast_to([128, 31, 128]),
                                                                 in1=cwT[:, c, :].unsqueeze(2).broadcast_to([128, 31, 128]),
                                                                 op=ALU.mult), reads=[R_const], writes=[R_diag[di]])
                    kb.dma("sp", dsc[c], diag[di][:].rearrange("p a b -> p (a b)"), s_dg[di], reads=[R_diag[di]],
                           writes=[R_dsc[c]])
                else:
                    kb.dma("sp", diag[di][:].rearrange("p a b -> p (a b)"), dsc[c], s_dg[di], reads=[R_dsc[c]],
                           writes=[R_diag[di]])
                bi, bk, br = bank()
                if samp:
                    rhsf = lambda j: uv[:, c, :, j:j + 8]
                else:
                    rhsf = lambda j: uT[:, c, j:j + NT]
                kb.mm([(lambda j=j: nc.tensor.matmul(bk[:, 0:NT], diag[di][:, j, :], rhsf(j), start=(j == 0), stop=(j == 30)))
                       for j in range(31)], reads=[R_diag[di], R_u], writes=[br])
                kb.op("act", lambda: nc.scalar.activation(out=dwf[:, c, :], in_=bk[:, 0:NT], func=AF.Identity,
                                                          bias=cbT[:, c:c + 1], scale=1.0), reads=[br, R_const], writes=[R_dw])
            im, bm, rm = pin()
            iq, bq, rq = pin()
            for c in range(8):
                si = c % 2
                kb.op("act", lambda: nc.scalar.activation(out=sqt[si][:], in_=dwf[:, c, :], func=AF.Square),
                      reads=[R_dw], writes=[R_sq[si]])
                kb.mm([lambda: nc.tensor.matmul(bm[:, 0:NT], onesf[:], dwf[:, c, :], start=(c == 0), stop=(c == 7),
                                                skip_group_check=True)], reads=[R_dw, R_const], writes=[rm])
                kb.mm([lambda: nc.tensor.matmul(bq[:, 0:NT], onesf[:], sqt[si][:], start=(c == 0), stop=(c == 7),
                                                skip_group_check=True)], reads=[R_sq[si], R_const], writes=[rq])
            kb.op("dve", lambda: nc.vector.tensor_copy(out=lnm[:], in_=bm[:, 0:NT]), reads=[rm], writes=[R_ln])
            kb.op("dve", lambda: nc.vector.scalar_tensor_tensor(out=lnv[:], in0=lnm[:], scalar=-1.0, in1=lnm[:],
                                                                op0=ALU.mult, op1=ALU.mult), reads=[R_ln], writes=[R_ln])
            kb.op("dve", lambda: nc.vector.tensor_tensor(out=lnv[:], in0=bq[:, 0:NT], in1=lnv[:], op=ALU.add),
                  reads=[rq, R_ln], writes=[R_ln])
            kb.op("act", lambda: nc.scalar.activation(out=lnv[:], in_=lnv[:], func=AF.Sqrt, bias=epsT[:, 0:1], scale=1.0),
                  reads=[R_ln, R_const], writes=[R_ln])
            kb.op("dve", lambda: nc.vector.reciprocal(out=lnr[:], in_=lnv[:]), reads=[R_ln], writes=[R_ln])
            pinned.discard(im)
            pinned.discard(iq)
            for c in range(8):
                i = c % 2
                kb.op("dve", lambda: nc.vector.tensor_tensor(out=t1[i][:], in0=dwf[:, c, :], in1=lnm[:], op=ALU.subtract),
                      reads=[R_dw, R_ln], writes=[R_t1[i]])
                kb.op("dve", lambda: nc.vector.tensor_tensor(out=t1[i][:], in0=t1[i][:], in1=lnr[:], op=ALU.mult),
                      reads=[R_ln], writes=[R_t1[i]])
                kb.op("act", lambda: nc.scalar.activation(out=aT[:, c, :], in_=t1[i][:], func=AF.Silu,
                                                          bias=cnbT[:, c:c + 1], scale=cngT[:, c:c + 1]),
                      reads=[R_t1[i], R_const], writes=[R_a])
        jobs.append((None, conv_job))
        for b2 in range(2):
            jobs.append(((f"zc{b2}", [(0, w_in_v[:, :, C_ZC + b2 * 512: C_ZC + (b2 + 1) * 512])], 16, 512),
                         lambda wv, wr, b2=b2: formF(wv, wr, 4, 128, copy_evac(lambda c: zcT[:, b2 * 4 + c, :], R_zc, AF.Silu))))

        for b2 in range(2):
            def pw_job(wv, wr, b2=b2):
                def ev(c, ps, br):
                    cc = b2 * 4 + c
                    kb.op("dve", lambda: nc.vector.scalar_tensor_tensor(out=cT[:, cc, :], in0=ps, scalar=bpwT[:, cc:cc + 1],
                                                                        in1=zcT[:, cc, :], op0=ALU.add, op1=ALU.mult),
                          reads=[br, R_zc, R_const], writes=[R_c])
                formF(wv, wr, 4, 128, ev, kcn=8, src=aT, R_src=R_a)
            jobs.append(((f"pw{b2}", [(0, w_pw_v[:, :, b2 * 512:(b2 + 1) * 512])], 8, 512), pw_job))
        bcb = {}
        for b4 in range(4):
            def uc_job(wv, wr, b4=b4):
                for c in range(4):
                    bi, bk, br = pin()
                    kb.mm([(lambda kc=kc: nc.tensor.matmul(bk[:, 0:NT], wv[:, kc, c * 128:(c + 1) * 128], cT[:, kc, :],
                                                           start=(kc == 0), stop=(kc == 7))) for kc in range(8)],
                          reads=[wr, R_c], writes=[br])
                    bcb[(b4, c)] = (bi, bk, br)
            jobs.append(((f"uc{b4}", [(0, w_uc_v[:, :, b4 * 512:(b4 + 1) * 512])], 8, 512), uc_job))

            def gc_job(wv, wr, b4=b4):
                def ev(c, ps, br):
                    i = sgc["n"] % 2
                    sgc["n"] += 1
                    bi2, bk2, br2 = bcb.pop((b4, c))
                    kb.op("act", lambda: nc.scalar.activation(out=sgt2[i][:], in_=ps, func=AF.Sigmoid), reads=[br], writes=[R_sg2[i]])
                    kb.op("dve", lambda: nc.vector.tensor_tensor(out=tt2[i][:], in0=bk2[:, 0:NT], in1=sgt2[i][:], op=ALU.mult),
                          reads=[br2, R_sg2[i]], writes=[R_tt2[i]])
                    pinned.discard(bi2)
                    kb.op("dve", lambda: nc.vector.tensor_tensor(out=m_T[:, b4 * 4 + c, :], in0=m_T[:, b4 * 4 + c, :],
                                                                 in1=tt2[i][:], op=ALU.add), reads=[R_tt2[i]], writes=[R_m])
                formF(wv, wr, 4, 128, ev)
            jobs.append(((f"gc{b4}", [(0, w_in_v[:, :, C_GC + b4 * 512: C_GC + (b4 + 1) * 512])], 16, 512), gc_job))
        run_jobs(jobs, ("wo0", [], 16, 512))
        kb.barrier()

        ar.reset()
        yb = [ar.get([128, D], F32) for _ in range(ntile)]
        gbc = ar.get([128, D], F32)
        jk = ar.get([128, D], F32)
        st2 = ar.get([128, 16], F32)
        R_yb = [Res() for _ in range(ntile)]
        R_gbc, R_jk = Res(), Res()
        R_st2 = [Res() for _ in range(ntile)]
        s_y = [kb.slot(f"y_{i}", scratch=True) for i in range(ntile)]
        s_g = kb.slot(f"g", scratch=True)
        for t in range(ntile):
            kb.dma("sp", yb[t][:], x[tok0 + t * 128: tok0 + (t + 1) * 128, :], s_y[t], writes=[R_yb[t]])
        kb.dma("sp", gbc[:], final_g.broadcast_to([128, D]), s_g, writes=[R_gbc])

        def final_norm(t):
            c0 = 4 * t
            kb.op("act", lambda: nc.scalar.activation(out=jk[:], in_=yb[t][:], func=AF.Square, accum_out=st2[:, c0:c0 + 1]),
                  reads=[R_yb[t]], writes=[R_jk, R_st2[t]])
            kb.op("act", lambda: nc.scalar.activation(out=st2[:, c0 + 1:c0 + 2], in_=st2[:, c0:c0 + 1], func=AF.Sqrt,
                                                      bias=epsT[:, 0:1], scale=1.0 / D), reads=[R_st2[t], R_const],
                  writes=[R_st2[t]])
            kb.op("dve", lambda: nc.vector.reciprocal(out=st2[:, c0 + 2:c0 + 3], in_=st2[:, c0 + 1:c0 + 2]), reads=[R_st2[t]],
                  writes=[R_st2[t]])
            kb.op("dve", lambda: nc.vector.scalar_tensor_tensor(out=yb[t][:], in0=yb[t][:], scalar=st2[:, c0 + 2:c0 + 3],
                                                                in1=gbc[:], op0=ALU.mult, op1=ALU.mult),
                  reads=[R_st2[t], R_gbc], writes=[R_yb[t]])
            kb.dma("sp", y[tok0 + t * 128: tok0 + (t + 1) * 128, :], yb[t][:], s_y[t], reads=[R_yb[t]], is_out=True)

        jobs = []
        for b4 in range(4):
            def wo_job(wv, wr, b4=b4):
                for t in range(ntile):
                    bi, bk, br = bank()
                    kb.mm([(lambda kc=kc: nc.tensor.matmul(bk[:, 0:512], m_T[:, kc, t * 128:(t + 1) * 128], wv[:, kc, :],
                                                           start=(kc == 0), stop=(kc == 15))) for kc in range(16)],
                          reads=[wr, R_m], writes=[br])
                    ysl = yb[t][:, b4 * 512:(b4 + 1) * 512]
                    kb.op("dve", lambda: nc.vector.tensor_tensor(out=ysl, in0=bk[:, 0:512], in1=ysl, op=ALU.add),
                          reads=[br], writes=[R_yb[t]])
                    if b4 == 3:
                        final_norm(t)
            jobs.append(((f"wo{b4}", [(0, w_out_v[:, :, b4 * 512:(b4 + 1) * 512])], 16, 512), wo_job))
        if gi < 4:
            pbn = p1_alloc(gi + 1)
            run_jobs(jobs, ("kv", [], 16, 512), hook=(1, lambda: p1_norm(gi + 1, pbn)))
            p1_tr(gi + 1, pbn)
        else:
            run_jobs(jobs, None)
        kb.barrier()

    for gi in range(5):
        group(gi)
    last = {}
    for sem, val in kb.out_toks:
        last[id(sem)] = (sem, max(val, last.get(id(sem), (None, 0))[1]))
    for tok in last.values():
        kb.eng["sp"].wait(tok)
    return nc


_NC = None


def kernel(x_prompt, x_sample, cache_k, cache_v, cache_kidx, state_conv, page_table, ln_g, w_in, conv_w, conv_b,
           conv_norm_g, conv_norm_b, w_pw, b_pw, w_up_attn, w_up_conv, w_out, rel_bias, final_g):
    global _NC
    if _NC is None:
        _NC = build()
    nc = _NC
    f = lambda a: np.ascontiguousarray(np.asarray(a, dtype=np.float32))
    consts = host_consts()
    ck = f(cache_k)[0].reshape(NPHYS * 16, 2048)
    cv = f(cache_v)[0].reshape(NPHYS * 16, 2048)
    cki = f(cache_kidx)[0].reshape(NPHYS * 16, 512)
    shared = {
        "cache_k": ck, "cache_v": cv, "cache_ki": cki, "ln_g": f(ln_g)[0], "w_in": f(w_in)[0], "conv_w": f(conv_w)[0],
        "conv_b": f(conv_b)[0], "cn_g": f(conv_norm_g)[0], "cn_b": f(conv_norm_b)[0], "w_pw": f(w_pw)[0],
        "b_pw": f(b_pw)[0], "w_ua": f(w_up_attn)[0], "w_uc": f(w_up_conv)[0], "w_out": f(w_out)[0],
        "rel_bias": f(rel_bias), "final_g": f(final_g).reshape(1, D),
    }
    for k, v in consts.items():
        shared["c_" + k] = v
    xp = f(x_prompt)
    xs = f(x_sample)
    st = f(state_conv)[0]
    pt = np.ascontiguousarray(np.asarray(page_table, dtype=np.int32))
    in_maps = []
    for c in range(8):
        m = dict(shared)
        m["x"] = np.ascontiguousarray(np.concatenate([xp[c], xs[16 * c:16 * c + 16].reshape(128, D)], axis=0))
        m["state"] = np.ascontiguousarray(st[16 * c:16 * c + 16])
        m["ptab"] = np.ascontiguousarray(pt[16 * c:16 * c + 16].reshape(1, 256))
        in_maps.append(m)
    res = run_bass_kernel_spmd(nc, in_maps, core_ids=list(range(8)))
    R = res.results
    y_p = np.stack([R[c]["y"][0:2048] for c in range(8)])
    y_s = np.concatenate([R[c]["y"][2048:2176].reshape(16, 8, D) for c in range(8)], axis=0)
    k_p = np.stack([R[c]["kk"][0:2048].reshape(2048, 2, 128) for c in range(8)])[None]
    v_p = np.stack([R[c]["vv"][0:2048].reshape(2048, 2, 128) for c in range(8)])[None]
    ki_p = np.stack([R[c]["kio"][0:2048] for c in range(8)])[None]
    cp = np.stack([R[c]["convp"] for c in range(8)])[None]
    k_s = np.concatenate([R[c]["kk"][2048:2176].reshape(16, 8, 2, 128) for c in range(8)], axis=0)[None]
    v_s = np.concatenate([R[c]["vv"][2048:2176].reshape(16, 8, 2, 128) for c in range(8)], axis=0)[None]
    ki_s = np.concatenate([R[c]["kio"][2048:2176].reshape(16, 8, 64) for c in range(8)], axis=0)[None]
    cs = np.concatenate([R[c]["convs"] for c in range(8)], axis=0)[None]
    return (y_p.astype(np.float32), y_s.astype(np.float32), k_p.astype(np.float32), v_p.astype(np.float32),
            ki_p.astype(np.float32), cp.astype(np.float32), k_s.astype(np.float32), v_s.astype(np.float32),
            ki_s.astype(np.float32), cs.astype(np.float32))
```

```python
import math
import numpy as np
import concourse.bass as bass
import concourse.mybir as mybir
from concourse.bass_utils import run_bass_kernel_spmd

F32 = mybir.dt.float32
BF16 = mybir.dt.bfloat16
I32 = mybir.dt.int32
U8 = mybir.dt.uint8
AF = mybir.ActivationFunctionType
ALU = mybir.AluOpType
AX = mybir.AxisListType
ET = mybir.EngineType

D = 2048
DIN = 10312
C_Q, C_K, C_V, C_ZA, C_QI, C_KI, C_WI, C_GV, C_GG, C_ZC, C_GA, C_GC = (
    0, 1024, 1280, 1536, 2560, 3072, 3136, 3144, 4168, 5192, 6216, 8264)
NPHYS = 2560
NEGB = -30000.0
SCALE = 128.0 ** -0.5
SQ128 = 128.0 ** 0.5
NBIS = 18
EPS = 1e-6


class Res:
    __slots__ = ("w", "r")

    def __init__(self):
        self.w = None
        self.r = {}


class Eng:
    def __init__(self, nc, e, name):
        self.e = e
        self.sem = nc.alloc_semaphore("p_" + name)
        self.n = 0
        self.seen = {}

    def wait(self, tok):
        if tok is None:
            return
        sem, val = tok
        k = id(sem)
        if self.seen.get(k, 0) >= val:
            return
        self.e.wait_ge(sem, val)
        self.seen[k] = val


class Slot:
    def __init__(self, nc, name, scratch=False):
        self.sem = nc.alloc_semaphore("d_" + name)
        self.n = 0
        self.scratch = scratch


class KB:
    def __init__(self, nc):
        self.nc = nc
        self.eng = {
            "pe": Eng(nc, nc.tensor, "pe"), "act": Eng(nc, nc.scalar, "act"),
            "dve": Eng(nc, nc.vector, "dve"), "pool": Eng(nc, nc.gpsimd, "pool"),
            "sp": Eng(nc, nc.sync, "sp"),
        }
        self.slots = []
        self.slotd = {}
        self.out_toks = []
        self.flip = 0

    def slot(self, name, scratch=False):
        if name in self.slotd:
            return self.slotd[name]
        s = Slot(self.nc, name, scratch)
        self.slots.append(s)
        self.slotd[name] = s
        return s

    def _deps(self, E, reads, writes):
        for r in reads:
            if r.w:
                E.wait(r.w)
        for w in writes:
            if w.w:
                E.wait(w.w)
            for t in w.r.values():
                E.wait(t)

    def _mark(self, tok, reads, writes):
        for r in reads:
            r.r[id(tok[0])] = tok
        for w in writes:
            w.w = tok
            w.r = {}

    def op(self, en, fn, reads=(), writes=()):
        E = self.eng[en]
        self._deps(E, reads, writes)
        ins = fn()
        E.n += 1
        ins.then_inc(E.sem, 1)
        tok = (E.sem, E.n)
        self._mark(tok, reads, writes)
        return tok

    def mm(self, fns, reads=(), writes=()):
        E = self.eng["pe"]
        self._deps(E, reads, writes)
        ins = None
        for f in fns:
            ins = f()
        E.n += 1
        ins.then_inc(E.sem, 1)
        tok = (E.sem, E.n)
        self._mark(tok, reads, writes)
        return tok

    def dma(self, q, out, in_, slot, reads=(), writes=(), is_out=False, **kw):
        E = self.eng[q]
        self._deps(E, reads, writes)
        ins = E.e.dma_start(out=out, in_=in_, **kw)
        slot.n += 16
        ins.then_inc(slot.sem, 16)
        tok = (slot.sem, slot.n)
        self._mark(tok, reads, writes)
        if is_out:
            self.out_toks.append(tok)
        return tok

    def idma(self, out, in_, idx_ap, slot, reads=(), writes=()):
        E = self.eng["pool"]
        self._deps(E, reads, writes)
        ins = self.nc.gpsimd.indirect_dma_start(out=out, out_offset=None, in_=in_,
                                                in_offset=bass.IndirectOffsetOnAxis(ap=idx_ap, axis=0))
        slot.n += 16
        ins.then_inc(slot.sem, 16)
        tok = (slot.sem, slot.n)
        self._mark(tok, reads, writes)
        return tok

    def barrier(self):
        toks = [(E.sem, E.n) for k, E in self.eng.items() if E.n > 0 and k != "sp"]
        toks += [(s.sem, s.n) for s in self.slots if s.scratch and s.n > 0]
        for E in self.eng.values():
            for t in toks:
                E.wait(t)

    def ev(self):
        self.flip ^= 1
        return "act" if self.flip else "dve"


def t5_bucket_np(n):
    n = np.maximum(n, 0)
    max_exact = 16
    ratio = np.log(np.maximum(n, 1).astype(np.float32) / np.float32(max_exact)) / np.float32(math.log(128 / max_exact))
    large = np.minimum(max_exact + (ratio.astype(np.float32) * np.float32(16)).astype(np.int32), 31)
    return np.where(n < max_exact, n, large)


def host_consts():
    c = {}
    c["identf"] = np.eye(128, dtype=np.float32)
    m = np.arange(384)
    n = m - 128
    oh = np.zeros((33, 384), np.float32)
    bk = t5_bucket_np(n)
    for i in range(384):
        if n[i] >= 0:
            oh[bk[i], i] = 1.0
        else:
            oh[32, i] = 1.0
    c["oh33"] = oh
    q = np.arange(128)[:, None]
    s = np.arange(128)[None, :]
    c["causneg"] = np.where(s <= q, 0.0, -1e30).astype(np.float32)
    same = (q // 8) == (s // 8)
    c["sampneg"] = np.where(same & ((s % 8) <= (q % 8)), 0.0, -1e30).astype(np.float32)
    c["sampbm"] = same.astype(np.float32)
    c["sampnb"] = np.where(same, 0.0, NEGB * SQ128).astype(np.float32)
    tok = np.arange(128)
    dq = np.zeros((128, 8, 8), np.float32)
    for t in range(128):
        dq[t, :, t % 8] = 1.0
    c["dq"] = dq.reshape(128, 64)
    bs = np.zeros((128, 16), np.float32)
    bs[tok, tok // 8] = 1.0
    c["bsel"] = bs
    sb = np.zeros((64, 248), np.float32)
    for hq in range(64):
        sb[hq, 120 + hq % 8] = 1.0
    c["selbase"] = sb
    c["oh8T"] = (np.arange(128)[None, :] // 16 == np.arange(8)[:, None]).astype(np.float32)
    c["fvec"] = np.tile((2.0 ** -(np.arange(32, dtype=np.float64) + 1)).astype(np.float32)[None, :], (128, 1))
    c["pidmat"] = np.tile((np.arange(128, dtype=np.float32) % 16)[:, None], (1, 32))
    return c


CONST_SHAPES = {"identf": [128, 128], "oh33": [33, 384], "causneg": [128, 128], "sampneg": [128, 128],
                "sampbm": [128, 128], "sampnb": [128, 128], "dq": [128, 64], "bsel": [128, 16],
                "selbase": [64, 248], "pidmat": [128, 32], "fvec": [128, 32], "oh8T": [8, 128]}


def build():
    nc = bass.Bass("TRN2", target_bir_lowering=False)
    kb = KB(nc)

    def din(name, shape, dt=F32):
        return nc.dram_tensor(name, shape, dt, kind="ExternalInput").ap()

    def dout(name, shape):
        return nc.dram_tensor(name, shape, F32, kind="ExternalOutput").ap()

    x = din("x", [2176, D])
    cache_k = din("cache_k", [NPHYS * 16, 2048])
    cache_v = din("cache_v", [NPHYS * 16, 2048])
    cache_ki = din("cache_ki", [NPHYS * 16, 512])
    state = din("state", [16, 30, 1024])
    ptab = din("ptab", [1, 256], I32)
    ln_g = din("ln_g", [D])
    w_in = din("w_in", [D, DIN])
    conv_w = din("conv_w", [31, 1024])
    conv_b = din("conv_b", [1024])
    cn_g = din("cn_g", [1024])
    cn_b = din("cn_b", [1024])
    w_pw = din("w_pw", [1024, 1024])
    b_pw = din("b_pw", [1024])
    w_ua = din("w_ua", [1024, D])
    w_uc = din("w_uc", [1024, D])
    w_out = din("w_out", [D, D])
    rel_bias = din("rel_bias", [32, 8])
    final_g = din("final_g", [1, D])
    cin = {k: din("c_" + k, v) for k, v in CONST_SHAPES.items()}

    y = dout("y", [2176, D])
    kk = dout("kk", [2176, 256])
    vv = dout("vv", [2176, 256])
    kio = dout("kio", [2176, 64])
    convp = dout("convp", [30, 1024])
    convs = dout("convs", [16, 30, 1024])
    dper = nc.dram_tensor("dper", [8, 129 * 384], F32, kind="Internal").ap()

    w_in_v = w_in.rearrange("(kc p) f -> p kc f", p=128)
    w_out_v = w_out.rearrange("(kc p) f -> p kc f", p=128)
    w_pw_v = w_pw.rearrange("(kc p) f -> p kc f", p=128)
    w_ua_v = w_ua.rearrange("(kc p) f -> p kc f", p=128)
    w_uc_v = w_uc.rearrange("(kc p) f -> p kc f", p=128)

    wsc = {}

    def wconv(key, src, kcn, ncols, col0=0, wtot=None):
        wtot = ncols if wtot is None else wtot
        if key not in wsc:
            t = nc.dram_tensor("wsc_" + key, [128, kcn * wtot], BF16, kind="Internal").ap()
            wsc[key] = (t, Res(), kb.slot("cv_" + key))
        t, r, sl = wsc[key]
        tv = t.rearrange("p (k c) -> p k c", k=kcn)
        kb.dma("pool", tv[:, :, col0:col0 + ncols], src, sl, writes=[r])

    def conv_first():
        wconv("kv", w_in_v[:, :, C_K:C_K + 512], 16, 512)
        wconv("qi", w_in_v[:, :, C_QI:C_QI + 512], 16, 512)
        wconv("ki", w_in_v[:, :, C_KI:C_KI + 72], 16, 72)
        for b2 in range(2):
            wconv(f"q{b2}", w_in_v[:, :, C_Q + b2 * 512: C_Q + (b2 + 1) * 512], 16, 512)
        for b2 in range(2):
            wconv(f"za{b2}", w_in_v[:, :, C_ZA + b2 * 512: C_ZA + (b2 + 1) * 512], 16, 512)

    def conv_rest(after):
        kb._deps(kb.eng["pool"], after, [])
        for b4 in range(4):
            wconv(f"ua{b4}", w_ua_v[:, :, b4 * 512:(b4 + 1) * 512], 8, 512)
        for b4 in range(4):
            wconv(f"ga{b4}", w_in_v[:, :, C_GA + b4 * 512: C_GA + (b4 + 1) * 512], 16, 512)
        for b2 in range(2):
            wconv(f"gg{b2}", w_in_v[:, :, C_GG + b2 * 512: C_GG + (b2 + 1) * 512], 16, 512)
        for b2 in range(2):
            wconv(f"gv{b2}", w_in_v[:, :, C_GV + b2 * 512: C_GV + (b2 + 1) * 512], 16, 512)
        for b2 in range(2):
            wconv(f"zc{b2}", w_in_v[:, :, C_ZC + b2 * 512: C_ZC + (b2 + 1) * 512], 16, 512)
        for b2 in range(2):
            wconv(f"pw{b2}", w_pw_v[:, :, b2 * 512:(b2 + 1) * 512], 8, 512)
        for b4 in range(4):
            wconv(f"uc{b4}", w_uc_v[:, :, b4 * 512:(b4 + 1) * 512], 8, 512)
            wconv(f"gc{b4}", w_in_v[:, :, C_GC + b4 * 512: C_GC + (b4 + 1) * 512], 16, 512)
        for b4 in range(4):
            wconv(f"wo{b4}", w_out_v[:, :, b4 * 512:(b4 + 1) * 512], 16, 512)


    def sb(name, shape, dt):
        return nc.alloc_sbuf_tensor(name, shape, dt)

    identf = sb("identf", [128, 128], F32)
    identb = sb("identb", [128, 128], BF16)
    onesb = sb("onesb", [128, 128], BF16)
    onesf = sb("onesf", [128, 128], F32)
    causneg = sb("causneg", [128, 128], F32)
    sampneg = sb("sampneg", [128, 128], F32)
    sampbm = sb("sampbm", [128, 128], F32)
    sampnb = sb("sampnb", [128, 128], F32)
    dq = sb("dq", [128, 64], F32)
    bsel = sb("bsel", [128, 16], F32)
    selbase = sb("selbase", [64, 248], F32)
    prmT = sb("prmT", [128, 48], F32)
    lngT = prmT[:, 0:16]
    cbT = prmT[:, 16:24]
    cngT = prmT[:, 24:32]
    cnbT = prmT[:, 32:40]
    bpwT = prmT[:, 40:48]
    cwT = sb("cwT", [128, 8, 31], F32)
    oh8T = sb("oh8T", [8, 128], F32)
    epsT = sb("epsT", [128, 1], F32)
    pidmat = sb("pidmat", [128, 32], F32)
    fvec = sb("fvec", [128, 32], F32)
    idxall = sb("idxall", [128, 32], I32)
    biasT = sb("biasT", [128, 4, 8, 128], BF16)
    xnT = sb("xnT", [128, 16, 512], BF16)
    mT = sb("mT", [128, 16, 512], BF16)
    wbuf = [sb(f"wbuf{i}", [128, 8192], BF16) for i in range(2)]
    KT = sb("KT", [128, 2, 2048], BF16)
    VA = sb("VA", [128, 16, 2, 128], BF16)
    KiT = sb("KiT", [64, 2048], BF16)
    uhalo = sb("uhalo", [128, 8, 30], BF16)
    SCR = 102 * 1024
    scr = sb("scr", [128, SCR], U8)
    psum_all = nc.alloc_psum_tensor("ps", [128, 4096], F32)

    class Arena:
        def __init__(self):
            self.off = 0

        def reset(self):
            self.off = 0

        def get(self, shape, dt):
            esz = 4 if dt in (F32, I32) else 2
            n = int(np.prod(shape[1:])) * esz
            n = (n + 63) // 64 * 64
            assert self.off + n <= SCR, (self.off, n)
            ap = scr[0:shape[0], self.off:self.off + n].bitcast(dt)
            self.off += n
            if len(shape) == 3:
                ap = ap[:, 0:shape[1] * shape[2]].rearrange("p (a b) -> p a b", a=shape[1])
            elif len(shape) == 4:
                ap = ap[:, 0:shape[1] * shape[2] * shape[3]].rearrange("p (a b c) -> p a b c", a=shape[1], b=shape[2])
            else:
                ap = ap[:, 0:shape[1]]
            return ap

    ar = Arena()

    banks = [psum_all[:, i * 512:(i + 1) * 512] for i in range(8)]
    bres = [Res() for _ in range(8)]
    pinned = set()
    bstate = {"i": 0}

    def bank():
        while True:
            i = bstate["i"] % 8
            bstate["i"] += 1
            if i not in pinned:
                return i, banks[i], bres[i]

    s_setup = kb.slot("setup")
    R_const = Res()
    ar.reset()
    ar.off = 56 * 1024
    prm = ar.get([48, 128], F32)
    cwst = ar.get([31, 1024], F32)
    ptJ = ar.get([8, 32], I32)
    ptJf = ar.get([8, 32], F32)
    with nc.allow_non_contiguous_dma(reason="tiny page table transpose"):
        for t_, a_ in [(identf, cin["identf"]), (causneg, cin["causneg"]), (sampneg, cin["sampneg"]),
                       (sampbm, cin["sampbm"]), (sampnb, cin["sampnb"]), (dq, cin["dq"]), (bsel, cin["bsel"]),
                       (selbase, cin["selbase"]), (pidmat, cin["pidmat"]), (fvec, cin["fvec"]), (oh8T, cin["oh8T"])]:
            kb.dma("sp", t_[:], a_, s_setup, writes=[R_const])
        kb.dma("sp", ptJ[:], ptab.rearrange("o (bh j) -> (o j) bh", j=8), s_setup, writes=[R_const])
        kb.dma("sp", prm[0:16, :], ln_g.rearrange("(kc p) -> kc p", p=128), s_setup, writes=[R_const])
        for k_, a_ in enumerate([conv_b, cn_g, cn_b, b_pw]):
            kb.dma("sp", prm[16 + 8 * k_:24 + 8 * k_, :], a_.rearrange("(c p) -> c p", p=128), s_setup, writes=[R_const])
        kb.dma("sp", cwst[:], conv_w, s_setup, writes=[R_const])
    bi, bk_, br = bank()
    kb.mm([lambda: nc.tensor.transpose(out=bk_[:, 0:48], in_=prm[0:48, :], identity=identf[0:48, 0:48])],
          reads=[R_const], writes=[br])
    kb.op("dve", lambda: nc.vector.tensor_copy(out=prmT[:], in_=bk_[:, 0:48]), reads=[br], writes=[R_const])
    bi, bk_, br = bank()
    kb.mm([(lambda c_=c_: nc.tensor.transpose(out=bk_[:, c_ * 31:(c_ + 1) * 31], in_=cwst[0:31, c_ * 128:(c_ + 1) * 128],
                                              identity=identf[0:31, 0:31])) for c_ in range(8)],
          reads=[R_const], writes=[br])
    kb.op("dve", lambda: nc.vector.tensor_copy(out=cwT[:].rearrange("p a b -> p (a b)"), in_=bk_[:, 0:248]),
          reads=[br], writes=[R_const])
    kb.op("dve", lambda: nc.vector.tensor_copy(out=ptJf[:], in_=ptJ[:]), reads=[R_const], writes=[R_const])
    bi, bk_, br = bank()
    kb.mm([lambda: nc.tensor.matmul(bk_[:, 0:32], oh8T[:, :], ptJf[:, :], start=True, stop=True)],
          reads=[R_const], writes=[br])
    kb.op("dve", lambda: nc.vector.scalar_tensor_tensor(out=idxall[:], in0=bk_[:, 0:32], scalar=16.0, in1=pidmat[:],
                                                        op0=ALU.mult, op1=ALU.add), reads=[br, R_const], writes=[R_const])
    kb.op("dve", lambda: nc.vector.tensor_copy(out=identb[:], in_=identf[:]), reads=[R_const], writes=[R_const])
    kb.op("dve", lambda: nc.vector.memset(onesb[:], 1.0), writes=[R_const])
    kb.op("dve", lambda: nc.vector.memset(onesf[:], 1.0 / 1024.0), writes=[R_const])
    kb.op("dve", lambda: nc.vector.memset(epsT[:], EPS), writes=[R_const])
    kb.op("dve", lambda: nc.vector.memset(uhalo[:], 0.0), writes=[R_const])

    LS = {}

    rb33 = ar.get([33, 8], F32)
    oh33 = ar.get([33, 384], F32)
    dx = ar.get([8, 384], F32)
    dx2 = ar.get([8, 384], F32)
    R_b0 = Res()
    s_b = kb.slot("btab", scratch=True)
    kb.op("dve", lambda: nc.vector.memset(rb33[0:33, :], NEGB), writes=[R_b0])
    kb.dma("sp", rb33[0:32, :], rel_bias, s_b, writes=[R_b0])
    kb.dma("sp", oh33[:], cin["oh33"], s_b, writes=[R_b0])
    bi, bk_, br = bank()
    kb.mm([lambda: nc.tensor.matmul(bk_[0:8, 0:384], rb33[:, :], oh33[:, :], start=True, stop=True)],
          reads=[R_b0], writes=[br])
    kb.op("dve", lambda: nc.vector.tensor_copy(out=dx[:], in_=bk_[0:8, 0:384]), reads=[br], writes=[R_b0])
    kb.op("dve", lambda: nc.vector.tensor_scalar(out=dx2[:], in0=dx[:], scalar1=dx[:, 383:384], scalar2=SQ128,
                                                 op0=ALU.subtract, op1=ALU.mult), reads=[R_b0], writes=[R_b0])
    R_dper = Res()
    kb.dma("sp", dper.rearrange("h (r m) -> h r m", m=384), dx2[:].unsqueeze(1).broadcast_to([8, 129, 384]),
           s_b, reads=[R_b0], writes=[R_dper])

    def late_setup():
        ar.off = 72 * 1024
        tst = ar.get([128, 8, 128], F32)
        thf = ar.get([128, 8, 128], F32)
        R_b = Res()
        R_bias = Res()

        def mk_table(cofs, dst_hi, dst_lo):
            src = dper[:, cofs:cofs + 383 * 128].rearrange("h (s x) -> s h x", x=383)[:, :, 0:128]
            kb.dma("sp", tst[:], src, s_b, reads=[R_dper], writes=[R_b])
            kb.op("dve", lambda: nc.vector.tensor_copy(out=dst_hi, in_=tst[:]), reads=[R_b], writes=[R_bias])
            kb.op("dve", lambda: nc.vector.tensor_copy(out=thf[:], in_=dst_hi), reads=[R_bias], writes=[R_b])
            kb.op("dve", lambda: nc.vector.tensor_tensor(out=dst_lo, in0=tst[:], in1=thf[:], op=ALU.subtract),
                  reads=[R_b], writes=[R_bias])

        with nc.allow_non_contiguous_dma(reason="toeplitz"):
            mk_table(128, biasT[:, 0], biasT[:, 1])
            mk_table(256, biasT[:, 2], biasT[:, 3])
        dsc = [nc.dram_tensor(f"dsc{c}", [128, 31 * 128], BF16, kind="Internal").ap() for c in range(8)]
        R_dsc = [Res() for _ in range(8)]
        dgt = [ar.get([128, 31, 128], BF16) for _ in range(2)]
        R_dgt = [Res(), Res()]
        s_dg = [kb.slot("dg0", scratch=True), kb.slot("dg1", scratch=True)]
        for c in range(8):
            di = c % 2
            kb.op("dve", lambda: nc.vector.tensor_tensor(out=dgt[di][:], in0=identf[:].unsqueeze(1).broadcast_to([128, 31, 128]),
                                                          in1=cwT[:, c, :].unsqueeze(2).broadcast_to([128, 31, 128]), op=ALU.mult),
                  reads=[R_const], writes=[R_dgt[di]])
            kb.dma("sp", dsc[c], dgt[di][:].rearrange("p a b -> p (a b)"), s_dg[di], reads=[R_dgt[di]], writes=[R_dsc[c]])
        LS.update(dict(R_dper=R_dper, R_bias=R_bias, dsc=dsc, R_dsc=R_dsc, s_dg=s_dg))

    R_uh = Res()
    wres = [Res(), Res()]
    wsl = [kb.slot("w0"), kb.slot("w1")]
    wst = {"n": 0}

    def wload(key, parts, kcn, ncols):
        i = wst["n"] % 2
        wst["n"] += 1
        flat = wbuf[i][:, 0:kcn * ncols]
        view = flat.rearrange("p (k c) -> p k c", k=kcn)
        t, r, sl = wsc[key]
        kb.dma("sp", flat, t, wsl[i], reads=[r], writes=[wres[i]])
        return view, wres[i]

    prefetched = {}

    def run_jobs(jobs, next_first=None, hook=None):
        loaded = {}
        order = [k for k, j in enumerate(jobs) if j[0] is not None]

        def ensure(k):
            if k not in loaded:
                key = jobs[k][0][0]
                if key in prefetched:
                    loaded[k] = prefetched.pop(key)
                else:
                    loaded[k] = wload(*jobs[k][0])
        pos = 0
        for k, (ls, fn) in enumerate(jobs):
            if ls is not None:
                ensure(k)
                pos = order.index(k)
                if pos + 1 < len(order):
                    ensure(order[pos + 1])
                elif next_first is not None:
                    prefetched[next_first[0]] = wload(*next_first)
                fn(*loaded.pop(k))
            else:
                fn()
            if hook is not None and hook[0] == k:
                hook[1]()

    R_xn = Res()

    def p1_alloc(gj):
        nt_ = 1 if gj == 4 else 4
        pb = {}
        pb["xb"] = [ar.get([128, D], F32) for _ in range(2)]
        pb["xs"] = [ar.get([128, D], F32) for _ in range(nt_)]
        pb["st"] = ar.get([128, 4 * nt_], F32)
        pb["R_xb"] = [Res(), Res()]
        pb["R_xs"] = [Res() for _ in range(nt_)]
        pb["R_st"] = Res()
        pb["nt"] = nt_
        return pb

    def p1_norm(gj, pb):
        tok0_ = gj * 512
        s_x = [kb.slot(f"x_{i}", scratch=True) for i in range(2)]
        xb, xs, st = pb["xb"], pb["xs"], pb["st"]
        for t in range(pb["nt"]):
            i = t % 2
            c0 = 4 * t
            kb.dma("sp", xb[i][:], x[tok0_ + t * 128: tok0_ + (t + 1) * 128, :], s_x[i], writes=[pb["R_xb"][i]])
            kb.op("act", lambda: nc.scalar.activation(out=xs[t][:], in_=xb[i][:], func=AF.Square,
                                                      accum_out=st[:, c0:c0 + 1]), reads=[pb["R_xb"][i]],
                  writes=[pb["R_xs"][t], pb["R_st"]])
            kb.op("act", lambda: nc.scalar.activation(out=st[:, c0 + 1:c0 + 2], in_=st[:, c0:c0 + 1], func=AF.Sqrt,
                                                      bias=epsT[:, 0:1], scale=1.0 / D), reads=[pb["R_st"], R_const],
                  writes=[pb["R_st"]])
            kb.op("dve", lambda: nc.vector.reciprocal(out=st[:, c0 + 2:c0 + 3], in_=st[:, c0 + 1:c0 + 2]), reads=[pb["R_st"]],
                  writes=[pb["R_st"]])
            kb.op("act", lambda: nc.scalar.activation(out=xs[t][:], in_=xb[i][:], func=AF.Copy, scale=st[:, c0 + 2:c0 + 3]),
                  reads=[pb["R_xb"][i], pb["R_st"]], writes=[pb["R_xs"][t]])

    def p1_tr(gj, pb):
        NT_ = 128 if gj == 4 else 512
        xn_ = xnT[:, :, 0:NT_]
        xs = pb["xs"]
        for t in range(pb["nt"]):
            for q4 in range(4):
                bi, bk, br = bank()
                kb.mm([(lambda kc=q4 * 4 + k, k=k: nc.tensor.transpose(out=bk[:, k * 128:(k + 1) * 128],
                                                                       in_=xs[t][:, kc * 128:(kc + 1) * 128],
                                                                       identity=identf[:])) for k in range(4)],
                      reads=[pb["R_xs"][t], R_const], writes=[br])
                for k in range(4):
                    kc = q4 * 4 + k
                    if kb.ev() == "act":
                        kb.op("act", lambda: nc.scalar.activation(out=xn_[:, kc, t * 128:(t + 1) * 128],
                                                                  in_=bk[:, k * 128:(k + 1) * 128], func=AF.Copy,
                                                                  scale=lngT[:, kc:kc + 1]),
                              reads=[br, R_const], writes=[R_xn])
                    else:
                        kb.op("dve", lambda: nc.vector.tensor_scalar(out=xn_[:, kc, t * 128:(t + 1) * 128],
                                                                     in0=bk[:, k * 128:(k + 1) * 128],
                                                                     scalar1=lngT[:, kc:kc + 1], scalar2=None,
                                                                     op0=ALU.mult),
                              reads=[br, R_const], writes=[R_xn])

    def group(gi):
        samp = gi == 4
        NT = 128 if samp else 512
        ntile = NT // 128
        tok0 = gi * 512
        xn = xnT[:, :, 0:NT]
        m_T = mT[:, :, 0:NT]
        R_m = Res()

        if gi == 0:
            ar.reset()
            pb = p1_alloc(0)
            p1_norm(0, pb)
            kb._deps(kb.eng["pool"], [R_const, pb["R_xb"][0], pb["R_xb"][1]], [])
            with nc.allow_non_contiguous_dma(reason="narrow idx weight block"):
                conv_first()
            p1_tr(0, pb)
        if gi == 0:
            late_setup()
            conv_rest([LS["R_bias"]] + LS["R_dsc"])
        R_dper, R_bias, dsc, R_dsc, s_dg = LS["R_dper"], LS["R_bias"], LS["dsc"], LS["R_dsc"], LS["s_dg"]
        kb.barrier()

        ar.reset()
        qT = ar.get([128, 8, NT], BF16)
        zsT = ar.get([128, 8, NT], BF16)
        qiT = ar.get([64, 8, NT], BF16)
        gT = ar.get([128, 8, NT], BF16)
        kvst = [ar.get([128, 512], F32) for _ in range(2)]
        kist = [ar.get([128, 72], F32) for _ in range(2)]
        wab = ar.get([128, ntile, 16], F32)
        R_q, R_zs, R_qi, R_g, R_wab = Res(), Res(), Res(), Res(), Res()
        R_kvst = [Res(), Res()]
        R_kist = [Res(), Res()]
        s_kv = [kb.slot(f"kv_{i}", scratch=True) for i in range(2)]
        s_ki = [kb.slot(f"ki_{i}", scratch=True) for i in range(2)]
        R_KT, R_VA, R_KiT = Res(), Res(), Res()
        if samp:
            KTn = ar.get([128, 2, 128], BF16)
            VAn = ar.get([128, 2, 128], BF16)
            KiTn = ar.get([64, 128], BF16)

        def formF(wv, wr, nchunk, M, evac, kcn=16, src=None, R_src=None):
            src = xn if src is None else src
            R_src = R_xn if R_src is None else R_src
            for c in range(nchunk):
                bi, bk, br = bank()
                kb.mm([(lambda kc=kc: nc.tensor.matmul(bk[0:M, 0:NT], wv[:, kc, c * M:(c + 1) * M], src[:, kc, :],
                                                       start=(kc == 0), stop=(kc == kcn - 1))) for kc in range(kcn)],
                      reads=[wr, R_src], writes=[br])
                evac(c, bk[0:M, 0:NT], br)

        def formT(wv, wr, ncols, evac, c0=0, kcn=16):
            for t in range(ntile):
                bi, bk, br = bank()
                kb.mm([(lambda kc=kc: nc.tensor.matmul(bk[:, 0:ncols], xn[:, kc, t * 128:(t + 1) * 128],
                                                       wv[:, kc, c0:c0 + ncols],
                                                       start=(kc == 0), stop=(kc == kcn - 1))) for kc in range(kcn)],
                      reads=[wr, R_xn], writes=[br])
                evac(t, bk[:, 0:ncols], br)

        def copy_evac(dst_fn, R_dst, func=None):
            def f(c, ps, br):
                if func is not None:
                    kb.op("act", lambda: nc.scalar.activation(out=dst_fn(c), in_=ps, func=func), reads=[br], writes=[R_dst])
                elif kb.ev() == "act":
                    kb.op("act", lambda: nc.scalar.copy(out=dst_fn(c), in_=ps), reads=[br], writes=[R_dst])
                else:
                    kb.op("dve", lambda: nc.vector.tensor_copy(out=dst_fn(c), in_=ps), reads=[br], writes=[R_dst])
            return f

        jobs = []

        def kv_job(wv, wr):
            if samp:
                formF(wv, wr, 2, 128, copy_evac(lambda c: KTn[:, c, :], R_KT))
            else:
                formF(wv, wr, 2, 128, copy_evac(lambda c: KT[:, c, tok0:tok0 + NT], R_KT))

            def ev(t, ps, br):
                i = t % 2
                kb.op("act", lambda: nc.scalar.copy(out=kvst[i][:, :], in_=ps), reads=[br], writes=[R_kvst[i]])
                kb.dma("sp", kk[tok0 + t * 128: tok0 + (t + 1) * 128, :], kvst[i][:, 0:256], s_kv[i],
                       reads=[R_kvst[i]], is_out=True)
                kb.dma("sp", vv[tok0 + t * 128: tok0 + (t + 1) * 128, :], kvst[i][:, 256:512], s_kv[i],
                       reads=[R_kvst[i]], is_out=True)
                dst = VAn[:, :, :] if samp else VA[:, gi * 4 + t, :, :]
                kb.op("dve", lambda: nc.vector.tensor_copy(out=dst, in_=kvst[i][:, 256:512].rearrange("p (g d) -> p g d", g=2)),
                      reads=[R_kvst[i]], writes=[R_VA])
            formT(wv, wr, 512, ev)
        jobs.append((("kv", [(0, w_in_v[:, :, C_K:C_K + 512])], 16, 512), kv_job))
        jobs.append((("qi", [(0, w_in_v[:, :, C_QI: C_QI + 512])], 16, 512),
                     lambda wv, wr: formF(wv, wr, 8, 64, copy_evac(lambda c: qiT[:, c, :], R_qi))))

        def idx_job(wv, wr):
            if samp:
                formF(wv, wr, 1, 64, copy_evac(lambda c: KiTn[:, :], R_KiT))
            else:
                formF(wv, wr, 1, 64, copy_evac(lambda c: KiT[:, tok0:tok0 + NT], R_KiT))

            def ev(t, ps, br):
                i = t % 2
                kb.op("act", lambda: nc.scalar.copy(out=kist[i][:, :], in_=ps), reads=[br], writes=[R_kist[i]])
                kb.dma("sp", kio[tok0 + t * 128: tok0 + (t + 1) * 128, :], kist[i][:, 0:64], s_ki[i],
                       reads=[R_kist[i]], is_out=True)
                kb.op("act", lambda: nc.scalar.activation(out=wab[:, t, 0:8], in_=kist[i][:, 64:72], func=AF.Abs),
                      reads=[R_kist[i]], writes=[R_wab])
                kb.op("act", lambda: nc.scalar.activation(out=wab[:, t, 8:16], in_=kist[i][:, 64:72], func=AF.Sign),
                      reads=[R_kist[i]], writes=[R_wab])
            formT(wv, wr, 72, ev)
        jobs.append((("ki", [(0, w_in_v[:, :, C_KI:C_KI + 72])], 16, 72), idx_job))
        with nc.allow_non_contiguous_dma(reason="narrow idx weight block"):
            run_jobs(jobs, None)
        if gi == 0:
            for E_ in kb.eng.values():
                kb._deps(E_, [R_bias] + R_dsc, [])

        def gen_X2():
            for key, dst, Rd, fn_ in [("q0", lambda c: qT[:, c, :], R_q, None), ("q1", lambda c: qT[:, 4 + c, :], R_q, None),
                                      ("za0", lambda c: zsT[:, c, :], R_zs, AF.Silu),
                                      ("za1", lambda c: zsT[:, 4 + c, :], R_zs, AF.Silu)]:
                wv, wr = wload(key, [], 16, 512)
                ev_ = copy_evac(dst, Rd, fn_)
                for c in range(4):
                    bi, bk, br = bank()
                    kb.mm([(lambda kc=kc: nc.tensor.matmul(bk[:, 0:NT], wv[:, kc, c * 128:(c + 1) * 128], xn[:, kc, :],
                                                           start=(kc == 0), stop=(kc == 15))) for kc in range(16)],
                          reads=[wr, R_xn], writes=[br])
                    ev_(c, bk[:, 0:NT], br)
                    yield

        def gen_ga():
            for b4 in range(4):
                wv, wr = wload(f"ga{b4}", [], 16, 512)
                for c in range(4):
                    cc = b4 * 4 + c
                    bi, bk, br = bank()
                    kb.mm([(lambda kc=kc: nc.tensor.matmul(bk[:, 0:NT], wv[:, kc, c * 128:(c + 1) * 128], xn[:, kc, :],
                                                           start=(kc == 0), stop=(kc == 15))) for kc in range(16)],
                          reads=[wr, R_xn], writes=[br])
                    kb.op("act", lambda: nc.scalar.copy(out=m_T[:, cc, :], in_=bk[:, 0:NT]), reads=[br], writes=[R_m])
                    yield

        class SelBuf:
            pass
        nset = 1 if samp else 2
        SB = []
        for k_ in range(nset):
            S = SelBuf()
            S.score = ar.get([128, 2176], F32)
            S.junk = ar.get([128, 2176], BF16)
            S.m01 = ar.get([128, 2176], BF16)
            S.maskT = ar.get([128, 17, 128], BF16)
            S.bis = ar.get([128, 80], F32)
            S.R_score, S.R_junk, S.R_m01, S.R_maskT, S.R_bis = Res(), Res(), Res(), Res(), Res()
            SB.append(S)
        S0 = SB[0]
        score, maskT, bis = S0.score, S0.maskT, S0.bis
        R_score, R_maskT, R_bis = S0.R_score, S0.R_maskT, S0.R_bis
        rtmp = [ar.get([128, 512], F32) for _ in range(2)]
        et = [ar.get([128, 512], BF16) for _ in range(3)]
        pt = [ar.get([128, 512], BF16) for _ in range(2)]
        rd = ar.get([128, 512], F32)
        tmpf = ar.get([128, 512], F32)
        R_rd, R_tmpf = Res(), Res()
        R_rtmp = [Res(), Res()]
        R_et = [Res(), Res(), Res()]
        R_pt = [Res(), Res()]
        cnt = {"r": 0, "e": 0}
        NEGM = NEGB * SQ128

        def gen_select(S, L, nblk, additive):
            bis_ = S.bis
            Rb = S.R_bis
            kb.op("dve", lambda: nc.vector.tensor_reduce(out=bis_[:, 0:1], in_=S.score[:, 0:L], axis=AX.X, op=ALU.max),
                  reads=[S.R_score], writes=[Rb])
            yield
            kb.op("dve", lambda: nc.vector.tensor_sub(out=bis_[:, 2:3], in0=bis_[:, 0:1], in1=bis_[:, 1:2]),
                  reads=[Rb], writes=[Rb])
            for k in range(NBIS + 1):
                pass
            kb.op("dve", lambda: nc.vector.tensor_scalar(out=bis_[:, 40:40 + NBIS + 1],
                                                         in0=fvec[:, 0:NBIS + 1], scalar1=bis_[:, 2:3], scalar2=None,
                                                         op0=ALU.mult), reads=[Rb, R_const], writes=[Rb])
            kb.op("dve", lambda: nc.vector.tensor_tensor(out=bis_[:, 8:9], in0=bis_[:, 1:2], in1=bis_[:, 40:41], op=ALU.add),
                  reads=[Rb], writes=[Rb])
            yield
            for k in range(NBIS):
                mid = bis_[:, 8 + k:9 + k]
                kb.op("dve", lambda: nc.vector.tensor_scalar(out=S.junk[:, 0:L], in0=S.score[:, 0:L], scalar1=mid,
                                                             scalar2=0.0, op0=ALU.is_ge, op1=ALU.add,
                                                             accum_out=bis_[:, 4:5]),
                      reads=[Rb, S.R_score], writes=[S.R_junk, Rb])
                yield
                kb.op("dve", lambda: nc.vector.tensor_scalar(out=bis_[:, 5:6], in0=bis_[:, 4:5], scalar1=255.5, scalar2=-0.5,
                                                             op0=ALU.is_ge, op1=ALU.add), reads=[Rb], writes=[Rb])
                kb.op("dve", lambda: nc.vector.scalar_tensor_tensor(out=bis_[:, 9 + k:10 + k], in0=bis_[:, 5:6],
                                                                    scalar=bis_[:, 40 + k:41 + k], in1=mid,
                                                                    op0=ALU.mult, op1=ALU.add), reads=[Rb], writes=[Rb])
                yield
            kb.op("dve", lambda: nc.vector.tensor_sub(out=bis_[:, 6:7], in0=bis_[:, 8 + NBIS:9 + NBIS],
                                                      in1=bis_[:, 40 + NBIS:41 + NBIS]), reads=[Rb], writes=[Rb])
            if additive:
                kb.op("dve", lambda: nc.vector.tensor_scalar(out=S.m01[:, 0:L], in0=S.score[:, 0:L], scalar1=bis_[:, 6:7],
                                                             scalar2=NEGM, op0=ALU.is_lt, op1=ALU.mult),
                      reads=[Rb, S.R_score], writes=[S.R_m01])
            else:
                kb.op("dve", lambda: nc.vector.tensor_scalar(out=S.m01[:, 0:L], in0=S.score[:, 0:L], scalar1=bis_[:, 6:7],
                                                             scalar2=None, op0=ALU.is_ge),
                      reads=[Rb, S.R_score], writes=[S.R_m01])
            yield
            j = 0
            while j < nblk:
                n = min(8, nblk - j)
                bi, bk, br = bank()
                bkb = bk.bitcast(BF16)
                kb.mm([(lambda jj=jj: nc.tensor.transpose(out=bkb[:, jj * 128:(jj + 1) * 128],
                                                          in_=S.m01[:, (j + jj) * 128:(j + jj + 1) * 128],
                                                          identity=identb[:])) for jj in range(n)],
                      reads=[S.R_m01, R_const], writes=[br])
                kb.op("act", lambda: nc.scalar.copy(out=S.maskT[:, j:j + n, :],
                                                    in_=bkb[:, 0:n * 128].rearrange("p (a b) -> p a b", a=n)),
                      reads=[br], writes=[S.R_maskT])
                j += n
                yield

        def select_mask(L, nblk):
            for _ in gen_select(S0, L, nblk, False):
                pass

        def finalize(g, bo, ro, bd, rdn, cols):
            kb.op("dve", lambda: nc.vector.reciprocal(out=rd[:], in_=bd[:, :]), reads=[rdn], writes=[R_rd])
            kb.op("dve", lambda: nc.vector.tensor_tensor(out=tmpf[:], in0=bo[:, :], in1=rd[:], op=ALU.mult),
                  reads=[ro, R_rd], writes=[R_tmpf])
            kb.op("dve", lambda: nc.vector.tensor_tensor(out=gT[:, 4 * g:4 * g + 4, cols],
                                                         in0=tmpf[:].rearrange("p (h q) -> p h q", h=4),
                                                         in1=zsT[:, 4 * g:4 * g + 4, cols], op=ALU.mult),
                  reads=[R_tmpf, R_zs], writes=[R_g])

        def attn_block(g, lhsK, cols, near, tabs, mask_ap, bo, ro, bd, rdn, lhsV, first, rK, rV, addmask=None, R_mk=None):
            bi, bk, br = bank()
            nmm = 1 + (2 if near else 0) + (1 if addmask is not None else 0)
            fns = [lambda: nc.tensor.matmul(bk[:, :], lhsK, qT[:, 4 * g:4 * g + 4, cols], start=True, stop=(nmm == 1),
                                            skip_group_check=True)]
            if near:
                fns.append(lambda: nc.tensor.matmul(bk[:, :], identb[:], tabs[0][:, 4 * g:4 * g + 4, :], start=False,
                                                    stop=False, skip_group_check=True))
                fns.append(lambda: nc.tensor.matmul(bk[:, :], identb[:], tabs[1][:, 4 * g:4 * g + 4, :], start=False,
                                                    stop=(addmask is None), skip_group_check=True))
            rds = [rK, R_q, R_bias, R_const]
            if addmask is not None:
                fns.append(lambda: nc.tensor.matmul(bk[:, :], identb[:], addmask.unsqueeze(1).broadcast_to([128, 4, 128]),
                                                    start=False, stop=True, skip_group_check=True))
                rds.append(R_mk)
            kb.mm(fns, reads=rds, writes=[br])
            ei = cnt["e"] % 3
            cnt["e"] += 1
            kb.op("act", lambda: nc.scalar.activation(out=et[ei][:], in_=bk[:, :], func=AF.Exp, scale=SCALE),
                  reads=[br], writes=[R_et[ei]])
            if mask_ap is not None:
                pi = ei % 2
                kb.op("dve", lambda: nc.vector.tensor_tensor(out=pt[pi][:].rearrange("p (h q) -> p h q", h=4),
                                                             in0=et[ei][:].rearrange("p (h q) -> p h q", h=4),
                                                             in1=mask_ap.unsqueeze(1).broadcast_to([128, 4, 128]),
                                                             op=ALU.mult),
                      reads=[R_et[ei], R_maskT], writes=[R_pt[pi]])
                src, rs = pt[pi], R_pt[pi]
            else:
                src, rs = et[ei], R_et[ei]
            kb.mm([lambda: nc.tensor.matmul(bo[:, :], lhsV, src[:], start=first, stop=False, skip_group_check=True),
                   lambda: nc.tensor.matmul(bd[:, :], onesb[:], src[:], start=first, stop=False, skip_group_check=True)],
                  reads=[rs, rV, R_const], writes=[ro, rdn])

        def pin():
            i, b, r = bank()
            pinned.add(i)
            return i, b, r

        if not samp:
            def gen_sel_tile(t):
                ti = gi * 4 + t
                S = SB[t % 2]
                L = 128 * (ti + 1)
                cols = slice(t * 128, (t + 1) * 128)
                if ti < 2:
                    return
                nch = (L + 511) // 512
                for h in range(8):
                    for ch in range(nch):
                        w = min(512, L - ch * 512)
                        bi, bk, br = bank()
                        kb.mm([lambda: nc.tensor.matmul(bk[:, 0:w], qiT[:, h, cols], KiT[:, ch * 512:ch * 512 + w],
                                                        start=True, stop=True)], reads=[R_qi, R_KiT], writes=[br])
                        ri = cnt["r"] % 2
                        cnt["r"] += 1
                        kb.op("act", lambda: nc.scalar.activation(out=rtmp[ri][:, 0:w], in_=bk[:, 0:w], func=AF.Relu,
                                                                  scale=wab[:, t, h:h + 1]),
                              reads=[br, R_wab], writes=[R_rtmp[ri]])
                        sc = S.score[:, ch * 512:ch * 512 + w]
                        if h == 0:
                            kb.op("dve", lambda: nc.vector.tensor_scalar(out=sc, in0=rtmp[ri][:, 0:w],
                                                                         scalar1=wab[:, t, 8 + h:9 + h], scalar2=None,
                                                                         op0=ALU.mult),
                                  reads=[R_rtmp[ri], R_wab], writes=[S.R_score])
                        else:
                            kb.op("dve", lambda: nc.vector.scalar_tensor_tensor(out=sc, in0=rtmp[ri][:, 0:w],
                                                                                scalar=wab[:, t, 8 + h:9 + h], in1=sc,
                                                                                op0=ALU.mult, op1=ALU.add),
                                  reads=[R_rtmp[ri], R_wab], writes=[S.R_score])
                        yield
                kb.op("dve", lambda: nc.vector.tensor_reduce(out=S.bis[:, 1:2], in_=S.score[:, 0:L], axis=AX.X, op=ALU.min),
                      reads=[S.R_score], writes=[S.R_bis])
                kb.op("dve", lambda: nc.vector.tensor_tensor(out=S.score[:, L - 128:L], in0=S.score[:, L - 128:L],
                                                             in1=causneg[:], op=ALU.add),
                      reads=[R_const], writes=[S.R_score])
                yield
                for _ in gen_select(S, L, ti + 1, True):
                    yield

            def gen_attn_tile(t):
                ti = gi * 4 + t
                S = SB[t % 2]
                cols = slice(t * 128, (t + 1) * 128)
                for g in range(2):
                    io, bo, ro = pin()
                    idn, bd, rdn = pin()
                    for j in range(ti + 1):
                        near = ti - j <= 1
                        tabs = (biasT[:, 2 * (ti - j)], biasT[:, 2 * (ti - j) + 1]) if near else None
                        attn_block(g, KT[:, g, j * 128:(j + 1) * 128], cols, near, tabs, None, bo, ro, bd, rdn,
                                   VA[:, j, g, :], j == 0, R_KT, R_VA,
                                   addmask=(S.maskT[:, j, :] if ti >= 2 else None), R_mk=S.R_maskT)
                        yield
                    finalize(g, bo, ro, bd, rdn, cols)
                    pinned.discard(io)
                    pinned.discard(idn)
                    yield

            def nsteps_sel(t):
                ti = gi * 4 + t
                if ti < 2:
                    return 0
                L = 128 * (ti + 1)
                return 8 * ((L + 511) // 512) + 3 + 2 * NBIS + 1 + (ti + 8) // 8

            def zipper(ga, na, gb, nb):
                da = db = 0
                alive_a = alive_b = True
                while alive_a or alive_b:
                    fa = da / max(na, 1) if alive_a else 2.0
                    fb = db / max(nb, 1) if alive_b else 2.0
                    if fa <= fb:
                        try:
                            next(ga)
                            da += 1
                        except StopIteration:
                            alive_a = False
                    else:
                        try:
                            next(gb)
                            db += 1
                        except StopIteration:
                            alive_b = False

            zipper(gen_sel_tile(0), nsteps_sel(0), gen_X2(), 16)
            def with_side(main, side, every):
                n_ = 0
                for _ in main:
                    n_ += 1
                    if n_ % every == 0:
                        try:
                            next(side)
                        except StopIteration:
                            pass
                    yield

            gga = gen_ga()
            for t in range(4):
                na = 2 * (gi * 4 + t + 2)
                ga = with_side(gen_attn_tile(t), gga, max(1, na // 4))
                if t + 1 < 4:
                    zipper(ga, na, gen_sel_tile(t + 1), nsteps_sel(t + 1))
                else:
                    for _ in ga:
                        pass
            for _ in gga:
                pass
        else:
            for _ in gen_X2():
                pass
            cols = slice(0, 128)
            ts = ar.get([128, 2, 8, 128], BF16)
            tab16 = ar.get([128, 8, 2, 64], BF16)
            ar_mark = ar.off
            tst2 = ar.get([128, 8, 128], F32)
            thf2 = ar.get([128, 8, 128], F32)
            stg = ar.get([128, 8, 64], F32)
            stgh = ar.get([128, 8, 64], F32)
            R_ts = Res()
            s_ts = kb.slot("ts", scratch=True)
            with nc.allow_non_contiguous_dma(reason="toeplitz"):
                src_ = dper[:, 128:128 + 383 * 128].rearrange("h (s x) -> s h x", x=383)[:, :, 0:128]
                kb.dma("sp", tst2[:], src_, s_ts, reads=[R_dper], writes=[R_ts])
                kb.op("dve", lambda: nc.vector.memset(stg[:], 0.0), writes=[R_ts])
                for pl in range(8):
                    off = 256 - pl
                    srcp = dper[:, off:off + 376 * 16].rearrange("h (pp x) -> pp h x", x=376)[:, :, 0:8]
                    kb.dma("sp", stg[112:128, pl, :].rearrange("p (h q) -> p h q", h=8), srcp, s_ts, reads=[R_dper], writes=[R_ts])
            kb.op("dve", lambda: nc.vector.tensor_tensor(out=tst2[:], in0=tst2[:],
                                                         in1=sampbm[:].unsqueeze(1).broadcast_to([128, 8, 128]),
                                                         op=ALU.mult), reads=[R_const], writes=[R_ts])
            kb.op("dve", lambda: nc.vector.tensor_tensor(out=tst2[:], in0=tst2[:],
                                                         in1=sampnb[:].unsqueeze(1).broadcast_to([128, 8, 128]),
                                                         op=ALU.add), reads=[R_const], writes=[R_ts])
            kb.op("dve", lambda: nc.vector.tensor_copy(out=ts[:, 0], in_=tst2[:]), writes=[R_ts])
            kb.op("dve", lambda: nc.vector.tensor_copy(out=thf2[:], in_=ts[:, 0]), writes=[R_ts])
            kb.op("dve", lambda: nc.vector.tensor_tensor(out=ts[:, 1], in0=tst2[:], in1=thf2[:], op=ALU.subtract),
                  writes=[R_ts])
            kb.op("dve", lambda: nc.vector.tensor_copy(out=tab16[:, :, 0, :], in_=stg[:]), writes=[R_ts])
            kb.op("dve", lambda: nc.vector.tensor_copy(out=stgh[:], in_=tab16[:, :, 0, :]), writes=[R_ts])
            kb.op("dve", lambda: nc.vector.tensor_tensor(out=tab16[:, :, 1, :], in0=stg[:], in1=stgh[:], op=ALU.subtract),
                  writes=[R_ts, R_bias])
            kb.barrier()
            ar.off = ar_mark
            wcol = ar.get([64, 16], F32)
            amat = ar.get([128, 64], F32)
            lq = [ar.get([64, 64], BF16) for _ in range(2)]
            kic = [ar.get([128, 2, 8, 64], BF16) for _ in range(2)]
            kitb = [ar.get([64, 2048], BF16) for _ in range(2)]
            rwt = [ar.get([64, 512], F32) for _ in range(4)]
            rwh = [ar.get([64, 512], BF16) for _ in range(4)]
            rwl = [ar.get([64, 512], BF16) for _ in range(4)]
            selb = ar.get([64, 248], BF16)
            R_rwh = [Res() for _ in range(4)]
            R_rwl = [Res() for _ in range(4)]
            kb.op("dve", lambda: nc.vector.tensor_copy(out=selb[:], in_=selbase[:]), reads=[R_const], writes=[R_const])
            R_wcol = Res()
            R_lq = [Res(), Res()]
            R_kic = [Res(), Res()]
            R_kitb = [Res(), Res()]
            R_rwt = [Res() for _ in range(4)]
            s_kic = [kb.slot(f"kic{i}", scratch=True) for i in range(2)]
            kb.op("dve", lambda: nc.vector.tensor_tensor(out=amat[:].rearrange("p (h q) -> p h q", h=8),
                                                         in0=dq[:].rearrange("p (h q) -> p h q", h=8),
                                                         in1=kist[0][:, 64:72].unsqueeze(2).broadcast_to([128, 8, 8]),
                                                         op=ALU.mult), reads=[R_kist[0], R_const], writes=[R_wcol])
            bi, bk, br = bank()
            kb.mm([lambda: nc.tensor.matmul(bk[0:64, 0:16], amat[:, :], bsel[:, :], start=True, stop=True)],
                  reads=[R_wcol, R_const], writes=[br])
            kb.op("dve", lambda: nc.vector.tensor_copy(out=wcol[:], in_=bk[0:64, 0:16]), reads=[br], writes=[R_wcol])
            sbk = [pin() for _ in range(5)]
            prepared = set()

            def prep(b):
                i = b % 2
                for hf in range(2):
                    kb.idma(kic[i][:, hf].rearrange("p a e -> p (a e)"), cache_ki, idxall[:, 2 * b + hf:2 * b + hf + 1],
                            s_kic[i], reads=[R_const], writes=[R_kic[i]])
                for hf in range(2):
                    bi, bk, br = bank()
                    bkb = bk.bitcast(BF16)
                    kb.mm([(lambda pl=pl: nc.tensor.transpose(out=bkb[0:64, pl * 128:(pl + 1) * 128], in_=kic[i][:, hf, pl, :],
                                                              identity=identb[:])) for pl in range(8)],
                          reads=[R_kic[i], R_const], writes=[br])
                    if kb.ev() == "act":
                        kb.op("act", lambda: nc.scalar.copy(out=kitb[i][:, hf * 1024:(hf + 1) * 1024], in_=bkb[0:64, :]),
                              reads=[br], writes=[R_kitb[i]])
                    else:
                        kb.op("dve", lambda: nc.vector.tensor_copy(out=kitb[i][:, hf * 1024:(hf + 1) * 1024], in_=bkb[0:64, :]),
                              reads=[br], writes=[R_kitb[i]])
                kb.op("dve", lambda: nc.vector.tensor_copy(out=lq[i][:].rearrange("p (h q) -> p h q", h=8),
                                                           in_=qiT[:, :, 8 * b:8 * b + 8]), reads=[R_qi], writes=[R_lq[i]])

            items = [(b, ch) for b in range(16) for ch in range(5)]

            def stageA(k):
                b, ch = items[k]
                i = b % 2
                if b not in prepared:
                    prepared.add(b)
                    prep(b)
                w = 512 if ch < 4 else 128
                rhs = kitb[i][:, ch * 512:(ch + 1) * 512] if ch < 4 else KiTn[:, :]
                bi, bk, br = pin()
                kb.mm([lambda: nc.tensor.matmul(bk[0:64, 0:w], lq[i][:, :], rhs, start=True, stop=True)],
                      reads=[R_lq[i], R_kitb[i], R_KiT], writes=[br])
                return bk, br, bi
            pend = {}
            LOOK = 1
            for k in range(min(LOOK, len(items))):
                pend[k] = stageA(k)
            for k in range(len(items)):
                if k + LOOK < len(items):
                    pend[k + LOOK] = stageA(k + LOOK)
                b, ch = items[k]
                w = 512 if ch < 4 else 128
                bk, br, bi_ = pend.pop(k)
                ri = k % 4
                kb.op("dve", lambda: nc.vector.tensor_scalar(out=rwt[ri][:, 0:w], in0=bk[0:64, 0:w], scalar1=0.0,
                                                             scalar2=wcol[:, b:b + 1], op0=ALU.max, op1=ALU.mult),
                      reads=[br, R_wcol], writes=[R_rwt[ri]])
                pinned.discard(bi_)
                kb.op("act", lambda: nc.scalar.copy(out=rwh[ri][:, 0:w], in_=rwt[ri][:, 0:w]), reads=[R_rwt[ri]],
                      writes=[R_rwh[ri]])
                kb.op("dve", lambda: nc.vector.tensor_tensor(out=rwl[ri][:, 0:w], in0=rwt[ri][:, 0:w], in1=rwh[ri][:, 0:w],
                                                             op=ALU.subtract), reads=[R_rwt[ri], R_rwh[ri]], writes=[R_rwl[ri]])
                sbi, sbk_, sbr = sbk[ch]
                kb.mm([lambda: nc.tensor.matmul(sbk_[:, 0:w], selb[:, 120 - 8 * b:248 - 8 * b], rwh[ri][:, 0:w],
                                                start=(b == 0), stop=False, skip_group_check=True),
                       lambda: nc.tensor.matmul(sbk_[:, 0:w], selb[:, 120 - 8 * b:248 - 8 * b], rwl[ri][:, 0:w],
                                                start=False, stop=(b == 15), skip_group_check=True)],
                      reads=[R_rwh[ri], R_rwl[ri], R_const], writes=[sbr])
            for ch in range(5):
                w = 512 if ch < 4 else 128
                sbi, sbk_, sbr = sbk[ch]
                kb.op("act", lambda: nc.scalar.copy(out=score[:, ch * 512:ch * 512 + w], in_=sbk_[:, 0:w]),
                      reads=[sbr], writes=[R_score])
                pinned.discard(sbi)
            L = 2176
            kb.op("dve", lambda: nc.vector.tensor_reduce(out=bis[:, 1:2], in_=score[:, 0:L], axis=AX.X, op=ALU.min),
                  reads=[R_score], writes=[R_bis])
            kb.op("dve", lambda: nc.vector.tensor_tensor(out=score[:, 2048:2176], in0=score[:, 2048:2176],
                                                         in1=sampneg[:], op=ALU.add), reads=[R_const], writes=[R_score])
            gsel_s = gen_select(S0, L, 17, False)
            gga_s = gen_ga()
            n_s = 0
            for _ in gsel_s:
                n_s += 1
                if n_s % 2 == 0:
                    try:
                        next(gga_s)
                    except StopIteration:
                        pass
            for _ in gga_s:
                pass
            kb.barrier()
            ar.off = ar_mark
            acc = [(pin(), pin()) for _ in range(2)]
            for g in range(2):
                (io, bo, ro), (idn, bd, rdn) = acc[g]
                attn_block(g, KTn[:, g, :], cols, True, (ts[:, 0], ts[:, 1]), maskT[:, 16, :], bo, ro, bd, rdn,
                           VAn[:, g, :], True, R_KT, R_VA)
            kcb = [ar.get([128, 2, 8, 256], BF16) for _ in range(2)]
            vb = [ar.get([128, 2, 8, 256], BF16) for _ in range(2)]
            ktb = ar.get([128, 2, 2048], BF16)
            es = ar.get([128, 16, 64], BF16)
            ps_ = ar.get([128, 16, 64], BF16)
            R_kc = [Res(), Res()]
            R_vb = [Res(), Res()]
            R_ktb, R_es, R_ps = Res(), Res(), Res()
            s_kc = [kb.slot(f"kc{i}", scratch=True) for i in range(2)]
            s_vc = [kb.slot(f"vc{i}", scratch=True) for i in range(2)]
            il0, bl0, rl0 = pin()
            il1, bl1, rl1 = pin()
            lgb = [(bl0, rl0), (bl1, rl1)]

            def gather(b):
                i = b % 2
                for hf in range(2):
                    ix = idxall[:, 2 * b + hf:2 * b + hf + 1]
                    kb.idma(kcb[i][:, hf].rearrange("p a e -> p (a e)"), cache_k, ix, s_kc[i], reads=[R_const], writes=[R_kc[i]])
                    kb.idma(vb[i][:, hf].rearrange("p a e -> p (a e)"), cache_v, ix, s_vc[i], reads=[R_const], writes=[R_vb[i]])
            gather(0)
            for b in range(16):
                i = b % 2
                if b + 1 < 16:
                    gather(b + 1)
                for hf in range(2):
                    for g in range(2):
                        bi, bk, br = bank()
                        bkb = bk.bitcast(BF16)
                        kb.mm([(lambda pl=pl: nc.tensor.transpose(out=bkb[:, pl * 128:(pl + 1) * 128],
                                                                  in_=kcb[i][:, hf, pl, g * 128:(g + 1) * 128],
                                                                  identity=identb[:])) for pl in range(8)],
                              reads=[R_kc[i], R_const], writes=[br])
                        if kb.ev() == "act":
                            kb.op("act", lambda: nc.scalar.copy(out=ktb[:, g, hf * 1024:(hf + 1) * 1024], in_=bkb[:, :]),
                                  reads=[br], writes=[R_ktb])
                        else:
                            kb.op("dve", lambda: nc.vector.tensor_copy(out=ktb[:, g, hf * 1024:(hf + 1) * 1024], in_=bkb[:, :]),
                                  reads=[br], writes=[R_ktb])
                for hb in range(2):
                    bl, rl = lgb[hb]
                    blv = bl[:, :].rearrange("p (j c) -> p j c", j=8)
                    fns = []
                    for jj in range(8):
                        j = hb * 8 + jj
                        for g in range(2):
                            fns.append(lambda j=j, jj=jj, g=g, blv=blv: nc.tensor.matmul(
                                blv[:, jj, g * 32:(g + 1) * 32], ktb[:, g, j * 128:(j + 1) * 128],
                                qT[:, 4 * g:4 * g + 4, 8 * b:8 * b + 8], start=True, stop=(hb == 0), skip_group_check=True))
                            if hb == 1:
                                for hl in range(2):
                                    fns.append(lambda jj=jj, g=g, hl=hl, blv=blv: nc.tensor.matmul(
                                        blv[:, jj, g * 32:(g + 1) * 32], identb[:],
                                        tab16[:, jj, hl, :].rearrange("p (h q) -> p h q", h=8)[:, 4 * g:4 * g + 4, :],
                                        start=False, stop=(hl == 1), skip_group_check=True))
                    kb.mm(fns, reads=[R_ktb, R_q, R_bias, R_ts, R_const], writes=[rl])
                    kb.op("act", lambda: nc.scalar.activation(out=es[:, hb * 8:(hb + 1) * 8, :], in_=blv, func=AF.Exp,
                                                              scale=SCALE), reads=[rl], writes=[R_es])
                kb.op("dve", lambda: nc.vector.tensor_tensor(
                    out=ps_[:].rearrange("p j (h q) -> p j h q", h=8), in0=es[:].rearrange("p j (h q) -> p j h q", h=8),
                    in1=maskT[:, 0:16, 8 * b:8 * b + 8].unsqueeze(2).broadcast_to([128, 16, 8, 8]), op=ALU.mult),
                    reads=[R_es, R_maskT], writes=[R_ps])
                for g in range(2):
                    (io, bo, ro), (idn, bd, rdn) = acc[g]
                    bov = bo[:, :].rearrange("p (h q) -> p h q", h=4)[:, :, 8 * b:8 * b + 8]
                    bdv = bd[:, :].rearrange("p (h q) -> p h q", h=4)[:, :, 8 * b:8 * b + 8]
                    fns = []
                    for j in range(16):
                        fns.append(lambda j=j, g=g, bov=bov: nc.tensor.matmul(
                            bov, vb[i][:, j // 8, j % 8, g * 128:(g + 1) * 128], ps_[:, j, g * 32:(g + 1) * 32],
                            start=False, stop=False, skip_group_check=True))
                        fns.append(lambda j=j, g=g, bdv=bdv: nc.tensor.matmul(bdv, onesb[:], ps_[:, j, g * 32:(g + 1) * 32],
                                                                            start=False, stop=False, skip_group_check=True))
                    kb.mm(fns, reads=[R_ps, R_vb[i], R_const], writes=[ro, rdn])
            pinned.discard(il0)
            pinned.discard(il1)
            for g in range(2):
                (io, bo, ro), (idn, bd, rdn) = acc[g]
                finalize(g, bo, ro, bd, rdn, cols)
                pinned.discard(io)
                pinned.discard(idn)

        for cc in range(16):
            kb.op("act", lambda: nc.scalar.activation(out=m_T[:, cc, :], in_=m_T[:, cc, :], func=AF.Sigmoid),
                  reads=[R_m], writes=[R_m])
        jobs = []
        for b4 in range(4):
            def ua_job(wv, wr, b4=b4):
                def ev(c, ps, br):
                    cc = b4 * 4 + c
                    kb.op("dve", lambda: nc.vector.tensor_tensor(out=m_T[:, cc, :], in0=ps, in1=m_T[:, cc, :], op=ALU.mult),
                          reads=[br], writes=[R_m])
                formF(wv, wr, 4, 128, ev, kcn=8, src=gT, R_src=R_g)
            jobs.append(((f"ua{b4}", [(0, w_ua_v[:, :, b4 * 512:(b4 + 1) * 512])], 8, 512), ua_job))
        sgc = {"n": 0}
        run_jobs(jobs, ("gg0", [], 16, 512))
        kb.barrier()

        ar.reset()
        NB_ = 16 if samp else 1
        TW = 38 if samp else NT + 30
        uT = ar.get([128, 8, NB_ * TW], BF16)
        zcT = ar.get([128, 8, NT], BF16)
        dwf = ar.get([128, 8, NT], F32)
        aT = ar.get([128, 8, NT], BF16)
        cT = ar.get([128, 8, NT], BF16)
        diag = [ar.get([128, 31, 128], BF16) for _ in range(2)]
        sqt = [ar.get([128, NT], F32) for _ in range(2)]
        lnm = ar.get([128, NT], F32)
        lnv = ar.get([128, NT], F32)
        lnr = ar.get([128, NT], F32)
        t1 = [ar.get([128, NT], F32) for _ in range(2)]
        sgt2 = [ar.get([128, NT], F32) for _ in range(2)]
        tt2 = [ar.get([128, NT], F32) for _ in range(2)]
        R_u, R_zc, R_dw, R_a, R_c, R_ln = Res(), Res(), Res(), Res(), Res(), Res()
        R_diag = [Res(), Res()]
        R_sq = [Res(), Res()]
        R_t1 = [Res(), Res()]
        R_sg2 = [Res(), Res()]
        R_tt2 = [Res(), Res()]
        need_tok = samp or gi == 3
        if need_tok:
            sgtok = ar.get([128, 1024], F32)
            utok = ar.get([128, 1024], F32)
            R_sgtok, R_utok = Res(), Res()
            s_ut = kb.slot(f"ut", scratch=True)
        if samp:
            uv = uT[:].rearrange("p c (b w) -> p c b w", b=16)

            def ucols(c):
                return uv[:, c, :, 30:38]
            stt = ar.get([120, 1024], F32)
            R_stt = Res()
            s_stt = kb.slot("stt", scratch=True)
            for q4 in range(4):
                kb.dma("sp", stt[:], state[q4 * 4:(q4 + 1) * 4].rearrange("b r c -> (b r) c"), s_stt, writes=[R_stt])
                for c2 in range(2):
                    bi, bk, br = bank()
                    kb.mm([(lambda k=k: nc.tensor.transpose(out=bk[:, k * 120:(k + 1) * 120],
                                                            in_=stt[:, (c2 * 4 + k) * 128:(c2 * 4 + k + 1) * 128],
                                                            identity=identf[0:120, 0:120])) for k in range(4)],
                          reads=[R_stt, R_const], writes=[br])
                    kb.op("dve", lambda: nc.vector.tensor_copy(
                        out=uv[:, c2 * 4:(c2 + 1) * 4, q4 * 4:(q4 + 1) * 4, 0:30],
                        in_=bk[:, 0:480].rearrange("p (k b r) -> p k b r", k=4, b=4)), reads=[br], writes=[R_u])
            kb.dma("sp", convs[:, 0:22, :], state[:, 8:30, :], s_stt, is_out=True)
        else:
            def ucols(c):
                return uT[:, c, 30:30 + NT]
            kb.op("dve", lambda: nc.vector.tensor_copy(out=uT[:, :, 0:30], in_=uhalo[:]), reads=[R_const, R_uh], writes=[R_u])

        jobs = []
        for b2 in range(2):
            def gg_job(wv, wr, b2=b2):
                formF(wv, wr, 4, 128, copy_evac(lambda c: ucols(b2 * 4 + c), R_u, AF.Sigmoid))
                if need_tok:
                    t = ntile - 1
                    bi, bk, br = bank()
                    kb.mm([(lambda kc=kc: nc.tensor.matmul(bk[:, 0:512], xn[:, kc, t * 128:(t + 1) * 128], wv[:, kc, :],
                                                           start=(kc == 0), stop=(kc == 15))) for kc in range(16)],
                          reads=[wr, R_xn], writes=[br])
                    kb.op("act", lambda: nc.scalar.activation(out=sgtok[:, b2 * 512:(b2 + 1) * 512], in_=bk[:, 0:512],
                                                              func=AF.Sigmoid), reads=[br], writes=[R_sgtok])
            jobs.append(((f"gg{b2}", [(0, w_in_v[:, :, C_GG + b2 * 512: C_GG + (b2 + 1) * 512])], 16, 512), gg_job))
        for b2 in range(2):
            def gv_job(wv, wr, b2=b2):
                def ev(c, ps, br):
                    kb.op("dve", lambda: nc.vector.tensor_tensor(out=ucols(b2 * 4 + c), in0=ps if not samp else
                                                                 ps.rearrange("p (b t) -> p b t", b=16),
                                                                 in1=ucols(b2 * 4 + c), op=ALU.mult), reads=[br], writes=[R_u])
                formF(wv, wr, 4, 128, ev)
                if need_tok:
                    t = ntile - 1
                    bi, bk, br = bank()
                    kb.mm([(lambda kc=kc: nc.tensor.matmul(bk[:, 0:512], xn[:, kc, t * 128:(t + 1) * 128], wv[:, kc, :],
                                                           start=(kc == 0), stop=(kc == 15))) for kc in range(16)],
                          reads=[wr, R_xn], writes=[br])
                    kb.op("dve", lambda: nc.vector.tensor_tensor(out=utok[:, b2 * 512:(b2 + 1) * 512], in0=bk[:, 0:512],
                                                                 in1=sgtok[:, b2 * 512:(b2 + 1) * 512], op=ALU.mult),
                          reads=[br, R_sgtok], writes=[R_utok])
            jobs.append(((f"gv{b2}", [(0, w_in_v[:, :, C_GV + b2 * 512: C_GV + (b2 + 1) * 512])], 16, 512), gv_job))
        def conv_job():
            if need_tok:
                if samp:
                    for b in range(16):
                        kb.dma("sp", convs[b, 22:30, :], utok[8 * b:8 * b + 8, :], s_ut, reads=[R_utok], is_out=True)
                else:
                    kb.dma("sp", convp[:, :], utok[98:128, :], s_ut, reads=[R_utok], is_out=True)
            if not samp:
                kb.op("dve", lambda: nc.vector.tensor_copy(out=uhalo[:], in_=uT[:, :, NT:NT + 30]), reads=[R_u], writes=[R_uh])
            for c in range(8):
                di = c % 2
                kb.dma("sp", diag[di][:].rearrange("p a b -> p (a b)"), dsc[c], s_dg[di], reads=[R_dsc[c]], writes=[R_diag[di]])
                bi, bk, br = bank()
                if samp:
                    rhsf = lambda j: uv[:, c, :, j:j + 8]
                else:
                    rhsf = lambda j: uT[:, c, j:j + NT]
                kb.mm([(lambda j=j: nc.tensor.matmul(bk[:, 0:NT], diag[di][:, j, :], rhsf(j), start=(j == 0), stop=(j == 30)))
                       for j in range(31)], reads=[R_diag[di], R_u], writes=[br])
                kb.op("act", lambda: nc.scalar.activation(out=dwf[:, c, :], in_=bk[:, 0:NT], func=AF.Identity,
                                                          bias=cbT[:, c:c + 1], scale=1.0), reads=[br, R_const], writes=[R_dw])
            im, bm, rm = pin()
            iq, bq, rq = pin()
            for c in range(8):
                si = c % 2
                kb.op("act", lambda: nc.scalar.activation(out=sqt[si][:], in_=dwf[:, c, :], func=AF.Square),
                      reads=[R_dw], writes=[R_sq[si]])
                kb.mm([lambda: nc.tensor.matmul(bm[:, 0:NT], onesf[:], dwf[:, c, :], start=(c == 0), stop=(c == 7),
                                                skip_group_check=True)], reads=[R_dw, R_const], writes=[rm])
                kb.mm([lambda: nc.tensor.matmul(bq[:, 0:NT], onesf[:], sqt[si][:], start=(c == 0), stop=(c == 7),
                                                skip_group_check=True)], reads=[R_sq[si], R_const], writes=[rq])
            kb.op("dve", lambda: nc.vector.tensor_copy(out=lnm[:], in_=bm[:, 0:NT]), reads=[rm], writes=[R_ln])
            kb.op("dve", lambda: nc.vector.scalar_tensor_tensor(out=lnv[:], in0=lnm[:], scalar=-1.0, in1=lnm[:],
                                                                op0=ALU.mult, op1=ALU.mult), reads=[R_ln], writes=[R_ln])
            kb.op("dve", lambda: nc.vector.tensor_tensor(out=lnv[:], in0=bq[:, 0:NT], in1=lnv[:], op=ALU.add),
                  reads=[rq, R_ln], writes=[R_ln])
            kb.op("act", lambda: nc.scalar.activation(out=lnv[:], in_=lnv[:], func=AF.Sqrt, bias=epsT[:, 0:1], scale=1.0),
                  reads=[R_ln, R_const], writes=[R_ln])
            kb.op("dve", lambda: nc.vector.reciprocal(out=lnr[:], in_=lnv[:]), reads=[R_ln], writes=[R_ln])
            pinned.discard(im)
            pinned.discard(iq)
            for c in range(8):
                i = c % 2
                kb.op("dve", lambda: nc.vector.tensor_tensor(out=t1[i][:], in0=dwf[:, c, :], in1=lnm[:], op=ALU.subtract),
                      reads=[R_dw, R_ln], writes=[R_t1[i]])
                kb.op("dve", lambda: nc.vector.tensor_tensor(out=t1[i][:], in0=t1[i][:], in1=lnr[:], op=ALU.mult),
                      reads=[R_ln], writes=[R_t1[i]])
                kb.op("act", lambda: nc.scalar.activation(out=aT[:, c, :], in_=t1[i][:], func=AF.Silu,
                                                          bias=cnbT[:, c:c + 1], scale=cngT[:, c:c + 1]),
                      reads=[R_t1[i], R_const], writes=[R_a])
        jobs.append((None, conv_job))
        for b2 in range(2):
            jobs.append(((f"zc{b2}", [(0, w_in_v[:, :, C_ZC + b2 * 512: C_ZC + (b2 + 1) * 512])], 16, 512),
                         lambda wv, wr, b2=b2: formF(wv, wr, 4, 128, copy_evac(lambda c: zcT[:, b2 * 4 + c, :], R_zc, AF.Silu))))

        for b2 in range(2):
            def pw_job(wv, wr, b2=b2):
                def ev(c, ps, br):
                    cc = b2 * 4 + c
                    kb.op("dve", lambda: nc.vector.scalar_tensor_tensor(out=cT[:, cc, :], in0=ps, scalar=bpwT[:, cc:cc + 1],
                                                                        in1=zcT[:, cc, :], op0=ALU.add, op1=ALU.mult),
                          reads=[br, R_zc, R_const], writes=[R_c])
                formF(wv, wr, 4, 128, ev, kcn=8, src=aT, R_src=R_a)
            jobs.append(((f"pw{b2}", [(0, w_pw_v[:, :, b2 * 512:(b2 + 1) * 512])], 8, 512), pw_job))
        bcb = {}
        for b4 in range(4):
            def uc_job(wv, wr, b4=b4):
                for c in range(4):
                    bi, bk, br = pin()
                    kb.mm([(lambda kc=kc: nc.tensor.matmul(bk[:, 0:NT], wv[:, kc, c * 128:(c + 1) * 128], cT[:, kc, :],
                                                           start=(kc == 0), stop=(kc == 7))) for kc in range(8)],
                          reads=[wr, R_c], writes=[br])
                    bcb[(b4, c)] = (bi, bk, br)
            jobs.append(((f"uc{b4}", [(0, w_uc_v[:, :, b4 * 512:(b4 + 1) * 512])], 8, 512), uc_job))

            def gc_job(wv, wr, b4=b4):
                def ev(c, ps, br):
                    i = sgc["n"] % 2
                    sgc["n"] += 1
                    bi2, bk2, br2 = bcb.pop((b4, c))
                    kb.op("act", lambda: nc.scalar.activation(out=sgt2[i][:], in_=ps, func=AF.Sigmoid), reads=[br], writes=[R_sg2[i]])
                    kb.op("dve", lambda: nc.vector.tensor_tensor(out=tt2[i][:], in0=bk2[:, 0:NT], in1=sgt2[i][:], op=ALU.mult),
                          reads=[br2, R_sg2[i]], writes=[R_tt2[i]])
                    pinned.discard(bi2)
                    kb.op("dve", lambda: nc.vector.tensor_tensor(out=m_T[:, b4 * 4 + c, :], in0=m_T[:, b4 * 4 + c, :],
                                                                 in1=tt2[i][:], op=ALU.add), reads=[R_tt2[i]], writes=[R_m])
                formF(wv, wr, 4, 128, ev)
            jobs.append(((f"gc{b4}", [(0, w_in_v[:, :, C_GC + b4 * 512: C_GC + (b4 + 1) * 512])], 16, 512), gc_job))
        run_jobs(jobs, ("wo0", [], 16, 512))
        kb.barrier()

        ar.reset()
        yb = [ar.get([128, D], F32) for _ in range(ntile)]
        gbc = ar.get([128, D], F32)
        jk = ar.get([128, D], F32)
        st2 = ar.get([128, 16], F32)
        R_yb = [Res() for _ in range(ntile)]
        R_gbc, R_jk = Res(), Res()
        R_st2 = [Res() for _ in range(ntile)]
        s_y = [kb.slot(f"y_{i}", scratch=True) for i in range(ntile)]
        s_g = kb.slot(f"g", scratch=True)
        for t in range(ntile):
            kb.dma("sp", yb[t][:], x[tok0 + t * 128: tok0 + (t + 1) * 128, :], s_y[t], writes=[R_yb[t]])
        kb.dma("sp", gbc[:], final_g.broadcast_to([128, D]), s_g, writes=[R_gbc])

        def final_norm(t):
            c0 = 4 * t
            kb.op("act", lambda: nc.scalar.activation(out=jk[:], in_=yb[t][:], func=AF.Square, accum_out=st2[:, c0:c0 + 1]),
                  reads=[R_yb[t]], writes=[R_jk, R_st2[t]])
            kb.op("act", lambda: nc.scalar.activation(out=st2[:, c0 + 1:c0 + 2], in_=st2[:, c0:c0 + 1], func=AF.Sqrt,
                                                      bias=epsT[:, 0:1], scale=1.0 / D), reads=[R_st2[t], R_const],
                  writes=[R_st2[t]])
            kb.op("dve", lambda: nc.vector.reciprocal(out=st2[:, c0 + 2:c0 + 3], in_=st2[:, c0 + 1:c0 + 2]), reads=[R_st2[t]],
                  writes=[R_st2[t]])
            kb.op("dve", lambda: nc.vector.scalar_tensor_tensor(out=yb[t][:], in0=yb[t][:], scalar=st2[:, c0 + 2:c0 + 3],
                                                                in1=gbc[:], op0=ALU.mult, op1=ALU.mult),
                  reads=[R_st2[t], R_gbc], writes=[R_yb[t]])
            kb.dma("sp", y[tok0 + t * 128: tok0 + (t + 1) * 128, :], yb[t][:], s_y[t], reads=[R_yb[t]], is_out=True)

        jobs = []
        for b4 in range(4):
            def wo_job(wv, wr, b4=b4):
                for t in range(ntile):
                    bi, bk, br = bank()
                    kb.mm([(lambda kc=kc: nc.tensor.matmul(bk[:, 0:512], m_T[:, kc, t * 128:(t + 1) * 128], wv[:, kc, :],
                                                           start=(kc == 0), stop=(kc == 15))) for kc in range(16)],
                          reads=[wr, R_m], writes=[br])
                    ysl = yb[t][:, b4 * 512:(b4 + 1) * 512]
                    kb.op("dve", lambda: nc.vector.tensor_tensor(out=ysl, in0=bk[:, 0:512], in1=ysl, op=ALU.add),
                          reads=[br], writes=[R_yb[t]])
                    if b4 == 3:
                        final_norm(t)
            jobs.append(((f"wo{b4}", [(0, w_out_v[:, :, b4 * 512:(b4 + 1) * 512])], 16, 512), wo_job))
        if gi < 4:
            pbn = p1_alloc(gi + 1)
            run_jobs(jobs, ("kv", [], 16, 512), hook=(1, lambda: p1_norm(gi + 1, pbn)))
            p1_tr(gi + 1, pbn)
        else:
            run_jobs(jobs, None)
        kb.barrier()

    for gi in range(5):
        group(gi)
    last = {}
    for sem, val in kb.out_toks:
        last[id(sem)] = (sem, max(val, last.get(id(sem), (None, 0))[1]))
    for tok in last.values():
        kb.eng["sp"].wait(tok)
    return nc


_NC = None


def kernel(x_prompt, x_sample, cache_k, cache_v, cache_kidx, state_conv, page_table, ln_g, w_in, conv_w, conv_b,
           conv_norm_g, conv_norm_b, w_pw, b_pw, w_up_attn, w_up_conv, w_out, rel_bias, final_g):
    global _NC
    if _NC is None:
        _NC = build()
    nc = _NC
    f = lambda a: np.ascontiguousarray(np.asarray(a, dtype=np.float32))
    consts = host_consts()
    ck = f(cache_k)[0].reshape(NPHYS * 16, 2048)
    cv = f(cache_v)[0].reshape(NPHYS * 16, 2048)
    cki = f(cache_kidx)[0].reshape(NPHYS * 16, 512)
    shared = {
        "cache_k": ck, "cache_v": cv, "cache_ki": cki, "ln_g": f(ln_g)[0], "w_in": f(w_in)[0], "conv_w": f(conv_w)[0],
        "conv_b": f(conv_b)[0], "cn_g": f(conv_norm_g)[0], "cn_b": f(conv_norm_b)[0], "w_pw": f(w_pw)[0],
        "b_pw": f(b_pw)[0], "w_ua": f(w_up_attn)[0], "w_uc": f(w_up_conv)[0], "w_out": f(w_out)[0],
        "rel_bias": f(rel_bias), "final_g": f(final_g).reshape(1, D),
    }
    for k, v in consts.items():
        shared["c_" + k] = v
    xp = f(x_prompt)
    xs = f(x_sample)
    st = f(state_conv)[0]
    pt = np.ascontiguousarray(np.asarray(page_table, dtype=np.int32))
    in_maps = []
    for c in range(8):
        m = dict(shared)
        m["x"] = np.ascontiguousarray(np.concatenate([xp[c], xs[16 * c:16 * c + 16].reshape(128, D)], axis=0))
        m["state"] = np.ascontiguousarray(st[16 * c:16 * c + 16])
        m["ptab"] = np.ascontiguousarray(pt[16 * c:16 * c + 16].reshape(1, 256))
        in_maps.append(m)
    res = run_bass_kernel_spmd(nc, in_maps, core_ids=list(range(8)))
    R = res.results
    y_p = np.stack([R[c]["y"][0:2048] for c in range(8)])
    y_s = np.concatenate([R[c]["y"][2048:2176].reshape(16, 8, D) for c in range(8)], axis=0)
    k_p = np.stack([R[c]["kk"][0:2048].reshape(2048, 2, 128) for c in range(8)])[None]
    v_p = np.stack([R[c]["vv"][0:2048].reshape(2048, 2, 128) for c in range(8)])[None]
    ki_p = np.stack([R[c]["kio"][0:2048] for c in range(8)])[None]
    cp = np.stack([R[c]["convp"] for c in range(8)])[None]
    k_s = np.concatenate([R[c]["kk"][2048:2176].reshape(16, 8, 2, 128) for c in range(8)], axis=0)[None]
    v_s = np.concatenate([R[c]["vv"][2048:2176].reshape(16, 8, 2, 128) for c in range(8)], axis=0)[None]
    ki_s = np.concatenate([R[c]["kio"][2048:2176].reshape(16, 8, 64) for c in range(8)], axis=0)[None]
    cs = np.concatenate([R[c]["convs"] for c in range(8)], axis=0)[None]
    return (y_p.astype(np.float32), y_s.astype(np.float32), k_p.astype(np.float32), v_p.astype(np.float32),
            ki_p.astype(np.float32), cp.astype(np.float32), k_s.astype(np.float32), v_s.astype(np.float32),
            ki_s.astype(np.float32), cs.astype(np.float32))
```
